# Optimizing a Trainium2 kernel written in Bass

```python
import jax, jax.numpy as jnp
from jax import lax
import numpy as np

D_MODEL = 2048
BATCH = 16
SEQ = 2048
DEPTH = 4

GRID_W = 64
CTX_LEN = 256
N_MIXERS = 3
EPS = 1e-6
F32 = jnp.float32

S5_GROUP = 16
S5_GROUPS = D_MODEL // S5_GROUP
S5_STATE = 64

GDN_HEAD_DIM = 128
GDN_QK_HEADS = D_MODEL // GDN_HEAD_DIM
GDN_V_HEADS = 2 * GDN_QK_HEADS
GDN_QK_WIDTH = GDN_QK_HEADS * GDN_HEAD_DIM
GDN_V_WIDTH = GDN_V_HEADS * GDN_HEAD_DIM
GDN_CONV = 5
GDN_CHUNK = 64
GDN_IN_WIDTH = 2 * GDN_QK_WIDTH + 2 * GDN_V_WIDTH + 4 * GDN_V_HEADS

RET_QK_DIM = 256
RET_HEADS = D_MODEL // RET_QK_DIM
RET_V_DIM = 2 * RET_QK_DIM
RET_QK_WIDTH = RET_HEADS * RET_QK_DIM
RET_V_WIDTH = RET_HEADS * RET_V_DIM
RET_CHUNK = 64
RET_IN_WIDTH = 2 * RET_QK_WIDTH + 2 * RET_V_WIDTH
ROPE_BASE = 10000.0

D_FF = 4 * D_MODEL

kernel_name = 'hybrid_s5_gdn_retention_dit'


def rmsnorm(x, w):
    xf = x.astype(F32)
    y = xf * lax.rsqrt(jnp.mean(xf * xf, axis=-1, keepdims=True) + EPS) * w.astype(F32)
    return y.astype(x.dtype)


def squared_relu_mlp(h, w1, w2):
    return jnp.square(jax.nn.relu(h @ w1)) @ w2


def to_heads(t, n_heads):
    b, l, _ = t.shape
    return t.reshape(b, l, n_heads, -1).transpose(0, 2, 1, 3).astype(F32)


def from_heads(t):
    return t.transpose(0, 2, 1, 3)


def seq_flip(t, rev):
    return jnp.flip(t, axis=2) if rev else t


def centred_dwconv(x, w):
    k, ch = w.shape
    return lax.conv_general_dilated(x, w[:, None, :], window_strides=(1,), padding=[((k - 1) // 2, k // 2)],
                                    dimension_numbers=('NWC', 'WIO', 'NWC'), feature_group_count=ch)


def l2norm(t):
    return t * lax.rsqrt(jnp.sum(t * t, axis=-1, keepdims=True) + EPS)


def s5_discretise(lam_re, lam_im, log_dt):
    lam = lax.complex(jnp.minimum(lam_re.astype(F32), -1e-4), lam_im.astype(F32))
    dt = jnp.exp(log_dt.astype(F32))[:, None]
    a_bar = jnp.exp(lam * dt)
    return a_bar, (a_bar - 1.0) / lam


def diag_combine(left, right):
    a1, b1 = left
    a2, b2 = right
    return a1 * a2, a2 * b1 + b2


def diag_scan(a_bar, bu, h0, reverse):
    L = bu.shape[1]
    if h0 is not None:
        start = L - 1 if reverse else 0
        bu = bu.at[:, start].add(a_bar * h0)
    a_seq = jnp.broadcast_to(a_bar, (1, L) + a_bar.shape)
    _, h = lax.associative_scan(diag_combine, (a_seq, bu), reverse=reverse, axis=1)
    return h


def s5_core(u, lam_re, lam_im, log_dt, b_re, b_im, c_re, c_im, d_skip, h0s):
    bsz, L, d = u.shape
    uf = u.astype(F32)
    ug = uf.reshape(bsz, L, S5_GROUPS, S5_GROUP)
    y = uf * d_skip.astype(F32)
    finals = []
    for di in range(2):
        rev = di == 1
        a_bar, b_scale = s5_discretise(lam_re[di], lam_im[di], log_dt[di])
        bu = lax.complex(jnp.einsum('blgc,gnc->blgn', ug, b_re[di].astype(F32)),
                         jnp.einsum('blgc,gnc->blgn', ug, b_im[di].astype(F32))) * b_scale
        h = diag_scan(a_bar, bu, None if h0s is None else h0s[di], rev)
        y = y + (jnp.einsum('blgn,gcn->blgc', h.real, c_re[di].astype(F32))
                 - jnp.einsum('blgn,gcn->blgc', h.imag, c_im[di].astype(F32))).reshape(bsz, L, d)
        finals.append(h[:, 0] if rev else h[:, -1])
    return y, finals


def s5_mixer(hc, hx, lam_re, lam_im, log_dt, b_re, b_im, c_re, c_im, d_skip, glu_w, ctx_out):
    params = (lam_re, lam_im, log_dt, b_re, b_im, c_re, c_im, d_skip)
    yc, ctx_states = s5_core(hc, *params, None)
    yx, _ = s5_core(hx, *params, ctx_states)

    def glu(y, dtype):
        z = jax.nn.gelu(y)
        val, gate = jnp.split(z @ glu_w.astype(F32), 2, axis=-1)
        return (val * jax.nn.sigmoid(gate)).astype(dtype)

    return (glu(yc, hc.dtype) if ctx_out else None), glu(yx, hx.dtype)


def gated_delta_chunk(q, k, v, g, beta, s0):
    bsz, nh, L, dk = q.shape
    dv = v.shape[-1]
    C = GDN_CHUNK
    n = L // C
    q = q.reshape(bsz, nh, n, C, dk)
    k = k.reshape(bsz, nh, n, C, dk)
    v = v.reshape(bsz, nh, n, C, dv)
    g = g.reshape(bsz, nh, n, C)
    beta = beta.reshape(bsz, nh, n, C)
    gc = jnp.cumsum(g, axis=-1)
    lower_incl = jnp.tril(jnp.ones((C, C), dtype=bool))
    lower_strict = jnp.tril(jnp.ones((C, C), dtype=bool), -1)
    decay = jnp.exp(jnp.where(lower_incl, gc[..., :, None] - gc[..., None, :], -jnp.inf))
    kb = k * beta[..., None]
    m = jnp.where(lower_strict, jnp.einsum('bhnid,bhnjd->bhnij', kb, k) * decay, 0.0)
    rhs = jnp.concatenate([v * beta[..., None], kb * jnp.exp(gc)[..., None]], axis=-1)
    uw = lax.linalg.triangular_solve(m, rhs, left_side=True, lower=True, unit_diagonal=True)
    u, w = uw[..., :dv], uw[..., dv:]
    attn = jnp.einsum('bhnid,bhnjd->bhnij', q, k) * decay
    q_dec = q * jnp.exp(gc)[..., None]
    k_dec = k * jnp.exp(gc[..., -1:] - gc)[..., None]
    g_last = jnp.exp(gc[..., -1])[..., None, None]
    if s0 is None:
        s0 = jnp.zeros((bsz, nh, dk, dv), F32)

    def step(S, xs):
        u_n, w_n, a_n, qd_n, kd_n, gl_n = xs
        v_new = u_n - jnp.einsum('bhcd,bhde->bhce', w_n, S)
        o = jnp.einsum('bhcd,bhde->bhce', qd_n, S) + jnp.einsum('bhij,bhje->bhie', a_n, v_new)
        S = gl_n * S + jnp.einsum('bhcd,bhce->bhde', kd_n, v_new)
        return S, o

    xs = tuple(jnp.moveaxis(t, 2, 0) for t in (u, w, attn, q_dec, k_dec, g_last))
    S, o = lax.scan(step, s0, xs)
    return jnp.moveaxis(o, 0, 2).reshape(bsz, nh, L, dv), S


def gdn_prepare(h, w_in, conv_w, a_log, dt_bias):
    bsz, L, _ = h.shape
    proj = h @ w_in
    n_qkv = 2 * GDN_QK_WIDTH + GDN_V_WIDTH
    qkv = jax.nn.silu(centred_dwconv(proj[..., :n_qkv], conv_w))
    z = proj[..., n_qkv:n_qkv + GDN_V_WIDTH].astype(F32).reshape(bsz, L, GDN_V_HEADS, GDN_HEAD_DIM)
    ab = proj[..., n_qkv + GDN_V_WIDTH:].astype(F32).reshape(bsz, L, 2, 2, GDN_V_HEADS)
    ab = jnp.transpose(ab, (2, 3, 0, 4, 1))
    rep = GDN_V_HEADS // GDN_QK_HEADS
    q = jnp.repeat(l2norm(to_heads(qkv[..., :GDN_QK_WIDTH], GDN_QK_HEADS)), rep, axis=1) * GDN_HEAD_DIM ** -0.5
    k = jnp.repeat(l2norm(to_heads(qkv[..., GDN_QK_WIDTH:2 * GDN_QK_WIDTH], GDN_QK_HEADS)), rep, axis=1)
    v = to_heads(qkv[..., 2 * GDN_QK_WIDTH:], GDN_V_HEADS)
    g = -jnp.exp(a_log.astype(F32))[:, None, :, None] * jax.nn.softplus(ab[:, 0] + dt_bias.astype(F32)[:, None, :, None])
    beta = jax.nn.sigmoid(ab[:, 1])
    return q, k, v, g, beta, z


def gdn_bidir(q, k, v, g, beta, init):
    out = 0.0
    finals = []
    for di in range(2):
        rev = di == 1
        o, s = gated_delta_chunk(seq_flip(q, rev), seq_flip(k, rev), seq_flip(v, rev), seq_flip(g[di], rev),
                                 seq_flip(beta[di], rev), None if init is None else init[di])
        out = out + seq_flip(o, rev)
        finals.append(s)
    return out, finals


def gdn_mixer(hc, hx, w_in, conv_w, a_log, dt_bias, norm_w, w_out, ctx_out):
    qc, kc, vc, gc, bc, zc = gdn_prepare(hc, w_in, conv_w, a_log, dt_bias)
    oc, ctx_states = gdn_bidir(qc, kc, vc, gc, bc, None)
    qx, kx, vx, gx, bx, zx = gdn_prepare(hx, w_in, conv_w, a_log, dt_bias)
    ox, _ = gdn_bidir(qx, kx, vx, gx, bx, ctx_states)

    def readout(o, z, dtype):
        o = from_heads(o)
        o = o * lax.rsqrt(jnp.mean(o * o, axis=-1, keepdims=True) + EPS) * norm_w.astype(F32) * jax.nn.silu(z)
        b, l = o.shape[:2]
        return (o.reshape(b, l, GDN_V_WIDTH) @ w_out.astype(F32)).astype(dtype)

    return (readout(oc, zc, hc.dtype) if ctx_out else None), readout(ox, zx, hx.dtype)


def grid_rope(t, rows):
    d_axis = t.shape[-1] // 2
    half = d_axis // 2
    inv_freq = ROPE_BASE ** (-jnp.arange(half, dtype=F32) / half)
    row = jnp.repeat(jnp.arange(rows, dtype=F32), GRID_W)
    col = jnp.tile(jnp.arange(GRID_W, dtype=F32), rows)

    def rotate(ta, pos):
        ang = pos[:, None] * inv_freq[None, :]
        cos, sin = jnp.cos(ang), jnp.sin(ang)
        t1, t2 = ta[..., :half], ta[..., half:]
        return jnp.concatenate([t1 * cos - t2 * sin, t1 * sin + t2 * cos], axis=-1)

    return jnp.concatenate([rotate(t[..., :d_axis], row), rotate(t[..., d_axis:], col)], axis=-1)


def retention_chunk(q, k, v, log_gamma, s0):
    bsz, nh, L, dk = q.shape
    dv = v.shape[-1]
    C = RET_CHUNK
    n = L // C
    q = q.reshape(bsz, nh, n, C, dk)
    k = k.reshape(bsz, nh, n, C, dk)
    v = v.reshape(bsz, nh, n, C, dv)
    lg = log_gamma.astype(F32)[:, None]
    pos = jnp.arange(C, dtype=F32)
    rel = pos[:, None] - pos[None, :]
    decay = jnp.where(rel >= 0, jnp.exp(lg[:, :, None] * jnp.maximum(rel, 0.0)), 0.0)
    o_intra = jnp.einsum('bhnij,bhnje->bhnie', jnp.einsum('bhnid,bhnjd->bhnij', q, k) * decay[None, :, None], v)
    q_dec = q * jnp.exp(lg * (pos + 1.0))[None, :, None, :, None]
    k_dec = k * jnp.exp(lg * (C - 1.0 - pos))[None, :, None, :, None]
    chunk_decay = jnp.exp(lg * C)[None, :, :, None]
    if s0 is None:
        s0 = jnp.zeros((bsz, nh, dk, dv), F32)

    def step(S, xs):
        qd_n, kd_n, v_n, oi_n = xs
        o = oi_n + jnp.einsum('bhcd,bhde->bhce', qd_n, S)
        S = chunk_decay * S + jnp.einsum('bhcd,bhce->bhde', kd_n, v_n)
        return S, o

    xs = tuple(jnp.moveaxis(t, 2, 0) for t in (q_dec, k_dec, v, o_intra))
    S, o = lax.scan(step, s0, xs)
    return jnp.moveaxis(o, 0, 2).reshape(bsz, nh, L, dv), S


def ret_prepare(h, w_in, rows):
    proj = h @ w_in
    q = to_heads(proj[..., :RET_QK_WIDTH], RET_HEADS)
    k = to_heads(proj[..., RET_QK_WIDTH:2 * RET_QK_WIDTH], RET_HEADS) * RET_QK_DIM ** -0.5
    v = to_heads(proj[..., 2 * RET_QK_WIDTH:2 * RET_QK_WIDTH + RET_V_WIDTH], RET_HEADS)
    gate = proj[..., 2 * RET_QK_WIDTH + RET_V_WIDTH:].astype(F32)
    if rows is not None:
        q, k = grid_rope(q, rows), grid_rope(k, rows)
    return q, k, v, gate


def ret_bidir(q, k, v, log_decay, init):
    out = 0.0
    finals = []
    for di in range(2):
        rev = di == 1
        o, s = retention_chunk(seq_flip(q, rev), seq_flip(k, rev), seq_flip(v, rev), log_decay[di],
                               None if init is None else init[di])
        out = out + seq_flip(o, rev)
        finals.append(s)
    return out, finals


def retention_mixer(hc, hx, rows, w_in, log_decay, gn_w, w_out, ctx_out):
    qc, kc, vc, gatec = ret_prepare(hc, w_in, None)
    oc, ctx_states = ret_bidir(qc, kc, vc, log_decay, None)
    qx, kx, vx, gatex = ret_prepare(hx, w_in, rows)
    ox, _ = ret_bidir(qx, kx, vx, log_decay, ctx_states)

    def readout(o, gate, dtype):
        o = from_heads(o)
        mu = jnp.mean(o, axis=-1, keepdims=True)
        o = (o - mu) * lax.rsqrt(jnp.mean(jnp.square(o - mu), axis=-1, keepdims=True) + EPS) * gn_w.astype(F32)
        b, l = o.shape[:2]
        return ((jax.nn.silu(gate) * o.reshape(b, l, RET_V_WIDTH)) @ w_out.astype(F32)).astype(dtype)

    return (readout(oc, gatec, hc.dtype) if ctx_out else None), readout(ox, gatex, hx.dtype)


def setup_inputs(seed: int = 0) -> dict:
    key = jax.random.key(seed)
    keys = iter(jax.random.split(key, 40))

    def normal(shape, std):
        return jax.random.normal(next(keys), shape, F32) * std

    def uniform(shape, lo, hi):
        return jax.random.uniform(next(keys), shape, F32, lo, hi)

    d = D_MODEL
    n_a = len(range(0, DEPTH, N_MIXERS))
    n_b = len(range(1, DEPTH, N_MIXERS))
    n_c = len(range(2, DEPTH, N_MIXERS))
    log_dt_lo, log_dt_hi = float(np.log(1e-3)), float(np.log(1e-1))
    gdn_dt = jnp.exp(uniform((n_b, 2, GDN_V_HEADS), log_dt_lo, log_dt_hi))
    ret_exponent = 5.0 + jnp.arange(RET_HEADS, dtype=F32) + normal((n_c, 2, RET_HEADS), 0.05)
    return {
        'x': normal((BATCH, SEQ, d), 1.0),
        'c': normal((BATCH, d), 1.0),
        'ctx': normal((BATCH, CTX_LEN, d), 1.0),
        'c_ctx': normal((d,), 1.0),
        'mod_w': normal((DEPTH, d, 6 * d), 0.5 * d ** -0.5),
        'mod_b': normal((DEPTH, 6 * d), 0.01),
        'norm_w': 1.0 + normal((DEPTH, 2, d), 0.02),
        'final_norm_w': 1.0 + normal((d,), 0.02),
        'ffn_w1': normal((DEPTH, d, D_FF), d ** -0.5),
        'ffn_w2': normal((DEPTH, D_FF, d), D_FF ** -0.5),
        's5_lam_re': -0.5 + normal((n_a, 2, S5_GROUPS, S5_STATE), 0.01),
        's5_lam_im': jnp.broadcast_to(jnp.pi * jnp.arange(S5_STATE, dtype=F32), (n_a, 2, S5_GROUPS, S5_STATE)),
        's5_log_dt': uniform((n_a, 2, S5_GROUPS), log_dt_lo, log_dt_hi),
        's5_b_re': normal((n_a, 2, S5_GROUPS, S5_STATE, S5_GROUP), (2 * S5_GROUP) ** -0.5),
        's5_b_im': normal((n_a, 2, S5_GROUPS, S5_STATE, S5_GROUP), (2 * S5_GROUP) ** -0.5),
        's5_c_re': normal((n_a, 2, S5_GROUPS, S5_GROUP, S5_STATE), 0.5 ** 0.5),
        's5_c_im': normal((n_a, 2, S5_GROUPS, S5_GROUP, S5_STATE), 0.5 ** 0.5),
        's5_d': normal((n_a, d), 1.0),
        's5_glu_w': normal((n_a, d, 2 * d), d ** -0.5),
        'gdn_w_in': normal((n_b, d, GDN_IN_WIDTH), d ** -0.5),
        'gdn_conv_w': normal((n_b, GDN_CONV, 2 * GDN_QK_WIDTH + GDN_V_WIDTH), GDN_CONV ** -0.5),
        'gdn_a_log': jnp.log(uniform((n_b, 2, GDN_V_HEADS), 1.0, 16.0)),
        'gdn_dt_bias': gdn_dt + jnp.log(-jnp.expm1(-gdn_dt)),
        'gdn_norm_w': 1.0 + normal((n_b, GDN_HEAD_DIM), 0.02),
        'gdn_w_out': normal((n_b, GDN_V_WIDTH, d), GDN_V_WIDTH ** -0.5),
        'ret_w_in': normal((n_c, d, RET_IN_WIDTH), d ** -0.5),
        'ret_log_decay': jnp.log1p(-jnp.exp2(-ret_exponent)),
        'ret_gn_w': 1.0 + normal((n_c, RET_V_DIM), 0.02),
        'ret_w_out': normal((n_c, RET_V_WIDTH, d), RET_V_WIDTH ** -0.5),
    }


def reference(x, c, ctx, c_ctx, mod_w, mod_b, norm_w, final_norm_w, ffn_w1, ffn_w2,
              s5_lam_re, s5_lam_im, s5_log_dt, s5_b_re, s5_b_im, s5_c_re, s5_c_im, s5_d, s5_glu_w,
              gdn_w_in, gdn_conv_w, gdn_a_log, gdn_dt_bias, gdn_norm_w, gdn_w_out,
              ret_w_in, ret_log_decay, ret_gn_w, ret_w_out):
    rows = x.shape[1] // GRID_W
    silu_c = jax.nn.silu(c)
    silu_cc = jax.nn.silu(c_ctx)
    for i in range(DEPTH):
        kind, j = i % N_MIXERS, i // N_MIXERS
        last = i == DEPTH - 1
        sh1, sc1, g1, sh2, sc2, g2 = jnp.split((silu_c @ mod_w[i] + mod_b[i])[:, None, :], 6, axis=-1)
        csh1, csc1, cg1, csh2, csc2, cg2 = jnp.split(silu_cc @ mod_w[i] + mod_b[i], 6, axis=-1)
        hx = rmsnorm(x, norm_w[i, 0]) * (1.0 + sc1) + sh1
        hc = rmsnorm(ctx, norm_w[i, 0]) * (1.0 + csc1) + csh1
        if kind == 0:
            yc, yx = s5_mixer(hc, hx, s5_lam_re[j], s5_lam_im[j], s5_log_dt[j], s5_b_re[j], s5_b_im[j],
                              s5_c_re[j], s5_c_im[j], s5_d[j], s5_glu_w[j], not last)
        elif kind == 1:
            yc, yx = gdn_mixer(hc, hx, gdn_w_in[j], gdn_conv_w[j], gdn_a_log[j], gdn_dt_bias[j],
                               gdn_norm_w[j], gdn_w_out[j], not last)
        else:
            yc, yx = retention_mixer(hc, hx, rows, ret_w_in[j], ret_log_decay[j], ret_gn_w[j], ret_w_out[j], not last)
        x = x + g1 * yx
        x = x + g2 * squared_relu_mlp(rmsnorm(x, norm_w[i, 1]) * (1.0 + sc2) + sh2, ffn_w1[i], ffn_w2[i])
        if not last:
            ctx = ctx + cg1 * yc
            ctx = ctx + cg2 * squared_relu_mlp(rmsnorm(ctx, norm_w[i, 1]) * (1.0 + csc2) + csh2, ffn_w1[i], ffn_w2[i])
    return rmsnorm(x, final_norm_w)
```

```python
import contextlib
import math
import numpy as np
import ml_dtypes
import concourse.bass as bass
import concourse.mybir as mybir
from concourse.bass_utils import run_bass_kernel_spmd

F32 = mybir.dt.float32
BF16 = mybir.dt.bfloat16
ALU = mybir.AluOpType
AF = mybir.ActivationFunctionType
AX = mybir.AxisListType

D = 2048
SEQ = 2048
CTXL = 256
TOK = SEQ + CTXL
NT = TOK // 128
DEPTH = 4
DFF = 8192
EPS = 1e-6
NCORES = 8
BPC = 2


class Tok:
    __slots__ = ("name", "lw", "rs")

    def __init__(self, name=""):
        self.name = name
        self.lw = None
        self.rs = {}


class TB:
    def __init__(self, t, name=""):
        self.t = t
        self.tok = Tok(name)

    def __getitem__(self, idx):
        return self.t[idx]


def _toks(lst):
    out = []
    for x in lst:
        if x is None:
            continue
        out.append(x.tok if isinstance(x, TB) else x)
    return out


class Prog:
    STREAMS = ["pe", "act", "dve", "pool", "sp"]
    NDMA = 8

    def __init__(self, nc, stack):
        self.nc = nc
        self.stack = stack
        self.cur = stack
        self.streams = {q: [] for q in self.STREAMS}
        self.count = {}
        self.known = {q: {} for q in self.STREAMS}
        self.dma_rr = {q: 0 for q in self.STREAMS}
        self.semh = {}
        self.nops = 0
        self.nblocks = 0
        for q in self.STREAMS:
            self._sem((q, -1, 0))
        for q in ("sp", "pool"):
            for k in range(self.NDMA):
                self._sem((q, k, 0))

    def _sem(self, key):
        if key not in self.semh:
            nm = "s_%s_%d_%d" % (key[0], key[1] + 1, key[2])
            self.semh[key] = self.stack.enter_context(self.nc.semaphore(nm))
            self.count.setdefault(key, 0)
        return self.semh[key]

    def sbuf(self, name, shape, dtype):
        self.uid = getattr(self, "uid", 0) + 1
        name = "%s_u%d" % (name, self.uid)
        t = self.cur.enter_context(self.nc.sbuf_tensor(name, list(shape), dtype))
        return TB(t, name)

    def psum(self, name, shape, dtype=F32):
        self.uid = getattr(self, "uid", 0) + 1
        name = "%s_u%d" % (name, self.uid)
        t = self.cur.enter_context(self.nc.psum_tensor(name, list(shape), dtype))
        return TB(t, name)

    def dram(self, name, shape, dtype, kind="Internal"):
        t = self.nc.dram_tensor(name, list(shape), dtype, kind=kind)
        return TB(t.ap(), name)

    @contextlib.contextmanager
    def phase(self):
        prev = self.cur
        with contextlib.ExitStack() as st:
            self.cur = st
            yield
            self.barrier()
            self.flush()
        self.cur = prev

    LIMIT = 20000

    def op(self, stream, fn, reads=(), writes=(), dma=False):
        gens = self.__dict__.setdefault("gens", {})
        if dma:
            k = self.dma_rr[stream]
            self.dma_rr[stream] = (k + 1) % self.NDMA
            base = (stream, k)
            step = 16
        else:
            base = (stream, -1)
            step = 1
        g = gens.get(base, 0)
        key = base + (g,)
        if self.count.get(key, 0) + step > self.LIMIT:
            g += 1
            gens[base] = g
            key = base + (g,)
        self._sem(key)
        n = self.count.get(key, 0) + step
        self.count[key] = n
        deps = {}
        reads = _toks(reads)
        writes = _toks(writes)
        for b in reads:
            if b.lw is not None and deps.get(b.lw[0], 0) < b.lw[1]:
                deps[b.lw[0]] = b.lw[1]
        for b in writes:
            if b.lw is not None and deps.get(b.lw[0], 0) < b.lw[1]:
                deps[b.lw[0]] = b.lw[1]
            for kk, v in b.rs.items():
                if deps.get(kk, 0) < v:
                    deps[kk] = v
        known = self.known[stream]
        waits = []
        for kk, v in deps.items():
            if stream == "pe" and kk[0] == "pe" and kk[1] == -1:
                continue
            if known.get(kk, 0) >= v:
                continue
            known[kk] = v
            waits.append((kk, v))
        self.streams[stream].append((waits, fn, key, step))
        for b in reads:
            if b.rs.get(key, 0) < n:
                b.rs[key] = n
        for b in writes:
            b.lw = (key, n)
            b.rs = {}
        self.nops += 1

    def barrier(self):
        for s in self.STREAMS:
            known = self.known[s]
            waits = []
            for kk, v in self.count.items():
                if v > 0 and known.get(kk, 0) < v:
                    known[kk] = v
                    waits.append((kk, v))
            if waits:
                self.streams[s].append((waits, None, None, 0))

    def flush(self):
        nc = self.nc
        semh = self.semh
        st = self.streams
        self.streams = {q: [] for q in self.STREAMS}
        if not any(st.values()):
            return
        self.nblocks += 1

        def replay(eng, lst):
            for waits, fn, key, step in lst:
                for kk, v in waits:
                    eng.wait_ge(semh[kk], v)
                if fn is not None:
                    ins = fn(eng)
                    ins.then_inc(semh[key], step)

        with nc.Block() as block:
            @block.tensor
            def _(e):
                replay(e, st["pe"])

            @block.scalar
            def _(e):
                replay(e, st["act"])

            @block.vector
            def _(e):
                replay(e, st["dve"])

            @block.gpsimd
            def _(e):
                replay(e, st["pool"])

            @block.sync
            def _(e):
                replay(e, st["sp"])

    def dma(self, out, in_, reads, writes, q="sp"):
        self.op(q, lambda e: e.dma_start(out=out, in_=in_), reads, writes, dma=True)

    def matmul(self, out, lhsT, rhs, start, stop, reads, writes):
        self.op("pe", lambda e: e.matmul(out, lhsT, rhs, start=start, stop=stop), reads, writes)

    def transpose(self, out, in_, ident, reads, writes):
        self.op("pe", lambda e: e.transpose(out, in_, ident), reads, writes)

    def act(self, out, in_, func, reads, writes, bias=None, scale=None, accum_out=None):
        kw = {}
        if bias is not None:
            kw["bias"] = bias
        if scale is not None:
            kw["scale"] = scale
        if accum_out is not None:
            kw["accum_out"] = accum_out
        self.op("act", lambda e: e.activation(out, in_, func, **kw), reads, writes)

    def tt(self, out, in0, in1, op, reads, writes, q="dve"):
        self.op(q, lambda e: e.tensor_tensor(out, in0, in1, op), reads, writes)

    def ts(self, out, in0, s1, s2, op0, op1, reads, writes, q="dve"):
        if op1 is None:
            self.op(q, lambda e: e.tensor_scalar(out, in0, s1, None, op0), reads, writes)
        else:
            self.op(q, lambda e: e.tensor_scalar(out, in0, s1, s2, op0, op1), reads, writes)

    def stt(self, out, in0, scalar, in1, op0, op1, reads, writes, q="dve"):
        self.op(q, lambda e: e.scalar_tensor_tensor(out, in0, scalar, in1, op0, op1), reads, writes)

    def copy(self, out, in_, reads, writes, q="dve"):
        if q == "act":
            self.op(q, lambda e: e.activation(out, in_, AF.Copy), reads, writes)
        else:
            self.op(q, lambda e: e.tensor_copy(out, in_), reads, writes)

    def memset(self, ap, val, writes, q="dve"):
        self.op(q, lambda e: e.memset(ap, val), (), writes)

    def recip(self, out, in_, reads, writes):
        self.op("dve", lambda e: e.reciprocal(out, in_), reads, writes)


WEIGHT_NAMES = ["mod_w", "mod_b", "norm_w", "final_norm_w", "ffn_w1", "ffn_w2",
                "s5_lam_re", "s5_lam_im", "s5_log_dt", "s5_b_re", "s5_b_im", "s5_c_re", "s5_c_im", "s5_d",
                "s5_glu_w", "gdn_w_in", "gdn_conv_w", "gdn_a_log", "gdn_dt_bias", "gdn_norm_w", "gdn_w_out",
                "ret_w_in", "ret_log_decay", "ret_gn_w", "ret_w_out"]
WEIGHT_SHAPES = {
    "mod_w": (4, 2048, 12288), "mod_b": (4, 12288), "norm_w": (4, 2, 2048), "final_norm_w": (2048,),
    "ffn_w1": (4, 2048, 8192), "ffn_w2": (4, 8192, 2048),
    "s5_lam_re": (2, 2, 128, 64), "s5_lam_im": (2, 2, 128, 64), "s5_log_dt": (2, 2, 128),
    "s5_b_re": (2, 2, 128, 64, 16), "s5_b_im": (2, 2, 128, 64, 16),
    "s5_c_re": (2, 2, 128, 16, 64), "s5_c_im": (2, 2, 128, 16, 64), "s5_d": (2, 2048),
    "s5_glu_w": (2, 2048, 4096), "gdn_w_in": (1, 2048, 12416), "gdn_conv_w": (1, 5, 8192),
    "gdn_a_log": (1, 2, 32), "gdn_dt_bias": (1, 2, 32), "gdn_norm_w": (1, 128), "gdn_w_out": (1, 4096, 2048),
    "ret_w_in": (1, 2048, 12288), "ret_log_decay": (1, 2, 8), "ret_gn_w": (1, 512), "ret_w_out": (1, 4096, 2048),
}


def host_constants():
    c = {}
    c["ident"] = np.eye(128, dtype=np.float32)
    c["ones"] = np.ones((128, 128), dtype=np.float32)
    R = np.zeros((128, 128), np.float32)
    for m in range(64):
        R[m, m + 64] = -1.0
        R[m + 64, m] = 1.0
    c["rrotT"] = np.ascontiguousarray(R.T)
    half = 64
    inv_freq = (10000.0 ** (-np.arange(half, dtype=np.float32) / half)).astype(np.float32)
    tok = np.arange(SEQ)
    row = (tok // 64).astype(np.float32)
    col = (tok % 64).astype(np.float32)
    fr = np.concatenate([inv_freq, inv_freq])
    ar = (row[None, :] * fr[:, None]).astype(np.float32)
    ac = (col[None, :] * fr[:, None]).astype(np.float32)
    c["rope"] = np.stack([np.cos(ar), np.sin(ar), np.cos(ac), np.sin(ac)], axis=1).astype(np.float32)
    jj = np.arange(128)[:, None]
    ii = np.arange(128)[None, :]
    c["relF"] = np.maximum(ii - jj, 0).astype(np.float32)
    c["maskF"] = (ii >= jj).astype(np.float32)
    c["relB"] = np.maximum(jj - ii, 0).astype(np.float32)
    c["maskB"] = (jj >= ii).astype(np.float32)
    c["posrowF"] = np.broadcast_to((ii + 1.0), (128, 128)).astype(np.float32).copy()
    c["posrowB"] = np.broadcast_to((128.0 - ii), (128, 128)).astype(np.float32).copy()
    pc = np.zeros((128, 4), np.float32)
    pc[:, 0] = 127.0 - np.arange(128)
    pc[:, 1] = np.arange(128)
    c["poscol"] = pc
    r_ = np.arange(128)[:, None]
    c_ = np.arange(128)[None, :]
    c["LI"] = (r_ >= c_).astype(np.float32)
    c["LS"] = (r_ > c_).astype(np.float32)
    c["UI"] = (r_ <= c_).astype(np.float32)
    c["US"] = (r_ < c_).astype(np.float32)
    idx = np.arange(128)
    c["blk16"] = ((idx[:, None] // 16) == (idx[None, :] // 16)).astype(np.float32)
    for bs in (16, 32, 64):
        c["lvl%d" % bs] = (((idx[:, None] // (2 * bs)) == (idx[None, :] // (2 * bs)))
                           & ((idx[:, None] // bs) != (idx[None, :] // bs))).astype(np.float32)
    rr = np.arange(128)
    s_of = rr // 16
    c["s5maskF"] = (s_of[None, :] >= s_of[:, None]).astype(np.float32)
    c["s5maskB"] = (s_of[:, None] >= s_of[None, :]).astype(np.float32)
    c["iota288"] = np.broadcast_to(np.arange(288, dtype=np.float32), (128, 288)).copy()
    sel16 = np.zeros((16, 128), np.float32)
    for r in range(128):
        sel16[r % 16, r] = 1.0
    c["s5sel16"] = sel16
    selT = np.zeros((128, 64, 128), np.float32)
    for g8 in range(8):
        for t in range(8):
            for cc in range(16):
                selT[t * 16 + cc, g8 * 8 + t, g8 * 16 + cc] = 1.0
    c["s5selT"] = selT
    return c


class Model:
    def __init__(self, nc, stack, cfg):
        self.cfg = cfg
        P = self.P = Prog(nc, stack)
        self.x_in = P.dram("x", [BPC, SEQ, D], F32, kind="ExternalInput")
        self.ctx_in = P.dram("ctx", [BPC, CTXL, D], F32, kind="ExternalInput")
        self.cvec = P.dram("cvec", [3, D], F32, kind="ExternalInput")
        self.W = {}
        for nm in WEIGHT_NAMES:
            kind = "ExternalInput"
            if cfg.get("internal_weights") and nm not in cfg.get("keep_weights", []):
                self.W[nm] = P.dram(nm, [2, 2], F32, kind="Internal")
                continue
            self.W[nm] = P.dram(nm, list(WEIGHT_SHAPES[nm]), F32, kind=kind)
        self.C = {}
        for nm, arr in host_constants().items():
            self.C[nm] = P.dram("k_" + nm, list(arr.shape), F32, kind="ExternalInput")
        self.out = P.dram("out", [BPC, SEQ, D], F32, kind="ExternalOutput")
        self.ctxs = P.dram("ctxs", [BPC, CTXL, D], F32)
        self.rtok = [[Tok("r%d_%d" % (b, t)) for t in range(NT)] for b in range(BPC)]
        self.ident = P.sbuf("ident", [128, 128], F32)
        self.ones = P.sbuf("ones", [128, 128], F32)
        self.identb = P.sbuf("identb", [128, 128], BF16)
        self.modT = P.sbuf("modT", [128, DEPTH, 96, 3], F32)
        self.nwT = P.sbuf("nwT", [128, DEPTH * 2 + 1, 16], F32)
        self.A = P.sbuf("Amod", [128, DEPTH, 2, 16, 3], F32)
        P.dma(self.ident[:, :], self.C["ident"][:, :], [self.C["ident"]], [self.ident])
        P.dma(self.ones[:, :], self.C["ones"][:, :], [self.C["ones"]], [self.ones])
        P.copy(self.identb[:, :], self.ident[:, :], [self.ident], [self.identb])

    def res_ap(self, b, t):
        if t < 2:
            return self.ctxs[b, t * 128:(t + 1) * 128, :]
        return self.out[b, (t - 2) * 128:(t - 1) * 128, :]

    def copy_inputs(self):
        P = self.P
        for b in range(BPC):
            P.dma(self.ctxs[b, :, :], self.ctx_in[b, :, :], [self.ctx_in], [self.rtok[b][0], self.rtok[b][1]])
            for q in range(4):
                P.dma(self.out[b, q * 512:(q + 1) * 512, :], self.x_in[b, q * 512:(q + 1) * 512, :], [self.x_in],
                      [self.rtok[b][2 + q * 4 + k] for k in range(4)])

    def modulation(self):
        P = self.P
        with P.phase():
            cs = P.sbuf("cs", [3, D], F32)
            scT = P.sbuf("scT", [128, 16, 3], F32)
            pst = P.psum("pst", [128, 512], F32)
            P.dma(cs[:, :], self.cvec[:, :], [self.cvec], [cs])
            P.act(cs[:, :], cs[:, :], AF.Silu, [cs], [cs])
            for k in range(16):
                P.transpose(pst[:, k * 3:(k + 1) * 3], cs[0:3, k * 128:(k + 1) * 128], self.ident[0:3, 0:3],
                            [cs, self.ident], [pst])
            P.copy(scT[:, :, :], pst[:, 0:48].rearrange("p (k m) -> p k m", m=3), [pst], [scT])
            nwr = P.sbuf("nwr", [128, 128], F32)
            nwr2 = P.sbuf("nwr2", [16, 128], F32)
            P.dma(nwr[:, :], self.W["norm_w"].t.rearrange("l w (c p) -> (l w c) p", p=128),
                  [self.W["norm_w"]], [nwr])
            P.dma(nwr2[:, :], self.W["final_norm_w"].t.rearrange("(c p) -> c p", p=128),
                  [self.W["final_norm_w"]], [nwr2])
            pst2 = P.psum("pst2", [128, 512], F32)
            P.transpose(pst2[:, 0:128], nwr[:, :], self.ident[:, :], [nwr, self.ident], [pst2])
            P.transpose(pst2[:, 128:144], nwr2[:, :], self.ident[0:16, 0:16], [nwr2, self.ident], [pst2])
            P.copy(self.nwT[:, :, :], pst2[:, 0:144].rearrange("p (l c) -> p l c", c=16), [pst2], [self.nwT])
            wp = [P.sbuf("modw%d" % i, [128, 16, 512], F32) for i in range(2)]
            pm = [P.psum("pm%d" % i, [128, 512], F32) for i in range(2)]
            mbr = P.sbuf("mbr", [96, 128], F32)
            mbT = P.sbuf("mbT", [128, 96], F32)
            pb = P.psum("pb", [128, 512], F32)
            for i in self.cfg.get("layers", range(DEPTH)):
                P.dma(mbr[:, :], self.W["mod_b"].t[i].rearrange("(c p) -> c p", p=128), [self.W["mod_b"]], [mbr])
                P.transpose(pb[:, 0:96], mbr[0:96, :], self.ident[0:96, 0:96], [mbr, self.ident], [pb])
                P.copy(mbT[:, :], pb[:, 0:96], [pb], [mbT])
                wv = self.W["mod_w"].t[i].rearrange("(c p) n -> p c n", p=128)
                for pn in range(24):
                    w = wp[pn % 2]
                    P.dma(w[:, :, :], wv[:, :, pn * 512:(pn + 1) * 512], [self.W["mod_w"]], [w])
                    ps = pm[pn % 2]
                    for nb in range(4):
                        for k in range(16):
                            P.matmul(ps[:, nb * 3:(nb + 1) * 3], w[:, k, nb * 128:(nb + 1) * 128], scT[:, k, :],
                                     k == 0, k == 15, [w, scT], [ps])
                    for nb in range(4):
                        P.ts(self.modT[:, i, pn * 4 + nb, :], ps[:, nb * 3:(nb + 1) * 3], mbT[:, pn * 4 + nb:pn * 4 + nb + 1],
                             None, ALU.add, None, [ps, mbT], [self.modT])
                for wh in range(2):
                    sc = self.modT[:, i, (1 + 3 * wh) * 16:(2 + 3 * wh) * 16, :]
                    for m in range(3):
                        P.stt(self.A[:, i, wh, :, m], self.modT[:, i, (1 + 3 * wh) * 16:(2 + 3 * wh) * 16, m], 1.0,
                              self.nwT[:, i * 2 + wh, :], ALU.add, ALU.mult, [self.modT, self.nwT], [self.A])

    def gate_rows(self, i, wh, m, gtile, ps):
        P = self.P
        base = (2 + 3 * wh) * 16
        for c4 in range(4):
            for cc in range(4):
                c = c4 * 4 + cc
                dg = self._dg[c % 2]
                P.ts(dg[:, :], self.ident[:, :], self.modT[:, i, base + c, m:m + 1], None, ALU.mult, None,
                     [self.ident, self.modT], [dg])
                P.matmul(ps[:, cc * 128:(cc + 1) * 128], self.ones[:, :], dg[:, :], True, True, [self.ones, dg], [ps])
            P.copy(gtile[:, c4 * 512:(c4 + 1) * 512], ps[:, :], [ps], [gtile], q="act")

    def norm_tiles(self, i, wh, b, tiles, hT, hT_off, bufs, deint=False):
        P = self.P
        xt, xn, ss, pss = bufs
        for k, t in enumerate(tiles):
            m = 2 if t < 2 else b
            X = xt[k % 2]
            XN = xn[k % len(xn)]
            S = ss[k % 2]
            P.dma(X[:, :], self.res_ap(b, t), [self.rtok[b][t]], [X])
            P.memset(S[:, 0:1], 0.0, [S])
            P.act(XN[:, :], X[:, :], AF.Square, [X], [XN, S], accum_out=S[:, 0:1])
            P.ts(S[:, 1:2], S[:, 0:1], 1.0 / D, EPS, ALU.mult, ALU.add, [S], [S])
            P.act(S[:, 2:3], S[:, 1:2], AF.Sqrt, [S], [S])
            P.recip(S[:, 3:4], S[:, 2:3], [S], [S])
            P.act(XN[:, :], X[:, :], AF.Copy, [X, S], [XN], scale=S[:, 3:4])
            for c4 in range(4):
                ps = pss[(k * 4 + c4) % len(pss)]
                for cc in range(4):
                    c = c4 * 4 + cc
                    P.transpose(ps[:, cc * 128:(cc + 1) * 128], XN[:, c * 128:(c + 1) * 128], self.ident[:, :],
                                [XN, self.ident], [ps])
                for cc in range(4):
                    c = c4 * 4 + cc
                    if deint:
                        o = hT[:, c, :, k * 16:(k + 1) * 16]
                        pin = ps[:, cc * 128:(cc + 1) * 128].rearrange("p (j s) -> p s j", s=8)
                    else:
                        o = hT[:, c, hT_off + k * 128:hT_off + (k + 1) * 128]
                        pin = ps[:, cc * 128:(cc + 1) * 128]
                    P.act(o, pin, AF.Identity, [ps, self.A, self.modT], [hT],
                          scale=self.A[:, i, wh, c, m:m + 1], bias=self.modT[:, i, (3 * wh) * 16 + c, m:m + 1])

    def ffn(self, i):
        P = self.P
        TBT = 6
        NTOK = TBT * 128
        with P.phase():
            self._dg = [P.sbuf("dg%d" % k, [128, 128], F32) for k in range(2)]
            hT = P.sbuf("f_hT", [128, 16, NTOK], BF16)
            yacc = P.sbuf("f_yacc", [128, TBT, D], F32)
            uT = [P.sbuf("f_uT%d" % k, [128, 4, NTOK], BF16) for k in range(2)]
            w1p = [P.sbuf("f_w1p%d" % k, [128, 16, 512], BF16) for k in range(2)]
            w2p = [P.sbuf("f_w2p%d" % k, [128, 4, D], BF16) for k in range(2)]
            rl = [P.sbuf("f_rl%d" % k, [128, NTOK], F32) for k in range(1)]
            G = [P.sbuf("f_G%d" % k, [128, D], F32) for k in range(2)]
            xt = [P.sbuf("f_xt%d" % k, [128, D], F32) for k in range(2)]
            xn = [P.sbuf("f_xn%d" % k, [128, D], F32) for k in range(1)]
            ss = [P.sbuf("f_ss%d" % k, [128, 4], F32) for k in range(2)]
            psu = [P.psum("f_psu%d" % k, [128, 1024], F32) for k in range(2)]
            psd = [P.psum("f_psd%d" % k, [128, 512], F32) for k in range(4)]
            w1v = self.W["ffn_w1"].t[i].rearrange("(c p) n -> p c n", p=128)
            w2v = self.W["ffn_w2"].t[i].rearrange("(c p) n -> p c n", p=128)
            for b in range(BPC):
                self.gate_rows(i, 1, b, G[0], psd[0])
                self.gate_rows(i, 1, 2, G[1], psd[1])
                for blk in range(NT // TBT):
                    tiles = list(range(blk * TBT, (blk + 1) * TBT))
                    self.norm_tiles(i, 1, b, tiles, hT, 0, (xt, xn, ss, psd))
                    for hb in range(DFF // 512):
                        W1 = w1p[hb % 2]
                        W2 = w2p[hb % 2]
                        U = uT[hb % 2]
                        P.dma(W1[:, :, :], w1v[:, :, hb * 512:(hb + 1) * 512], [self.W["ffn_w1"]], [W1], q="pool")
                        P.dma(W2[:, :, :], w2v[:, hb * 4:(hb + 1) * 4, :], [self.W["ffn_w2"]], [W2], q="pool")
                        for nb in range(4):
                            ps = psu[nb % 2]
                            R = rl[0]
                            for k in range(16):
                                P.matmul(ps[:, 0:512], W1[:, k, nb * 128:(nb + 1) * 128], hT[:, k, 0:512], k == 0, k == 15,
                                         [W1, hT], [ps])
                            for k in range(16):
                                P.matmul(ps[:, 512:512 + NTOK - 512], W1[:, k, nb * 128:(nb + 1) * 128], hT[:, k, 512:NTOK],
                                         k == 0, k == 15, [W1, hT], [ps])
                            P.act(R[:, :], ps[:, 0:NTOK], AF.Relu, [ps], [R])
                            P.tt(U[:, nb, :], R[:, :], R[:, :], ALU.mult, [R], [U])
                        for k, t in enumerate(tiles):
                            for db in range(4):
                                ps = psd[(k * 4 + db) % 4]
                                for kc in range(4):
                                    P.matmul(ps[:, :], U[:, kc, k * 128:(k + 1) * 128], W2[:, kc, db * 512:(db + 1) * 512],
                                             kc == 0, kc == 3, [U, W2], [ps])
                                ya = yacc[:, k, db * 512:(db + 1) * 512]
                                if hb == 0:
                                    P.copy(ya, ps[:, :], [ps], [yacc])
                                else:
                                    P.tt(ya, ps[:, :], ya, ALU.add, [ps, yacc], [yacc])
                    for k, t in enumerate(tiles):
                        X = xt[k % 2]
                        g = G[1] if t < 2 else G[0]
                        P.dma(X[:, :], self.res_ap(b, t), [self.rtok[b][t]], [X])
                        P.tt(yacc[:, k, :], yacc[:, k, :], g[:, :], ALU.mult, [yacc, g], [yacc])
                        P.tt(X[:, :], X[:, :], yacc[:, k, :], ALU.add, [X, yacc], [X])
                        P.dma(self.res_ap(b, t), X[:, :], [X], [self.rtok[b][t]])

    def final_norm(self):
        P = self.P
        with P.phase():
            xt = [P.sbuf("n_xt%d" % k, [128, D], F32) for k in range(2)]
            xn = [P.sbuf("n_xn%d" % k, [128, D], F32) for k in range(2)]
            ss = [P.sbuf("n_ss%d" % k, [128, 4], F32) for k in range(2)]
            wrow = P.sbuf("n_wrow", [128, D], F32)
            ps = P.psum("n_ps", [128, 512], F32)
            dg = [P.sbuf("n_dg%d" % k, [128, 128], F32) for k in range(2)]
            for c4 in range(4):
                for cc in range(4):
                    c = c4 * 4 + cc
                    P.ts(dg[c % 2][:, :], self.ident[:, :], self.nwT[:, 2 * DEPTH, c:c + 1], None, ALU.mult, None,
                         [self.ident, self.nwT], [dg[c % 2]])
                    P.matmul(ps[:, cc * 128:(cc + 1) * 128], self.ones[:, :], dg[c % 2][:, :], True, True,
                             [self.ones, dg[c % 2]], [ps])
                P.copy(wrow[:, c4 * 512:(c4 + 1) * 512], ps[:, :], [ps], [wrow])
            k = 0
            for b in range(BPC):
                for t in range(2, NT):
                    X = xt[k % 2]
                    XN = xn[k % 2]
                    S = ss[k % 2]
                    k += 1
                    P.dma(X[:, :], self.res_ap(b, t), [self.rtok[b][t]], [X])
                    P.memset(S[:, 0:1], 0.0, [S])
                    P.act(XN[:, :], X[:, :], AF.Square, [X], [XN, S], accum_out=S[:, 0:1])
                    P.ts(S[:, 1:2], S[:, 0:1], 1.0 / D, EPS, ALU.mult, ALU.add, [S], [S])
                    P.act(S[:, 2:3], S[:, 1:2], AF.Sqrt, [S], [S])
                    P.recip(S[:, 3:4], S[:, 2:3], [S], [S])
                    P.stt(XN[:, :], X[:, :], S[:, 3:4], wrow[:, :], ALU.mult, ALU.mult, [X, S, wrow], [XN])
                    P.dma(self.res_ap(b, t), XN[:, :], [XN], [self.rtok[b][t]])


    def scratch(self, name, shape, dtype):
        if not hasattr(self, "_scr"):
            self._scr = {}
        if name not in self._scr:
            self._scr[name] = self.P.dram(name, shape, dtype)
        return self._scr[name]

    def norm_to_dram(self, i, wh, b):
        P = self.P
        hT_d = self.scratch("hT_d", [128, 16, TOK], BF16)
        with P.phase():
            hT = P.sbuf("nd_hT", [128, 16, TOK], BF16)
            xt = [P.sbuf("nd_xt%d" % k, [128, D], F32) for k in range(2)]
            xn = [P.sbuf("nd_xn%d" % k, [128, D], F32) for k in range(2)]
            ss = [P.sbuf("nd_ss%d" % k, [128, 4], F32) for k in range(2)]
            pss = [P.psum("nd_ps%d" % k, [128, 512], F32) for k in range(4)]
            self.norm_tiles(i, wh, b, list(range(NT)), hT, 0, (xt, xn, ss, pss))
            P.dma(hT_d[:, :, :], hT[:, :, :], [hT], [hT_d])
        return hT_d

    def outproj_residual(self, i, b, aT_d, wout):
        P = self.P
        TBT = 6
        NTB = TBT * 128
        with P.phase():
            self._dg = [P.sbuf("dg%d" % k, [128, 128], F32) for k in range(2)]
            aT = P.sbuf("op_aT", [128, 32, NTB], BF16)
            yacc = P.sbuf("op_y", [128, TBT, D], F32)
            wp = [P.sbuf("op_wp%d" % k, [128, 32, 256], BF16) for k in range(2)]
            G = [P.sbuf("op_G%d" % k, [128, D], F32) for k in range(2)]
            xt = [P.sbuf("op_xt%d" % k, [128, D], F32) for k in range(2)]
            ps = [P.psum("op_ps%d" % k, [128, 512], F32) for k in range(4)]
            wv = wout.rearrange("(c p) n -> p c n", p=128)
            av = aT_d.t.rearrange("(c p) t -> p c t", p=128)
            self.gate_rows(i, 0, b, G[0], ps[0])
            self.gate_rows(i, 0, 2, G[1], ps[1])
            cnt = 0
            for blk in range(NT // TBT):
                tiles = list(range(blk * TBT, (blk + 1) * TBT))
                P.dma(aT[:, :, :], av[:, :, blk * NTB:(blk + 1) * NTB], [aT_d], [aT])
                for pn in range(8):
                    Wp = wp[pn % 2]
                    P.dma(Wp[:, :, :], wv[:, :, pn * 256:(pn + 1) * 256], [], [Wp], q="pool")
                    for k, t in enumerate(tiles):
                        pp = ps[cnt % 4]
                        cnt += 1
                        for kc in range(32):
                            P.matmul(pp[:, 0:256], aT[:, kc, k * 128:(k + 1) * 128], Wp[:, kc, :], kc == 0, kc == 31,
                                     [aT, Wp], [pp])
                        P.copy(yacc[:, k, pn * 256:(pn + 1) * 256], pp[:, 0:256], [pp], [yacc],
                               q="act" if cnt % 2 else "dve")
                for k, t in enumerate(tiles):
                    X = xt[k % 2]
                    g = G[1] if t < 2 else G[0]
                    P.dma(X[:, :], self.res_ap(b, t), [self.rtok[b][t]], [X])
                    P.tt(yacc[:, k, :], yacc[:, k, :], g[:, :], ALU.mult, [yacc, g], [yacc])
                    P.tt(X[:, :], X[:, :], yacc[:, k, :], ALU.add, [X, yacc], [X])
                    P.dma(self.res_ap(b, t), X[:, :], [X], [self.rtok[b][t]])

    def mixer(self, i):
        kind, j = i % 3, i // 3
        for b in range(BPC):
            if kind == 2:
                self.norm_to_dram(i, 0, b)
                self.ret_proj(i, j, b)
                self.ret_core(i, j, b)
                self.outproj_residual(i, b, self.scratch("r_ogT", [4096, TOK], BF16), self.W["ret_w_out"].t[j])
            elif kind == 0:
                self.s5(i, j, b)
            else:
                self.gdn(i, j, b)


    def sincos(self, x, t, r, sin_out, cos_out, halfpi, rd, tk):
        P = self.P
        TWO_PI = 2.0 * math.pi
        MAGIC = 12582912.0
        P.ts(t, x, 1.0 / TWO_PI, MAGIC, ALU.mult, ALU.add, rd, [tk["t"]])
        P.ts(t, t, -MAGIC, None, ALU.add, None, [tk["t"]], [tk["t"]])
        P.stt(r, t, -TWO_PI, x, ALU.mult, ALU.add, [tk["t"]] + rd, [tk["r"]])
        P.act(sin_out, r, AF.Sin, [tk["r"]], [tk["sin"]])
        P.stt(t, r, -1.0, r, ALU.mult, ALU.max, [tk["r"]], [tk["t"]])
        P.act(cos_out, t, AF.Sin, [tk["t"], halfpi], [tk["cos"]], scale=-1.0, bias=halfpi[:, 0:1])

    def s5_prep(self, i, j):
        P = self.P
        PI = math.pi
        s5F = [self.scratch("s5F%d" % d, [128, 16384], F32) for d in range(2)]
        s5FF = [self.scratch("s5FF%d" % d, [128, 16384], F32) for d in range(2)]
        s5EN = [self.scratch("s5EN%d" % d, [128, 16384], F32) for d in range(2)]
        s5E = [self.scratch("s5E%d" % d, [128, 16384], F32) for d in range(2)]
        s5M = self.scratch("s5M", [128, 16384], F32)
        if not hasattr(self, "s5cols"):
            self.s5cols = P.sbuf("s5cols", [128, 2, 2, 128], F32)
        with P.phase():
            big = P.sbuf("sp_big", [128, 16384], F32)
            PW = P.sbuf("sp_PW", [128, 16, 2, 64], F32)
            bre = P.sbuf("sp_bre", [128, 64, 16], F32)
            bim = P.sbuf("sp_bim", [128, 64, 16], F32)
            Btr = P.sbuf("sp_Btr", [128, 64, 16], F32)
            Bti = P.sbuf("sp_Bti", [128, 64, 16], F32)
            cre = P.sbuf("sp_cre", [128, 16, 64], F32)
            cim = P.sbuf("sp_cim", [128, 16, 64], F32)
            tA = [P.sbuf("sp_t%d" % k, [128, 1024], F32) for k in range(4)]
            sm = P.sbuf("sp_sm", [128, 16, 64], F32)
            dtc = P.sbuf("sp_dt", [128, 4], F32)
            dup = P.sbuf("sp_dup", [128, 128], F32)
            negpi = P.sbuf("sp_negpi", [128, 1], F32)
            P.memset(negpi[:, :], 0.5 * PI, [negpi])
            pst = P.psum("sp_pst", [128, 512], F32)
            X1, Y1, T, MAG, SN, CS, U2, V2, DEN, XR = [sm[:, k, :] for k in range(10)]
            XK, RR = sm[:, 14, :], sm[:, 15, :]
            for d in range(2):
                P.dma(sm[:, 10, :], self.W["s5_lam_re"].t[j, d], [], [sm])
                P.dma(sm[:, 11, :], self.W["s5_lam_im"].t[j, d], [], [sm])
                P.dma(dtc[:, 0:1], self.W["s5_log_dt"].t[j, d].rearrange("(g o) -> g o", o=1), [], [dtc])
                P.dma(bre[:, :, :], self.W["s5_b_re"].t[j, d], [], [bre])
                P.dma(bim[:, :, :], self.W["s5_b_im"].t[j, d], [], [bim])
                P.dma(cre[:, :, :], self.W["s5_c_re"].t[j, d], [], [cre])
                P.dma(cim[:, :, :], self.W["s5_c_im"].t[j, d], [], [cim])
                LRE, LIM = sm[:, 10, :], sm[:, 11, :]
                P.ts(LRE, LRE, -1e-4, None, ALU.min, None, [sm], [sm])
                P.act(dtc[:, 1:2], dtc[:, 0:1], AF.Exp, [dtc], [dtc])
                P.ts(X1, LRE, dtc[:, 1:2], None, ALU.mult, None, [sm, dtc], [sm])
                P.ts(Y1, LIM, dtc[:, 1:2], None, ALU.mult, None, [sm, dtc], [sm])
                P.copy(dup[:, 0:64], Y1, [sm], [dup])
                P.copy(dup[:, 64:128], Y1, [sm], [dup])
                P.transpose(pst[:, 0:128], dup[:, :], self.ident[:, :], [dup, self.ident], [pst])
                P.ts(self.s5cols[:, d, 0, :], pst[:, 0:128], 8.0, None, ALU.mult, None, [pst], [self.s5cols])
                P.copy(dup[:, 0:64], X1, [sm], [dup])
                P.copy(dup[:, 64:128], X1, [sm], [dup])
                P.transpose(pst[:, 128:256], dup[:, :], self.ident[:, :], [dup, self.ident], [pst])
                P.act(self.s5cols[:, d, 1, :], pst[:, 128:256], AF.Exp, [pst], [self.s5cols], scale=8.0)
                for k in range(0, 9):
                    if k == 0:
                        P.memset(PW[:, 7, 0, :], 1.0, [PW])
                        P.memset(PW[:, 7, 1, :], 0.0, [PW])
                        continue
                    P.ts(XK, Y1, float(k), None, ALU.mult, None, [sm], [sm])
                    self.sincos(XK, T, RR, SN, CS, negpi, [sm], {"t": sm, "r": sm, "sin": sm, "cos": sm})
                    P.act(MAG, X1, AF.Exp, [sm], [sm], scale=float(k))
                    P.tt(PW[:, 7 + k, 0, :], MAG, CS, ALU.mult, [sm], [PW])
                    P.tt(PW[:, 7 + k, 1, :], MAG, SN, ALU.mult, [sm], [PW])
                    if k <= 7:
                        P.act(MAG, X1, AF.Exp, [sm], [sm], scale=-float(k))
                        P.tt(PW[:, 7 - k, 0, :], MAG, CS, ALU.mult, [sm], [PW])
                        P.stt(PW[:, 7 - k, 1, :], MAG, -1.0, SN, ALU.mult, ALU.mult, [sm], [PW])
                P.ts(XR, PW[:, 8, 0, :], -1.0, None, ALU.add, None, [PW], [sm])
                YI = PW[:, 8, 1, :]
                P.tt(U2, LRE, LRE, ALU.mult, [sm], [sm])
                P.tt(V2, LIM, LIM, ALU.mult, [sm], [sm])
                P.tt(DEN, U2, V2, ALU.add, [sm], [sm])
                P.recip(DEN, DEN, [sm], [sm])
                BSR, BSI = sm[:, 12, :], sm[:, 13, :]
                P.tt(U2, XR, LRE, ALU.mult, [sm], [sm])
                P.tt(V2, YI, LIM, ALU.mult, [sm, PW], [sm])
                P.tt(U2, U2, V2, ALU.add, [sm], [sm])
                P.tt(BSR, U2, DEN, ALU.mult, [sm], [sm])
                P.tt(U2, YI, LRE, ALU.mult, [sm, PW], [sm])
                P.tt(V2, XR, LIM, ALU.mult, [sm], [sm])
                P.tt(U2, U2, V2, ALU.subtract, [sm], [sm])
                P.tt(BSI, U2, DEN, ALU.mult, [sm], [sm])

                def bc_n_c(ap):
                    return ap.unsqueeze(2).broadcast_to([128, 64, 16])

                def bc_c_n(ap):
                    return ap.unsqueeze(1).broadcast_to([128, 16, 64])

                t3 = [t[:, :].rearrange("g (a b) -> g a b", b=16) for t in tA]
                t3c = [t[:, :].rearrange("g (a b) -> g a b", b=64) for t in tA]

                def cprod(o_re, o_im, a_re, a_im, m_re, m_im, tv, neg_im, rds):
                    P.tt(tv[0], a_re, m_re, ALU.mult, rds, [tA[0]])
                    P.tt(tv[1], a_im, m_im, ALU.mult, rds, [tA[1]])
                    P.tt(o_re, tv[0], tv[1], ALU.subtract, [tA[0], tA[1]], [big])
                    P.tt(tv[2], a_re, m_im, ALU.mult, rds, [tA[2]])
                    P.tt(tv[3], a_im, m_re, ALU.mult, rds, [tA[3]])
                    if neg_im:
                        P.stt(o_im, tv[2], -1.0, tv[3], ALU.mult, ALU.subtract, [tA[2], tA[3]], [big])
                    else:
                        P.tt(o_im, tv[2], tv[3], ALU.add, [tA[2], tA[3]], [big])

                P.tt(t3[0], bc_n_c(BSR), bre[:, :, :], ALU.mult, [sm, bre], [tA[0]])
                P.tt(t3[1], bc_n_c(BSI), bim[:, :, :], ALU.mult, [sm, bim], [tA[1]])
                P.tt(Btr[:, :, :], t3[0], t3[1], ALU.subtract, [tA[0], tA[1]], [Btr])
                P.tt(t3[2], bc_n_c(BSR), bim[:, :, :], ALU.mult, [sm, bim], [tA[2]])
                P.tt(t3[3], bc_n_c(BSI), bre[:, :, :], ALU.mult, [sm, bre], [tA[3]])
                P.tt(Bti[:, :, :], t3[2], t3[3], ALU.add, [tA[2], tA[3]], [Bti])

                v1 = big[:, :].rearrange("g (s c r n) -> g s c r n", s=8, c=16, r=2, n=64)
                v2 = big[:, :].rearrange("g (r n s c) -> g r n s c", s=8, c=16, r=2, n=64)
                for layout, dst in ((1, s5F[d]), (2, s5FF[d])):
                    for sidx in range(8):
                        k = (7 - sidx) if d == 0 else sidx
                        if layout == 1:
                            o_re = v1[:, sidx, :, 0, :].rearrange("g c n -> g n c")
                            o_im = v1[:, sidx, :, 1, :].rearrange("g c n -> g n c")
                        else:
                            o_re = v2[:, 0, :, sidx, :]
                            o_im = v2[:, 1, :, sidx, :]
                        cprod(o_re, o_im, bc_n_c(PW[:, 7 + k, 0, :]), bc_n_c(PW[:, 7 + k, 1, :]), Btr[:, :, :], Bti[:, :, :],
                              t3, False, [PW, Btr, Bti])
                    P.dma(dst[:, :], big[:, :], [big], [dst])
                for layout, dst in ((3, s5EN[d]), (4, s5E[d])):
                    for tidx in range(8):
                        if layout == 3:
                            k = (tidx - 7) if d == 0 else (-tidx)
                        else:
                            k = (tidx + 1) if d == 0 else (8 - tidx)
                        o_re = v2[:, 0, :, tidx, :].rearrange("g n c -> g c n")
                        o_im = v2[:, 1, :, tidx, :].rearrange("g n c -> g c n")
                        cprod(o_re, o_im, bc_c_n(PW[:, 7 + k, 0, :]), bc_c_n(PW[:, 7 + k, 1, :]), cre[:, :, :], cim[:, :, :],
                              t3c, True, [PW, cre, cim])
                    P.dma(dst[:, :], big[:, :], [big], [dst])
        with P.phase():
            maskF = P.sbuf("sm_maskF", [128, 128], F32)
            maskB = P.sbuf("sm_maskB", [128, 128], F32)
            P.dma(maskF[:, :], self.C["s5maskF"][:, :], [], [maskF])
            P.dma(maskB[:, :], self.C["s5maskB"][:, :], [], [maskB])
            sel16 = P.sbuf("sm_sel16", [16, 128], F32)
            P.dma(sel16[:, :], self.C["s5sel16"][:, :], [], [sel16])
            Dg = P.sbuf("sm_Dg", [128, 16], F32)
            P.dma(Dg[:, :], self.W["s5_d"].t[j].rearrange("(g c) -> g c", c=16), [], [Dg])
            DgT = P.sbuf("sm_DgT", [16, 128], F32)
            Dall = P.sbuf("sm_Dall", [128, 128], F32)
            ps0 = P.psum("sm_ps0", [128, 512], F32)
            P.transpose(ps0[0:16, 0:128], Dg[:, :], self.ident[:, :], [Dg, self.ident], [ps0])
            P.copy(DgT[:, :], ps0[0:16, 0:128], [ps0], [DgT])
            P.matmul(ps0[:, 128:256], sel16[:, :], DgT[:, :], True, True, [sel16, DgT], [ps0])
            P.copy(Dall[:, :], ps0[:, 128:256], [ps0], [Dall])
            GB = 8
            ld = [[P.sbuf("sm_ld%d_%d" % (k, q), [128, GB, 128], F32) for q in range(4)] for k in range(2)]
            Mst = [P.sbuf("sm_Mst%d" % k, [128, GB, 128], F32) for k in range(2)]
            T1 = [P.sbuf("sm_T1%d" % k, [128, 128], F32) for k in range(2)]
            T2 = [P.sbuf("sm_T2%d" % k, [128, 128], F32) for k in range(2)]
            psf = [P.psum("sm_psf%d" % k, [128, 512], F32) for k in range(2)]
            psb = [P.psum("sm_psb%d" % k, [128, 512], F32) for k in range(2)]
            srcs = [s5FF[0], s5EN[0], s5FF[1], s5EN[1]]
            for bt in range(128 // GB):
                L = ld[bt % 2]
                for q in range(4):
                    P.dma(L[q][:, :, :], srcs[q].t[bt * GB:(bt + 1) * GB, :].rearrange("g (r c) -> r g c", c=128),
                          [srcs[q]], [L[q]])
                MS = Mst[bt % 2]
                for gi in range(GB):
                    g = bt * GB + gi
                    pf, pb = psf[gi % 2], psb[gi % 2]
                    P.matmul(pf[:, 0:128], L[0][:, gi, :], L[1][:, gi, :], True, True, [L[0], L[1]], [pf])
                    P.matmul(pb[:, 0:128], L[2][:, gi, :], L[3][:, gi, :], True, True, [L[2], L[3]], [pb])
                    a, c2 = T1[gi % 2], T2[gi % 2]
                    P.tt(a[:, :], pf[:, 0:128], maskF[:, :], ALU.mult, [pf, maskF], [a])
                    P.tt(c2[:, :], pb[:, 0:128], maskB[:, :], ALU.mult, [pb, maskB], [c2])
                    P.tt(a[:, :], a[:, :], c2[:, :], ALU.add, [a, c2], [a])
                    P.stt(MS[:, gi, :], self.ident[:, :], Dall[:, g:g + 1], a[:, :], ALU.mult, ALU.add,
                          [self.ident, Dall, a], [MS])
                P.dma(s5M.t[bt * GB:(bt + 1) * GB, :].rearrange("g (r c) -> r g c", c=128), MS[:, :, :], [MS], [s5M])

    def s5(self, i, j, b):
        P = self.P
        PI = math.pi
        if b == 0:
            self.s5_prep(i, j)
        s5F = [self.scratch("s5F%d" % d, [128, 16384], F32) for d in range(2)]
        s5E = [self.scratch("s5E%d" % d, [128, 16384], F32) for d in range(2)]
        s5M = self.scratch("s5M", [128, 16384], F32)
        U_d = self.scratch("s5U", [128, 8, 16, 288], BF16)
        zT_d = self.scratch("hT_d", [128, 16, TOK], BF16)
        with P.phase():
            hTd = P.sbuf("s1_hTd", [128, 16, 8, 288], BF16)
            xt = [P.sbuf("s1_xt%d" % k, [128, D], F32) for k in range(2)]
            xn = [P.sbuf("s1_xn%d" % k, [128, D], F32) for k in range(2)]
            ss = [P.sbuf("s1_ss%d" % k, [128, 4], F32) for k in range(2)]
            pss = [P.psum("s1_ps%d" % k, [128, 512], F32) for k in range(4)]
            self.norm_tiles(i, 0, b, list(range(NT)), hTd, 0, (xt, xn, ss, pss), deint=True)
            for fc in range(16):
                for g8 in range(8):
                    g = fc * 8 + g8
                    P.dma(U_d[g].rearrange("s c j -> c s j"), hTd[g8 * 16:(g8 + 1) * 16, fc, :, :], [hTd], [U_d])
        with P.phase():
            iota = P.sbuf("s2_iota", [128, 288], F32)
            P.dma(iota[:, :], self.C["iota288"][:, :], [], [iota])
            negpi = P.sbuf("s2_negpi", [128, 1], F32)
            P.memset(negpi[:, :], 0.5 * PI, [negpi])
            selT = P.sbuf("s2_selT", [128, 64, 128], BF16)
            P.dma(selT[:, :, :], self.C["s5selT"][:, :, :], [], [selT], q="pool")
            Ub = [P.sbuf("s2_Ub%d" % k, [128, 8, 288], BF16) for k in range(2)]
            Wm = [P.sbuf("s2_Wm%d" % k, [128, 8, 128], BF16) for k in range(2)]
            WF = [[P.sbuf("s2_WF%d_%d" % (k, d), [128, 8, 128], BF16) for d in range(2)] for k in range(2)]
            WE = [[P.sbuf("s2_WE%d_%d" % (k, d), [128, 8, 128], BF16) for d in range(2)] for k in range(2)]
            ANG = P.sbuf("s2_ANG", [128, 8, 288], F32)
            ANG2 = P.sbuf("s2_ANG2", [128, 8, 288], F32)
            SINT = P.sbuf("s2_SINT", [128, 8, 288], F32)
            COST = P.sbuf("s2_COST", [128, 8, 288], F32)
            V = P.sbuf("s2_V", [128, 8, 288], F32)
            T1 = P.sbuf("s2_T1", [128, 8, 288], F32)
            T2 = P.sbuf("s2_T2", [128, 8, 288], F32)
            Gt = P.sbuf("s2_Gt", [128, 8, 288], F32)
            RT = P.sbuf("s2_RT", [128, 8, 288], F32)
            Zp = [P.sbuf("s2_Zp%d" % d, [128, 8, 288], BF16) for d in range(2)]
            Yg = P.sbuf("s2_Yg", [128, 8, 288], BF16)
            zfc = [P.sbuf("s2_zfc%d" % k, [128, TOK], BF16) for k in range(2)]
            gt = [P.sbuf("s2_gt%d" % k, [128, 288], F32) for k in range(3)]
            pv = [P.psum("s2_pv%d" % k, [128, 512], F32) for k in range(2)]
            py = [P.psum("s2_py%d" % k, [128, 512], F32) for k in range(2)]
            pt = [P.psum("s2_pt%d" % k, [128, 512], F32) for k in range(2)]
            for fc in range(16):
                UB, WM = Ub[fc % 2], Wm[fc % 2]
                P.dma(UB[:, :, :], U_d[fc * 8:(fc + 1) * 8].rearrange("g s c j -> (s c) g j"), [U_d], [UB])
                P.dma(WM[:, :, :], s5M.t[fc * 8:(fc + 1) * 8, :].rearrange("g (r c) -> r g c", c=128), [s5M], [WM], q="pool")
                for d in range(2):
                    P.dma(WF[fc % 2][d][:, :, :], s5F[d].t[fc * 8:(fc + 1) * 8, :].rearrange("g (r c) -> r g c", c=128),
                          [s5F[d]], [WF[fc % 2][d]], q="pool")
                    P.dma(WE[fc % 2][d][:, :, :], s5E[d].t[fc * 8:(fc + 1) * 8, :].rearrange("g (r c) -> r g c", c=128),
                          [s5E[d]], [WE[fc % 2][d]], q="pool")
                for d in range(2):
                    wf, we = WF[fc % 2][d], WE[fc % 2][d]
                    for gi in range(8):
                        g = fc * 8 + gi
                        P.ts(ANG[:, gi, :], iota[:, :], self.s5cols[:, d, 0, g:g + 1], None, ALU.mult, None,
                             [iota, self.s5cols], [ANG])
                        P.ts(RT[:, gi, :], iota[:, :], 0.0, self.s5cols[:, d, 1, g:g + 1], ALU.mult, ALU.add,
                             [iota, self.s5cols], [RT])
                    self.sincos(ANG[:, :, :], ANG2[:, :, :], T1[:, :, :], SINT[:, :, :], COST[:, :, :], negpi, [ANG],
                                {"t": ANG2, "r": T1, "sin": SINT, "cos": COST})
                    P.ts(SINT[64:128, :, :], SINT[64:128, :, :], -1.0, None, ALU.mult, None, [SINT], [SINT])
                    for gi in range(8):
                        p = pv[gi % 2]
                        P.matmul(p[:, 0:288], wf[:, gi, :], UB[:, gi, :], True, True, [wf, UB], [p])
                        if d == 0:
                            P.copy(V[:, gi, :], p[:, 0:288], [p], [V], q="act")
                        else:
                            P.copy(V[:, gi, 0:32], p[:, 0:32][:, ::-1], [p], [V], q="act")
                            P.copy(V[:, gi, 32:288], p[:, 32:288][:, ::-1], [p], [V], q="act")
                    P.tt(T1[:, :, :], V[:, :, :], COST[:, :, :], ALU.mult, [V, COST], [T1])
                    P.copy(T2[0:64, :, :], V[64:128, :, :], [V], [T2])
                    P.copy(T2[64:128, :, :], V[0:64, :, :], [V], [T2])
                    P.tt(T2[:, :, :], T2[:, :, :], SINT[:, :, :], ALU.mult, [T2, SINT], [T2])
                    P.tt(V[:, :, :], T1[:, :, :], T2[:, :, :], ALU.add, [T1, T2], [V])
                    for gi in range(8):
                        P.op("dve", (lambda gi=gi: (lambda e: e.tensor_tensor_scan(Gt[:, gi, :], RT[:, gi, :], V[:, gi, :], 0.0,
                                                                                    ALU.mult, ALU.add)))(), [RT, V], [Gt])
                    P.tt(T1[:, :, :], Gt[:, :, :], COST[:, :, :], ALU.mult, [Gt, COST], [T1])
                    P.copy(T2[0:64, :, :], Gt[64:128, :, :], [Gt], [T2])
                    P.copy(T2[64:128, :, :], Gt[0:64, :, :], [Gt], [T2])
                    P.tt(T2[:, :, :], T2[:, :, :], SINT[:, :, :], ALU.mult, [T2, SINT], [T2])
                    P.tt(T1[:, :, :], T1[:, :, :], T2[:, :, :], ALU.subtract, [T1, T2], [T1])
                    ZP = Zp[d]
                    if d == 0:
                        P.memset(ZP[:, :, 0:1], 0.0, [ZP])
                        P.copy(ZP[:, :, 1:288], T1[:, :, 0:287], [T1], [ZP])
                    else:
                        P.memset(ZP[:, :, 31:32], 0.0, [ZP])
                        P.copy(ZP[:, :, 0:31], T1[:, :, 0:31][:, :, ::-1], [T1], [ZP])
                        P.copy(ZP[:, :, 32:288], T1[:, :, 31:287][:, :, ::-1], [T1], [ZP])
                for gi in range(8):
                    p = py[gi % 2]
                    P.matmul(p[:, 0:288], WM[:, gi, :], UB[:, gi, :], True, False, [WM, UB], [p])
                    P.matmul(p[:, 0:288], WE[fc % 2][0][:, gi, :], Zp[0][:, gi, :], False, False, [WE[fc % 2][0], Zp[0]], [p])
                    P.matmul(p[:, 0:288], WE[fc % 2][1][:, gi, :], Zp[1][:, gi, :], False, True, [WE[fc % 2][1], Zp[1]], [p])
                    P.copy(Yg[:, gi, :], p[:, 0:288], [p], [Yg], q="act")
                Z = zfc[fc % 2]
                zv = Z[:, :].rearrange("p (j s) -> p s j", s=8)
                for t in range(8):
                    p = pt[t % 2]
                    for g8 in range(8):
                        P.matmul(p[:, 0:288], selT[:, g8 * 8 + t, :], Yg[:, g8, :], g8 == 0, g8 == 7, [selT, Yg], [p])
                    P.act(gt[0][:, :], p[:, 0:288], AF.Square, [p], [gt[0]])
                    P.ts(gt[0][:, :], gt[0][:, :], 0.044715, 1.0, ALU.mult, ALU.add, [gt[0]], [gt[0]])
                    P.tt(gt[1][:, :], gt[0][:, :], p[:, 0:288], ALU.mult, [gt[0], p], [gt[1]])
                    P.act(gt[2][:, :], gt[1][:, :], AF.Sigmoid, [gt[1]], [gt[2]], scale=1.5957691216057308)
                    P.tt(zv[:, t, :], gt[2][:, :], p[:, 0:288], ALU.mult, [gt[2], p], [Z])
                P.dma(zT_d[:, fc, :], Z[:, :], [Z], [zT_d])
        self.s5_glu(i, j, b)

    def s5_glu(self, i, j, b):
        P = self.P
        zT_d = self.scratch("hT_d", [128, 16, TOK], BF16)
        TBT = 6
        NTB = TBT * 128
        gw = self.W["s5_glu_w"].t[j].rearrange("(c p) n -> p c n", p=128)
        with P.phase():
            self._dg = [P.sbuf("dg%d" % k, [128, 128], F32) for k in range(2)]
            zT = P.sbuf("sg_zT", [128, 16, NTB], BF16)
            yacc = P.sbuf("sg_y", [128, TBT, D], F32)
            wp = [P.sbuf("sg_wp%d" % k, [128, 16, 512], BF16) for k in range(2)]
            G = [P.sbuf("sg_G%d" % k, [128, D], F32) for k in range(2)]
            xt = [P.sbuf("sg_xt%d" % k, [128, D], F32) for k in range(2)]
            sg = [P.sbuf("sg_sg%d" % k, [128, 256], F32) for k in range(2)]
            ps = [P.psum("sg_ps%d" % k, [128, 512], F32) for k in range(4)]
            self.gate_rows(i, 0, b, G[0], ps[0])
            self.gate_rows(i, 0, 2, G[1], ps[1])
            cnt = 0
            for blk in range(NT // TBT):
                tiles = list(range(blk * TBT, (blk + 1) * TBT))
                P.dma(zT[:, :, :], zT_d[:, :, blk * NTB:(blk + 1) * NTB], [zT_d], [zT])
                for pn in range(8):
                    Wp = wp[pn % 2]
                    P.dma(Wp[:, :, 0:256], gw[:, :, pn * 256:(pn + 1) * 256], [], [Wp], q="pool")
                    P.dma(Wp[:, :, 256:512], gw[:, :, 2048 + pn * 256:2048 + (pn + 1) * 256], [], [Wp], q="pool")
                    for k, t in enumerate(tiles):
                        pp = ps[cnt % 4]
                        S = sg[cnt % 2]
                        cnt += 1
                        for kc in range(16):
                            P.matmul(pp[:, :], zT[:, kc, k * 128:(k + 1) * 128], Wp[:, kc, :], kc == 0, kc == 15, [zT, Wp], [pp])
                        P.act(S[:, :], pp[:, 256:512], AF.Sigmoid, [pp], [S])
                        P.tt(yacc[:, k, pn * 256:(pn + 1) * 256], pp[:, 0:256], S[:, :], ALU.mult, [pp, S], [yacc])
                for k, t in enumerate(tiles):
                    X = xt[k % 2]
                    g = G[1] if t < 2 else G[0]
                    P.dma(X[:, :], self.res_ap(b, t), [self.rtok[b][t]], [X])
                    P.tt(yacc[:, k, :], yacc[:, k, :], g[:, :], ALU.mult, [yacc, g], [yacc])
                    P.tt(X[:, :], X[:, :], yacc[:, k, :], ALU.add, [X, yacc], [X])
                    P.dma(self.res_ap(b, t), X[:, :], [X], [self.rtok[b][t]])


    def gdn(self, i, j, b):
        self.norm_to_dram(i, 0, b)
        self.gdn_proj(i, j, b)
        if self.cfg.get("gdn_stop") == "proj":
            return
        self.gdn_core(i, j, b)
        self.outproj_residual(i, b, self.scratch("r_ogT", [4096, TOK], BF16), self.W["gdn_w_out"].t[j])

    def gdn_proj(self, i, j, b):
        P = self.P
        hT_d = self.scratch("hT_d", [128, 16, TOK], BF16)
        g_qkvT = self.scratch("g_qkvT", [8192, TOK], BF16)
        g_z = self.scratch("r_gate", [TOK, 4096], BF16)
        g_ab = self.scratch("g_ab", [TOK, 128], F32)
        win = self.W["gdn_w_in"].t[j].rearrange("(c p) n -> p c n", p=128)
        blocks = [(0, 256)] + [(256 + 512 * k, 512) for k in range(4)]
        PADW = 4 + 256 + 4 + 2048 + 4
        with P.phase():
            hT = P.sbuf("gp_hT", [128, 16, TOK], BF16)
            P.dma(hT[:, :, :], hT_d[:, :, :], [hT_d], [hT])
            cwr = P.sbuf("gp_cwr", [5, 2048], F32)
            cw = P.sbuf("gp_cw", [128, 64, 5], F32)
            pcw = P.psum("gp_pcw", [128, 512], F32)
            for c4 in range(4):
                P.dma(cwr[:, :], self.W["gdn_conv_w"].t[j][:, c4 * 2048:(c4 + 1) * 2048], [], [cwr])
                for c in range(16):
                    cc = c4 * 16 + c
                    P.transpose(pcw[:, cc * 5:(cc + 1) * 5], cwr[0:5, c * 128:(c + 1) * 128], self.ident[0:5, 0:5],
                                [cwr, self.ident], [pcw])
            P.copy(cw[:, :, :], pcw[:, 0:320].rearrange("p (c k) -> p c k", k=5), [pcw], [cw])
            wp = [P.sbuf("gp_wp%d" % k, [128, 16, 512], BF16) for k in range(2)]
            pad = [P.sbuf("gp_pad%d" % k, [128, PADW], F32) for k in range(2)]
            acc = [P.sbuf("gp_acc%d" % k, [128, TOK], F32) for k in range(2)]
            sq = [P.sbuf("gp_sq%d" % k, [128, 512], F32) for k in range(2)]
            rin = [P.sbuf("gp_rin%d" % k, [128, 512], F32) for k in range(2)]
            ost = [P.sbuf("gp_ost%d" % k, [128, TOK], BF16) for k in range(2)]
            epsc = P.sbuf("gp_eps", [128, 1], F32)
            P.memset(epsc[:, :], EPS, [epsc])
            for k in range(2):
                P.memset(pad[k][:, :], 0.0, [pad[k]])
            psA = [P.psum("gp_psA%d" % k, [128, 512], F32) for k in range(4)]
            psN = [P.psum("gp_psN%d" % k, [128, 512], F32) for k in range(2)]
            cnt = 0
            for pn in range(16):
                Wp = wp[pn % 2]
                P.dma(Wp[:, :, :], win[:, :, pn * 512:(pn + 1) * 512], [], [Wp], q="pool")
                for nb in range(4):
                    chunk = pn * 4 + nb
                    PD, AC, OS = pad[chunk % 2], acc[chunk % 2], ost[chunk % 2]
                    for bi, (t0, n) in enumerate(blocks):
                        ps = psA[cnt % 4]
                        cnt += 1
                        for k in range(16):
                            P.matmul(ps[:, 0:n], Wp[:, k, nb * 128:(nb + 1) * 128], hT[:, k, t0:t0 + n], k == 0, k == 15,
                                     [Wp, hT], [ps])
                        o0 = 2 + t0 if bi == 0 else 264 + 2 + (t0 - 256)
                        P.copy(PD[:, o0:o0 + n], ps[:, 0:n], [ps], [PD], q="act")
                    for (po, ao, n) in ((0, 0, 256), (264, 256, 2048)):
                        for k in range(5):
                            src = PD[:, po + k:po + k + n]
                            if k == 0:
                                P.ts(AC[:, ao:ao + n], src, cw[:, chunk, 0:1], None, ALU.mult, None, [PD, cw], [AC])
                            else:
                                P.stt(AC[:, ao:ao + n], src, cw[:, chunk, k:k + 1], AC[:, ao:ao + n], ALU.mult, ALU.add,
                                      [PD, cw, AC], [AC])
                    if chunk >= 32:
                        P.act(OS[:, :], AC[:, :], AF.Silu, [AC], [OS])
                    else:
                        P.act(AC[:, :], AC[:, :], AF.Silu, [AC], [AC])
                        qs = (128.0 ** -0.5) if chunk < 16 else 1.0
                        for bi, (t0, n) in enumerate(blocks):
                            SQ, RI = sq[bi % 2], rin[bi % 2]
                            pN = psN[bi % 2]
                            P.tt(SQ[:, 0:n], AC[:, t0:t0 + n], AC[:, t0:t0 + n], ALU.mult, [AC], [SQ])
                            P.matmul(pN[:, 0:n], self.ones[:, :], SQ[:, 0:n], True, True, [self.ones, SQ], [pN])
                            P.act(RI[:, 0:n], pN[:, 0:n], AF.Sqrt, [pN, epsc], [RI], bias=epsc[:, 0:1])
                            P.recip(RI[:, 0:n], RI[:, 0:n], [RI], [RI])
                            P.stt(OS[:, t0:t0 + n], AC[:, t0:t0 + n], qs, RI[:, 0:n], ALU.mult, ALU.mult, [AC, RI], [OS])
                    P.dma(g_qkvT[chunk * 128:(chunk + 1) * 128, :], OS[:, :], [OS], [g_qkvT])
        with P.phase():
            hT = P.sbuf("gz_hT", [128, 16, TOK], BF16)
            P.dma(hT[:, :, :], hT_d[:, :, :], [hT_d], [hT])
            wp = [P.sbuf("gz_wp%d" % k, [128, 16, 512], BF16) for k in range(2)]
            stage = [P.sbuf("gz_st%d" % k, [128, NT, 512], BF16) for k in range(2)]
            psA = [P.psum("gz_psA%d" % k, [128, 512], F32) for k in range(4)]
            cnt = 0
            for pn in range(8):
                Wp = wp[pn % 2]
                ST = stage[pn % 2]
                c0 = 8192 + pn * 512
                P.dma(Wp[:, :, :], win[:, :, c0:c0 + 512], [], [Wp], q="pool")
                for t in range(NT):
                    ps = psA[cnt % 4]
                    cnt += 1
                    for k in range(16):
                        P.matmul(ps[:, :], hT[:, k, t * 128:(t + 1) * 128], Wp[:, k, :], k == 0, k == 15, [hT, Wp], [ps])
                    P.act(ST[:, t, :], ps[:, :], AF.Silu, [ps], [ST])
                P.dma(g_z.t.rearrange("(t p) e -> p t e", p=128)[:, :, pn * 512:(pn + 1) * 512], ST[:, :, :], [ST], [g_z])
            Wab = P.sbuf("gz_wab", [128, 16, 128], BF16)
            P.dma(Wab[:, :, :], win[:, :, 12288:12416], [], [Wab], q="pool")
            par = P.sbuf("gz_par", [128, 2, 64], F32)
            P.dma(par[:, 0, :], self.W["gdn_a_log"].t[j].rearrange("d h -> (d h)").partition_broadcast(128), [], [par])
            P.dma(par[:, 1, :], self.W["gdn_dt_bias"].t[j].rearrange("d h -> (d h)").partition_broadcast(128), [], [par])
            nA = P.sbuf("gz_nA", [128, 2, 32], F32)
            P.act(nA[:, :, :], par[:, 0, :].rearrange("p (d h) -> p d h", d=2), AF.Exp, [par], [nA])
            ab = P.sbuf("gz_ab", [128, NT, 128], F32)
            tmp = [P.sbuf("gz_tmp%d" % k, [128, 2, 32], F32) for k in range(2)]
            for t in range(NT):
                ps = psA[cnt % 4]
                cnt += 1
                for k in range(16):
                    P.matmul(ps[:, 0:128], hT[:, k, t * 128:(t + 1) * 128], Wab[:, k, :], k == 0, k == 15, [hT, Wab], [ps])
                pv = ps[:, 0:128].rearrange("p (d a h) -> p d a h", d=2, a=2)
                av = ab[:, t, :].rearrange("p (d a h) -> p d a h", d=2, a=2)
                T = tmp[t % 2]
                P.tt(T[:, :, :], pv[:, :, 0, :], par[:, 1, :].rearrange("p (d h) -> p d h", d=2), ALU.add, [ps, par], [T])
                P.act(T[:, :, :], T[:, :, :], AF.Exp, [T], [T])
                P.act(T[:, :, :], T[:, :, :], AF.Ln, [T], [T], bias=self.ones[:, 0:1])
                P.stt(av[:, :, 0, :], T[:, :, :], -1.0, nA[:, :, :], ALU.mult, ALU.mult, [T, nA], [ab])
                P.act(av[:, :, 1, :], pv[:, :, 1, :], AF.Sigmoid, [ps], [ab])
            P.dma(g_ab.t.rearrange("(t p) e -> p t e", p=128), ab[:, :, :], [ab], [g_ab])

    def gdn_core(self, i, j, b):
        P = self.P
        g_qkvT = self.scratch("g_qkvT", [8192, TOK], BF16)
        g_z = self.scratch("r_gate", [TOK, 4096], BF16)
        g_ab = self.scratch("g_ab", [TOK, 128], F32)
        r_ogT = self.scratch("r_ogT", [4096, TOK], BF16)

        class Ring:
            def __init__(self, items):
                self.items = items
                self.k = 0

            def next(self):
                it = self.items[self.k % len(self.items)]
                self.k += 1
                return it

        with P.phase():
            def cload(nm):
                t = P.sbuf("gc_" + nm, [128, 128], F32)
                P.dma(t[:, :], self.C[nm][:, :], [], [t])
                return t
            LI, LS, UI, US = cload("LI"), cload("LS"), cload("UI"), cload("US")
            gnw = P.sbuf("gc_gnw", [128, 128], F32)
            P.dma(gnw[:, :], self.W["gdn_norm_w"].t[j].partition_broadcast(128), [], [gnw])
            gb = P.sbuf("gc_gb", [128, NT, 128], F32)
            P.dma(gb[:, :, :], g_ab.t.rearrange("(t p) e -> p t e", p=128), [g_ab], [gb])
            qT = P.sbuf("gc_qT", [128, TOK], BF16)
            kT = P.sbuf("gc_kT", [128, TOK], BF16)
            vT = P.sbuf("gc_vT", [128, TOK], BF16)
            zh = P.sbuf("gc_zh", [128, NT, 128], BF16)
            ktm = P.sbuf("gc_ktm", [128, NT, 128], BF16)
            vtm = P.sbuf("gc_vtm", [128, NT, 128], BF16)
            oacc = P.sbuf("gc_oacc", [128, NT, 128], F32)
            ogT = P.sbuf("gc_ogT", [128, TOK], BF16)
            junk = P.sbuf("gc_junk", [128, 128], F32)
            def ring(nm, n, shape, dt):
                return Ring([P.sbuf("gc_%s%d" % (nm, k), shape, dt) for k in range(n)])
            rGm = ring("Gm", 4, [128, 128], F32)
            rcols = ring("cols", 6, [128, 8], F32)
            rDec = ring("Dec", 4, [128, 128], F32)
            rDecT = ring("DecT", 4, [128, 128], F32)
            rsq = ring("sqb", 24, [128, 128], BF16)
            rU = ring("U", 4, [128, 128], F32)
            rAT = ring("ATr", 4, [128, 128], BF16)
            rM = ring("Mr", 3, [128, 128], BF16)
            rN = ring("Nr", 3, [128, 128], BF16)
            def cloadb(nm):
                tf = P.sbuf("gc_f_" + nm, [128, 128], F32)
                P.dma(tf[:, :], self.C[nm][:, :], [], [tf])
                tb = P.sbuf("gc_b_" + nm, [128, 128], BF16)
                P.copy(tb[:, :], tf[:, :], [tf], [tb])
                return tb
            blk16 = cloadb("blk16")
            lvlm = [cloadb("lvl16"), cloadb("lvl32"), cloadb("lvl64")]
            rkd = ring("kdr", 4, [128, 128], BF16)
            rWT = ring("WTr", 4, [128, 128], BF16)
            rtmp = ring("tmp", 4, [128, 128], F32)
            S32 = [P.sbuf("gc_S32_%d" % k, [128, 128], F32) for k in range(2)]
            S16 = [P.sbuf("gc_S16_%d" % k, [128, 128], BF16) for k in range(2)]
            st = ring("st", 2, [128, 8], F32)
            banks = []
            for k in range(5):
                bank = P.psum("gc_pq%d" % k, [128, 512], F32)
                banks.append((bank.t, Tok("pq%d" % k)))
            rbank = Ring(banks)

            class _PS:
                def __init__(self):
                    self.cur = None
                    self.q = 0

                def next(self, newbank=True):
                    if newbank or self.cur is None or self.q >= 4:
                        self.cur = rbank.next()
                        self.q = 0
                    ap = self.cur[0][:, self.q * 128:(self.q + 1) * 128]
                    self.q += 1
                    return ap, self.cur[1]
            rps = _PS()
            pcb = P.psum("gc_pc", [128, 512], F32)
            rpc = Ring([(pcb.t[:, 0:4], Tok("pc0"))])
            pbb = [P.psum("gc_pb%d" % k, [128, 512], BF16) for k in range(2)]
            rpbank = Ring([(pbb[k].t, Tok("pb%d" % k)) for k in range(2)])

            class _PB:
                def next(self):
                    bk = rpbank.next()
                    return bk[0][:, 0:128], bk[1]
            rpb = _PB()
            qkv = g_qkvT.t.rearrange("(c p) t -> p c t", p=128)
            zv = g_z.t.rearrange("(t p) e -> p t e", p=128)
            ov = r_ogT.t.rearrange("(c p) t -> p c t", p=128)
            identb = self.identb
            order = [list(range(NT)), [1, 0] + list(range(NT - 1, 1, -1))]

            def pre(d, hv, t):
                sl = slice(t * 128, (t + 1) * 128)
                LU = UI if d == 0 else LI
                GMASK = LS if d == 0 else US
                mS = LS if d == 0 else US
                mTI = UI if d == 0 else LI
                gcol = gb[:, t, d * 64 + hv:d * 64 + hv + 1]
                bcol = gb[:, t, d * 64 + 32 + hv:d * 64 + 32 + hv + 1]
                Gm = rGm.next()
                P.ts(Gm[:, :], GMASK[:, :], gcol, None, ALU.mult, None, [GMASK, gb], [Gm])
                pgd, tgd = rps.next()
                pgdT, tgdT = rps.next(False)
                P.matmul(pgd, LU[:, :], Gm[:, :], True, True, [LU, Gm], [tgd])
                P.matmul(pgdT, Gm[:, :], LU[:, :], True, True, [LU, Gm], [tgdT])
                pc, tpc = rpc.next()
                gcol2 = gb[:, t, d * 64 + hv:d * 64 + hv + 2]
                P.matmul(pc[:, 0:2], LU[:, :], gcol2, True, True, [LU, gb], [tpc])
                P.matmul(pc[:, 2:4], self.ones[:, :], gcol2, True, True, [self.ones, gb], [tpc])
                cols = rcols.next()
                P.copy(cols[:, 0:2], pc[:, 0:4:2], [tpc], [cols], q="act")
                P.act(cols[:, 2:3], cols[:, 0:1], AF.Exp, [cols], [cols])
                P.act(cols[:, 3:4], cols[:, 0:1], AF.Exp, [cols], [cols], scale=-1.0, bias=cols[:, 1:2])
                P.act(cols[:, 4:5], cols[:, 1:2], AF.Exp, [cols], [cols])
                P.tt(cols[:, 5:6], cols[:, 2:3], bcol, ALU.mult, [cols, gb], [cols])
                if self.cfg.get("gdn_cut", 9) < 2:
                    return None
                Dec, DecT = rDec.next(), rDecT.next()
                P.act(Dec[:, :], pgd, AF.Exp, [tgd], [Dec])
                P.act(DecT[:, :], pgdT, AF.Exp, [tgdT], [DecT])
                P.tt(Dec[:, :], Dec[:, :], mS[:, :], ALU.mult, [Dec, mS], [Dec])
                P.tt(DecT[:, :], DecT[:, :], mTI[:, :], ALU.mult, [DecT, mTI], [DecT])
                pkk, tkk = rps.next()
                pkq, tkq = rps.next(False)
                P.matmul(pkk, kT[:, sl], kT[:, sl], True, True, [kT], [tkk])
                P.matmul(pkq, kT[:, sl], qT[:, sl], True, True, [kT, qT], [tkq])
                M = rM.next()
                AT = rAT.next()
                P.stt(M[:, :], pkk, bcol, Dec[:, :], ALU.mult, ALU.mult, [tkk, gb, Dec], [M])
                P.tt(AT[:, :], pkq, DecT[:, :], ALU.mult, [tkq, DecT], [AT])
                pN, tN = rpb.next()
                P.transpose(pN, M[:, :], identb[:, :], [M, identb], [tN])
                N = rN.next()
                P.copy(N[:, :], pN, [tN], [N], q="act")
                M0, N0 = rsq.next(), rsq.next()
                P.tt(M0[:, :], M[:, :], blk16[:, :], ALU.mult, [M, blk16], [M0])
                P.tt(N0[:, :], N[:, :], blk16[:, :], ALU.mult, [N, blk16], [N0])
                T_, R = rsq.next(), rsq.next()
                P.tt(T_[:, :], identb[:, :], M0[:, :], ALU.subtract, [identb, M0], [T_])
                P.tt(R[:, :], identb[:, :], N0[:, :], ALU.subtract, [identb, N0], [R])
                Pm, Qm = N0, M0
                for lvl in range(3):
                    pp, tp = rps.next()
                    P.matmul(pp, Qm[:, :], Pm[:, :], True, True, [Qm, Pm], [tp])
                    pq, tq = rps.next(False)
                    P.matmul(pq, Pm[:, :], Qm[:, :], True, True, [Qm, Pm], [tq])
                    P2, Q2 = rsq.next(), rsq.next()
                    P.copy(P2[:, :], pp, [tp], [P2], q="act")
                    P.copy(Q2[:, :], pq, [tq], [Q2], q="act")
                    pt_, tt_ = rps.next()
                    P.matmul(pt_, P2[:, :], T_[:, :], True, True, [P2, T_], [tt_])
                    prr, trr = rps.next(False)
                    P.matmul(prr, T_[:, :], P2[:, :], True, True, [P2, T_], [trr])
                    Tn, Rn = rsq.next(), rsq.next()
                    P.tt(Tn[:, :], pt_, T_[:, :], ALU.add, [tt_, T_], [Tn])
                    P.tt(Rn[:, :], prr, R[:, :], ALU.add, [trr, R], [Rn])
                    T_, R, Pm, Qm = Tn, Rn, P2, Q2
                for li, msk in enumerate(lvlm):
                    last = li == len(lvlm) - 1
                    B1, B1T = rsq.next(), rsq.next()
                    P.tt(B1[:, :], M[:, :], msk[:, :], ALU.mult, [M, msk], [B1])
                    P.tt(B1T[:, :], N[:, :], msk[:, :], ALU.mult, [N, msk], [B1T])
                    px, tx = rps.next()
                    P.matmul(px, B1[:, :], R[:, :], True, True, [B1, R], [tx])
                    if not last:
                        px2, tx2 = rps.next(False)
                        P.matmul(px2, B1T[:, :], T_[:, :], True, True, [B1T, T_], [tx2])
                    Xp = rsq.next()
                    P.copy(Xp[:, :], px, [tx], [Xp], q="act")
                    if not last:
                        X = rsq.next()
                        P.copy(X[:, :], px2, [tx2], [X], q="act")
                    pr, tr = rps.next()
                    P.matmul(pr, T_[:, :], Xp[:, :], True, True, [T_, Xp], [tr])
                    if not last:
                        pt_, tt_ = rps.next(False)
                        P.matmul(pt_, R[:, :], X[:, :], True, True, [R, X], [tt_])
                    Rn = rsq.next()
                    P.tt(Rn[:, :], R[:, :], pr, ALU.subtract, [tr, R], [Rn])
                    if not last:
                        Tn = rsq.next()
                        P.tt(Tn[:, :], T_[:, :], pt_, ALU.subtract, [tt_, T_], [Tn])
                        T_ = Tn
                    R = Rn
                RHSu, RHSw, kdec = rsq.next(), rsq.next(), rkd.next()
                P.ts(RHSu[:, :], vtm[:, t, :], bcol, None, ALU.mult, None, [vtm, gb], [RHSu])
                P.ts(RHSw[:, :], ktm[:, t, :], cols[:, 5:6], None, ALU.mult, None, [ktm, cols], [RHSw])
                P.ts(kdec[:, :], ktm[:, t, :], cols[:, 3:4], None, ALU.mult, None, [ktm, cols], [kdec])
                pU, tU = rps.next()
                pW, tW = rps.next(False)
                P.matmul(pU, R[:, :], RHSu[:, :], True, True, [R, RHSu], [tU])
                P.matmul(pW, RHSw[:, :], R[:, :], True, True, [R, RHSw], [tW])
                U = rU.next()
                WT = rWT.next()
                P.copy(U[:, :], pU, [tU], [U], q="act")
                P.copy(WT[:, :], pW, [tW], [WT], q="act")
                return dict(U=U, WT=WT, AT=AT, kdec=kdec, cols=cols, t=t)

            def seq(d, hv, pr_, first, first_dir):
                t = pr_["t"]
                sl = slice(t * 128, (t + 1) * 128)
                S, Sb = S32[d], S16[d]
                cols = pr_["cols"]
                vnew = rsq.next()
                if first:
                    P.copy(vnew[:, :], pr_["U"][:, :], [pr_["U"]], [vnew])
                else:
                    pws, tws = rps.next()
                    P.matmul(pws, pr_["WT"][:, :], Sb[:, :], True, True, [pr_["WT"], Sb], [tws])
                    P.tt(vnew[:, :], pr_["U"][:, :], pws, ALU.subtract, [pr_["U"], tws], [vnew])
                po2, to2 = rps.next()
                P.matmul(po2, pr_["AT"][:, :], vnew[:, :], True, True, [pr_["AT"], vnew], [to2])
                if not first:
                    po1, to1 = rps.next(False)
                    P.matmul(po1, qT[:, sl], Sb[:, :], True, True, [qT, Sb], [to1])
                    tm = rtmp.next()
                    P.act(tm[:, :], po1, AF.Copy, [to1, cols], [tm], scale=cols[:, 2:3])
                    if first_dir:
                        P.tt(oacc[:, t, :], po2, tm[:, :], ALU.add, [to2, tm], [oacc])
                    else:
                        P.tt(tm[:, :], po2, tm[:, :], ALU.add, [to2, tm], [tm])
                        P.tt(oacc[:, t, :], oacc[:, t, :], tm[:, :], ALU.add, [oacc, tm], [oacc])
                else:
                    if first_dir:
                        P.copy(oacc[:, t, :], po2, [to2], [oacc])
                    else:
                        P.tt(oacc[:, t, :], oacc[:, t, :], po2, ALU.add, [oacc, to2], [oacc])
                pS, tS = rps.next()
                P.matmul(pS, pr_["kdec"][:, :], vnew[:, :], True, True, [pr_["kdec"], vnew], [tS])
                if first:
                    P.copy(S[:, :], pS, [tS], [S])
                else:
                    P.stt(S[:, :], S[:, :], cols[:, 4:5], pS, ALU.mult, ALU.add, [S, cols, tS], [S])
                P.copy(Sb[:, :], S[:, :], [S], [Sb], q="act")

            for hv in range(self.cfg.get("gdn_heads", 32)):
                hq = hv // 2
                if hv % 2 == 0:
                    P.dma(qT[:, :], qkv[:, hq, :], [g_qkvT], [qT])
                    P.dma(kT[:, :], qkv[:, 16 + hq, :], [g_qkvT], [kT])
                    for t0 in range(0, NT, 4):
                        bk, tk_ = rpbank.next()
                        nt = min(4, NT - t0)
                        for q4 in range(nt):
                            t = t0 + q4
                            P.transpose(bk[:, q4 * 128:(q4 + 1) * 128], kT[:, t * 128:(t + 1) * 128], identb[:, :],
                                        [kT, identb], [tk_])
                        P.copy(ktm[:, t0:t0 + nt, :], bk[:, 0:nt * 128].rearrange("p (a b) -> p a b", b=128), [tk_], [ktm],
                               q="act")
                P.dma(vT[:, :], qkv[:, 32 + hv, :], [g_qkvT], [vT])
                P.dma(zh[:, :, :], zv[:, :, hv * 128:(hv + 1) * 128], [g_z], [zh])
                for t0 in range(0, NT, 4):
                    bk, tk_ = rpbank.next()
                    nt = min(4, NT - t0)
                    for q4 in range(nt):
                        t = t0 + q4
                        P.transpose(bk[:, q4 * 128:(q4 + 1) * 128], vT[:, t * 128:(t + 1) * 128], identb[:, :],
                                    [vT, identb], [tk_])
                    P.copy(vtm[:, t0:t0 + nt, :], bk[:, 0:nt * 128].rearrange("p (a b) -> p a b", b=128), [tk_], [vtm],
                           q="act")
                if self.cfg.get("gdn_cut", 9) < 1:
                    continue
                for d in range(2):
                    nxt = pre(d, hv, order[d][0])
                    if self.cfg.get("gdn_cut", 9) < 3:
                        continue
                    for n in range(NT):
                        cur = nxt
                        if n + 1 < NT:
                            nxt = pre(d, hv, order[d][n + 1])
                        seq(d, hv, cur, n == 0, d == 0)
                for t in range(NT):
                    S_ = st.next()
                    o = oacc[:, t, :]
                    P.memset(S_[:, 0:1], 0.0, [S_])
                    P.act(junk[:, :], o, AF.Square, [oacc], [junk, S_], accum_out=S_[:, 0:1])
                    P.ts(S_[:, 1:2], S_[:, 0:1], 1.0 / 128, EPS, ALU.mult, ALU.add, [S_], [S_])
                    P.act(S_[:, 2:3], S_[:, 1:2], AF.Sqrt, [S_], [S_])
                    P.recip(S_[:, 3:4], S_[:, 2:3], [S_], [S_])
                    tm = rtmp.next()
                    P.stt(tm[:, :], o, S_[:, 3:4], gnw[:, :], ALU.mult, ALU.mult, [oacc, S_, gnw], [tm])
                    og = rsq.next()
                    P.tt(og[:, :], tm[:, :], zh[:, t, :], ALU.mult, [tm, zh], [og])
                    pb_, tb_ = rpb.next()
                    P.transpose(pb_, og[:, :], identb[:, :], [og, identb], [tb_])
                    P.copy(ogT[:, t * 128:(t + 1) * 128], pb_, [tb_], [ogT], q="act")
                P.dma(ov[:, hv, :], ogT[:, :], [ogT], [r_ogT])

    def ret_proj(self, i, j, b):
        P = self.P
        hT_d = self.scratch("hT_d", [128, 16, TOK], BF16)
        r_qT = self.scratch("r_qT", [D, TOK], BF16)
        r_kT = self.scratch("r_kT", [D, TOK], BF16)
        r_v = self.scratch("r_v", [TOK, 4096], BF16)
        r_gate = self.scratch("r_gate", [TOK, 4096], BF16)
        win = self.W["ret_w_in"].t[j].rearrange("(c p) n -> p c n", p=128)
        blocks = [(0, 256)] + [(256 + 512 * k, 512) for k in range(4)]
        with P.phase():
            hT = P.sbuf("rp_hT", [128, 16, TOK], BF16)
            P.dma(hT[:, :, :], hT_d[:, :, :], [hT_d], [hT])
            rope = P.sbuf("rp_rope", [128, 4, SEQ], F32)
            P.dma(rope[:, :, :], self.C["rope"][:, :, :], [], [rope])
            rrf = P.sbuf("rp_rrf", [128, 128], F32)
            rrb = P.sbuf("rp_rrb", [128, 128], BF16)
            P.dma(rrf[:, :], self.C["rrotT"][:, :], [], [rrf])
            P.copy(rrb[:, :], rrf[:, :], [rrf], [rrb])
            wp = [P.sbuf("rp_wp%d" % k, [128, 16, 512], BF16) for k in range(2)]
            qraw = [P.sbuf("rp_qraw%d" % k, [128, 512], BF16) for k in range(2)]
            T1 = [P.sbuf("rp_t1%d" % k, [128, 512], F32) for k in range(2)]
            T2 = [P.sbuf("rp_t2%d" % k, [128, 512], F32) for k in range(2)]
            qst = [P.sbuf("rp_qst%d" % k, [128, TOK], BF16) for k in range(2)]
            psA = [P.psum("rp_psA%d" % k, [128, 512], F32) for k in range(4)]
            psR = [P.psum("rp_psR%d" % k, [128, 512], F32) for k in range(2)]
            cnt = 0
            for pn in range(8):
                Wp = wp[pn % 2]
                P.dma(Wp[:, :, :], win[:, :, pn * 512:(pn + 1) * 512], [], [Wp], q="pool")
                for nb in range(4):
                    chunk = pn * 4 + nb
                    isk = chunk >= 16
                    typ = chunk % 2
                    sc = 0.0625 if isk else 1.0
                    QS = qst[chunk % 2]
                    for bi, (t0, n) in enumerate(blocks):
                        ps = psA[cnt % 4]
                        for k in range(16):
                            P.matmul(ps[:, 0:n], Wp[:, k, nb * 128:(nb + 1) * 128], hT[:, k, t0:t0 + n], k == 0, k == 15,
                                     [Wp, hT], [ps])
                        if bi == 0:
                            P.act(QS[:, 0:256], ps[:, 0:256], AF.Copy, [ps], [QS], scale=sc)
                        else:
                            QR = qraw[cnt % 2]
                            P.act(QR[:, :], ps[:, :], AF.Copy, [ps], [QR], scale=sc)
                            pr = psR[cnt % 2]
                            P.matmul(pr[:, :], rrb[:, :], QR[:, :], True, True, [rrb, QR], [pr])
                            x0 = t0 - 256
                            P.tt(T1[cnt % 2][:, :], QR[:, :], rope[:, 2 * typ, x0:x0 + 512], ALU.mult, [QR, rope], [T1[cnt % 2]])
                            P.tt(T2[cnt % 2][:, :], pr[:, :], rope[:, 2 * typ + 1, x0:x0 + 512], ALU.mult, [pr, rope],
                                 [T2[cnt % 2]])
                            P.tt(QS[:, t0:t0 + 512], T1[cnt % 2][:, :], T2[cnt % 2][:, :], ALU.add,
                                 [T1[cnt % 2], T2[cnt % 2]], [QS])
                        cnt += 1
                    dst = r_kT if isk else r_qT
                    row0 = (chunk % 16) * 128
                    P.dma(dst[row0:row0 + 128, :], QS[:, :], [QS], [dst])
        with P.phase():
            hT = P.sbuf("rv_hT", [128, 16, TOK], BF16)
            P.dma(hT[:, :, :], hT_d[:, :, :], [hT_d], [hT])
            wp = [P.sbuf("rv_wp%d" % k, [128, 16, 512], BF16) for k in range(2)]
            stage = [P.sbuf("rv_st%d" % k, [128, NT, 512], BF16) for k in range(2)]
            psA = [P.psum("rv_psA%d" % k, [128, 512], F32) for k in range(4)]
            cnt = 0
            for pn in range(16):
                Wp = wp[pn % 2]
                ST = stage[pn % 2]
                c0 = 4096 + pn * 512
                P.dma(Wp[:, :, :], win[:, :, c0:c0 + 512], [], [Wp], q="pool")
                for t in range(NT):
                    ps = psA[cnt % 4]
                    cnt += 1
                    for k in range(16):
                        P.matmul(ps[:, :], hT[:, k, t * 128:(t + 1) * 128], Wp[:, k, :], k == 0, k == 15, [hT, Wp], [ps])
                    P.act(ST[:, t, :], ps[:, :], AF.Copy if pn < 8 else AF.Silu, [ps], [ST])
                dst = r_v if pn < 8 else r_gate
                cc = (pn % 8) * 512
                P.dma(dst.t.rearrange("(t p) e -> p t e", p=128)[:, :, cc:cc + 512], ST[:, :, :], [ST], [dst])

    def ret_core(self, i, j, b):
        P = self.P
        r_qT = self.scratch("r_qT", [D, TOK], BF16)
        r_kT = self.scratch("r_kT", [D, TOK], BF16)
        r_v = self.scratch("r_v", [TOK, 4096], BF16)
        r_gate = self.scratch("r_gate", [TOK, 4096], BF16)
        r_ogT = self.scratch("r_ogT", [4096, TOK], BF16)
        with P.phase():
            def cload(nm, shape):
                t = P.sbuf("rc_" + nm, shape, F32)
                P.dma(t[:, :], self.C[nm][:, :], [], [t])
                return t
            relF = cload("relF", [128, 128])
            maskF = cload("maskF", [128, 128])
            relB = cload("relB", [128, 128])
            maskB = cload("maskB", [128, 128])
            posrowF = cload("posrowF", [128, 128])
            posrowB = cload("posrowB", [128, 128])
            poscol = cload("poscol", [128, 4])
            lg = P.sbuf("rc_lg", [128, 16], F32)
            P.dma(lg[:, :], self.W["ret_log_decay"].t[j].rearrange("d h -> (d h)").partition_broadcast(128), [], [lg])
            gnw = P.sbuf("rc_gnw", [128, 512], F32)
            P.dma(gnw[:, :], self.W["ret_gn_w"].t[j].partition_broadcast(128), [], [gnw])
            E1 = P.sbuf("rc_E1", [128, 128], F32)
            E2 = P.sbuf("rc_E2", [128, 128], F32)
            DT = P.sbuf("rc_DT", [128, 128], F32)
            TFq = P.sbuf("rc_TFq", [128, 128], F32)
            TBq = P.sbuf("rc_TBq", [128, 128], F32)
            cols = P.sbuf("rc_cols", [128, 4], F32)
            qT = P.sbuf("rc_qT", [128, 2, TOK], BF16)
            kT = P.sbuf("rc_kT", [128, 2, TOK], BF16)
            vh = P.sbuf("rc_v", [128, NT, 512], BF16)
            gh = P.sbuf("rc_g", [128, NT, 512], BF16)
            qdf = P.sbuf("rc_qdf", [128, 2, TOK], BF16)
            qdb = P.sbuf("rc_qdb", [128, 2, TOK], BF16)
            kdb_all = P.sbuf("rc_kdb", [128, NT, 256], BF16)
            kdf = [P.sbuf("rc_kdf%d" % k, [128, 256], BF16) for k in range(2)]
            ATs = [P.sbuf("rc_AT%d" % k, [128, 128], BF16) for k in range(2)]
            oacc = P.sbuf("rc_oacc", [128, NT, 512], F32)
            S32 = [P.sbuf("rc_S32_%d" % k, [128, 2, 512], F32) for k in range(2)]
            S16 = [P.sbuf("rc_S16_%d" % k, [128, 2, 512], BF16) for k in range(2)]
            ogT = P.sbuf("rc_ogT", [128, 4, TOK], BF16)
            ogt = [P.sbuf("rc_ogt%d" % k, [128, 512], BF16) for k in range(2)]
            tmp = [P.sbuf("rc_tmp%d" % k, [128, 512], F32) for k in range(2)]
            junk = P.sbuf("rc_junk", [128, 512], F32)
            st = [P.sbuf("rc_st%d" % k, [128, 8], F32) for k in range(2)]
            psAT = [P.psum("rc_psAT%d" % k, [128, 512], F32) for k in range(2)]
            psK = [P.psum("rc_psK%d" % k, [128, 512], BF16) for k in range(2)]
            psO = [P.psum("rc_psO%d" % k, [128, 512], F32) for k in range(2)]
            psS = [P.psum("rc_psS%d" % k, [128, 512], F32) for k in range(2)]
            order_f = list(range(NT))
            order_b = [1, 0] + list(range(NT - 1, 1, -1))
            qv = r_qT.t.rearrange("(c p) t -> p c t", p=128)
            kv = r_kT.t.rearrange("(c p) t -> p c t", p=128)
            vv = r_v.t.rearrange("(t p) e -> p t e", p=128)
            gv = r_gate.t.rearrange("(t p) e -> p t e", p=128)
            ov = r_ogT.t.rearrange("(c p) t -> p c t", p=128)
            for h in range(8):
                lgf = lg[:, h:h + 1]
                lgb = lg[:, 8 + h:9 + h]
                P.dma(qT[:, :, :], qv[:, 2 * h:2 * h + 2, :], [r_qT], [qT])
                P.dma(kT[:, :, :], kv[:, 2 * h:2 * h + 2, :], [r_kT], [kT])
                P.dma(vh[:, :, :], vv[:, :, h * 512:(h + 1) * 512], [r_v], [vh])
                P.dma(gh[:, :, :], gv[:, :, h * 512:(h + 1) * 512], [r_gate], [gh])
                P.act(E1[:, :], relF[:, :], AF.Exp, [relF, lg], [E1], scale=lgf)
                P.tt(E1[:, :], E1[:, :], maskF[:, :], ALU.mult, [E1, maskF], [E1])
                P.act(E2[:, :], relB[:, :], AF.Exp, [relB, lg], [E2], scale=lgb)
                P.tt(E2[:, :], E2[:, :], maskB[:, :], ALU.mult, [E2, maskB], [E2])
                P.tt(DT[:, :], E1[:, :], E2[:, :], ALU.add, [E1, E2], [DT])
                P.act(TFq[:, :], posrowF[:, :], AF.Exp, [posrowF, lg], [TFq], scale=lgf)
                P.act(TBq[:, :], posrowB[:, :], AF.Exp, [posrowB, lg], [TBq], scale=lgb)
                P.act(cols[:, 0:1], poscol[:, 0:1], AF.Exp, [poscol, lg], [cols], scale=lgf)
                P.act(cols[:, 1:2], poscol[:, 1:2], AF.Exp, [poscol, lg], [cols], scale=lgb)
                P.act(cols[:, 2:3], lgf, AF.Exp, [lg], [cols], scale=128.0)
                P.act(cols[:, 3:4], lgb, AF.Exp, [lg], [cols], scale=128.0)
                for dc in range(2):
                    P.tt(qdf[:, dc, :].rearrange("p (t i) -> p t i", i=128), qT[:, dc, :].rearrange("p (t i) -> p t i", i=128),
                         TFq[:, :].unsqueeze(1).broadcast_to([128, NT, 128]), ALU.mult, [qT, TFq], [qdf])
                    P.tt(qdb[:, dc, :].rearrange("p (t i) -> p t i", i=128), qT[:, dc, :].rearrange("p (t i) -> p t i", i=128),
                         TBq[:, :].unsqueeze(1).broadcast_to([128, NT, 128]), ALU.mult, [qT, TBq], [qdb])
                Sf, Sfb = S32[0], S16[0]
                for n, t in enumerate(order_f):
                    sl = slice(t * 128, (t + 1) * 128)
                    pa = psAT[n % 2]
                    for dc in range(2):
                        P.matmul(pa[:, 0:128], kT[:, dc, sl], qT[:, dc, sl], dc == 0, dc == 1, [kT, qT], [pa])
                    AT = ATs[n % 2]
                    P.tt(AT[:, :], pa[:, 0:128], DT[:, :], ALU.mult, [pa, DT], [AT])
                    pk = psK[n % 2]
                    for dc in range(2):
                        P.transpose(pk[:, dc * 128:(dc + 1) * 128], kT[:, dc, sl], self.identb[:, :], [kT, self.identb], [pk])
                    KF = kdf[n % 2]
                    P.act(KF[:, :], pk[:, 0:256], AF.Copy, [pk, cols], [KF], scale=cols[:, 0:1])
                    P.ts(kdb_all[:, t, :], pk[:, 0:256], cols[:, 1:2], None, ALU.mult, None, [pk, cols], [kdb_all])
                    po = psO[n % 2]
                    P.matmul(po[:, :], AT[:, :], vh[:, t, :], True, n == 0, [AT, vh], [po])
                    if n > 0:
                        for dc in range(2):
                            P.matmul(po[:, :], qdf[:, dc, sl], Sfb[:, dc, :], False, dc == 1, [qdf, Sfb], [po])
                    P.copy(oacc[:, t, :], po[:, :], [po], [oacc], q="act")
                    for dc in range(2):
                        pS = psS[dc]
                        P.matmul(pS[:, :], KF[:, dc * 128:(dc + 1) * 128], vh[:, t, :], True, True, [KF, vh], [pS])
                        if n == 0:
                            P.copy(Sf[:, dc, :], pS[:, :], [pS], [Sf])
                        else:
                            P.stt(Sf[:, dc, :], Sf[:, dc, :], cols[:, 2:3], pS[:, :], ALU.mult, ALU.add, [Sf, cols, pS], [Sf])
                    if n < NT - 1:
                        P.copy(Sfb[:, :, :], Sf[:, :, :], [Sf], [Sfb], q="act")
                Sb, Sbb = S32[1], S16[1]
                for n, t in enumerate(order_b):
                    sl = slice(t * 128, (t + 1) * 128)
                    if n > 0:
                        po = psO[n % 2]
                        for dc in range(2):
                            P.matmul(po[:, :], qdb[:, dc, sl], Sbb[:, dc, :], dc == 0, dc == 1, [qdb, Sbb], [po])
                        P.tt(oacc[:, t, :], oacc[:, t, :], po[:, :], ALU.add, [oacc, po], [oacc])
                    for dc in range(2):
                        pS = psS[dc]
                        P.matmul(pS[:, :], kdb_all[:, t, dc * 128:(dc + 1) * 128], vh[:, t, :], True, True, [kdb_all, vh], [pS])
                        if n == 0:
                            P.copy(Sb[:, dc, :], pS[:, :], [pS], [Sb])
                        else:
                            P.stt(Sb[:, dc, :], Sb[:, dc, :], cols[:, 3:4], pS[:, :], ALU.mult, ALU.add, [Sb, cols, pS], [Sb])
                    if n < NT - 1:
                        P.copy(Sbb[:, :, :], Sb[:, :, :], [Sb], [Sbb], q="act")
                for t in range(NT):
                    S = st[t % 2]
                    o = oacc[:, t, :]
                    P.memset(S[:, 0:2], 0.0, [S])
                    P.act(junk[:, :], o, AF.Identity, [oacc], [junk, S], accum_out=S[:, 0:1])
                    P.act(junk[:, :], o, AF.Square, [oacc], [junk, S], accum_out=S[:, 1:2])
                    P.ts(S[:, 2:3], S[:, 0:1], 1.0 / 512, None, ALU.mult, None, [S], [S])
                    P.tt(S[:, 3:4], S[:, 2:3], S[:, 2:3], ALU.mult, [S], [S])
                    P.stt(S[:, 4:5], S[:, 1:2], 1.0 / 512, S[:, 3:4], ALU.mult, ALU.subtract, [S], [S])
                    P.ts(S[:, 4:5], S[:, 4:5], EPS, None, ALU.add, None, [S], [S])
                    P.act(S[:, 5:6], S[:, 4:5], AF.Sqrt, [S], [S])
                    P.recip(S[:, 6:7], S[:, 5:6], [S], [S])
                    TM = tmp[t % 2]
                    P.ts(TM[:, :], o, S[:, 2:3], S[:, 6:7], ALU.subtract, ALU.mult, [oacc, S], [TM])
                    P.tt(TM[:, :], TM[:, :], gnw[:, :], ALU.mult, [TM, gnw], [TM])
                    OG = ogt[t % 2]
                    P.tt(OG[:, :], TM[:, :], gh[:, t, :], ALU.mult, [TM, gh], [OG])
                    pk = psK[t % 2]
                    for ec in range(4):
                        P.transpose(pk[:, ec * 128:(ec + 1) * 128], OG[:, ec * 128:(ec + 1) * 128], self.identb[:, :],
                                    [OG, self.identb], [pk])
                    P.copy(ogT[:, :, t * 128:(t + 1) * 128], pk[:, :].rearrange("p (e i) -> p e i", e=4), [pk], [ogT],
                           q="act")
                P.dma(ov[:, 4 * h:4 * h + 4, :], ogT[:, :, :], [ogT], [r_ogT])


def build(cfg):
    nc = bass.Bass("TRN2", target_bir_lowering=False)
    with contextlib.ExitStack() as stack:
        M = Model(nc, stack, cfg)
        P = M.P
        M.copy_inputs()
        if cfg.get("only_gdn_core"):
            M.gdn_core(1, 0, 0)
            P.barrier()
            P.flush()
            return nc
        M.modulation()
        for i in cfg.get("layers", range(DEPTH)):
            if "mixer" in cfg.get("phases", ["mixer", "ffn"]):
                M.mixer(i)
            if "ffn" in cfg.get("phases", ["mixer", "ffn"]):
                M.ffn(i)
        if cfg.get("final", True):
            M.final_norm()
        for nm in cfg.get("dump", []):
            src = M._scr[nm]
            shp = list(src.t.shape)
            dt = src.t.dtype
            do = P.dram("dump_" + nm, shp, dt, kind="ExternalOutput")
            if len(shp) == 2:
                P.dma(do[:, :], src[:, :], [src], [do])
            else:
                P.dma(do[:, :, :], src[:, :, :], [src], [do])
        if cfg.get("debug_ctx"):
            co = P.dram("ctx_out", [BPC, CTXL, D], F32, kind="ExternalOutput")
            for b in range(BPC):
                P.dma(co[b, :, :], M.ctxs[b, :, :], [M.rtok[b][0], M.rtok[b][1]], [co])
        P.barrier()
        P.flush()
        print("ops", P.nops, "blocks", P.nblocks)
    return nc


def make_in_maps(inputs, ncores=NCORES):
    consts = host_constants()
    maps = []
    for r in range(ncores):
        m = {}
        m["x"] = np.ascontiguousarray(inputs["x"][r * BPC:(r + 1) * BPC])
        m["ctx"] = np.ascontiguousarray(inputs["ctx"][r * BPC:(r + 1) * BPC])
        m["cvec"] = np.ascontiguousarray(
            np.concatenate([inputs["c"][r * BPC:(r + 1) * BPC], inputs["c_ctx"][None, :]], axis=0))
        for nm in WEIGHT_NAMES:
            m[nm] = np.ascontiguousarray(inputs[nm])
        for nm, arr in consts.items():
            m["k_" + nm] = arr
        maps.append(m)
    return maps


GROUP = 4


def kernel(**inputs):
    inputs = {k: np.asarray(v) for k, v in inputs.items()}
    nc = build({})
    maps = make_in_maps(inputs)
    outs = []
    for g0 in range(0, NCORES, GROUP):
        res = run_bass_kernel_spmd(nc, maps[g0:g0 + GROUP], core_ids=list(range(GROUP)))
        outs += [res.results[r]["out"] for r in range(GROUP)]
    return np.concatenate(outs, axis=0).astype(np.float32)
```

```python
import contextlib
import math
import numpy as np
import ml_dtypes
import concourse.bass as bass
import concourse.mybir as mybir
from concourse.bass_utils import run_bass_kernel_spmd

F32 = mybir.dt.float32
BF16 = mybir.dt.bfloat16
ALU = mybir.AluOpType
AF = mybir.ActivationFunctionType
AX = mybir.AxisListType

D = 2048
SEQ = 2048
CTXL = 256
TOK = SEQ + CTXL
NT = TOK // 128
DEPTH = 4
DFF = 8192
EPS = 1e-6
NCORES = 4
BPC = 4
NM = BPC + 1


class Tok:
    __slots__ = ("name", "lw", "rs")

    def __init__(self, name=""):
        self.name = name
        self.lw = None
        self.rs = {}


class TB:
    def __init__(self, t, name=""):
        self.t = t
        self.tok = Tok(name)

    def __getitem__(self, idx):
        return self.t[idx]


def _toks(lst):
    out = []
    for x in lst:
        if x is None:
            continue
        out.append(x.tok if isinstance(x, TB) else x)
    return out


class Prog:
    STREAMS = ["pe", "act", "dve", "pool", "sp"]
    NDMA = 8

    def __init__(self, nc, stack):
        self.nc = nc
        self.stack = stack
        self.cur = stack
        self.streams = {q: [] for q in self.STREAMS}
        self.count = {}
        self.known = {q: {} for q in self.STREAMS}
        self.dma_rr = {q: 0 for q in self.STREAMS}
        self.semh = {}
        self.nops = 0
        self.nblocks = 0
        for q in self.STREAMS:
            self._sem((q, -1, 0))
        for q in ("sp", "pool"):
            for k in range(self.NDMA):
                self._sem((q, k, 0))

    def _sem(self, key):
        if key not in self.semh:
            nm = "s_%s_%d_%d" % (key[0], key[1] + 1, key[2])
            self.semh[key] = self.stack.enter_context(self.nc.semaphore(nm))
            self.count.setdefault(key, 0)
        return self.semh[key]

    def sbuf(self, name, shape, dtype):
        self.uid = getattr(self, "uid", 0) + 1
        name = "%s_u%d" % (name, self.uid)
        t = self.cur.enter_context(self.nc.sbuf_tensor(name, list(shape), dtype))
        return TB(t, name)

    def psum(self, name, shape, dtype=F32):
        self.uid = getattr(self, "uid", 0) + 1
        name = "%s_u%d" % (name, self.uid)
        t = self.cur.enter_context(self.nc.psum_tensor(name, list(shape), dtype))
        return TB(t, name)

    def dram(self, name, shape, dtype, kind="Internal"):
        t = self.nc.dram_tensor(name, list(shape), dtype, kind=kind)
        return TB(t.ap(), name)

    @contextlib.contextmanager
    def phase(self):
        prev = self.cur
        with contextlib.ExitStack() as st:
            self.cur = st
            yield
            self.barrier()
            self.flush()
        self.cur = prev

    LIMIT = 20000

    def op(self, stream, fn, reads=(), writes=(), dma=False):
        gens = self.__dict__.setdefault("gens", {})
        if dma:
            k = self.dma_rr[stream]
            self.dma_rr[stream] = (k + 1) % self.NDMA
            base = (stream, k)
            step = 16
        else:
            base = (stream, -1)
            step = 1
        g = gens.get(base, 0)
        key = base + (g,)
        if self.count.get(key, 0) + step > self.LIMIT:
            g += 1
            gens[base] = g
            key = base + (g,)
        self._sem(key)
        n = self.count.get(key, 0) + step
        self.count[key] = n
        deps = {}
        reads = _toks(reads)
        writes = _toks(writes)
        for b in reads:
            if b.lw is not None and deps.get(b.lw[0], 0) < b.lw[1]:
                deps[b.lw[0]] = b.lw[1]
        for b in writes:
            if b.lw is not None and deps.get(b.lw[0], 0) < b.lw[1]:
                deps[b.lw[0]] = b.lw[1]
            for kk, v in b.rs.items():
                if deps.get(kk, 0) < v:
                    deps[kk] = v
        known = self.known[stream]
        waits = []
        for kk, v in deps.items():
            if stream == "pe" and kk[0] == "pe" and kk[1] == -1:
                continue
            if known.get(kk, 0) >= v:
                continue
            known[kk] = v
            waits.append((kk, v))
        self.streams[stream].append((waits, fn, key, step))
        for b in reads:
            if b.rs.get(key, 0) < n:
                b.rs[key] = n
        for b in writes:
            b.lw = (key, n)
            b.rs = {}
        self.nops += 1

    def barrier(self):
        for s in self.STREAMS:
            known = self.known[s]
            waits = []
            for kk, v in self.count.items():
                if v > 0 and known.get(kk, 0) < v:
                    known[kk] = v
                    waits.append((kk, v))
            if waits:
                self.streams[s].append((waits, None, None, 0))

    def flush(self):
        nc = self.nc
        semh = self.semh
        st = self.streams
        self.streams = {q: [] for q in self.STREAMS}
        if not any(st.values()):
            return
        self.nblocks += 1

        def replay(eng, lst):
            for waits, fn, key, step in lst:
                for kk, v in waits:
                    eng.wait_ge(semh[kk], v)
                if fn is not None:
                    ins = fn(eng)
                    ins.then_inc(semh[key], step)

        with nc.Block() as block:
            @block.tensor
            def _(e):
                replay(e, st["pe"])

            @block.scalar
            def _(e):
                replay(e, st["act"])

            @block.vector
            def _(e):
                replay(e, st["dve"])

            @block.gpsimd
            def _(e):
                replay(e, st["pool"])

            @block.sync
            def _(e):
                replay(e, st["sp"])

    def dma(self, out, in_, reads, writes, q="sp"):
        self.op(q, lambda e: e.dma_start(out=out, in_=in_), reads, writes, dma=True)

    def matmul(self, out, lhsT, rhs, start, stop, reads, writes):
        self.op("pe", lambda e: e.matmul(out, lhsT, rhs, start=start, stop=stop), reads, writes)

    def transpose(self, out, in_, ident, reads, writes):
        self.op("pe", lambda e: e.transpose(out, in_, ident), reads, writes)

    def act(self, out, in_, func, reads, writes, bias=None, scale=None, accum_out=None):
        kw = {}
        if bias is not None:
            kw["bias"] = bias
        if scale is not None:
            kw["scale"] = scale
        if accum_out is not None:
            kw["accum_out"] = accum_out
        self.op("act", lambda e: e.activation(out, in_, func, **kw), reads, writes)

    def tt(self, out, in0, in1, op, reads, writes, q="dve"):
        self.op(q, lambda e: e.tensor_tensor(out, in0, in1, op), reads, writes)

    def ts(self, out, in0, s1, s2, op0, op1, reads, writes, q="dve"):
        if op1 is None:
            self.op(q, lambda e: e.tensor_scalar(out, in0, s1, None, op0), reads, writes)
        else:
            self.op(q, lambda e: e.tensor_scalar(out, in0, s1, s2, op0, op1), reads, writes)

    def stt(self, out, in0, scalar, in1, op0, op1, reads, writes, q="dve"):
        self.op(q, lambda e: e.scalar_tensor_tensor(out, in0, scalar, in1, op0, op1), reads, writes)

    def copy(self, out, in_, reads, writes, q="dve"):
        if q == "act":
            self.op(q, lambda e: e.activation(out, in_, AF.Copy), reads, writes)
        else:
            self.op(q, lambda e: e.tensor_copy(out, in_), reads, writes)

    def memset(self, ap, val, writes, q="dve"):
        self.op(q, lambda e: e.memset(ap, val), (), writes)

    def recip(self, out, in_, reads, writes):
        self.op("dve", lambda e: e.reciprocal(out, in_), reads, writes)


WEIGHT_NAMES = ["mod_w", "mod_b", "norm_w", "final_norm_w", "ffn_w1", "ffn_w2",
                "s5_lam_re", "s5_lam_im", "s5_log_dt", "s5_b_re", "s5_b_im", "s5_c_re", "s5_c_im", "s5_d",
                "s5_glu_w", "gdn_w_in", "gdn_conv_w", "gdn_a_log", "gdn_dt_bias", "gdn_norm_w", "gdn_w_out",
                "ret_w_in", "ret_log_decay", "ret_gn_w", "ret_w_out"]
WEIGHT_SHAPES = {
    "mod_w": (4, 2048, 12288), "mod_b": (4, 12288), "norm_w": (4, 2, 2048), "final_norm_w": (2048,),
    "ffn_w1": (4, 2048, 8192), "ffn_w2": (4, 8192, 2048),
    "s5_lam_re": (2, 2, 128, 64), "s5_lam_im": (2, 2, 128, 64), "s5_log_dt": (2, 2, 128),
    "s5_b_re": (2, 2, 128, 64, 16), "s5_b_im": (2, 2, 128, 64, 16),
    "s5_c_re": (2, 2, 128, 16, 64), "s5_c_im": (2, 2, 128, 16, 64), "s5_d": (2, 2048),
    "s5_glu_w": (2, 2048, 4096), "gdn_w_in": (1, 2048, 12416), "gdn_conv_w": (1, 5, 8192),
    "gdn_a_log": (1, 2, 32), "gdn_dt_bias": (1, 2, 32), "gdn_norm_w": (1, 128), "gdn_w_out": (1, 4096, 2048),
    "ret_w_in": (1, 2048, 12288), "ret_log_decay": (1, 2, 8), "ret_gn_w": (1, 512), "ret_w_out": (1, 4096, 2048),
}


def host_constants():
    c = {}
    c["ident"] = np.eye(128, dtype=np.float32)
    c["ones"] = np.ones((128, 128), dtype=np.float32)
    R = np.zeros((128, 128), np.float32)
    for m in range(64):
        R[m, m + 64] = -1.0
        R[m + 64, m] = 1.0
    c["rrotT"] = np.ascontiguousarray(R.T)
    half = 64
    inv_freq = (10000.0 ** (-np.arange(half, dtype=np.float32) / half)).astype(np.float32)
    tok = np.arange(SEQ)
    row = (tok // 64).astype(np.float32)
    col = (tok % 64).astype(np.float32)
    fr = np.concatenate([inv_freq, inv_freq])
    ar = (row[None, :] * fr[:, None]).astype(np.float32)
    ac = (col[None, :] * fr[:, None]).astype(np.float32)
    c["rope"] = np.stack([np.cos(ar), np.sin(ar), np.cos(ac), np.sin(ac)], axis=1).astype(np.float32)
    jj = np.arange(128)[:, None]
    ii = np.arange(128)[None, :]
    c["relF"] = np.maximum(ii - jj, 0).astype(np.float32)
    c["maskF"] = (ii >= jj).astype(np.float32)
    c["relB"] = np.maximum(jj - ii, 0).astype(np.float32)
    c["maskB"] = (jj >= ii).astype(np.float32)
    c["posrowF"] = np.broadcast_to((ii + 1.0), (128, 128)).astype(np.float32).copy()
    c["posrowB"] = np.broadcast_to((128.0 - ii), (128, 128)).astype(np.float32).copy()
    pc = np.zeros((128, 4), np.float32)
    pc[:, 0] = 127.0 - np.arange(128)
    pc[:, 1] = np.arange(128)
    c["poscol"] = pc
    r_ = np.arange(128)[:, None]
    c_ = np.arange(128)[None, :]
    c["LI"] = (r_ >= c_).astype(np.float32)
    c["LS"] = (r_ > c_).astype(np.float32)
    c["UI"] = (r_ <= c_).astype(np.float32)
    c["US"] = (r_ < c_).astype(np.float32)
    idx = np.arange(128)
    c["blk16"] = ((idx[:, None] // 16) == (idx[None, :] // 16)).astype(np.float32)
    for bs in (16, 32, 64):
        c["lvl%d" % bs] = (((idx[:, None] // (2 * bs)) == (idx[None, :] // (2 * bs)))
                           & ((idx[:, None] // bs) != (idx[None, :] // bs))).astype(np.float32)
    rr = np.arange(128)
    s_of = rr // 16
    c["s5maskF"] = (s_of[None, :] >= s_of[:, None]).astype(np.float32)
    c["s5maskB"] = (s_of[:, None] >= s_of[None, :]).astype(np.float32)
    c["iota288"] = np.broadcast_to(np.arange(288, dtype=np.float32), (128, 288)).copy()
    sel16 = np.zeros((16, 128), np.float32)
    for r in range(128):
        sel16[r % 16, r] = 1.0
    c["s5sel16"] = sel16
    selT = np.zeros((128, 64, 128), np.float32)
    for g8 in range(8):
        for t in range(8):
            for cc in range(16):
                selT[t * 16 + cc, g8 * 8 + t, g8 * 16 + cc] = 1.0
    c["s5selT"] = selT
    return c


class Model:
    def __init__(self, nc, stack, cfg):
        self.cfg = cfg
        P = self.P = Prog(nc, stack)
        self.x_in = P.dram("x", [BPC, SEQ, D], F32, kind="ExternalInput")
        self.ctx_in = P.dram("ctx", [BPC, CTXL, D], F32, kind="ExternalInput")
        self.cvec = P.dram("cvec", [NM, D], F32, kind="ExternalInput")
        self.W = {}
        for nm in WEIGHT_NAMES:
            kind = "ExternalInput"
            if cfg.get("internal_weights") and nm not in cfg.get("keep_weights", []):
                self.W[nm] = P.dram(nm, [2, 2], F32, kind="Internal")
                continue
            self.W[nm] = P.dram(nm, list(WEIGHT_SHAPES[nm]), F32, kind=kind)
        self.C = {}
        for nm, arr in host_constants().items():
            self.C[nm] = P.dram("k_" + nm, list(arr.shape), F32, kind="ExternalInput")
        self.out = P.dram("out", [BPC, SEQ, D], F32, kind="ExternalOutput")
        self.ctxs = P.dram("ctxs", [BPC, CTXL, D], F32)
        self.rtok = [[Tok("r%d_%d" % (b, t)) for t in range(NT)] for b in range(BPC)]
        self.ident = P.sbuf("ident", [128, 128], F32)
        self.ones = P.sbuf("ones", [128, 128], F32)
        self.identb = P.sbuf("identb", [128, 128], BF16)
        self.modT = P.sbuf("modT", [128, DEPTH, 96, NM], F32)
        self.nwT = P.sbuf("nwT", [128, DEPTH * 2 + 1, 16], F32)
        self.A = P.sbuf("Amod", [128, DEPTH, 2, 16, NM], F32)
        P.dma(self.ident[:, :], self.C["ident"][:, :], [self.C["ident"]], [self.ident])
        P.dma(self.ones[:, :], self.C["ones"][:, :], [self.C["ones"]], [self.ones])
        P.copy(self.identb[:, :], self.ident[:, :], [self.ident], [self.identb])

    def res_ap(self, b, t):
        if t < 2:
            return self.ctxs[b, t * 128:(t + 1) * 128, :]
        return self.out[b, (t - 2) * 128:(t - 1) * 128, :]

    def copy_inputs(self):
        P = self.P
        for b in range(BPC):
            P.dma(self.ctxs[b, :, :], self.ctx_in[b, :, :], [self.ctx_in], [self.rtok[b][0], self.rtok[b][1]])
            for q in range(4):
                P.dma(self.out[b, q * 512:(q + 1) * 512, :], self.x_in[b, q * 512:(q + 1) * 512, :], [self.x_in],
                      [self.rtok[b][2 + q * 4 + k] for k in range(4)])

    def modulation(self):
        P = self.P
        with P.phase():
            cs = P.sbuf("cs", [NM, D], F32)
            scT = P.sbuf("scT", [128, 16, NM], F32)
            pst = P.psum("pst", [128, 512], F32)
            P.dma(cs[:, :], self.cvec[:, :], [self.cvec], [cs])
            P.act(cs[:, :], cs[:, :], AF.Silu, [cs], [cs])
            for k in range(16):
                P.transpose(pst[:, k * NM:(k + 1) * NM], cs[0:NM, k * 128:(k + 1) * 128], self.ident[0:NM, 0:NM],
                            [cs, self.ident], [pst])
            P.copy(scT[:, :, :], pst[:, 0:16 * NM].rearrange("p (k m) -> p k m", m=NM), [pst], [scT])
            nwr = P.sbuf("nwr", [128, 128], F32)
            nwr2 = P.sbuf("nwr2", [16, 128], F32)
            P.dma(nwr[:, :], self.W["norm_w"].t.rearrange("l w (c p) -> (l w c) p", p=128),
                  [self.W["norm_w"]], [nwr])
            P.dma(nwr2[:, :], self.W["final_norm_w"].t.rearrange("(c p) -> c p", p=128),
                  [self.W["final_norm_w"]], [nwr2])
            pst2 = P.psum("pst2", [128, 512], F32)
            P.transpose(pst2[:, 0:128], nwr[:, :], self.ident[:, :], [nwr, self.ident], [pst2])
            P.transpose(pst2[:, 128:144], nwr2[:, :], self.ident[0:16, 0:16], [nwr2, self.ident], [pst2])
            P.copy(self.nwT[:, :, :], pst2[:, 0:144].rearrange("p (l c) -> p l c", c=16), [pst2], [self.nwT])
            wp = [P.sbuf("modw%d" % i, [128, 16, 512], F32) for i in range(2)]
            pm = [P.psum("pm%d" % i, [128, 512], F32) for i in range(2)]
            mbr = P.sbuf("mbr", [96, 128], F32)
            mbT = P.sbuf("mbT", [128, 96], F32)
            pb = P.psum("pb", [128, 512], F32)
            for i in self.cfg.get("layers", range(DEPTH)):
                P.dma(mbr[:, :], self.W["mod_b"].t[i].rearrange("(c p) -> c p", p=128), [self.W["mod_b"]], [mbr])
                P.transpose(pb[:, 0:96], mbr[0:96, :], self.ident[0:96, 0:96], [mbr, self.ident], [pb])
                P.copy(mbT[:, :], pb[:, 0:96], [pb], [mbT])
                wv = self.W["mod_w"].t[i].rearrange("(c p) n -> p c n", p=128)
                for pn in range(24):
                    w = wp[pn % 2]
                    P.dma(w[:, :, :], wv[:, :, pn * 512:(pn + 1) * 512], [self.W["mod_w"]], [w])
                    ps = pm[pn % 2]
                    for nb in range(4):
                        for k in range(16):
                            P.matmul(ps[:, nb * NM:(nb + 1) * NM], w[:, k, nb * 128:(nb + 1) * 128], scT[:, k, :],
                                     k == 0, k == 15, [w, scT], [ps])
                    for nb in range(4):
                        P.ts(self.modT[:, i, pn * 4 + nb, :], ps[:, nb * NM:(nb + 1) * NM], mbT[:, pn * 4 + nb:pn * 4 + nb + 1],
                             None, ALU.add, None, [ps, mbT], [self.modT])
                for wh in range(2):
                    sc = self.modT[:, i, (1 + 3 * wh) * 16:(2 + 3 * wh) * 16, :]
                    for m in range(NM):
                        P.stt(self.A[:, i, wh, :, m], self.modT[:, i, (1 + 3 * wh) * 16:(2 + 3 * wh) * 16, m], 1.0,
                              self.nwT[:, i * 2 + wh, :], ALU.add, ALU.mult, [self.modT, self.nwT], [self.A])

    def gate_rows(self, i, wh, m, gtile, ps):
        P = self.P
        base = (2 + 3 * wh) * 16
        for c4 in range(4):
            for cc in range(4):
                c = c4 * 4 + cc
                dg = self._dg[c % 2]
                P.ts(dg[:, :], self.ident[:, :], self.modT[:, i, base + c, m:m + 1], None, ALU.mult, None,
                     [self.ident, self.modT], [dg])
                P.matmul(ps[:, cc * 128:(cc + 1) * 128], self.ones[:, :], dg[:, :], True, True, [self.ones, dg], [ps])
            P.copy(gtile[:, c4 * 512:(c4 + 1) * 512], ps[:, :], [ps], [gtile], q="act")

    def norm_tiles(self, i, wh, b, tiles, hT, hT_off, bufs, deint=False):
        P = self.P
        xt, xn, ss, pss = bufs
        for k, t in enumerate(tiles):
            m = BPC if t < 2 else b
            X = xt[k % 2]
            XN = xn[k % len(xn)]
            S = ss[k % 2]
            P.dma(X[:, :], self.res_ap(b, t), [self.rtok[b][t]], [X])
            P.memset(S[:, 0:1], 0.0, [S])
            P.act(XN[:, :], X[:, :], AF.Square, [X], [XN, S], accum_out=S[:, 0:1])
            P.ts(S[:, 1:2], S[:, 0:1], 1.0 / D, EPS, ALU.mult, ALU.add, [S], [S])
            P.act(S[:, 2:3], S[:, 1:2], AF.Sqrt, [S], [S])
            P.recip(S[:, 3:4], S[:, 2:3], [S], [S])
            P.act(XN[:, :], X[:, :], AF.Copy, [X, S], [XN], scale=S[:, 3:4])
            for c4 in range(4):
                ps = pss[(k * 4 + c4) % len(pss)]
                for cc in range(4):
                    c = c4 * 4 + cc
                    P.transpose(ps[:, cc * 128:(cc + 1) * 128], XN[:, c * 128:(c + 1) * 128], self.ident[:, :],
                                [XN, self.ident], [ps])
                for cc in range(4):
                    c = c4 * 4 + cc
                    if deint:
                        o = hT[:, c, :, k * 16:(k + 1) * 16]
                        pin = ps[:, cc * 128:(cc + 1) * 128].rearrange("p (j s) -> p s j", s=8)
                    else:
                        o = hT[:, c, hT_off + k * 128:hT_off + (k + 1) * 128]
                        pin = ps[:, cc * 128:(cc + 1) * 128]
                    P.act(o, pin, AF.Identity, [ps, self.A, self.modT], [hT],
                          scale=self.A[:, i, wh, c, m:m + 1], bias=self.modT[:, i, (3 * wh) * 16 + c, m:m + 1])

    def ffn(self, i):
        P = self.P
        TBT = 6
        NTOK = TBT * 128
        with P.phase():
            self._dg = [P.sbuf("dg%d" % k, [128, 128], F32) for k in range(2)]
            hT = P.sbuf("f_hT", [128, 16, NTOK], BF16)
            yacc = P.sbuf("f_yacc", [128, TBT, D], F32)
            uT = [P.sbuf("f_uT%d" % k, [128, 4, NTOK], BF16) for k in range(2)]
            w1p = [P.sbuf("f_w1p%d" % k, [128, 16, 512], BF16) for k in range(2)]
            w2p = [P.sbuf("f_w2p%d" % k, [128, 4, D], BF16) for k in range(2)]
            rl = [P.sbuf("f_rl%d" % k, [128, NTOK], F32) for k in range(1)]
            G = [P.sbuf("f_G%d" % k, [128, D], F32) for k in range(2)]
            xt = [P.sbuf("f_xt%d" % k, [128, D], F32) for k in range(2)]
            xn = [P.sbuf("f_xn%d" % k, [128, D], F32) for k in range(1)]
            ss = [P.sbuf("f_ss%d" % k, [128, 4], F32) for k in range(2)]
            psu = [P.psum("f_psu%d" % k, [128, 1024], F32) for k in range(2)]
            psd = [P.psum("f_psd%d" % k, [128, 512], F32) for k in range(4)]
            w1v = self.W["ffn_w1"].t[i].rearrange("(c p) n -> p c n", p=128)
            w2v = self.W["ffn_w2"].t[i].rearrange("(c p) n -> p c n", p=128)
            for b in range(BPC):
                self.gate_rows(i, 1, b, G[0], psd[0])
                self.gate_rows(i, 1, BPC, G[1], psd[1])
                for blk in range(NT // TBT):
                    tiles = list(range(blk * TBT, (blk + 1) * TBT))
                    self.norm_tiles(i, 1, b, tiles, hT, 0, (xt, xn, ss, psd))
                    for hb in range(DFF // 512):
                        W1 = w1p[hb % 2]
                        W2 = w2p[hb % 2]
                        U = uT[hb % 2]
                        P.dma(W1[:, :, :], w1v[:, :, hb * 512:(hb + 1) * 512], [self.W["ffn_w1"]], [W1], q="pool")
                        P.dma(W2[:, :, :], w2v[:, hb * 4:(hb + 1) * 4, :], [self.W["ffn_w2"]], [W2], q="pool")
                        for nb in range(4):
                            ps = psu[nb % 2]
                            R = rl[0]
                            for k in range(16):
                                P.matmul(ps[:, 0:512], W1[:, k, nb * 128:(nb + 1) * 128], hT[:, k, 0:512], k == 0, k == 15,
                                         [W1, hT], [ps])
                            for k in range(16):
                                P.matmul(ps[:, 512:512 + NTOK - 512], W1[:, k, nb * 128:(nb + 1) * 128], hT[:, k, 512:NTOK],
                                         k == 0, k == 15, [W1, hT], [ps])
                            P.act(R[:, :], ps[:, 0:NTOK], AF.Relu, [ps], [R])
                            P.tt(U[:, nb, :], R[:, :], R[:, :], ALU.mult, [R], [U])
                        for k, t in enumerate(tiles):
                            for db in range(4):
                                ps = psd[(k * 4 + db) % 4]
                                for kc in range(4):
                                    P.matmul(ps[:, :], U[:, kc, k * 128:(k + 1) * 128], W2[:, kc, db * 512:(db + 1) * 512],
                                             kc == 0, kc == 3, [U, W2], [ps])
                                ya = yacc[:, k, db * 512:(db + 1) * 512]
                                if hb == 0:
                                    P.copy(ya, ps[:, :], [ps], [yacc])
                                else:
                                    P.tt(ya, ps[:, :], ya, ALU.add, [ps, yacc], [yacc])
                    for k, t in enumerate(tiles):
                        X = xt[k % 2]
                        g = G[1] if t < 2 else G[0]
                        P.dma(X[:, :], self.res_ap(b, t), [self.rtok[b][t]], [X])
                        P.tt(yacc[:, k, :], yacc[:, k, :], g[:, :], ALU.mult, [yacc, g], [yacc])
                        P.tt(X[:, :], X[:, :], yacc[:, k, :], ALU.add, [X, yacc], [X])
                        P.dma(self.res_ap(b, t), X[:, :], [X], [self.rtok[b][t]])

    def final_norm(self):
        P = self.P
        with P.phase():
            xt = [P.sbuf("n_xt%d" % k, [128, D], F32) for k in range(2)]
            xn = [P.sbuf("n_xn%d" % k, [128, D], F32) for k in range(2)]
            ss = [P.sbuf("n_ss%d" % k, [128, 4], F32) for k in range(2)]
            wrow = P.sbuf("n_wrow", [128, D], F32)
            ps = P.psum("n_ps", [128, 512], F32)
            dg = [P.sbuf("n_dg%d" % k, [128, 128], F32) for k in range(2)]
            for c4 in range(4):
                for cc in range(4):
                    c = c4 * 4 + cc
                    P.ts(dg[c % 2][:, :], self.ident[:, :], self.nwT[:, 2 * DEPTH, c:c + 1], None, ALU.mult, None,
                         [self.ident, self.nwT], [dg[c % 2]])
                    P.matmul(ps[:, cc * 128:(cc + 1) * 128], self.ones[:, :], dg[c % 2][:, :], True, True,
                             [self.ones, dg[c % 2]], [ps])
                P.copy(wrow[:, c4 * 512:(c4 + 1) * 512], ps[:, :], [ps], [wrow])
            k = 0
            for b in range(BPC):
                for t in range(2, NT):
                    X = xt[k % 2]
                    XN = xn[k % 2]
                    S = ss[k % 2]
                    k += 1
                    P.dma(X[:, :], self.res_ap(b, t), [self.rtok[b][t]], [X])
                    P.memset(S[:, 0:1], 0.0, [S])
                    P.act(XN[:, :], X[:, :], AF.Square, [X], [XN, S], accum_out=S[:, 0:1])
                    P.ts(S[:, 1:2], S[:, 0:1], 1.0 / D, EPS, ALU.mult, ALU.add, [S], [S])
                    P.act(S[:, 2:3], S[:, 1:2], AF.Sqrt, [S], [S])
                    P.recip(S[:, 3:4], S[:, 2:3], [S], [S])
                    P.stt(XN[:, :], X[:, :], S[:, 3:4], wrow[:, :], ALU.mult, ALU.mult, [X, S, wrow], [XN])
                    P.dma(self.res_ap(b, t), XN[:, :], [XN], [self.rtok[b][t]])


    def scratch(self, name, shape, dtype):
        if not hasattr(self, "_scr"):
            self._scr = {}
        if name not in self._scr:
            self._scr[name] = self.P.dram(name, shape, dtype)
        return self._scr[name]

    def norm_to_dram(self, i, wh, b):
        P = self.P
        hT_d = self.scratch("hT_d", [128, 16, TOK], BF16)
        with P.phase():
            hT = P.sbuf("nd_hT", [128, 16, TOK], BF16)
            xt = [P.sbuf("nd_xt%d" % k, [128, D], F32) for k in range(2)]
            xn = [P.sbuf("nd_xn%d" % k, [128, D], F32) for k in range(2)]
            ss = [P.sbuf("nd_ss%d" % k, [128, 4], F32) for k in range(2)]
            pss = [P.psum("nd_ps%d" % k, [128, 512], F32) for k in range(4)]
            self.norm_tiles(i, wh, b, list(range(NT)), hT, 0, (xt, xn, ss, pss))
            P.dma(hT_d[:, :, :], hT[:, :, :], [hT], [hT_d])
        return hT_d

    def outproj_residual(self, i, b, aT_d, wout):
        P = self.P
        TBT = 6
        NTB = TBT * 128
        with P.phase():
            self._dg = [P.sbuf("dg%d" % k, [128, 128], F32) for k in range(2)]
            aT = P.sbuf("op_aT", [128, 32, NTB], BF16)
            yacc = P.sbuf("op_y", [128, TBT, D], F32)
            wp = [P.sbuf("op_wp%d" % k, [128, 32, 256], BF16) for k in range(2)]
            G = [P.sbuf("op_G%d" % k, [128, D], F32) for k in range(2)]
            xt = [P.sbuf("op_xt%d" % k, [128, D], F32) for k in range(2)]
            ps = [P.psum("op_ps%d" % k, [128, 512], F32) for k in range(4)]
            wv = wout.rearrange("(c p) n -> p c n", p=128)
            av = aT_d.t.rearrange("(c p) t -> p c t", p=128)
            self.gate_rows(i, 0, b, G[0], ps[0])
            self.gate_rows(i, 0, BPC, G[1], ps[1])
            cnt = 0
            for blk in range(NT // TBT):
                tiles = list(range(blk * TBT, (blk + 1) * TBT))
                P.dma(aT[:, :, :], av[:, :, blk * NTB:(blk + 1) * NTB], [aT_d], [aT])
                for pn in range(8):
                    Wp = wp[pn % 2]
                    P.dma(Wp[:, :, :], wv[:, :, pn * 256:(pn + 1) * 256], [], [Wp], q="pool")
                    for k, t in enumerate(tiles):
                        pp = ps[cnt % 4]
                        cnt += 1
                        for kc in range(32):
                            P.matmul(pp[:, 0:256], aT[:, kc, k * 128:(k + 1) * 128], Wp[:, kc, :], kc == 0, kc == 31,
                                     [aT, Wp], [pp])
                        P.copy(yacc[:, k, pn * 256:(pn + 1) * 256], pp[:, 0:256], [pp], [yacc],
                               q="act" if cnt % 2 else "dve")
                for k, t in enumerate(tiles):
                    X = xt[k % 2]
                    g = G[1] if t < 2 else G[0]
                    P.dma(X[:, :], self.res_ap(b, t), [self.rtok[b][t]], [X])
                    P.tt(yacc[:, k, :], yacc[:, k, :], g[:, :], ALU.mult, [yacc, g], [yacc])
                    P.tt(X[:, :], X[:, :], yacc[:, k, :], ALU.add, [X, yacc], [X])
                    P.dma(self.res_ap(b, t), X[:, :], [X], [self.rtok[b][t]])

    def mixer(self, i):
        kind, j = i % 3, i // 3
        for b in range(BPC):
            if kind == 2:
                self.norm_to_dram(i, 0, b)
                self.ret_proj(i, j, b)
                self.ret_core(i, j, b)
                self.outproj_residual(i, b, self.scratch("r_ogT", [4096, TOK], BF16), self.W["ret_w_out"].t[j])
            elif kind == 0:
                self.s5(i, j, b)
            else:
                self.gdn(i, j, b)


    def sincos(self, x, t, r, sin_out, cos_out, halfpi, rd, tk):
        P = self.P
        TWO_PI = 2.0 * math.pi
        MAGIC = 12582912.0
        P.ts(t, x, 1.0 / TWO_PI, MAGIC, ALU.mult, ALU.add, rd, [tk["t"]])
        P.ts(t, t, -MAGIC, None, ALU.add, None, [tk["t"]], [tk["t"]])
        P.stt(r, t, -TWO_PI, x, ALU.mult, ALU.add, [tk["t"]] + rd, [tk["r"]])
        P.act(sin_out, r, AF.Sin, [tk["r"]], [tk["sin"]])
        P.stt(t, r, -1.0, r, ALU.mult, ALU.max, [tk["r"]], [tk["t"]])
        P.act(cos_out, t, AF.Sin, [tk["t"], halfpi], [tk["cos"]], scale=-1.0, bias=halfpi[:, 0:1])

    def s5_prep(self, i, j):
        P = self.P
        PI = math.pi
        s5F = [self.scratch("s5F%d" % d, [128, 16384], F32) for d in range(2)]
        s5FF = [self.scratch("s5FF%d" % d, [128, 16384], F32) for d in range(2)]
        s5EN = [self.scratch("s5EN%d" % d, [128, 16384], F32) for d in range(2)]
        s5E = [self.scratch("s5E%d" % d, [128, 16384], F32) for d in range(2)]
        s5M = self.scratch("s5M", [128, 16384], F32)
        if not hasattr(self, "s5cols"):
            self.s5cols = P.sbuf("s5cols", [128, 2, 2, 128], F32)
        with P.phase():
            big = P.sbuf("sp_big", [128, 16384], F32)
            PW = P.sbuf("sp_PW", [128, 16, 2, 64], F32)
            bre = P.sbuf("sp_bre", [128, 64, 16], F32)
            bim = P.sbuf("sp_bim", [128, 64, 16], F32)
            Btr = P.sbuf("sp_Btr", [128, 64, 16], F32)
            Bti = P.sbuf("sp_Bti", [128, 64, 16], F32)
            cre = P.sbuf("sp_cre", [128, 16, 64], F32)
            cim = P.sbuf("sp_cim", [128, 16, 64], F32)
            tA = [P.sbuf("sp_t%d" % k, [128, 1024], F32) for k in range(4)]
            sm = P.sbuf("sp_sm", [128, 16, 64], F32)
            dtc = P.sbuf("sp_dt", [128, 4], F32)
            dup = P.sbuf("sp_dup", [128, 128], F32)
            negpi = P.sbuf("sp_negpi", [128, 1], F32)
            P.memset(negpi[:, :], 0.5 * PI, [negpi])
            pst = P.psum("sp_pst", [128, 512], F32)
            X1, Y1, T, MAG, SN, CS, U2, V2, DEN, XR = [sm[:, k, :] for k in range(10)]
            XK, RR = sm[:, 14, :], sm[:, 15, :]
            for d in range(2):
                P.dma(sm[:, 10, :], self.W["s5_lam_re"].t[j, d], [], [sm])
                P.dma(sm[:, 11, :], self.W["s5_lam_im"].t[j, d], [], [sm])
                P.dma(dtc[:, 0:1], self.W["s5_log_dt"].t[j, d].rearrange("(g o) -> g o", o=1), [], [dtc])
                P.dma(bre[:, :, :], self.W["s5_b_re"].t[j, d], [], [bre])
                P.dma(bim[:, :, :], self.W["s5_b_im"].t[j, d], [], [bim])
                P.dma(cre[:, :, :], self.W["s5_c_re"].t[j, d], [], [cre])
                P.dma(cim[:, :, :], self.W["s5_c_im"].t[j, d], [], [cim])
                LRE, LIM = sm[:, 10, :], sm[:, 11, :]
                P.ts(LRE, LRE, -1e-4, None, ALU.min, None, [sm], [sm])
                P.act(dtc[:, 1:2], dtc[:, 0:1], AF.Exp, [dtc], [dtc])
                P.ts(X1, LRE, dtc[:, 1:2], None, ALU.mult, None, [sm, dtc], [sm])
                P.ts(Y1, LIM, dtc[:, 1:2], None, ALU.mult, None, [sm, dtc], [sm])
                P.copy(dup[:, 0:64], Y1, [sm], [dup])
                P.copy(dup[:, 64:128], Y1, [sm], [dup])
                P.transpose(pst[:, 0:128], dup[:, :], self.ident[:, :], [dup, self.ident], [pst])
                P.ts(self.s5cols[:, d, 0, :], pst[:, 0:128], 8.0, None, ALU.mult, None, [pst], [self.s5cols])
                P.copy(dup[:, 0:64], X1, [sm], [dup])
                P.copy(dup[:, 64:128], X1, [sm], [dup])
                P.transpose(pst[:, 128:256], dup[:, :], self.ident[:, :], [dup, self.ident], [pst])
                P.act(self.s5cols[:, d, 1, :], pst[:, 128:256], AF.Exp, [pst], [self.s5cols], scale=8.0)
                for k in range(0, 9):
                    if k == 0:
                        P.memset(PW[:, 7, 0, :], 1.0, [PW])
                        P.memset(PW[:, 7, 1, :], 0.0, [PW])
                        continue
                    P.ts(XK, Y1, float(k), None, ALU.mult, None, [sm], [sm])
                    self.sincos(XK, T, RR, SN, CS, negpi, [sm], {"t": sm, "r": sm, "sin": sm, "cos": sm})
                    P.act(MAG, X1, AF.Exp, [sm], [sm], scale=float(k))
                    P.tt(PW[:, 7 + k, 0, :], MAG, CS, ALU.mult, [sm], [PW])
                    P.tt(PW[:, 7 + k, 1, :], MAG, SN, ALU.mult, [sm], [PW])
                    if k <= 7:
                        P.act(MAG, X1, AF.Exp, [sm], [sm], scale=-float(k))
                        P.tt(PW[:, 7 - k, 0, :], MAG, CS, ALU.mult, [sm], [PW])
                        P.stt(PW[:, 7 - k, 1, :], MAG, -1.0, SN, ALU.mult, ALU.mult, [sm], [PW])
                P.ts(XR, PW[:, 8, 0, :], -1.0, None, ALU.add, None, [PW], [sm])
                YI = PW[:, 8, 1, :]
                P.tt(U2, LRE, LRE, ALU.mult, [sm], [sm])
                P.tt(V2, LIM, LIM, ALU.mult, [sm], [sm])
                P.tt(DEN, U2, V2, ALU.add, [sm], [sm])
                P.recip(DEN, DEN, [sm], [sm])
                BSR, BSI = sm[:, 12, :], sm[:, 13, :]
                P.tt(U2, XR, LRE, ALU.mult, [sm], [sm])
                P.tt(V2, YI, LIM, ALU.mult, [sm, PW], [sm])
                P.tt(U2, U2, V2, ALU.add, [sm], [sm])
                P.tt(BSR, U2, DEN, ALU.mult, [sm], [sm])
                P.tt(U2, YI, LRE, ALU.mult, [sm, PW], [sm])
                P.tt(V2, XR, LIM, ALU.mult, [sm], [sm])
                P.tt(U2, U2, V2, ALU.subtract, [sm], [sm])
                P.tt(BSI, U2, DEN, ALU.mult, [sm], [sm])

                def bc_n_c(ap):
                    return ap.unsqueeze(2).broadcast_to([128, 64, 16])

                def bc_c_n(ap):
                    return ap.unsqueeze(1).broadcast_to([128, 16, 64])

                t3 = [t[:, :].rearrange("g (a b) -> g a b", b=16) for t in tA]
                t3c = [t[:, :].rearrange("g (a b) -> g a b", b=64) for t in tA]

                def cprod(o_re, o_im, a_re, a_im, m_re, m_im, tv, neg_im, rds):
                    P.tt(tv[0], a_re, m_re, ALU.mult, rds, [tA[0]])
                    P.tt(tv[1], a_im, m_im, ALU.mult, rds, [tA[1]])
                    P.tt(o_re, tv[0], tv[1], ALU.subtract, [tA[0], tA[1]], [big])
                    P.tt(tv[2], a_re, m_im, ALU.mult, rds, [tA[2]])
                    P.tt(tv[3], a_im, m_re, ALU.mult, rds, [tA[3]])
                    if neg_im:
                        P.stt(o_im, tv[2], -1.0, tv[3], ALU.mult, ALU.subtract, [tA[2], tA[3]], [big])
                    else:
                        P.tt(o_im, tv[2], tv[3], ALU.add, [tA[2], tA[3]], [big])

                P.tt(t3[0], bc_n_c(BSR), bre[:, :, :], ALU.mult, [sm, bre], [tA[0]])
                P.tt(t3[1], bc_n_c(BSI), bim[:, :, :], ALU.mult, [sm, bim], [tA[1]])
                P.tt(Btr[:, :, :], t3[0], t3[1], ALU.subtract, [tA[0], tA[1]], [Btr])
                P.tt(t3[2], bc_n_c(BSR), bim[:, :, :], ALU.mult, [sm, bim], [tA[2]])
                P.tt(t3[3], bc_n_c(BSI), bre[:, :, :], ALU.mult, [sm, bre], [tA[3]])
                P.tt(Bti[:, :, :], t3[2], t3[3], ALU.add, [tA[2], tA[3]], [Bti])

                v1 = big[:, :].rearrange("g (s c r n) -> g s c r n", s=8, c=16, r=2, n=64)
                v2 = big[:, :].rearrange("g (r n s c) -> g r n s c", s=8, c=16, r=2, n=64)
                for layout, dst in ((1, s5F[d]), (2, s5FF[d])):
                    for sidx in range(8):
                        k = (7 - sidx) if d == 0 else sidx
                        if layout == 1:
                            o_re = v1[:, sidx, :, 0, :].rearrange("g c n -> g n c")
                            o_im = v1[:, sidx, :, 1, :].rearrange("g c n -> g n c")
                        else:
                            o_re = v2[:, 0, :, sidx, :]
                            o_im = v2[:, 1, :, sidx, :]
                        cprod(o_re, o_im, bc_n_c(PW[:, 7 + k, 0, :]), bc_n_c(PW[:, 7 + k, 1, :]), Btr[:, :, :], Bti[:, :, :],
                              t3, False, [PW, Btr, Bti])
                    P.dma(dst[:, :], big[:, :], [big], [dst])
                for layout, dst in ((3, s5EN[d]), (4, s5E[d])):
                    for tidx in range(8):
                        if layout == 3:
                            k = (tidx - 7) if d == 0 else (-tidx)
                        else:
                            k = (tidx + 1) if d == 0 else (8 - tidx)
                        o_re = v2[:, 0, :, tidx, :].rearrange("g n c -> g c n")
                        o_im = v2[:, 1, :, tidx, :].rearrange("g n c -> g c n")
                        cprod(o_re, o_im, bc_c_n(PW[:, 7 + k, 0, :]), bc_c_n(PW[:, 7 + k, 1, :]), cre[:, :, :], cim[:, :, :],
                              t3c, True, [PW, cre, cim])
                    P.dma(dst[:, :], big[:, :], [big], [dst])
        with P.phase():
            maskF = P.sbuf("sm_maskF", [128, 128], F32)
            maskB = P.sbuf("sm_maskB", [128, 128], F32)
            P.dma(maskF[:, :], self.C["s5maskF"][:, :], [], [maskF])
            P.dma(maskB[:, :], self.C["s5maskB"][:, :], [], [maskB])
            sel16 = P.sbuf("sm_sel16", [16, 128], F32)
            P.dma(sel16[:, :], self.C["s5sel16"][:, :], [], [sel16])
            Dg = P.sbuf("sm_Dg", [128, 16], F32)
            P.dma(Dg[:, :], self.W["s5_d"].t[j].rearrange("(g c) -> g c", c=16), [], [Dg])
            DgT = P.sbuf("sm_DgT", [16, 128], F32)
            Dall = P.sbuf("sm_Dall", [128, 128], F32)
            ps0 = P.psum("sm_ps0", [128, 512], F32)
            P.transpose(ps0[0:16, 0:128], Dg[:, :], self.ident[:, :], [Dg, self.ident], [ps0])
            P.copy(DgT[:, :], ps0[0:16, 0:128], [ps0], [DgT])
            P.matmul(ps0[:, 128:256], sel16[:, :], DgT[:, :], True, True, [sel16, DgT], [ps0])
            P.copy(Dall[:, :], ps0[:, 128:256], [ps0], [Dall])
            GB = 8
            ld = [[P.sbuf("sm_ld%d_%d" % (k, q), [128, GB, 128], F32) for q in range(4)] for k in range(2)]
            Mst = [P.sbuf("sm_Mst%d" % k, [128, GB, 128], F32) for k in range(2)]
            T1 = [P.sbuf("sm_T1%d" % k, [128, 128], F32) for k in range(2)]
            T2 = [P.sbuf("sm_T2%d" % k, [128, 128], F32) for k in range(2)]
            psf = [P.psum("sm_psf%d" % k, [128, 512], F32) for k in range(2)]
            psb = [P.psum("sm_psb%d" % k, [128, 512], F32) for k in range(2)]
            srcs = [s5FF[0], s5EN[0], s5FF[1], s5EN[1]]
            for bt in range(128 // GB):
                L = ld[bt % 2]
                for q in range(4):
                    P.dma(L[q][:, :, :], srcs[q].t[bt * GB:(bt + 1) * GB, :].rearrange("g (r c) -> r g c", c=128),
                          [srcs[q]], [L[q]])
                MS = Mst[bt % 2]
                for gi in range(GB):
                    g = bt * GB + gi
                    pf, pb = psf[gi % 2], psb[gi % 2]
                    P.matmul(pf[:, 0:128], L[0][:, gi, :], L[1][:, gi, :], True, True, [L[0], L[1]], [pf])
                    P.matmul(pb[:, 0:128], L[2][:, gi, :], L[3][:, gi, :], True, True, [L[2], L[3]], [pb])
                    a, c2 = T1[gi % 2], T2[gi % 2]
                    P.tt(a[:, :], pf[:, 0:128], maskF[:, :], ALU.mult, [pf, maskF], [a])
                    P.tt(c2[:, :], pb[:, 0:128], maskB[:, :], ALU.mult, [pb, maskB], [c2])
                    P.tt(a[:, :], a[:, :], c2[:, :], ALU.add, [a, c2], [a])
                    P.stt(MS[:, gi, :], self.ident[:, :], Dall[:, g:g + 1], a[:, :], ALU.mult, ALU.add,
                          [self.ident, Dall, a], [MS])
                P.dma(s5M.t[bt * GB:(bt + 1) * GB, :].rearrange("g (r c) -> r g c", c=128), MS[:, :, :], [MS], [s5M])

    def s5(self, i, j, b):
        P = self.P
        PI = math.pi
        if b == 0:
            self.s5_prep(i, j)
        s5F = [self.scratch("s5F%d" % d, [128, 16384], F32) for d in range(2)]
        s5E = [self.scratch("s5E%d" % d, [128, 16384], F32) for d in range(2)]
        s5M = self.scratch("s5M", [128, 16384], F32)
        U_d = self.scratch("s5U", [128, 8, 16, 288], BF16)
        zT_d = self.scratch("hT_d", [128, 16, TOK], BF16)
        with P.phase():
            hTd = P.sbuf("s1_hTd", [128, 16, 8, 288], BF16)
            xt = [P.sbuf("s1_xt%d" % k, [128, D], F32) for k in range(2)]
            xn = [P.sbuf("s1_xn%d" % k, [128, D], F32) for k in range(2)]
            ss = [P.sbuf("s1_ss%d" % k, [128, 4], F32) for k in range(2)]
            pss = [P.psum("s1_ps%d" % k, [128, 512], F32) for k in range(4)]
            self.norm_tiles(i, 0, b, list(range(NT)), hTd, 0, (xt, xn, ss, pss), deint=True)
            for fc in range(16):
                for g8 in range(8):
                    g = fc * 8 + g8
                    P.dma(U_d[g].rearrange("s c j -> c s j"), hTd[g8 * 16:(g8 + 1) * 16, fc, :, :], [hTd], [U_d])
        with P.phase():
            iota = P.sbuf("s2_iota", [128, 288], F32)
            P.dma(iota[:, :], self.C["iota288"][:, :], [], [iota])
            negpi = P.sbuf("s2_negpi", [128, 1], F32)
            P.memset(negpi[:, :], 0.5 * PI, [negpi])
            selT = P.sbuf("s2_selT", [128, 64, 128], BF16)
            P.dma(selT[:, :, :], self.C["s5selT"][:, :, :], [], [selT], q="pool")
            Ub = [P.sbuf("s2_Ub%d" % k, [128, 8, 288], BF16) for k in range(2)]
            Wm = [P.sbuf("s2_Wm%d" % k, [128, 8, 128], BF16) for k in range(2)]
            WF = [[P.sbuf("s2_WF%d_%d" % (k, d), [128, 8, 128], BF16) for d in range(2)] for k in range(2)]
            WE = [[P.sbuf("s2_WE%d_%d" % (k, d), [128, 8, 128], BF16) for d in range(2)] for k in range(2)]
            ANG = P.sbuf("s2_ANG", [128, 8, 288], F32)
            ANG2 = P.sbuf("s2_ANG2", [128, 8, 288], F32)
            SINT = P.sbuf("s2_SINT", [128, 8, 288], F32)
            COST = P.sbuf("s2_COST", [128, 8, 288], F32)
            V = P.sbuf("s2_V", [128, 8, 288], F32)
            T1 = P.sbuf("s2_T1", [128, 8, 288], F32)
            T2 = P.sbuf("s2_T2", [128, 8, 288], F32)
            Gt = P.sbuf("s2_Gt", [128, 8, 288], F32)
            RT = P.sbuf("s2_RT", [128, 8, 288], F32)
            Zp = [P.sbuf("s2_Zp%d" % d, [128, 8, 288], BF16) for d in range(2)]
            Yg = P.sbuf("s2_Yg", [128, 8, 288], BF16)
            zfc = [P.sbuf("s2_zfc%d" % k, [128, TOK], BF16) for k in range(2)]
            gt = [P.sbuf("s2_gt%d" % k, [128, 288], F32) for k in range(3)]
            pv = [P.psum("s2_pv%d" % k, [128, 512], F32) for k in range(2)]
            py = [P.psum("s2_py%d" % k, [128, 512], F32) for k in range(2)]
            pt = [P.psum("s2_pt%d" % k, [128, 512], F32) for k in range(2)]
            for fc in range(16):
                UB, WM = Ub[fc % 2], Wm[fc % 2]
                P.dma(UB[:, :, :], U_d[fc * 8:(fc + 1) * 8].rearrange("g s c j -> (s c) g j"), [U_d], [UB])
                P.dma(WM[:, :, :], s5M.t[fc * 8:(fc + 1) * 8, :].rearrange("g (r c) -> r g c", c=128), [s5M], [WM], q="pool")
                for d in range(2):
                    P.dma(WF[fc % 2][d][:, :, :], s5F[d].t[fc * 8:(fc + 1) * 8, :].rearrange("g (r c) -> r g c", c=128),
                          [s5F[d]], [WF[fc % 2][d]], q="pool")
                    P.dma(WE[fc % 2][d][:, :, :], s5E[d].t[fc * 8:(fc + 1) * 8, :].rearrange("g (r c) -> r g c", c=128),
                          [s5E[d]], [WE[fc % 2][d]], q="pool")
                for d in range(2):
                    wf, we = WF[fc % 2][d], WE[fc % 2][d]
                    for gi in range(8):
                        g = fc * 8 + gi
                        P.ts(ANG[:, gi, :], iota[:, :], self.s5cols[:, d, 0, g:g + 1], None, ALU.mult, None,
                             [iota, self.s5cols], [ANG])
                        P.ts(RT[:, gi, :], iota[:, :], 0.0, self.s5cols[:, d, 1, g:g + 1], ALU.mult, ALU.add,
                             [iota, self.s5cols], [RT])
                    self.sincos(ANG[:, :, :], ANG2[:, :, :], T1[:, :, :], SINT[:, :, :], COST[:, :, :], negpi, [ANG],
                                {"t": ANG2, "r": T1, "sin": SINT, "cos": COST})
                    P.ts(SINT[64:128, :, :], SINT[64:128, :, :], -1.0, None, ALU.mult, None, [SINT], [SINT])
                    for gi in range(8):
                        p = pv[gi % 2]
                        P.matmul(p[:, 0:288], wf[:, gi, :], UB[:, gi, :], True, True, [wf, UB], [p])
                        if d == 0:
                            P.copy(V[:, gi, :], p[:, 0:288], [p], [V], q="act")
                        else:
                            P.copy(V[:, gi, 0:32], p[:, 0:32][:, ::-1], [p], [V], q="act")
                            P.copy(V[:, gi, 32:288], p[:, 32:288][:, ::-1], [p], [V], q="act")
                    P.tt(T1[:, :, :], V[:, :, :], COST[:, :, :], ALU.mult, [V, COST], [T1])
                    P.copy(T2[0:64, :, :], V[64:128, :, :], [V], [T2])
                    P.copy(T2[64:128, :, :], V[0:64, :, :], [V], [T2])
                    P.tt(T2[:, :, :], T2[:, :, :], SINT[:, :, :], ALU.mult, [T2, SINT], [T2])
                    P.tt(V[:, :, :], T1[:, :, :], T2[:, :, :], ALU.add, [T1, T2], [V])
                    for gi in range(8):
                        P.op("dve", (lambda gi=gi: (lambda e: e.tensor_tensor_scan(Gt[:, gi, :], RT[:, gi, :], V[:, gi, :], 0.0,
                                                                                    ALU.mult, ALU.add)))(), [RT, V], [Gt])
                    P.tt(T1[:, :, :], Gt[:, :, :], COST[:, :, :], ALU.mult, [Gt, COST], [T1])
                    P.copy(T2[0:64, :, :], Gt[64:128, :, :], [Gt], [T2])
                    P.copy(T2[64:128, :, :], Gt[0:64, :, :], [Gt], [T2])
                    P.tt(T2[:, :, :], T2[:, :, :], SINT[:, :, :], ALU.mult, [T2, SINT], [T2])
                    P.tt(T1[:, :, :], T1[:, :, :], T2[:, :, :], ALU.subtract, [T1, T2], [T1])
                    ZP = Zp[d]
                    if d == 0:
                        P.memset(ZP[:, :, 0:1], 0.0, [ZP])
                        P.copy(ZP[:, :, 1:288], T1[:, :, 0:287], [T1], [ZP])
                    else:
                        P.memset(ZP[:, :, 31:32], 0.0, [ZP])
                        P.copy(ZP[:, :, 0:31], T1[:, :, 0:31][:, :, ::-1], [T1], [ZP])
                        P.copy(ZP[:, :, 32:288], T1[:, :, 31:287][:, :, ::-1], [T1], [ZP])
                for gi in range(8):
                    p = py[gi % 2]
                    P.matmul(p[:, 0:288], WM[:, gi, :], UB[:, gi, :], True, False, [WM, UB], [p])
                    P.matmul(p[:, 0:288], WE[fc % 2][0][:, gi, :], Zp[0][:, gi, :], False, False, [WE[fc % 2][0], Zp[0]], [p])
                    P.matmul(p[:, 0:288], WE[fc % 2][1][:, gi, :], Zp[1][:, gi, :], False, True, [WE[fc % 2][1], Zp[1]], [p])
                    P.copy(Yg[:, gi, :], p[:, 0:288], [p], [Yg], q="act")
                Z = zfc[fc % 2]
                zv = Z[:, :].rearrange("p (j s) -> p s j", s=8)
                for t in range(8):
                    p = pt[t % 2]
                    for g8 in range(8):
                        P.matmul(p[:, 0:288], selT[:, g8 * 8 + t, :], Yg[:, g8, :], g8 == 0, g8 == 7, [selT, Yg], [p])
                    P.act(gt[0][:, :], p[:, 0:288], AF.Square, [p], [gt[0]])
                    P.ts(gt[0][:, :], gt[0][:, :], 0.044715, 1.0, ALU.mult, ALU.add, [gt[0]], [gt[0]])
                    P.tt(gt[1][:, :], gt[0][:, :], p[:, 0:288], ALU.mult, [gt[0], p], [gt[1]])
                    P.act(gt[2][:, :], gt[1][:, :], AF.Sigmoid, [gt[1]], [gt[2]], scale=1.5957691216057308)
                    P.tt(zv[:, t, :], gt[2][:, :], p[:, 0:288], ALU.mult, [gt[2], p], [Z])
                P.dma(zT_d[:, fc, :], Z[:, :], [Z], [zT_d])
        self.s5_glu(i, j, b)

    def s5_glu(self, i, j, b):
        P = self.P
        zT_d = self.scratch("hT_d", [128, 16, TOK], BF16)
        TBT = 6
        NTB = TBT * 128
        gw = self.W["s5_glu_w"].t[j].rearrange("(c p) n -> p c n", p=128)
        with P.phase():
            self._dg = [P.sbuf("dg%d" % k, [128, 128], F32) for k in range(2)]
            zT = P.sbuf("sg_zT", [128, 16, NTB], BF16)
            yacc = P.sbuf("sg_y", [128, TBT, D], F32)
            wp = [P.sbuf("sg_wp%d" % k, [128, 16, 512], BF16) for k in range(2)]
            G = [P.sbuf("sg_G%d" % k, [128, D], F32) for k in range(2)]
            xt = [P.sbuf("sg_xt%d" % k, [128, D], F32) for k in range(2)]
            sg = [P.sbuf("sg_sg%d" % k, [128, 256], F32) for k in range(2)]
            ps = [P.psum("sg_ps%d" % k, [128, 512], F32) for k in range(4)]
            self.gate_rows(i, 0, b, G[0], ps[0])
            self.gate_rows(i, 0, BPC, G[1], ps[1])
            cnt = 0
            for blk in range(NT // TBT):
                tiles = list(range(blk * TBT, (blk + 1) * TBT))
                P.dma(zT[:, :, :], zT_d[:, :, blk * NTB:(blk + 1) * NTB], [zT_d], [zT])
                for pn in range(8):
                    Wp = wp[pn % 2]
                    P.dma(Wp[:, :, 0:256], gw[:, :, pn * 256:(pn + 1) * 256], [], [Wp], q="pool")
                    P.dma(Wp[:, :, 256:512], gw[:, :, 2048 + pn * 256:2048 + (pn + 1) * 256], [], [Wp], q="pool")
                    for k, t in enumerate(tiles):
                        pp = ps[cnt % 4]
                        S = sg[cnt % 2]
                        cnt += 1
                        for kc in range(16):
                            P.matmul(pp[:, :], zT[:, kc, k * 128:(k + 1) * 128], Wp[:, kc, :], kc == 0, kc == 15, [zT, Wp], [pp])
                        P.act(S[:, :], pp[:, 256:512], AF.Sigmoid, [pp], [S])
                        P.tt(yacc[:, k, pn * 256:(pn + 1) * 256], pp[:, 0:256], S[:, :], ALU.mult, [pp, S], [yacc])
                for k, t in enumerate(tiles):
                    X = xt[k % 2]
                    g = G[1] if t < 2 else G[0]
                    P.dma(X[:, :], self.res_ap(b, t), [self.rtok[b][t]], [X])
                    P.tt(yacc[:, k, :], yacc[:, k, :], g[:, :], ALU.mult, [yacc, g], [yacc])
                    P.tt(X[:, :], X[:, :], yacc[:, k, :], ALU.add, [X, yacc], [X])
                    P.dma(self.res_ap(b, t), X[:, :], [X], [self.rtok[b][t]])


    def gdn(self, i, j, b):
        self.norm_to_dram(i, 0, b)
        self.gdn_proj(i, j, b)
        if self.cfg.get("gdn_stop") == "proj":
            return
        self.gdn_core(i, j, b)
        self.outproj_residual(i, b, self.scratch("r_ogT", [4096, TOK], BF16), self.W["gdn_w_out"].t[j])

    def gdn_proj(self, i, j, b):
        P = self.P
        hT_d = self.scratch("hT_d", [128, 16, TOK], BF16)
        g_qkvT = self.scratch("g_qkvT", [8192, TOK], BF16)
        g_z = self.scratch("r_gate", [TOK, 4096], BF16)
        g_ab = self.scratch("g_ab", [TOK, 128], F32)
        win = self.W["gdn_w_in"].t[j].rearrange("(c p) n -> p c n", p=128)
        blocks = [(0, 256)] + [(256 + 512 * k, 512) for k in range(4)]
        PADW = 4 + 256 + 4 + 2048 + 4
        with P.phase():
            hT = P.sbuf("gp_hT", [128, 16, TOK], BF16)
            P.dma(hT[:, :, :], hT_d[:, :, :], [hT_d], [hT])
            cwr = P.sbuf("gp_cwr", [5, 2048], F32)
            cw = P.sbuf("gp_cw", [128, 64, 5], F32)
            pcw = P.psum("gp_pcw", [128, 512], F32)
            for c4 in range(4):
                P.dma(cwr[:, :], self.W["gdn_conv_w"].t[j][:, c4 * 2048:(c4 + 1) * 2048], [], [cwr])
                for c in range(16):
                    cc = c4 * 16 + c
                    P.transpose(pcw[:, cc * 5:(cc + 1) * 5], cwr[0:5, c * 128:(c + 1) * 128], self.ident[0:5, 0:5],
                                [cwr, self.ident], [pcw])
            P.copy(cw[:, :, :], pcw[:, 0:320].rearrange("p (c k) -> p c k", k=5), [pcw], [cw])
            wp = [P.sbuf("gp_wp%d" % k, [128, 16, 512], BF16) for k in range(2)]
            pad = [P.sbuf("gp_pad%d" % k, [128, PADW], F32) for k in range(2)]
            acc = [P.sbuf("gp_acc%d" % k, [128, TOK], F32) for k in range(2)]
            sq = [P.sbuf("gp_sq%d" % k, [128, 512], F32) for k in range(2)]
            rin = [P.sbuf("gp_rin%d" % k, [128, 512], F32) for k in range(2)]
            ost = [P.sbuf("gp_ost%d" % k, [128, TOK], BF16) for k in range(2)]
            epsc = P.sbuf("gp_eps", [128, 1], F32)
            P.memset(epsc[:, :], EPS, [epsc])
            for k in range(2):
                P.memset(pad[k][:, :], 0.0, [pad[k]])
            psA = [P.psum("gp_psA%d" % k, [128, 512], F32) for k in range(4)]
            psN = [P.psum("gp_psN%d" % k, [128, 512], F32) for k in range(2)]
            cnt = 0
            for pn in range(16):
                Wp = wp[pn % 2]
                P.dma(Wp[:, :, :], win[:, :, pn * 512:(pn + 1) * 512], [], [Wp], q="pool")
                for nb in range(4):
                    chunk = pn * 4 + nb
                    PD, AC, OS = pad[chunk % 2], acc[chunk % 2], ost[chunk % 2]
                    for bi, (t0, n) in enumerate(blocks):
                        ps = psA[cnt % 4]
                        cnt += 1
                        for k in range(16):
                            P.matmul(ps[:, 0:n], Wp[:, k, nb * 128:(nb + 1) * 128], hT[:, k, t0:t0 + n], k == 0, k == 15,
                                     [Wp, hT], [ps])
                        o0 = 2 + t0 if bi == 0 else 264 + 2 + (t0 - 256)
                        P.copy(PD[:, o0:o0 + n], ps[:, 0:n], [ps], [PD], q="act")
                    for (po, ao, n) in ((0, 0, 256), (264, 256, 2048)):
                        for k in range(5):
                            src = PD[:, po + k:po + k + n]
                            if k == 0:
                                P.ts(AC[:, ao:ao + n], src, cw[:, chunk, 0:1], None, ALU.mult, None, [PD, cw], [AC])
                            else:
                                P.stt(AC[:, ao:ao + n], src, cw[:, chunk, k:k + 1], AC[:, ao:ao + n], ALU.mult, ALU.add,
                                      [PD, cw, AC], [AC])
                    if chunk >= 32:
                        P.act(OS[:, :], AC[:, :], AF.Silu, [AC], [OS])
                    else:
                        P.act(AC[:, :], AC[:, :], AF.Silu, [AC], [AC])
                        qs = (128.0 ** -0.5) if chunk < 16 else 1.0
                        for bi, (t0, n) in enumerate(blocks):
                            SQ, RI = sq[bi % 2], rin[bi % 2]
                            pN = psN[bi % 2]
                            P.tt(SQ[:, 0:n], AC[:, t0:t0 + n], AC[:, t0:t0 + n], ALU.mult, [AC], [SQ])
                            P.matmul(pN[:, 0:n], self.ones[:, :], SQ[:, 0:n], True, True, [self.ones, SQ], [pN])
                            P.act(RI[:, 0:n], pN[:, 0:n], AF.Sqrt, [pN, epsc], [RI], bias=epsc[:, 0:1])
                            P.recip(RI[:, 0:n], RI[:, 0:n], [RI], [RI])
                            P.stt(OS[:, t0:t0 + n], AC[:, t0:t0 + n], qs, RI[:, 0:n], ALU.mult, ALU.mult, [AC, RI], [OS])
                    P.dma(g_qkvT[chunk * 128:(chunk + 1) * 128, :], OS[:, :], [OS], [g_qkvT])
        with P.phase():
            hT = P.sbuf("gz_hT", [128, 16, TOK], BF16)
            P.dma(hT[:, :, :], hT_d[:, :, :], [hT_d], [hT])
            wp = [P.sbuf("gz_wp%d" % k, [128, 16, 512], BF16) for k in range(2)]
            stage = [P.sbuf("gz_st%d" % k, [128, NT, 512], BF16) for k in range(2)]
            psA = [P.psum("gz_psA%d" % k, [128, 512], F32) for k in range(4)]
            cnt = 0
            for pn in range(8):
                Wp = wp[pn % 2]
                ST = stage[pn % 2]
                c0 = 8192 + pn * 512
                P.dma(Wp[:, :, :], win[:, :, c0:c0 + 512], [], [Wp], q="pool")
                for t in range(NT):
                    ps = psA[cnt % 4]
                    cnt += 1
                    for k in range(16):
                        P.matmul(ps[:, :], hT[:, k, t * 128:(t + 1) * 128], Wp[:, k, :], k == 0, k == 15, [hT, Wp], [ps])
                    P.act(ST[:, t, :], ps[:, :], AF.Silu, [ps], [ST])
                P.dma(g_z.t.rearrange("(t p) e -> p t e", p=128)[:, :, pn * 512:(pn + 1) * 512], ST[:, :, :], [ST], [g_z])
            Wab = P.sbuf("gz_wab", [128, 16, 128], BF16)
            P.dma(Wab[:, :, :], win[:, :, 12288:12416], [], [Wab], q="pool")
            par = P.sbuf("gz_par", [128, 2, 64], F32)
            P.dma(par[:, 0, :], self.W["gdn_a_log"].t[j].rearrange("d h -> (d h)").partition_broadcast(128), [], [par])
            P.dma(par[:, 1, :], self.W["gdn_dt_bias"].t[j].rearrange("d h -> (d h)").partition_broadcast(128), [], [par])
            nA = P.sbuf("gz_nA", [128, 2, 32], F32)
            P.act(nA[:, :, :], par[:, 0, :].rearrange("p (d h) -> p d h", d=2), AF.Exp, [par], [nA])
            ab = P.sbuf("gz_ab", [128, NT, 128], F32)
            tmp = [P.sbuf("gz_tmp%d" % k, [128, 2, 32], F32) for k in range(2)]
            for t in range(NT):
                ps = psA[cnt % 4]
                cnt += 1
                for k in range(16):
                    P.matmul(ps[:, 0:128], hT[:, k, t * 128:(t + 1) * 128], Wab[:, k, :], k == 0, k == 15, [hT, Wab], [ps])
                pv = ps[:, 0:128].rearrange("p (d a h) -> p d a h", d=2, a=2)
                av = ab[:, t, :].rearrange("p (d a h) -> p d a h", d=2, a=2)
                T = tmp[t % 2]
                P.tt(T[:, :, :], pv[:, :, 0, :], par[:, 1, :].rearrange("p (d h) -> p d h", d=2), ALU.add, [ps, par], [T])
                P.act(T[:, :, :], T[:, :, :], AF.Exp, [T], [T])
                P.act(T[:, :, :], T[:, :, :], AF.Ln, [T], [T], bias=self.ones[:, 0:1])
                P.stt(av[:, :, 0, :], T[:, :, :], -1.0, nA[:, :, :], ALU.mult, ALU.mult, [T, nA], [ab])
                P.act(av[:, :, 1, :], pv[:, :, 1, :], AF.Sigmoid, [ps], [ab])
            P.dma(g_ab.t.rearrange("(t p) e -> p t e", p=128), ab[:, :, :], [ab], [g_ab])

    def gdn_core(self, i, j, b):
        P = self.P
        g_qkvT = self.scratch("g_qkvT", [8192, TOK], BF16)
        g_z = self.scratch("r_gate", [TOK, 4096], BF16)
        g_ab = self.scratch("g_ab", [TOK, 128], F32)
        r_ogT = self.scratch("r_ogT", [4096, TOK], BF16)

        class Ring:
            def __init__(self, items):
                self.items = items
                self.k = 0

            def next(self):
                it = self.items[self.k % len(self.items)]
                self.k += 1
                return it

        with P.phase():
            def cload(nm):
                t = P.sbuf("gc_" + nm, [128, 128], F32)
                P.dma(t[:, :], self.C[nm][:, :], [], [t])
                return t
            LI, LS, UI, US = cload("LI"), cload("LS"), cload("UI"), cload("US")
            gnw = P.sbuf("gc_gnw", [128, 128], F32)
            P.dma(gnw[:, :], self.W["gdn_norm_w"].t[j].partition_broadcast(128), [], [gnw])
            gb = P.sbuf("gc_gb", [128, NT, 128], F32)
            P.dma(gb[:, :, :], g_ab.t.rearrange("(t p) e -> p t e", p=128), [g_ab], [gb])
            qT = P.sbuf("gc_qT", [128, TOK], BF16)
            kT = P.sbuf("gc_kT", [128, TOK], BF16)
            vT = P.sbuf("gc_vT", [128, TOK], BF16)
            zh = P.sbuf("gc_zh", [128, NT, 128], BF16)
            ktm = P.sbuf("gc_ktm", [128, NT, 128], BF16)
            vtm = P.sbuf("gc_vtm", [128, NT, 128], BF16)
            oacc = P.sbuf("gc_oacc", [128, NT, 128], F32)
            ogT = P.sbuf("gc_ogT", [128, TOK], BF16)
            junk = P.sbuf("gc_junk", [128, 128], F32)
            def ring(nm, n, shape, dt):
                return Ring([P.sbuf("gc_%s%d" % (nm, k), shape, dt) for k in range(n)])
            rGm = ring("Gm", 4, [128, 128], F32)
            rcols = ring("cols", 6, [128, 8], F32)
            rDec = ring("Dec", 4, [128, 128], F32)
            rDecT = ring("DecT", 4, [128, 128], F32)
            rsq = ring("sqb", 24, [128, 128], BF16)
            rU = ring("U", 4, [128, 128], F32)
            rAT = ring("ATr", 4, [128, 128], BF16)
            rM = ring("Mr", 3, [128, 128], BF16)
            rN = ring("Nr", 3, [128, 128], BF16)
            def cloadb(nm):
                tf = P.sbuf("gc_f_" + nm, [128, 128], F32)
                P.dma(tf[:, :], self.C[nm][:, :], [], [tf])
                tb = P.sbuf("gc_b_" + nm, [128, 128], BF16)
                P.copy(tb[:, :], tf[:, :], [tf], [tb])
                return tb
            blk16 = cloadb("blk16")
            lvlm = [cloadb("lvl16"), cloadb("lvl32"), cloadb("lvl64")]
            rkd = ring("kdr", 4, [128, 128], BF16)
            rWT = ring("WTr", 4, [128, 128], BF16)
            rtmp = ring("tmp", 4, [128, 128], F32)
            S32 = [P.sbuf("gc_S32_%d" % k, [128, 128], F32) for k in range(2)]
            S16 = [P.sbuf("gc_S16_%d" % k, [128, 128], BF16) for k in range(2)]
            st = ring("st", 2, [128, 8], F32)
            banks = []
            for k in range(5):
                bank = P.psum("gc_pq%d" % k, [128, 512], F32)
                banks.append((bank.t, Tok("pq%d" % k)))
            rbank = Ring(banks)

            class _PS:
                def __init__(self):
                    self.cur = None
                    self.q = 0

                def next(self, newbank=True):
                    if newbank or self.cur is None or self.q >= 4:
                        self.cur = rbank.next()
                        self.q = 0
                    ap = self.cur[0][:, self.q * 128:(self.q + 1) * 128]
                    self.q += 1
                    return ap, self.cur[1]
            rps = _PS()
            pcb = P.psum("gc_pc", [128, 512], F32)
            rpc = Ring([(pcb.t[:, 0:4], Tok("pc0"))])
            pbb = [P.psum("gc_pb%d" % k, [128, 512], BF16) for k in range(2)]
            rpbank = Ring([(pbb[k].t, Tok("pb%d" % k)) for k in range(2)])

            class _PB:
                def next(self):
                    bk = rpbank.next()
                    return bk[0][:, 0:128], bk[1]
            rpb = _PB()
            qkv = g_qkvT.t.rearrange("(c p) t -> p c t", p=128)
            zv = g_z.t.rearrange("(t p) e -> p t e", p=128)
            ov = r_ogT.t.rearrange("(c p) t -> p c t", p=128)
            identb = self.identb
            order = [list(range(NT)), [1, 0] + list(range(NT - 1, 1, -1))]

            def pre(d, hv, t):
                sl = slice(t * 128, (t + 1) * 128)
                LU = UI if d == 0 else LI
                GMASK = LS if d == 0 else US
                mS = LS if d == 0 else US
                mTI = UI if d == 0 else LI
                gcol = gb[:, t, d * 64 + hv:d * 64 + hv + 1]
                bcol = gb[:, t, d * 64 + 32 + hv:d * 64 + 32 + hv + 1]
                Gm = rGm.next()
                P.ts(Gm[:, :], GMASK[:, :], gcol, None, ALU.mult, None, [GMASK, gb], [Gm])
                pgd, tgd = rps.next()
                pgdT, tgdT = rps.next(False)
                P.matmul(pgd, LU[:, :], Gm[:, :], True, True, [LU, Gm], [tgd])
                P.matmul(pgdT, Gm[:, :], LU[:, :], True, True, [LU, Gm], [tgdT])
                pc, tpc = rpc.next()
                gcol2 = gb[:, t, d * 64 + hv:d * 64 + hv + 2]
                P.matmul(pc[:, 0:2], LU[:, :], gcol2, True, True, [LU, gb], [tpc])
                P.matmul(pc[:, 2:4], self.ones[:, :], gcol2, True, True, [self.ones, gb], [tpc])
                cols = rcols.next()
                P.copy(cols[:, 0:2], pc[:, 0:4:2], [tpc], [cols], q="act")
                P.act(cols[:, 2:3], cols[:, 0:1], AF.Exp, [cols], [cols])
                P.act(cols[:, 3:4], cols[:, 0:1], AF.Exp, [cols], [cols], scale=-1.0, bias=cols[:, 1:2])
                P.act(cols[:, 4:5], cols[:, 1:2], AF.Exp, [cols], [cols])
                P.tt(cols[:, 5:6], cols[:, 2:3], bcol, ALU.mult, [cols, gb], [cols])
                if self.cfg.get("gdn_cut", 9) < 2:
                    return None
                Dec, DecT = rDec.next(), rDecT.next()
                P.act(Dec[:, :], pgd, AF.Exp, [tgd], [Dec])
                P.act(DecT[:, :], pgdT, AF.Exp, [tgdT], [DecT])
                P.tt(Dec[:, :], Dec[:, :], mS[:, :], ALU.mult, [Dec, mS], [Dec])
                P.tt(DecT[:, :], DecT[:, :], mTI[:, :], ALU.mult, [DecT, mTI], [DecT])
                pkk, tkk = rps.next()
                pkq, tkq = rps.next(False)
                P.matmul(pkk, kT[:, sl], kT[:, sl], True, True, [kT], [tkk])
                P.matmul(pkq, kT[:, sl], qT[:, sl], True, True, [kT, qT], [tkq])
                M = rM.next()
                AT = rAT.next()
                P.stt(M[:, :], pkk, bcol, Dec[:, :], ALU.mult, ALU.mult, [tkk, gb, Dec], [M])
                P.tt(AT[:, :], pkq, DecT[:, :], ALU.mult, [tkq, DecT], [AT])
                pN, tN = rpb.next()
                P.transpose(pN, M[:, :], identb[:, :], [M, identb], [tN])
                N = rN.next()
                P.copy(N[:, :], pN, [tN], [N], q="act")
                M0, N0 = rsq.next(), rsq.next()
                P.tt(M0[:, :], M[:, :], blk16[:, :], ALU.mult, [M, blk16], [M0])
                P.tt(N0[:, :], N[:, :], blk16[:, :], ALU.mult, [N, blk16], [N0])
                T_, R = rsq.next(), rsq.next()
                P.tt(T_[:, :], identb[:, :], M0[:, :], ALU.subtract, [identb, M0], [T_])
                P.tt(R[:, :], identb[:, :], N0[:, :], ALU.subtract, [identb, N0], [R])
                Pm, Qm = N0, M0
                for lvl in range(3):
                    pp, tp = rps.next()
                    P.matmul(pp, Qm[:, :], Pm[:, :], True, True, [Qm, Pm], [tp])
                    pq, tq = rps.next(False)
                    P.matmul(pq, Pm[:, :], Qm[:, :], True, True, [Qm, Pm], [tq])
                    P2, Q2 = rsq.next(), rsq.next()
                    P.copy(P2[:, :], pp, [tp], [P2], q="act")
                    P.copy(Q2[:, :], pq, [tq], [Q2], q="act")
                    pt_, tt_ = rps.next()
                    P.matmul(pt_, P2[:, :], T_[:, :], True, True, [P2, T_], [tt_])
                    prr, trr = rps.next(False)
                    P.matmul(prr, T_[:, :], P2[:, :], True, True, [P2, T_], [trr])
                    Tn, Rn = rsq.next(), rsq.next()
                    P.tt(Tn[:, :], pt_, T_[:, :], ALU.add, [tt_, T_], [Tn])
                    P.tt(Rn[:, :], prr, R[:, :], ALU.add, [trr, R], [Rn])
                    T_, R, Pm, Qm = Tn, Rn, P2, Q2
                for li, msk in enumerate(lvlm):
                    last = li == len(lvlm) - 1
                    B1, B1T = rsq.next(), rsq.next()
                    P.tt(B1[:, :], M[:, :], msk[:, :], ALU.mult, [M, msk], [B1])
                    P.tt(B1T[:, :], N[:, :], msk[:, :], ALU.mult, [N, msk], [B1T])
                    px, tx = rps.next()
                    P.matmul(px, B1[:, :], R[:, :], True, True, [B1, R], [tx])
                    if not last:
                        px2, tx2 = rps.next(False)
                        P.matmul(px2, B1T[:, :], T_[:, :], True, True, [B1T, T_], [tx2])
                    Xp = rsq.next()
                    P.copy(Xp[:, :], px, [tx], [Xp], q="act")
                    if not last:
                        X = rsq.next()
                        P.copy(X[:, :], px2, [tx2], [X], q="act")
                    pr, tr = rps.next()
                    P.matmul(pr, T_[:, :], Xp[:, :], True, True, [T_, Xp], [tr])
                    if not last:
                        pt_, tt_ = rps.next(False)
                        P.matmul(pt_, R[:, :], X[:, :], True, True, [R, X], [tt_])
                    Rn = rsq.next()
                    P.tt(Rn[:, :], R[:, :], pr, ALU.subtract, [tr, R], [Rn])
                    if not last:
                        Tn = rsq.next()
                        P.tt(Tn[:, :], T_[:, :], pt_, ALU.subtract, [tt_, T_], [Tn])
                        T_ = Tn
                    R = Rn
                RHSu, RHSw, kdec = rsq.next(), rsq.next(), rkd.next()
                P.ts(RHSu[:, :], vtm[:, t, :], bcol, None, ALU.mult, None, [vtm, gb], [RHSu])
                P.ts(RHSw[:, :], ktm[:, t, :], cols[:, 5:6], None, ALU.mult, None, [ktm, cols], [RHSw])
                P.ts(kdec[:, :], ktm[:, t, :], cols[:, 3:4], None, ALU.mult, None, [ktm, cols], [kdec])
                pU, tU = rps.next()
                pW, tW = rps.next(False)
                P.matmul(pU, R[:, :], RHSu[:, :], True, True, [R, RHSu], [tU])
                P.matmul(pW, RHSw[:, :], R[:, :], True, True, [R, RHSw], [tW])
                U = rU.next()
                WT = rWT.next()
                P.copy(U[:, :], pU, [tU], [U], q="act")
                P.copy(WT[:, :], pW, [tW], [WT], q="act")
                return dict(U=U, WT=WT, AT=AT, kdec=kdec, cols=cols, t=t)

            def seq(d, hv, pr_, first, first_dir):
                t = pr_["t"]
                sl = slice(t * 128, (t + 1) * 128)
                S, Sb = S32[d], S16[d]
                cols = pr_["cols"]
                vnew = rsq.next()
                if first:
                    P.copy(vnew[:, :], pr_["U"][:, :], [pr_["U"]], [vnew])
                else:
                    pws, tws = rps.next()
                    P.matmul(pws, pr_["WT"][:, :], Sb[:, :], True, True, [pr_["WT"], Sb], [tws])
                    P.tt(vnew[:, :], pr_["U"][:, :], pws, ALU.subtract, [pr_["U"], tws], [vnew])
                po2, to2 = rps.next()
                P.matmul(po2, pr_["AT"][:, :], vnew[:, :], True, True, [pr_["AT"], vnew], [to2])
                if not first:
                    po1, to1 = rps.next(False)
                    P.matmul(po1, qT[:, sl], Sb[:, :], True, True, [qT, Sb], [to1])
                    tm = rtmp.next()
                    P.act(tm[:, :], po1, AF.Copy, [to1, cols], [tm], scale=cols[:, 2:3])
                    if first_dir:
                        P.tt(oacc[:, t, :], po2, tm[:, :], ALU.add, [to2, tm], [oacc])
                    else:
                        P.tt(tm[:, :], po2, tm[:, :], ALU.add, [to2, tm], [tm])
                        P.tt(oacc[:, t, :], oacc[:, t, :], tm[:, :], ALU.add, [oacc, tm], [oacc])
                else:
                    if first_dir:
                        P.copy(oacc[:, t, :], po2, [to2], [oacc])
                    else:
                        P.tt(oacc[:, t, :], oacc[:, t, :], po2, ALU.add, [oacc, to2], [oacc])
                pS, tS = rps.next()
                P.matmul(pS, pr_["kdec"][:, :], vnew[:, :], True, True, [pr_["kdec"], vnew], [tS])
                if first:
                    P.copy(S[:, :], pS, [tS], [S])
                else:
                    P.stt(S[:, :], S[:, :], cols[:, 4:5], pS, ALU.mult, ALU.add, [S, cols, tS], [S])
                P.copy(Sb[:, :], S[:, :], [S], [Sb], q="act")

            for hv in range(self.cfg.get("gdn_heads", 32)):
                hq = hv // 2
                if hv % 2 == 0:
                    P.dma(qT[:, :], qkv[:, hq, :], [g_qkvT], [qT])
                    P.dma(kT[:, :], qkv[:, 16 + hq, :], [g_qkvT], [kT])
                    for t0 in range(0, NT, 4):
                        bk, tk_ = rpbank.next()
                        nt = min(4, NT - t0)
                        for q4 in range(nt):
                            t = t0 + q4
                            P.transpose(bk[:, q4 * 128:(q4 + 1) * 128], kT[:, t * 128:(t + 1) * 128], identb[:, :],
                                        [kT, identb], [tk_])
                        P.copy(ktm[:, t0:t0 + nt, :], bk[:, 0:nt * 128].rearrange("p (a b) -> p a b", b=128), [tk_], [ktm],
                               q="act")
                P.dma(vT[:, :], qkv[:, 32 + hv, :], [g_qkvT], [vT])
                P.dma(zh[:, :, :], zv[:, :, hv * 128:(hv + 1) * 128], [g_z], [zh])
                for t0 in range(0, NT, 4):
                    bk, tk_ = rpbank.next()
                    nt = min(4, NT - t0)
                    for q4 in range(nt):
                        t = t0 + q4
                        P.transpose(bk[:, q4 * 128:(q4 + 1) * 128], vT[:, t * 128:(t + 1) * 128], identb[:, :],
                                    [vT, identb], [tk_])
                    P.copy(vtm[:, t0:t0 + nt, :], bk[:, 0:nt * 128].rearrange("p (a b) -> p a b", b=128), [tk_], [vtm],
                           q="act")
                if self.cfg.get("gdn_cut", 9) < 1:
                    continue
                for d in range(2):
                    nxt = pre(d, hv, order[d][0])
                    if self.cfg.get("gdn_cut", 9) < 3:
                        continue
                    for n in range(NT):
                        cur = nxt
                        if n + 1 < NT:
                            nxt = pre(d, hv, order[d][n + 1])
                        seq(d, hv, cur, n == 0, d == 0)
                for t in range(NT):
                    S_ = st.next()
                    o = oacc[:, t, :]
                    P.memset(S_[:, 0:1], 0.0, [S_])
                    P.act(junk[:, :], o, AF.Square, [oacc], [junk, S_], accum_out=S_[:, 0:1])
                    P.ts(S_[:, 1:2], S_[:, 0:1], 1.0 / 128, EPS, ALU.mult, ALU.add, [S_], [S_])
                    P.act(S_[:, 2:3], S_[:, 1:2], AF.Sqrt, [S_], [S_])
                    P.recip(S_[:, 3:4], S_[:, 2:3], [S_], [S_])
                    tm = rtmp.next()
                    P.stt(tm[:, :], o, S_[:, 3:4], gnw[:, :], ALU.mult, ALU.mult, [oacc, S_, gnw], [tm])
                    og = rsq.next()
                    P.tt(og[:, :], tm[:, :], zh[:, t, :], ALU.mult, [tm, zh], [og])
                    pb_, tb_ = rpb.next()
                    P.transpose(pb_, og[:, :], identb[:, :], [og, identb], [tb_])
                    P.copy(ogT[:, t * 128:(t + 1) * 128], pb_, [tb_], [ogT], q="act")
                P.dma(ov[:, hv, :], ogT[:, :], [ogT], [r_ogT])

    def ret_proj(self, i, j, b):
        P = self.P
        hT_d = self.scratch("hT_d", [128, 16, TOK], BF16)
        r_qT = self.scratch("r_qT", [D, TOK], BF16)
        r_kT = self.scratch("r_kT", [D, TOK], BF16)
        r_v = self.scratch("r_v", [TOK, 4096], BF16)
        r_gate = self.scratch("r_gate", [TOK, 4096], BF16)
        win = self.W["ret_w_in"].t[j].rearrange("(c p) n -> p c n", p=128)
        blocks = [(0, 256)] + [(256 + 512 * k, 512) for k in range(4)]
        with P.phase():
            hT = P.sbuf("rp_hT", [128, 16, TOK], BF16)
            P.dma(hT[:, :, :], hT_d[:, :, :], [hT_d], [hT])
            rope = P.sbuf("rp_rope", [128, 4, SEQ], F32)
            P.dma(rope[:, :, :], self.C["rope"][:, :, :], [], [rope])
            rrf = P.sbuf("rp_rrf", [128, 128], F32)
            rrb = P.sbuf("rp_rrb", [128, 128], BF16)
            P.dma(rrf[:, :], self.C["rrotT"][:, :], [], [rrf])
            P.copy(rrb[:, :], rrf[:, :], [rrf], [rrb])
            wp = [P.sbuf("rp_wp%d" % k, [128, 16, 512], BF16) for k in range(2)]
            qraw = [P.sbuf("rp_qraw%d" % k, [128, 512], BF16) for k in range(2)]
            T1 = [P.sbuf("rp_t1%d" % k, [128, 512], F32) for k in range(2)]
            T2 = [P.sbuf("rp_t2%d" % k, [128, 512], F32) for k in range(2)]
            qst = [P.sbuf("rp_qst%d" % k, [128, TOK], BF16) for k in range(2)]
            psA = [P.psum("rp_psA%d" % k, [128, 512], F32) for k in range(4)]
            psR = [P.psum("rp_psR%d" % k, [128, 512], F32) for k in range(2)]
            cnt = 0
            for pn in range(8):
                Wp = wp[pn % 2]
                P.dma(Wp[:, :, :], win[:, :, pn * 512:(pn + 1) * 512], [], [Wp], q="pool")
                for nb in range(4):
                    chunk = pn * 4 + nb
                    isk = chunk >= 16
                    typ = chunk % 2
                    sc = 0.0625 if isk else 1.0
                    QS = qst[chunk % 2]
                    for bi, (t0, n) in enumerate(blocks):
                        ps = psA[cnt % 4]
                        for k in range(16):
                            P.matmul(ps[:, 0:n], Wp[:, k, nb * 128:(nb + 1) * 128], hT[:, k, t0:t0 + n], k == 0, k == 15,
                                     [Wp, hT], [ps])
                        if bi == 0:
                            P.act(QS[:, 0:256], ps[:, 0:256], AF.Copy, [ps], [QS], scale=sc)
                        else:
                            QR = qraw[cnt % 2]
                            P.act(QR[:, :], ps[:, :], AF.Copy, [ps], [QR], scale=sc)
                            pr = psR[cnt % 2]
                            P.matmul(pr[:, :], rrb[:, :], QR[:, :], True, True, [rrb, QR], [pr])
                            x0 = t0 - 256
                            P.tt(T1[cnt % 2][:, :], QR[:, :], rope[:, 2 * typ, x0:x0 + 512], ALU.mult, [QR, rope], [T1[cnt % 2]])
                            P.tt(T2[cnt % 2][:, :], pr[:, :], rope[:, 2 * typ + 1, x0:x0 + 512], ALU.mult, [pr, rope],
                                 [T2[cnt % 2]])
                            P.tt(QS[:, t0:t0 + 512], T1[cnt % 2][:, :], T2[cnt % 2][:, :], ALU.add,
                                 [T1[cnt % 2], T2[cnt % 2]], [QS])
                        cnt += 1
                    dst = r_kT if isk else r_qT
                    row0 = (chunk % 16) * 128
                    P.dma(dst[row0:row0 + 128, :], QS[:, :], [QS], [dst])
        with P.phase():
            hT = P.sbuf("rv_hT", [128, 16, TOK], BF16)
            P.dma(hT[:, :, :], hT_d[:, :, :], [hT_d], [hT])
            wp = [P.sbuf("rv_wp%d" % k, [128, 16, 512], BF16) for k in range(2)]
            stage = [P.sbuf("rv_st%d" % k, [128, NT, 512], BF16) for k in range(2)]
            psA = [P.psum("rv_psA%d" % k, [128, 512], F32) for k in range(4)]
            cnt = 0
            for pn in range(16):
                Wp = wp[pn % 2]
                ST = stage[pn % 2]
                c0 = 4096 + pn * 512
                P.dma(Wp[:, :, :], win[:, :, c0:c0 + 512], [], [Wp], q="pool")
                for t in range(NT):
                    ps = psA[cnt % 4]
                    cnt += 1
                    for k in range(16):
                        P.matmul(ps[:, :], hT[:, k, t * 128:(t + 1) * 128], Wp[:, k, :], k == 0, k == 15, [hT, Wp], [ps])
                    P.act(ST[:, t, :], ps[:, :], AF.Copy if pn < 8 else AF.Silu, [ps], [ST])
                dst = r_v if pn < 8 else r_gate
                cc = (pn % 8) * 512
                P.dma(dst.t.rearrange("(t p) e -> p t e", p=128)[:, :, cc:cc + 512], ST[:, :, :], [ST], [dst])

    def ret_core(self, i, j, b):
        P = self.P
        r_qT = self.scratch("r_qT", [D, TOK], BF16)
        r_kT = self.scratch("r_kT", [D, TOK], BF16)
        r_v = self.scratch("r_v", [TOK, 4096], BF16)
        r_gate = self.scratch("r_gate", [TOK, 4096], BF16)
        r_ogT = self.scratch("r_ogT", [4096, TOK], BF16)
        with P.phase():
            def cload(nm, shape):
                t = P.sbuf("rc_" + nm, shape, F32)
                P.dma(t[:, :], self.C[nm][:, :], [], [t])
                return t
            relF = cload("relF", [128, 128])
            maskF = cload("maskF", [128, 128])
            relB = cload("relB", [128, 128])
            maskB = cload("maskB", [128, 128])
            posrowF = cload("posrowF", [128, 128])
            posrowB = cload("posrowB", [128, 128])
            poscol = cload("poscol", [128, 4])
            lg = P.sbuf("rc_lg", [128, 16], F32)
            P.dma(lg[:, :], self.W["ret_log_decay"].t[j].rearrange("d h -> (d h)").partition_broadcast(128), [], [lg])
            gnw = P.sbuf("rc_gnw", [128, 512], F32)
            P.dma(gnw[:, :], self.W["ret_gn_w"].t[j].partition_broadcast(128), [], [gnw])
            E1 = P.sbuf("rc_E1", [128, 128], F32)
            E2 = P.sbuf("rc_E2", [128, 128], F32)
            DT = P.sbuf("rc_DT", [128, 128], F32)
            TFq = P.sbuf("rc_TFq", [128, 128], F32)
            TBq = P.sbuf("rc_TBq", [128, 128], F32)
            cols = P.sbuf("rc_cols", [128, 4], F32)
            qT = P.sbuf("rc_qT", [128, 2, TOK], BF16)
            kT = P.sbuf("rc_kT", [128, 2, TOK], BF16)
            vh = P.sbuf("rc_v", [128, NT, 512], BF16)
            gh = P.sbuf("rc_g", [128, NT, 512], BF16)
            qdf = P.sbuf("rc_qdf", [128, 2, TOK], BF16)
            qdb = P.sbuf("rc_qdb", [128, 2, TOK], BF16)
            kdb_all = P.sbuf("rc_kdb", [128, NT, 256], BF16)
            kdf = [P.sbuf("rc_kdf%d" % k, [128, 256], BF16) for k in range(2)]
            ATs = [P.sbuf("rc_AT%d" % k, [128, 128], BF16) for k in range(2)]
            oacc = P.sbuf("rc_oacc", [128, NT, 512], F32)
            S32 = [P.sbuf("rc_S32_%d" % k, [128, 2, 512], F32) for k in range(2)]
            S16 = [P.sbuf("rc_S16_%d" % k, [128, 2, 512], BF16) for k in range(2)]
            ogT = P.sbuf("rc_ogT", [128, 4, TOK], BF16)
            ogt = [P.sbuf("rc_ogt%d" % k, [128, 512], BF16) for k in range(2)]
            tmp = [P.sbuf("rc_tmp%d" % k, [128, 512], F32) for k in range(2)]
            junk = P.sbuf("rc_junk", [128, 512], F32)
            st = [P.sbuf("rc_st%d" % k, [128, 8], F32) for k in range(2)]
            psAT = [P.psum("rc_psAT%d" % k, [128, 512], F32) for k in range(2)]
            psK = [P.psum("rc_psK%d" % k, [128, 512], BF16) for k in range(2)]
            psO = [P.psum("rc_psO%d" % k, [128, 512], F32) for k in range(2)]
            psS = [P.psum("rc_psS%d" % k, [128, 512], F32) for k in range(2)]
            order_f = list(range(NT))
            order_b = [1, 0] + list(range(NT - 1, 1, -1))
            qv = r_qT.t.rearrange("(c p) t -> p c t", p=128)
            kv = r_kT.t.rearrange("(c p) t -> p c t", p=128)
            vv = r_v.t.rearrange("(t p) e -> p t e", p=128)
            gv = r_gate.t.rearrange("(t p) e -> p t e", p=128)
            ov = r_ogT.t.rearrange("(c p) t -> p c t", p=128)
            for h in range(8):
                lgf = lg[:, h:h + 1]
                lgb = lg[:, 8 + h:9 + h]
                P.dma(qT[:, :, :], qv[:, 2 * h:2 * h + 2, :], [r_qT], [qT])
                P.dma(kT[:, :, :], kv[:, 2 * h:2 * h + 2, :], [r_kT], [kT])
                P.dma(vh[:, :, :], vv[:, :, h * 512:(h + 1) * 512], [r_v], [vh])
                P.dma(gh[:, :, :], gv[:, :, h * 512:(h + 1) * 512], [r_gate], [gh])
                P.act(E1[:, :], relF[:, :], AF.Exp, [relF, lg], [E1], scale=lgf)
                P.tt(E1[:, :], E1[:, :], maskF[:, :], ALU.mult, [E1, maskF], [E1])
                P.act(E2[:, :], relB[:, :], AF.Exp, [relB, lg], [E2], scale=lgb)
                P.tt(E2[:, :], E2[:, :], maskB[:, :], ALU.mult, [E2, maskB], [E2])
                P.tt(DT[:, :], E1[:, :], E2[:, :], ALU.add, [E1, E2], [DT])
                P.act(TFq[:, :], posrowF[:, :], AF.Exp, [posrowF, lg], [TFq], scale=lgf)
                P.act(TBq[:, :], posrowB[:, :], AF.Exp, [posrowB, lg], [TBq], scale=lgb)
                P.act(cols[:, 0:1], poscol[:, 0:1], AF.Exp, [poscol, lg], [cols], scale=lgf)
                P.act(cols[:, 1:2], poscol[:, 1:2], AF.Exp, [poscol, lg], [cols], scale=lgb)
                P.act(cols[:, 2:3], lgf, AF.Exp, [lg], [cols], scale=128.0)
                P.act(cols[:, 3:4], lgb, AF.Exp, [lg], [cols], scale=128.0)
                for dc in range(2):
                    P.tt(qdf[:, dc, :].rearrange("p (t i) -> p t i", i=128), qT[:, dc, :].rearrange("p (t i) -> p t i", i=128),
                         TFq[:, :].unsqueeze(1).broadcast_to([128, NT, 128]), ALU.mult, [qT, TFq], [qdf])
                    P.tt(qdb[:, dc, :].rearrange("p (t i) -> p t i", i=128), qT[:, dc, :].rearrange("p (t i) -> p t i", i=128),
                         TBq[:, :].unsqueeze(1).broadcast_to([128, NT, 128]), ALU.mult, [qT, TBq], [qdb])
                Sf, Sfb = S32[0], S16[0]
                for n, t in enumerate(order_f):
                    sl = slice(t * 128, (t + 1) * 128)
                    pa = psAT[n % 2]
                    for dc in range(2):
                        P.matmul(pa[:, 0:128], kT[:, dc, sl], qT[:, dc, sl], dc == 0, dc == 1, [kT, qT], [pa])
                    AT = ATs[n % 2]
                    P.tt(AT[:, :], pa[:, 0:128], DT[:, :], ALU.mult, [pa, DT], [AT])
                    pk = psK[n % 2]
                    for dc in range(2):
                        P.transpose(pk[:, dc * 128:(dc + 1) * 128], kT[:, dc, sl], self.identb[:, :], [kT, self.identb], [pk])
                    KF = kdf[n % 2]
                    P.act(KF[:, :], pk[:, 0:256], AF.Copy, [pk, cols], [KF], scale=cols[:, 0:1])
                    P.ts(kdb_all[:, t, :], pk[:, 0:256], cols[:, 1:2], None, ALU.mult, None, [pk, cols], [kdb_all])
                    po = psO[n % 2]
                    P.matmul(po[:, :], AT[:, :], vh[:, t, :], True, n == 0, [AT, vh], [po])
                    if n > 0:
                        for dc in range(2):
                            P.matmul(po[:, :], qdf[:, dc, sl], Sfb[:, dc, :], False, dc == 1, [qdf, Sfb], [po])
                    P.copy(oacc[:, t, :], po[:, :], [po], [oacc], q="act")
                    for dc in range(2):
                        pS = psS[dc]
                        P.matmul(pS[:, :], KF[:, dc * 128:(dc + 1) * 128], vh[:, t, :], True, True, [KF, vh], [pS])
                        if n == 0:
                            P.copy(Sf[:, dc, :], pS[:, :], [pS], [Sf])
                        else:
                            P.stt(Sf[:, dc, :], Sf[:, dc, :], cols[:, 2:3], pS[:, :], ALU.mult, ALU.add, [Sf, cols, pS], [Sf])
                    if n < NT - 1:
                        P.copy(Sfb[:, :, :], Sf[:, :, :], [Sf], [Sfb], q="act")
                Sb, Sbb = S32[1], S16[1]
                for n, t in enumerate(order_b):
                    sl = slice(t * 128, (t + 1) * 128)
                    if n > 0:
                        po = psO[n % 2]
                        for dc in range(2):
                            P.matmul(po[:, :], qdb[:, dc, sl], Sbb[:, dc, :], dc == 0, dc == 1, [qdb, Sbb], [po])
                        P.tt(oacc[:, t, :], oacc[:, t, :], po[:, :], ALU.add, [oacc, po], [oacc])
                    for dc in range(2):
                        pS = psS[dc]
                        P.matmul(pS[:, :], kdb_all[:, t, dc * 128:(dc + 1) * 128], vh[:, t, :], True, True, [kdb_all, vh], [pS])
                        if n == 0:
                            P.copy(Sb[:, dc, :], pS[:, :], [pS], [Sb])
                        else:
                            P.stt(Sb[:, dc, :], Sb[:, dc, :], cols[:, 3:4], pS[:, :], ALU.mult, ALU.add, [Sb, cols, pS], [Sb])
                    if n < NT - 1:
                        P.copy(Sbb[:, :, :], Sb[:, :, :], [Sb], [Sbb], q="act")
                for t in range(NT):
                    S = st[t % 2]
                    o = oacc[:, t, :]
                    P.memset(S[:, 0:2], 0.0, [S])
                    P.act(junk[:, :], o, AF.Identity, [oacc], [junk, S], accum_out=S[:, 0:1])
                    P.act(junk[:, :], o, AF.Square, [oacc], [junk, S], accum_out=S[:, 1:2])
                    P.ts(S[:, 2:3], S[:, 0:1], 1.0 / 512, None, ALU.mult, None, [S], [S])
                    P.tt(S[:, 3:4], S[:, 2:3], S[:, 2:3], ALU.mult, [S], [S])
                    P.stt(S[:, 4:5], S[:, 1:2], 1.0 / 512, S[:, 3:4], ALU.mult, ALU.subtract, [S], [S])
                    P.ts(S[:, 4:5], S[:, 4:5], EPS, None, ALU.add, None, [S], [S])
                    P.act(S[:, 5:6], S[:, 4:5], AF.Sqrt, [S], [S])
                    P.recip(S[:, 6:7], S[:, 5:6], [S], [S])
                    TM = tmp[t % 2]
                    P.ts(TM[:, :], o, S[:, 2:3], S[:, 6:7], ALU.subtract, ALU.mult, [oacc, S], [TM])
                    P.tt(TM[:, :], TM[:, :], gnw[:, :], ALU.mult, [TM, gnw], [TM])
                    OG = ogt[t % 2]
                    P.tt(OG[:, :], TM[:, :], gh[:, t, :], ALU.mult, [TM, gh], [OG])
                    pk = psK[t % 2]
                    for ec in range(4):
                        P.transpose(pk[:, ec * 128:(ec + 1) * 128], OG[:, ec * 128:(ec + 1) * 128], self.identb[:, :],
                                    [OG, self.identb], [pk])
                    P.copy(ogT[:, :, t * 128:(t + 1) * 128], pk[:, :].rearrange("p (e i) -> p e i", e=4), [pk], [ogT],
                           q="act")
                P.dma(ov[:, 4 * h:4 * h + 4, :], ogT[:, :, :], [ogT], [r_ogT])


def build(cfg):
    nc = bass.Bass("TRN2", target_bir_lowering=False)
    with contextlib.ExitStack() as stack:
        M = Model(nc, stack, cfg)
        P = M.P
        M.copy_inputs()
        if cfg.get("only_gdn_core"):
            M.gdn_core(1, 0, 0)
            P.barrier()
            P.flush()
            return nc
        M.modulation()
        for i in cfg.get("layers", range(DEPTH)):
            if "mixer" in cfg.get("phases", ["mixer", "ffn"]):
                M.mixer(i)
            if "ffn" in cfg.get("phases", ["mixer", "ffn"]):
                M.ffn(i)
        if cfg.get("final", True):
            M.final_norm()
        for nm in cfg.get("dump", []):
            src = M._scr[nm]
            shp = list(src.t.shape)
            dt = src.t.dtype
            do = P.dram("dump_" + nm, shp, dt, kind="ExternalOutput")
            if len(shp) == 2:
                P.dma(do[:, :], src[:, :], [src], [do])
            else:
                P.dma(do[:, :, :], src[:, :, :], [src], [do])
        if cfg.get("debug_ctx"):
            co = P.dram("ctx_out", [BPC, CTXL, D], F32, kind="ExternalOutput")
            for b in range(BPC):
                P.dma(co[b, :, :], M.ctxs[b, :, :], [M.rtok[b][0], M.rtok[b][1]], [co])
        P.barrier()
        P.flush()
        print("ops", P.nops, "blocks", P.nblocks)
    return nc


def make_in_maps(inputs, ncores=NCORES):
    consts = host_constants()
    maps = []
    for r in range(ncores):
        m = {}
        m["x"] = np.ascontiguousarray(inputs["x"][r * BPC:(r + 1) * BPC])
        m["ctx"] = np.ascontiguousarray(inputs["ctx"][r * BPC:(r + 1) * BPC])
        m["cvec"] = np.ascontiguousarray(
            np.concatenate([inputs["c"][r * BPC:(r + 1) * BPC], inputs["c_ctx"][None, :]], axis=0))
        for nm in WEIGHT_NAMES:
            m[nm] = np.ascontiguousarray(inputs[nm])
        for nm, arr in consts.items():
            m["k_" + nm] = arr
        maps.append(m)
    return maps


GROUP = NCORES


def kernel(**inputs):
    inputs = {k: np.asarray(v) for k, v in inputs.items()}
    nc = build({})
    maps = make_in_maps(inputs)
    outs = []
    for g0 in range(0, NCORES, GROUP):
        res = run_bass_kernel_spmd(nc, maps[g0:g0 + GROUP], core_ids=list(range(GROUP)))
        outs += [res.results[r]["out"] for r in range(GROUP)]
    return np.concatenate(outs, axis=0).astype(np.float32)
```

```python
import contextlib
import math
import numpy as np
import ml_dtypes
import concourse.bass as bass
import concourse.mybir as mybir
from concourse.bass_utils import run_bass_kernel_spmd

F32 = mybir.dt.float32
BF16 = mybir.dt.bfloat16
ALU = mybir.AluOpType
AF = mybir.ActivationFunctionType
AX = mybir.AxisListType

D = 2048
SEQ = 2048
CTXL = 256
TOK = SEQ + CTXL
NT = TOK // 128
DEPTH = 4
DFF = 8192
EPS = 1e-6
NCORES = 8
BPC = 2
NM = BPC + 1


class Tok:
    __slots__ = ("name", "lw", "rs")

    def __init__(self, name=""):
        self.name = name
        self.lw = None
        self.rs = {}


class TB:
    def __init__(self, t, name=""):
        self.t = t
        self.tok = Tok(name)

    def __getitem__(self, idx):
        return self.t[idx]


def _toks(lst):
    out = []
    for x in lst:
        if x is None:
            continue
        out.append(x.tok if isinstance(x, TB) else x)
    return out


class Prog:
    STREAMS = ["pe", "act", "dve", "pool", "sp"]
    NDMA = 8

    def __init__(self, nc, stack):
        self.nc = nc
        self.stack = stack
        self.cur = stack
        self.streams = {q: [] for q in self.STREAMS}
        self.count = {}
        self.known = {q: {} for q in self.STREAMS}
        self.dma_rr = {q: 0 for q in self.STREAMS}
        self.semh = {}
        self.nops = 0
        self.nblocks = 0
        for q in self.STREAMS:
            self._sem((q, -1, 0))
        for q in ("sp", "pool"):
            for k in range(self.NDMA):
                self._sem((q, k, 0))

    def _sem(self, key):
        if key not in self.semh:
            nm = "s_%s_%d_%d" % (key[0], key[1] + 1, key[2])
            self.semh[key] = self.stack.enter_context(self.nc.semaphore(nm))
            self.count.setdefault(key, 0)
        return self.semh[key]

    def sbuf(self, name, shape, dtype):
        self.uid = getattr(self, "uid", 0) + 1
        name = "%s_u%d" % (name, self.uid)
        t = self.cur.enter_context(self.nc.sbuf_tensor(name, list(shape), dtype))
        return TB(t, name)

    def psum(self, name, shape, dtype=F32):
        self.uid = getattr(self, "uid", 0) + 1
        name = "%s_u%d" % (name, self.uid)
        t = self.cur.enter_context(self.nc.psum_tensor(name, list(shape), dtype))
        return TB(t, name)

    def dram(self, name, shape, dtype, kind="Internal"):
        t = self.nc.dram_tensor(name, list(shape), dtype, kind=kind)
        return TB(t.ap(), name)

    @contextlib.contextmanager
    def phase(self):
        prev = self.cur
        with contextlib.ExitStack() as st:
            self.cur = st
            yield
            self.barrier()
            self.flush()
        self.cur = prev

    LIMIT = 20000

    def op(self, stream, fn, reads=(), writes=(), dma=False):
        gens = self.__dict__.setdefault("gens", {})
        if dma:
            k = self.dma_rr[stream]
            self.dma_rr[stream] = (k + 1) % self.NDMA
            base = (stream, k)
            step = 16
        else:
            base = (stream, -1)
            step = 1
        g = gens.get(base, 0)
        key = base + (g,)
        if self.count.get(key, 0) + step > self.LIMIT:
            g += 1
            gens[base] = g
            key = base + (g,)
        self._sem(key)
        n = self.count.get(key, 0) + step
        self.count[key] = n
        deps = {}
        reads = _toks(reads)
        writes = _toks(writes)
        for b in reads:
            if b.lw is not None and deps.get(b.lw[0], 0) < b.lw[1]:
                deps[b.lw[0]] = b.lw[1]
        for b in writes:
            if b.lw is not None and deps.get(b.lw[0], 0) < b.lw[1]:
                deps[b.lw[0]] = b.lw[1]
            for kk, v in b.rs.items():
                if deps.get(kk, 0) < v:
                    deps[kk] = v
        known = self.known[stream]
        waits = []
        for kk, v in deps.items():
            if stream == "pe" and kk[0] == "pe" and kk[1] == -1:
                continue
            if known.get(kk, 0) >= v:
                continue
            known[kk] = v
            waits.append((kk, v))
        self.streams[stream].append((waits, fn, key, step))
        for b in reads:
            if b.rs.get(key, 0) < n:
                b.rs[key] = n
        for b in writes:
            b.lw = (key, n)
            b.rs = {}
        self.nops += 1

    def barrier(self):
        for s in self.STREAMS:
            known = self.known[s]
            waits = []
            for kk, v in self.count.items():
                if v > 0 and known.get(kk, 0) < v:
                    known[kk] = v
                    waits.append((kk, v))
            if waits:
                self.streams[s].append((waits, None, None, 0))

    def flush(self):
        nc = self.nc
        semh = self.semh
        st = self.streams
        self.streams = {q: [] for q in self.STREAMS}
        if not any(st.values()):
            return
        self.nblocks += 1

        def replay(eng, lst):
            for waits, fn, key, step in lst:
                for kk, v in waits:
                    eng.wait_ge(semh[kk], v)
                if fn is not None:
                    ins = fn(eng)
                    ins.then_inc(semh[key], step)

        with nc.Block() as block:
            @block.tensor
            def _(e):
                replay(e, st["pe"])

            @block.scalar
            def _(e):
                replay(e, st["act"])

            @block.vector
            def _(e):
                replay(e, st["dve"])

            @block.gpsimd
            def _(e):
                replay(e, st["pool"])

            @block.sync
            def _(e):
                replay(e, st["sp"])

    def dma(self, out, in_, reads, writes, q="sp"):
        self.op(q, lambda e: e.dma_start(out=out, in_=in_), reads, writes, dma=True)

    def matmul(self, out, lhsT, rhs, start, stop, reads, writes):
        self.op("pe", lambda e: e.matmul(out, lhsT, rhs, start=start, stop=stop), reads, writes)

    def transpose(self, out, in_, ident, reads, writes):
        self.op("pe", lambda e: e.transpose(out, in_, ident), reads, writes)

    def act(self, out, in_, func, reads, writes, bias=None, scale=None, accum_out=None):
        kw = {}
        if bias is not None:
            kw["bias"] = bias
        if scale is not None:
            kw["scale"] = scale
        if accum_out is not None:
            kw["accum_out"] = accum_out
        self.op("act", lambda e: e.activation(out, in_, func, **kw), reads, writes)

    def tt(self, out, in0, in1, op, reads, writes, q="dve"):
        self.op(q, lambda e: e.tensor_tensor(out, in0, in1, op), reads, writes)

    def ts(self, out, in0, s1, s2, op0, op1, reads, writes, q="dve"):
        if op1 is None:
            self.op(q, lambda e: e.tensor_scalar(out, in0, s1, None, op0), reads, writes)
        else:
            self.op(q, lambda e: e.tensor_scalar(out, in0, s1, s2, op0, op1), reads, writes)

    def stt(self, out, in0, scalar, in1, op0, op1, reads, writes, q="dve"):
        self.op(q, lambda e: e.scalar_tensor_tensor(out, in0, scalar, in1, op0, op1), reads, writes)

    def copy(self, out, in_, reads, writes, q="dve"):
        if q == "act":
            self.op(q, lambda e: e.activation(out, in_, AF.Copy), reads, writes)
        else:
            self.op(q, lambda e: e.tensor_copy(out, in_), reads, writes)

    def memset(self, ap, val, writes, q="dve"):
        self.op(q, lambda e: e.memset(ap, val), (), writes)

    def recip(self, out, in_, reads, writes):
        self.op("dve", lambda e: e.reciprocal(out, in_), reads, writes)


WEIGHT_NAMES = ["mod_w", "mod_b", "norm_w", "final_norm_w", "ffn_w1", "ffn_w2",
                "s5_lam_re", "s5_lam_im", "s5_log_dt", "s5_b_re", "s5_b_im", "s5_c_re", "s5_c_im", "s5_d",
                "s5_glu_w", "gdn_w_in", "gdn_conv_w", "gdn_a_log", "gdn_dt_bias", "gdn_norm_w", "gdn_w_out",
                "ret_w_in", "ret_log_decay", "ret_gn_w", "ret_w_out"]
WEIGHT_SHAPES = {
    "mod_w": (4, 2048, 12288), "mod_b": (4, 12288), "norm_w": (4, 2, 2048), "final_norm_w": (2048,),
    "ffn_w1": (4, 2048, 8192), "ffn_w2": (4, 8192, 2048),
    "s5_lam_re": (2, 2, 128, 64), "s5_lam_im": (2, 2, 128, 64), "s5_log_dt": (2, 2, 128),
    "s5_b_re": (2, 2, 128, 64, 16), "s5_b_im": (2, 2, 128, 64, 16),
    "s5_c_re": (2, 2, 128, 16, 64), "s5_c_im": (2, 2, 128, 16, 64), "s5_d": (2, 2048),
    "s5_glu_w": (2, 2048, 4096), "gdn_w_in": (1, 2048, 12416), "gdn_conv_w": (1, 5, 8192),
    "gdn_a_log": (1, 2, 32), "gdn_dt_bias": (1, 2, 32), "gdn_norm_w": (1, 128), "gdn_w_out": (1, 4096, 2048),
    "ret_w_in": (1, 2048, 12288), "ret_log_decay": (1, 2, 8), "ret_gn_w": (1, 512), "ret_w_out": (1, 4096, 2048),
}


def host_constants():
    c = {}
    c["ident"] = np.eye(128, dtype=np.float32)
    c["ones"] = np.ones((128, 128), dtype=np.float32)
    R = np.zeros((128, 128), np.float32)
    for m in range(64):
        R[m, m + 64] = -1.0
        R[m + 64, m] = 1.0
    c["rrotT"] = np.ascontiguousarray(R.T)
    half = 64
    inv_freq = (10000.0 ** (-np.arange(half, dtype=np.float32) / half)).astype(np.float32)
    tok = np.arange(SEQ)
    row = (tok // 64).astype(np.float32)
    col = (tok % 64).astype(np.float32)
    fr = np.concatenate([inv_freq, inv_freq])
    ar = (row[None, :] * fr[:, None]).astype(np.float32)
    ac = (col[None, :] * fr[:, None]).astype(np.float32)
    c["rope"] = np.stack([np.cos(ar), np.sin(ar), np.cos(ac), np.sin(ac)], axis=1).astype(np.float32)
    jj = np.arange(128)[:, None]
    ii = np.arange(128)[None, :]
    c["relF"] = np.maximum(ii - jj, 0).astype(np.float32)
    c["maskF"] = (ii >= jj).astype(np.float32)
    c["relB"] = np.maximum(jj - ii, 0).astype(np.float32)
    c["maskB"] = (jj >= ii).astype(np.float32)
    c["posrowF"] = np.broadcast_to((ii + 1.0), (128, 128)).astype(np.float32).copy()
    c["posrowB"] = np.broadcast_to((128.0 - ii), (128, 128)).astype(np.float32).copy()
    pc = np.zeros((128, 4), np.float32)
    pc[:, 0] = 127.0 - np.arange(128)
    pc[:, 1] = np.arange(128)
    c["poscol"] = pc
    r_ = np.arange(128)[:, None]
    c_ = np.arange(128)[None, :]
    c["LI"] = (r_ >= c_).astype(np.float32)
    c["LS"] = (r_ > c_).astype(np.float32)
    c["UI"] = (r_ <= c_).astype(np.float32)
    c["US"] = (r_ < c_).astype(np.float32)
    idx = np.arange(128)
    c["blk16"] = ((idx[:, None] // 16) == (idx[None, :] // 16)).astype(np.float32)
    for bs in (16, 32, 64):
        c["lvl%d" % bs] = (((idx[:, None] // (2 * bs)) == (idx[None, :] // (2 * bs)))
                           & ((idx[:, None] // bs) != (idx[None, :] // bs))).astype(np.float32)
    rr = np.arange(128)
    s_of = rr // 16
    c["s5maskF"] = (s_of[None, :] >= s_of[:, None]).astype(np.float32)
    c["s5maskB"] = (s_of[:, None] >= s_of[None, :]).astype(np.float32)
    c["iota288"] = np.broadcast_to(np.arange(288, dtype=np.float32), (128, 288)).copy()
    sel16 = np.zeros((16, 128), np.float32)
    for r in range(128):
        sel16[r % 16, r] = 1.0
    c["s5sel16"] = sel16
    selT = np.zeros((128, 64, 128), np.float32)
    for g8 in range(8):
        for t in range(8):
            for cc in range(16):
                selT[t * 16 + cc, g8 * 8 + t, g8 * 16 + cc] = 1.0
    c["s5selT"] = selT
    return c


class Model:
    def __init__(self, nc, stack, cfg):
        self.cfg = cfg
        P = self.P = Prog(nc, stack)
        self.x_in = P.dram("x", [BPC, SEQ, D], F32, kind="ExternalInput")
        self.ctx_in = P.dram("ctx", [BPC, CTXL, D], F32, kind="ExternalInput")
        self.cvec = P.dram("cvec", [NM, D], F32, kind="ExternalInput")
        self.W = {}
        for nm in WEIGHT_NAMES:
            kind = "ExternalInput"
            if cfg.get("internal_weights") and nm not in cfg.get("keep_weights", []):
                self.W[nm] = P.dram(nm, [2, 2], F32, kind="Internal")
                continue
            self.W[nm] = P.dram(nm, list(WEIGHT_SHAPES[nm]), F32, kind=kind)
        self.C = {}
        for nm, arr in host_constants().items():
            self.C[nm] = P.dram("k_" + nm, list(arr.shape), F32, kind="ExternalInput")
        self.out = P.dram("out", [BPC, SEQ, D], F32, kind="ExternalOutput")
        self.ctxs = P.dram("ctxs", [BPC, CTXL, D], F32)
        self.rtok = [[Tok("r%d_%d" % (b, t)) for t in range(NT)] for b in range(BPC)]
        self.ident = P.sbuf("ident", [128, 128], F32)
        self.ones = P.sbuf("ones", [128, 128], F32)
        self.identb = P.sbuf("identb", [128, 128], BF16)
        self.modT = P.sbuf("modT", [128, DEPTH, 96, NM], F32)
        self.nwT = P.sbuf("nwT", [128, DEPTH * 2 + 1, 16], F32)
        self.A = P.sbuf("Amod", [128, DEPTH, 2, 16, NM], F32)
        P.dma(self.ident[:, :], self.C["ident"][:, :], [self.C["ident"]], [self.ident])
        P.dma(self.ones[:, :], self.C["ones"][:, :], [self.C["ones"]], [self.ones])
        P.copy(self.identb[:, :], self.ident[:, :], [self.ident], [self.identb])

    def res_ap(self, b, t):
        if t < 2:
            return self.ctxs[b, t * 128:(t + 1) * 128, :]
        return self.out[b, (t - 2) * 128:(t - 1) * 128, :]

    def copy_inputs(self):
        P = self.P
        for b in range(BPC):
            P.dma(self.ctxs[b, :, :], self.ctx_in[b, :, :], [self.ctx_in], [self.rtok[b][0], self.rtok[b][1]])
            for q in range(4):
                P.dma(self.out[b, q * 512:(q + 1) * 512, :], self.x_in[b, q * 512:(q + 1) * 512, :], [self.x_in],
                      [self.rtok[b][2 + q * 4 + k] for k in range(4)])

    def modulation(self):
        P = self.P
        with P.phase():
            cs = P.sbuf("cs", [NM, D], F32)
            scT = P.sbuf("scT", [128, 16, NM], F32)
            pst = P.psum("pst", [128, 512], F32)
            P.dma(cs[:, :], self.cvec[:, :], [self.cvec], [cs])
            P.act(cs[:, :], cs[:, :], AF.Silu, [cs], [cs])
            for k in range(16):
                P.transpose(pst[:, k * NM:(k + 1) * NM], cs[0:NM, k * 128:(k + 1) * 128], self.ident[0:NM, 0:NM],
                            [cs, self.ident], [pst])
            P.copy(scT[:, :, :], pst[:, 0:16 * NM].rearrange("p (k m) -> p k m", m=NM), [pst], [scT])
            nwr = P.sbuf("nwr", [128, 128], F32)
            nwr2 = P.sbuf("nwr2", [16, 128], F32)
            P.dma(nwr[:, :], self.W["norm_w"].t.rearrange("l w (c p) -> (l w c) p", p=128),
                  [self.W["norm_w"]], [nwr])
            P.dma(nwr2[:, :], self.W["final_norm_w"].t.rearrange("(c p) -> c p", p=128),
                  [self.W["final_norm_w"]], [nwr2])
            pst2 = P.psum("pst2", [128, 512], F32)
            P.transpose(pst2[:, 0:128], nwr[:, :], self.ident[:, :], [nwr, self.ident], [pst2])
            P.transpose(pst2[:, 128:144], nwr2[:, :], self.ident[0:16, 0:16], [nwr2, self.ident], [pst2])
            P.copy(self.nwT[:, :, :], pst2[:, 0:144].rearrange("p (l c) -> p l c", c=16), [pst2], [self.nwT])
            wp = [P.sbuf("modw%d" % i, [128, 16, 512], F32) for i in range(2)]
            pm = [P.psum("pm%d" % i, [128, 512], F32) for i in range(2)]
            mbr = P.sbuf("mbr", [96, 128], F32)
            mbT = P.sbuf("mbT", [128, 96], F32)
            pb = P.psum("pb", [128, 512], F32)
            for i in self.cfg.get("layers", range(DEPTH)):
                P.dma(mbr[:, :], self.W["mod_b"].t[i].rearrange("(c p) -> c p", p=128), [self.W["mod_b"]], [mbr])
                P.transpose(pb[:, 0:96], mbr[0:96, :], self.ident[0:96, 0:96], [mbr, self.ident], [pb])
                P.copy(mbT[:, :], pb[:, 0:96], [pb], [mbT])
                wv = self.W["mod_w"].t[i].rearrange("(c p) n -> p c n", p=128)
                for pn in range(24):
                    w = wp[pn % 2]
                    P.dma(w[:, :, :], wv[:, :, pn * 512:(pn + 1) * 512], [self.W["mod_w"]], [w])
                    ps = pm[pn % 2]
                    for nb in range(4):
                        for k in range(16):
                            P.matmul(ps[:, nb * NM:(nb + 1) * NM], w[:, k, nb * 128:(nb + 1) * 128], scT[:, k, :],
                                     k == 0, k == 15, [w, scT], [ps])
                    for nb in range(4):
                        P.ts(self.modT[:, i, pn * 4 + nb, :], ps[:, nb * NM:(nb + 1) * NM], mbT[:, pn * 4 + nb:pn * 4 + nb + 1],
                             None, ALU.add, None, [ps, mbT], [self.modT])
                for wh in range(2):
                    sc = self.modT[:, i, (1 + 3 * wh) * 16:(2 + 3 * wh) * 16, :]
                    for m in range(NM):
                        P.stt(self.A[:, i, wh, :, m], self.modT[:, i, (1 + 3 * wh) * 16:(2 + 3 * wh) * 16, m], 1.0,
                              self.nwT[:, i * 2 + wh, :], ALU.add, ALU.mult, [self.modT, self.nwT], [self.A])

    def gate_rows(self, i, wh, m, gtile, ps):
        P = self.P
        base = (2 + 3 * wh) * 16
        for c4 in range(4):
            for cc in range(4):
                c = c4 * 4 + cc
                dg = self._dg[c % 2]
                P.ts(dg[:, :], self.ident[:, :], self.modT[:, i, base + c, m:m + 1], None, ALU.mult, None,
                     [self.ident, self.modT], [dg])
                P.matmul(ps[:, cc * 128:(cc + 1) * 128], self.ones[:, :], dg[:, :], True, True, [self.ones, dg], [ps])
            P.copy(gtile[:, c4 * 512:(c4 + 1) * 512], ps[:, :], [ps], [gtile], q="act")

    def norm_tiles(self, i, wh, b, tiles, hT, hT_off, bufs, deint=False):
        P = self.P
        xt, xn, ss, pss = bufs
        for k, t in enumerate(tiles):
            m = BPC if t < 2 else b
            X = xt[k % 2]
            XN = xn[k % len(xn)]
            S = ss[k % 2]
            P.dma(X[:, :], self.res_ap(b, t), [self.rtok[b][t]], [X])
            P.memset(S[:, 0:1], 0.0, [S])
            P.act(XN[:, :], X[:, :], AF.Square, [X], [XN, S], accum_out=S[:, 0:1])
            P.ts(S[:, 1:2], S[:, 0:1], 1.0 / D, EPS, ALU.mult, ALU.add, [S], [S])
            P.act(S[:, 2:3], S[:, 1:2], AF.Sqrt, [S], [S])
            P.recip(S[:, 3:4], S[:, 2:3], [S], [S])
            P.act(XN[:, :], X[:, :], AF.Copy, [X, S], [XN], scale=S[:, 3:4])
            for c4 in range(4):
                ps = pss[(k * 4 + c4) % len(pss)]
                for cc in range(4):
                    c = c4 * 4 + cc
                    P.transpose(ps[:, cc * 128:(cc + 1) * 128], XN[:, c * 128:(c + 1) * 128], self.ident[:, :],
                                [XN, self.ident], [ps])
                for cc in range(4):
                    c = c4 * 4 + cc
                    if deint:
                        o = hT[:, c, :, k * 16:(k + 1) * 16]
                        pin = ps[:, cc * 128:(cc + 1) * 128].rearrange("p (j s) -> p s j", s=8)
                    else:
                        o = hT[:, c, hT_off + k * 128:hT_off + (k + 1) * 128]
                        pin = ps[:, cc * 128:(cc + 1) * 128]
                    P.act(o, pin, AF.Identity, [ps, self.A, self.modT], [hT],
                          scale=self.A[:, i, wh, c, m:m + 1], bias=self.modT[:, i, (3 * wh) * 16 + c, m:m + 1])

    def ffn(self, i):
        P = self.P
        TBT = 6
        NTOK = TBT * 128
        with P.phase():
            self._dg = [P.sbuf("dg%d" % k, [128, 128], F32) for k in range(2)]
            hT = P.sbuf("f_hT", [128, 16, NTOK], BF16)
            yacc = P.sbuf("f_yacc", [128, TBT, D], F32)
            uT = [P.sbuf("f_uT%d" % k, [128, 4, NTOK], BF16) for k in range(2)]
            w1p = [P.sbuf("f_w1p%d" % k, [128, 16, 512], BF16) for k in range(2)]
            w2p = [P.sbuf("f_w2p%d" % k, [128, 4, D], BF16) for k in range(2)]
            rl = [P.sbuf("f_rl%d" % k, [128, NTOK], F32) for k in range(1)]
            G = [P.sbuf("f_G%d" % k, [128, D], F32) for k in range(2)]
            xt = [P.sbuf("f_xt%d" % k, [128, D], F32) for k in range(2)]
            xn = [P.sbuf("f_xn%d" % k, [128, D], F32) for k in range(1)]
            ss = [P.sbuf("f_ss%d" % k, [128, 4], F32) for k in range(2)]
            psu = [P.psum("f_psu%d" % k, [128, 1024], F32) for k in range(2)]
            psd = [P.psum("f_psd%d" % k, [128, 512], F32) for k in range(4)]
            w1v = self.W["ffn_w1"].t[i].rearrange("(c p) n -> p c n", p=128)
            w2v = self.W["ffn_w2"].t[i].rearrange("(c p) n -> p c n", p=128)
            for b in range(BPC):
                self.gate_rows(i, 1, b, G[0], psd[0])
                self.gate_rows(i, 1, BPC, G[1], psd[1])
                for blk in range(NT // TBT):
                    tiles = list(range(blk * TBT, (blk + 1) * TBT))
                    self.norm_tiles(i, 1, b, tiles, hT, 0, (xt, xn, ss, psd))
                    for hb in range(DFF // 512):
                        W1 = w1p[hb % 2]
                        W2 = w2p[hb % 2]
                        U = uT[hb % 2]
                        P.dma(W1[:, :, :], w1v[:, :, hb * 512:(hb + 1) * 512], [self.W["ffn_w1"]], [W1], q="pool")
                        P.dma(W2[:, :, :], w2v[:, hb * 4:(hb + 1) * 4, :], [self.W["ffn_w2"]], [W2], q="pool")
                        for nb in range(4):
                            ps = psu[nb % 2]
                            R = rl[0]
                            for k in range(16):
                                P.matmul(ps[:, 0:512], W1[:, k, nb * 128:(nb + 1) * 128], hT[:, k, 0:512], k == 0, k == 15,
                                         [W1, hT], [ps])
                            for k in range(16):
                                P.matmul(ps[:, 512:512 + NTOK - 512], W1[:, k, nb * 128:(nb + 1) * 128], hT[:, k, 512:NTOK],
                                         k == 0, k == 15, [W1, hT], [ps])
                            P.act(R[:, :], ps[:, 0:NTOK], AF.Relu, [ps], [R])
                            P.tt(U[:, nb, :], R[:, :], R[:, :], ALU.mult, [R], [U])
                        for k, t in enumerate(tiles):
                            for db in range(4):
                                ps = psd[(k * 4 + db) % 4]
                                for kc in range(4):
                                    P.matmul(ps[:, :], U[:, kc, k * 128:(k + 1) * 128], W2[:, kc, db * 512:(db + 1) * 512],
                                             kc == 0, kc == 3, [U, W2], [ps])
                                ya = yacc[:, k, db * 512:(db + 1) * 512]
                                if hb == 0:
                                    P.copy(ya, ps[:, :], [ps], [yacc])
                                else:
                                    P.tt(ya, ps[:, :], ya, ALU.add, [ps, yacc], [yacc])
                    for k, t in enumerate(tiles):
                        X = xt[k % 2]
                        g = G[1] if t < 2 else G[0]
                        P.dma(X[:, :], self.res_ap(b, t), [self.rtok[b][t]], [X])
                        P.tt(yacc[:, k, :], yacc[:, k, :], g[:, :], ALU.mult, [yacc, g], [yacc])
                        P.tt(X[:, :], X[:, :], yacc[:, k, :], ALU.add, [X, yacc], [X])
                        P.dma(self.res_ap(b, t), X[:, :], [X], [self.rtok[b][t]])

    def final_norm(self):
        P = self.P
        with P.phase():
            xt = [P.sbuf("n_xt%d" % k, [128, D], F32) for k in range(2)]
            xn = [P.sbuf("n_xn%d" % k, [128, D], F32) for k in range(2)]
            ss = [P.sbuf("n_ss%d" % k, [128, 4], F32) for k in range(2)]
            wrow = P.sbuf("n_wrow", [128, D], F32)
            ps = P.psum("n_ps", [128, 512], F32)
            dg = [P.sbuf("n_dg%d" % k, [128, 128], F32) for k in range(2)]
            for c4 in range(4):
                for cc in range(4):
                    c = c4 * 4 + cc
                    P.ts(dg[c % 2][:, :], self.ident[:, :], self.nwT[:, 2 * DEPTH, c:c + 1], None, ALU.mult, None,
                         [self.ident, self.nwT], [dg[c % 2]])
                    P.matmul(ps[:, cc * 128:(cc + 1) * 128], self.ones[:, :], dg[c % 2][:, :], True, True,
                             [self.ones, dg[c % 2]], [ps])
                P.copy(wrow[:, c4 * 512:(c4 + 1) * 512], ps[:, :], [ps], [wrow])
            k = 0
            for b in range(BPC):
                for t in range(2, NT):
                    X = xt[k % 2]
                    XN = xn[k % 2]
                    S = ss[k % 2]
                    k += 1
                    P.dma(X[:, :], self.res_ap(b, t), [self.rtok[b][t]], [X])
                    P.memset(S[:, 0:1], 0.0, [S])
                    P.act(XN[:, :], X[:, :], AF.Square, [X], [XN, S], accum_out=S[:, 0:1])
                    P.ts(S[:, 1:2], S[:, 0:1], 1.0 / D, EPS, ALU.mult, ALU.add, [S], [S])
                    P.act(S[:, 2:3], S[:, 1:2], AF.Sqrt, [S], [S])
                    P.recip(S[:, 3:4], S[:, 2:3], [S], [S])
                    P.stt(XN[:, :], X[:, :], S[:, 3:4], wrow[:, :], ALU.mult, ALU.mult, [X, S, wrow], [XN])
                    P.dma(self.res_ap(b, t), XN[:, :], [XN], [self.rtok[b][t]])


    def scratch(self, name, shape, dtype):
        if not hasattr(self, "_scr"):
            self._scr = {}
        if name not in self._scr:
            self._scr[name] = self.P.dram(name, shape, dtype)
        return self._scr[name]

    def norm_to_dram(self, i, wh, b):
        P = self.P
        hT_d = self.scratch("hT_d", [128, 16, TOK], BF16)
        with P.phase():
            hT = P.sbuf("nd_hT", [128, 16, TOK], BF16)
            xt = [P.sbuf("nd_xt%d" % k, [128, D], F32) for k in range(2)]
            xn = [P.sbuf("nd_xn%d" % k, [128, D], F32) for k in range(2)]
            ss = [P.sbuf("nd_ss%d" % k, [128, 4], F32) for k in range(2)]
            pss = [P.psum("nd_ps%d" % k, [128, 512], F32) for k in range(4)]
            self.norm_tiles(i, wh, b, list(range(NT)), hT, 0, (xt, xn, ss, pss))
            P.dma(hT_d[:, :, :], hT[:, :, :], [hT], [hT_d])
        return hT_d

    def outproj_residual(self, i, b, aT_d, wout):
        P = self.P
        TBT = 6
        NTB = TBT * 128
        with P.phase():
            self._dg = [P.sbuf("dg%d" % k, [128, 128], F32) for k in range(2)]
            aT = P.sbuf("op_aT", [128, 32, NTB], BF16)
            yacc = P.sbuf("op_y", [128, TBT, D], F32)
            wp = [P.sbuf("op_wp%d" % k, [128, 32, 256], BF16) for k in range(2)]
            G = [P.sbuf("op_G%d" % k, [128, D], F32) for k in range(2)]
            xt = [P.sbuf("op_xt%d" % k, [128, D], F32) for k in range(2)]
            ps = [P.psum("op_ps%d" % k, [128, 512], F32) for k in range(4)]
            wv = wout.rearrange("(c p) n -> p c n", p=128)
            av = aT_d.t.rearrange("(c p) t -> p c t", p=128)
            self.gate_rows(i, 0, b, G[0], ps[0])
            self.gate_rows(i, 0, BPC, G[1], ps[1])
            cnt = 0
            for blk in range(NT // TBT):
                tiles = list(range(blk * TBT, (blk + 1) * TBT))
                P.dma(aT[:, :, :], av[:, :, blk * NTB:(blk + 1) * NTB], [aT_d], [aT])
                for pn in range(8):
                    Wp = wp[pn % 2]
                    P.dma(Wp[:, :, :], wv[:, :, pn * 256:(pn + 1) * 256], [], [Wp], q="pool")
                    for k, t in enumerate(tiles):
                        pp = ps[cnt % 4]
                        cnt += 1
                        for kc in range(32):
                            P.matmul(pp[:, 0:256], aT[:, kc, k * 128:(k + 1) * 128], Wp[:, kc, :], kc == 0, kc == 31,
                                     [aT, Wp], [pp])
                        P.copy(yacc[:, k, pn * 256:(pn + 1) * 256], pp[:, 0:256], [pp], [yacc],
                               q="act" if cnt % 2 else "dve")
                for k, t in enumerate(tiles):
                    X = xt[k % 2]
                    g = G[1] if t < 2 else G[0]
                    P.dma(X[:, :], self.res_ap(b, t), [self.rtok[b][t]], [X])
                    P.tt(yacc[:, k, :], yacc[:, k, :], g[:, :], ALU.mult, [yacc, g], [yacc])
                    P.tt(X[:, :], X[:, :], yacc[:, k, :], ALU.add, [X, yacc], [X])
                    P.dma(self.res_ap(b, t), X[:, :], [X], [self.rtok[b][t]])

    def mixer(self, i):
        kind, j = i % 3, i // 3
        for b in range(BPC):
            if kind == 2:
                self.norm_to_dram(i, 0, b)
                self.ret_proj(i, j, b)
                self.ret_core(i, j, b)
                self.outproj_residual(i, b, self.scratch("r_ogT", [4096, TOK], BF16), self.W["ret_w_out"].t[j])
            elif kind == 0:
                self.s5(i, j, b)
            else:
                self.gdn(i, j, b)


    def sincos(self, x, t, r, sin_out, cos_out, halfpi, rd, tk):
        P = self.P
        TWO_PI = 2.0 * math.pi
        MAGIC = 12582912.0
        P.ts(t, x, 1.0 / TWO_PI, MAGIC, ALU.mult, ALU.add, rd, [tk["t"]])
        P.ts(t, t, -MAGIC, None, ALU.add, None, [tk["t"]], [tk["t"]])
        P.stt(r, t, -TWO_PI, x, ALU.mult, ALU.add, [tk["t"]] + rd, [tk["r"]])
        P.act(sin_out, r, AF.Sin, [tk["r"]], [tk["sin"]])
        P.stt(t, r, -1.0, r, ALU.mult, ALU.max, [tk["r"]], [tk["t"]])
        P.act(cos_out, t, AF.Sin, [tk["t"], halfpi], [tk["cos"]], scale=-1.0, bias=halfpi[:, 0:1])

    def s5_prep(self, i, j):
        P = self.P
        PI = math.pi
        s5F = [self.scratch("s5F%d" % d, [128, 16384], F32) for d in range(2)]
        s5FF = [self.scratch("s5FF%d" % d, [128, 16384], F32) for d in range(2)]
        s5EN = [self.scratch("s5EN%d" % d, [128, 16384], F32) for d in range(2)]
        s5E = [self.scratch("s5E%d" % d, [128, 16384], F32) for d in range(2)]
        s5M = self.scratch("s5M", [128, 16384], F32)
        if not hasattr(self, "s5cols"):
            self.s5cols = P.sbuf("s5cols", [128, 2, 2, 128], F32)
        with P.phase():
            big = P.sbuf("sp_big", [128, 16384], F32)
            PW = P.sbuf("sp_PW", [128, 16, 2, 64], F32)
            bre = P.sbuf("sp_bre", [128, 64, 16], F32)
            bim = P.sbuf("sp_bim", [128, 64, 16], F32)
            Btr = P.sbuf("sp_Btr", [128, 64, 16], F32)
            Bti = P.sbuf("sp_Bti", [128, 64, 16], F32)
            cre = P.sbuf("sp_cre", [128, 16, 64], F32)
            cim = P.sbuf("sp_cim", [128, 16, 64], F32)
            tA = [P.sbuf("sp_t%d" % k, [128, 1024], F32) for k in range(4)]
            sm = P.sbuf("sp_sm", [128, 16, 64], F32)
            dtc = P.sbuf("sp_dt", [128, 4], F32)
            dup = P.sbuf("sp_dup", [128, 128], F32)
            negpi = P.sbuf("sp_negpi", [128, 1], F32)
            P.memset(negpi[:, :], 0.5 * PI, [negpi])
            pst = P.psum("sp_pst", [128, 512], F32)
            X1, Y1, T, MAG, SN, CS, U2, V2, DEN, XR = [sm[:, k, :] for k in range(10)]
            XK, RR = sm[:, 14, :], sm[:, 15, :]
            for d in range(2):
                P.dma(sm[:, 10, :], self.W["s5_lam_re"].t[j, d], [], [sm])
                P.dma(sm[:, 11, :], self.W["s5_lam_im"].t[j, d], [], [sm])
                P.dma(dtc[:, 0:1], self.W["s5_log_dt"].t[j, d].rearrange("(g o) -> g o", o=1), [], [dtc])
                P.dma(bre[:, :, :], self.W["s5_b_re"].t[j, d], [], [bre])
                P.dma(bim[:, :, :], self.W["s5_b_im"].t[j, d], [], [bim])
                P.dma(cre[:, :, :], self.W["s5_c_re"].t[j, d], [], [cre])
                P.dma(cim[:, :, :], self.W["s5_c_im"].t[j, d], [], [cim])
                LRE, LIM = sm[:, 10, :], sm[:, 11, :]
                P.ts(LRE, LRE, -1e-4, None, ALU.min, None, [sm], [sm])
                P.act(dtc[:, 1:2], dtc[:, 0:1], AF.Exp, [dtc], [dtc])
                P.ts(X1, LRE, dtc[:, 1:2], None, ALU.mult, None, [sm, dtc], [sm])
                P.ts(Y1, LIM, dtc[:, 1:2], None, ALU.mult, None, [sm, dtc], [sm])
                P.copy(dup[:, 0:64], Y1, [sm], [dup])
                P.copy(dup[:, 64:128], Y1, [sm], [dup])
                P.transpose(pst[:, 0:128], dup[:, :], self.ident[:, :], [dup, self.ident], [pst])
                P.ts(self.s5cols[:, d, 0, :], pst[:, 0:128], 8.0, None, ALU.mult, None, [pst], [self.s5cols])
                P.copy(dup[:, 0:64], X1, [sm], [dup])
                P.copy(dup[:, 64:128], X1, [sm], [dup])
                P.transpose(pst[:, 128:256], dup[:, :], self.ident[:, :], [dup, self.ident], [pst])
                P.act(self.s5cols[:, d, 1, :], pst[:, 128:256], AF.Exp, [pst], [self.s5cols], scale=8.0)
                for k in range(0, 9):
                    if k == 0:
                        P.memset(PW[:, 7, 0, :], 1.0, [PW])
                        P.memset(PW[:, 7, 1, :], 0.0, [PW])
                        continue
                    P.ts(XK, Y1, float(k), None, ALU.mult, None, [sm], [sm])
                    self.sincos(XK, T, RR, SN, CS, negpi, [sm], {"t": sm, "r": sm, "sin": sm, "cos": sm})
                    P.act(MAG, X1, AF.Exp, [sm], [sm], scale=float(k))
                    P.tt(PW[:, 7 + k, 0, :], MAG, CS, ALU.mult, [sm], [PW])
                    P.tt(PW[:, 7 + k, 1, :], MAG, SN, ALU.mult, [sm], [PW])
                    if k <= 7:
                        P.act(MAG, X1, AF.Exp, [sm], [sm], scale=-float(k))
                        P.tt(PW[:, 7 - k, 0, :], MAG, CS, ALU.mult, [sm], [PW])
                        P.stt(PW[:, 7 - k, 1, :], MAG, -1.0, SN, ALU.mult, ALU.mult, [sm], [PW])
                P.ts(XR, PW[:, 8, 0, :], -1.0, None, ALU.add, None, [PW], [sm])
                YI = PW[:, 8, 1, :]
                P.tt(U2, LRE, LRE, ALU.mult, [sm], [sm])
                P.tt(V2, LIM, LIM, ALU.mult, [sm], [sm])
                P.tt(DEN, U2, V2, ALU.add, [sm], [sm])
                P.recip(DEN, DEN, [sm], [sm])
                BSR, BSI = sm[:, 12, :], sm[:, 13, :]
                P.tt(U2, XR, LRE, ALU.mult, [sm], [sm])
                P.tt(V2, YI, LIM, ALU.mult, [sm, PW], [sm])
                P.tt(U2, U2, V2, ALU.add, [sm], [sm])
                P.tt(BSR, U2, DEN, ALU.mult, [sm], [sm])
                P.tt(U2, YI, LRE, ALU.mult, [sm, PW], [sm])
                P.tt(V2, XR, LIM, ALU.mult, [sm], [sm])
                P.tt(U2, U2, V2, ALU.subtract, [sm], [sm])
                P.tt(BSI, U2, DEN, ALU.mult, [sm], [sm])

                def bc_n_c(ap):
                    return ap.unsqueeze(2).broadcast_to([128, 64, 16])

                def bc_c_n(ap):
                    return ap.unsqueeze(1).broadcast_to([128, 16, 64])

                t3 = [t[:, :].rearrange("g (a b) -> g a b", b=16) for t in tA]
                t3c = [t[:, :].rearrange("g (a b) -> g a b", b=64) for t in tA]

                def cprod(o_re, o_im, a_re, a_im, m_re, m_im, tv, neg_im, rds):
                    P.tt(tv[0], a_re, m_re, ALU.mult, rds, [tA[0]])
                    P.tt(tv[1], a_im, m_im, ALU.mult, rds, [tA[1]])
                    P.tt(o_re, tv[0], tv[1], ALU.subtract, [tA[0], tA[1]], [big])
                    P.tt(tv[2], a_re, m_im, ALU.mult, rds, [tA[2]])
                    P.tt(tv[3], a_im, m_re, ALU.mult, rds, [tA[3]])
                    if neg_im:
                        P.stt(o_im, tv[2], -1.0, tv[3], ALU.mult, ALU.subtract, [tA[2], tA[3]], [big])
                    else:
                        P.tt(o_im, tv[2], tv[3], ALU.add, [tA[2], tA[3]], [big])

                P.tt(t3[0], bc_n_c(BSR), bre[:, :, :], ALU.mult, [sm, bre], [tA[0]])
                P.tt(t3[1], bc_n_c(BSI), bim[:, :, :], ALU.mult, [sm, bim], [tA[1]])
                P.tt(Btr[:, :, :], t3[0], t3[1], ALU.subtract, [tA[0], tA[1]], [Btr])
                P.tt(t3[2], bc_n_c(BSR), bim[:, :, :], ALU.mult, [sm, bim], [tA[2]])
                P.tt(t3[3], bc_n_c(BSI), bre[:, :, :], ALU.mult, [sm, bre], [tA[3]])
                P.tt(Bti[:, :, :], t3[2], t3[3], ALU.add, [tA[2], tA[3]], [Bti])

                v1 = big[:, :].rearrange("g (s c r n) -> g s c r n", s=8, c=16, r=2, n=64)
                v2 = big[:, :].rearrange("g (r n s c) -> g r n s c", s=8, c=16, r=2, n=64)
                for layout, dst in ((1, s5F[d]), (2, s5FF[d])):
                    for sidx in range(8):
                        k = (7 - sidx) if d == 0 else sidx
                        if layout == 1:
                            o_re = v1[:, sidx, :, 0, :].rearrange("g c n -> g n c")
                            o_im = v1[:, sidx, :, 1, :].rearrange("g c n -> g n c")
                        else:
                            o_re = v2[:, 0, :, sidx, :]
                            o_im = v2[:, 1, :, sidx, :]
                        cprod(o_re, o_im, bc_n_c(PW[:, 7 + k, 0, :]), bc_n_c(PW[:, 7 + k, 1, :]), Btr[:, :, :], Bti[:, :, :],
                              t3, False, [PW, Btr, Bti])
                    P.dma(dst[:, :], big[:, :], [big], [dst])
                for layout, dst in ((3, s5EN[d]), (4, s5E[d])):
                    for tidx in range(8):
                        if layout == 3:
                            k = (tidx - 7) if d == 0 else (-tidx)
                        else:
                            k = (tidx + 1) if d == 0 else (8 - tidx)
                        o_re = v2[:, 0, :, tidx, :].rearrange("g n c -> g c n")
                        o_im = v2[:, 1, :, tidx, :].rearrange("g n c -> g c n")
                        cprod(o_re, o_im, bc_c_n(PW[:, 7 + k, 0, :]), bc_c_n(PW[:, 7 + k, 1, :]), cre[:, :, :], cim[:, :, :],
                              t3c, True, [PW, cre, cim])
                    P.dma(dst[:, :], big[:, :], [big], [dst])
        with P.phase():
            maskF = P.sbuf("sm_maskF", [128, 128], F32)
            maskB = P.sbuf("sm_maskB", [128, 128], F32)
            P.dma(maskF[:, :], self.C["s5maskF"][:, :], [], [maskF])
            P.dma(maskB[:, :], self.C["s5maskB"][:, :], [], [maskB])
            sel16 = P.sbuf("sm_sel16", [16, 128], F32)
            P.dma(sel16[:, :], self.C["s5sel16"][:, :], [], [sel16])
            Dg = P.sbuf("sm_Dg", [128, 16], F32)
            P.dma(Dg[:, :], self.W["s5_d"].t[j].rearrange("(g c) -> g c", c=16), [], [Dg])
            DgT = P.sbuf("sm_DgT", [16, 128], F32)
            Dall = P.sbuf("sm_Dall", [128, 128], F32)
            ps0 = P.psum("sm_ps0", [128, 512], F32)
            P.transpose(ps0[0:16, 0:128], Dg[:, :], self.ident[:, :], [Dg, self.ident], [ps0])
            P.copy(DgT[:, :], ps0[0:16, 0:128], [ps0], [DgT])
            P.matmul(ps0[:, 128:256], sel16[:, :], DgT[:, :], True, True, [sel16, DgT], [ps0])
            P.copy(Dall[:, :], ps0[:, 128:256], [ps0], [Dall])
            GB = 8
            ld = [[P.sbuf("sm_ld%d_%d" % (k, q), [128, GB, 128], F32) for q in range(4)] for k in range(2)]
            Mst = [P.sbuf("sm_Mst%d" % k, [128, GB, 128], F32) for k in range(2)]
            T1 = [P.sbuf("sm_T1%d" % k, [128, 128], F32) for k in range(2)]
            T2 = [P.sbuf("sm_T2%d" % k, [128, 128], F32) for k in range(2)]
            psf = [P.psum("sm_psf%d" % k, [128, 512], F32) for k in range(2)]
            psb = [P.psum("sm_psb%d" % k, [128, 512], F32) for k in range(2)]
            srcs = [s5FF[0], s5EN[0], s5FF[1], s5EN[1]]
            for bt in range(128 // GB):
                L = ld[bt % 2]
                for q in range(4):
                    P.dma(L[q][:, :, :], srcs[q].t[bt * GB:(bt + 1) * GB, :].rearrange("g (r c) -> r g c", c=128),
                          [srcs[q]], [L[q]])
                MS = Mst[bt % 2]
                for gi in range(GB):
                    g = bt * GB + gi
                    pf, pb = psf[gi % 2], psb[gi % 2]
                    P.matmul(pf[:, 0:128], L[0][:, gi, :], L[1][:, gi, :], True, True, [L[0], L[1]], [pf])
                    P.matmul(pb[:, 0:128], L[2][:, gi, :], L[3][:, gi, :], True, True, [L[2], L[3]], [pb])
                    a, c2 = T1[gi % 2], T2[gi % 2]
                    P.tt(a[:, :], pf[:, 0:128], maskF[:, :], ALU.mult, [pf, maskF], [a])
                    P.tt(c2[:, :], pb[:, 0:128], maskB[:, :], ALU.mult, [pb, maskB], [c2])
                    P.tt(a[:, :], a[:, :], c2[:, :], ALU.add, [a, c2], [a])
                    P.stt(MS[:, gi, :], self.ident[:, :], Dall[:, g:g + 1], a[:, :], ALU.mult, ALU.add,
                          [self.ident, Dall, a], [MS])
                P.dma(s5M.t[bt * GB:(bt + 1) * GB, :].rearrange("g (r c) -> r g c", c=128), MS[:, :, :], [MS], [s5M])

    def s5(self, i, j, b):
        P = self.P
        PI = math.pi
        if b == 0:
            self.s5_prep(i, j)
        s5F = [self.scratch("s5F%d" % d, [128, 16384], F32) for d in range(2)]
        s5E = [self.scratch("s5E%d" % d, [128, 16384], F32) for d in range(2)]
        s5M = self.scratch("s5M", [128, 16384], F32)
        U_d = self.scratch("s5U", [128, 8, 16, 288], BF16)
        zT_d = self.scratch("hT_d", [128, 16, TOK], BF16)
        with P.phase():
            hTd = P.sbuf("s1_hTd", [128, 16, 8, 288], BF16)
            xt = [P.sbuf("s1_xt%d" % k, [128, D], F32) for k in range(2)]
            xn = [P.sbuf("s1_xn%d" % k, [128, D], F32) for k in range(2)]
            ss = [P.sbuf("s1_ss%d" % k, [128, 4], F32) for k in range(2)]
            pss = [P.psum("s1_ps%d" % k, [128, 512], F32) for k in range(4)]
            self.norm_tiles(i, 0, b, list(range(NT)), hTd, 0, (xt, xn, ss, pss), deint=True)
            for fc in range(16):
                for g8 in range(8):
                    g = fc * 8 + g8
                    P.dma(U_d[g].rearrange("s c j -> c s j"), hTd[g8 * 16:(g8 + 1) * 16, fc, :, :], [hTd], [U_d])
        with P.phase():
            iota = P.sbuf("s2_iota", [128, 288], F32)
            P.dma(iota[:, :], self.C["iota288"][:, :], [], [iota])
            negpi = P.sbuf("s2_negpi", [128, 1], F32)
            P.memset(negpi[:, :], 0.5 * PI, [negpi])
            selT = P.sbuf("s2_selT", [128, 64, 128], BF16)
            P.dma(selT[:, :, :], self.C["s5selT"][:, :, :], [], [selT], q="pool")
            Ub = [P.sbuf("s2_Ub%d" % k, [128, 8, 288], BF16) for k in range(2)]
            Wm = [P.sbuf("s2_Wm%d" % k, [128, 8, 128], BF16) for k in range(2)]
            WF = [[P.sbuf("s2_WF%d_%d" % (k, d), [128, 8, 128], BF16) for d in range(2)] for k in range(2)]
            WE = [[P.sbuf("s2_WE%d_%d" % (k, d), [128, 8, 128], BF16) for d in range(2)] for k in range(2)]
            ANG = P.sbuf("s2_ANG", [128, 8, 288], F32)
            ANG2 = P.sbuf("s2_ANG2", [128, 8, 288], F32)
            SINT = P.sbuf("s2_SINT", [128, 8, 288], F32)
            COST = P.sbuf("s2_COST", [128, 8, 288], F32)
            V = P.sbuf("s2_V", [128, 8, 288], F32)
            T1 = P.sbuf("s2_T1", [128, 8, 288], F32)
            T2 = P.sbuf("s2_T2", [128, 8, 288], F32)
            Gt = P.sbuf("s2_Gt", [128, 8, 288], F32)
            RT = P.sbuf("s2_RT", [128, 8, 288], F32)
            Zp = [P.sbuf("s2_Zp%d" % d, [128, 8, 288], BF16) for d in range(2)]
            Yg = P.sbuf("s2_Yg", [128, 8, 288], BF16)
            zfc = [P.sbuf("s2_zfc%d" % k, [128, TOK], BF16) for k in range(2)]
            gt = [P.sbuf("s2_gt%d" % k, [128, 288], F32) for k in range(3)]
            pv = [P.psum("s2_pv%d" % k, [128, 512], F32) for k in range(2)]
            py = [P.psum("s2_py%d" % k, [128, 512], F32) for k in range(2)]
            pt = [P.psum("s2_pt%d" % k, [128, 512], F32) for k in range(2)]
            for fc in range(16):
                UB, WM = Ub[fc % 2], Wm[fc % 2]
                P.dma(UB[:, :, :], U_d[fc * 8:(fc + 1) * 8].rearrange("g s c j -> (s c) g j"), [U_d], [UB])
                P.dma(WM[:, :, :], s5M.t[fc * 8:(fc + 1) * 8, :].rearrange("g (r c) -> r g c", c=128), [s5M], [WM], q="pool")
                for d in range(2):
                    P.dma(WF[fc % 2][d][:, :, :], s5F[d].t[fc * 8:(fc + 1) * 8, :].rearrange("g (r c) -> r g c", c=128),
                          [s5F[d]], [WF[fc % 2][d]], q="pool")
                    P.dma(WE[fc % 2][d][:, :, :], s5E[d].t[fc * 8:(fc + 1) * 8, :].rearrange("g (r c) -> r g c", c=128),
                          [s5E[d]], [WE[fc % 2][d]], q="pool")
                for d in range(2):
                    wf, we = WF[fc % 2][d], WE[fc % 2][d]
                    for gi in range(8):
                        g = fc * 8 + gi
                        P.ts(ANG[:, gi, :], iota[:, :], self.s5cols[:, d, 0, g:g + 1], None, ALU.mult, None,
                             [iota, self.s5cols], [ANG])
                        P.ts(RT[:, gi, :], iota[:, :], 0.0, self.s5cols[:, d, 1, g:g + 1], ALU.mult, ALU.add,
                             [iota, self.s5cols], [RT])
                    self.sincos(ANG[:, :, :], ANG2[:, :, :], T1[:, :, :], SINT[:, :, :], COST[:, :, :], negpi, [ANG],
                                {"t": ANG2, "r": T1, "sin": SINT, "cos": COST})
                    P.ts(SINT[64:128, :, :], SINT[64:128, :, :], -1.0, None, ALU.mult, None, [SINT], [SINT])
                    for gi in range(8):
                        p = pv[gi % 2]
                        P.matmul(p[:, 0:288], wf[:, gi, :], UB[:, gi, :], True, True, [wf, UB], [p])
                        if d == 0:
                            P.copy(V[:, gi, :], p[:, 0:288], [p], [V], q="act")
                        else:
                            P.copy(V[:, gi, 0:32], p[:, 0:32][:, ::-1], [p], [V], q="act")
                            P.copy(V[:, gi, 32:288], p[:, 32:288][:, ::-1], [p], [V], q="act")
                    P.tt(T1[:, :, :], V[:, :, :], COST[:, :, :], ALU.mult, [V, COST], [T1])
                    P.copy(T2[0:64, :, :], V[64:128, :, :], [V], [T2])
                    P.copy(T2[64:128, :, :], V[0:64, :, :], [V], [T2])
                    P.tt(T2[:, :, :], T2[:, :, :], SINT[:, :, :], ALU.mult, [T2, SINT], [T2])
                    P.tt(V[:, :, :], T1[:, :, :], T2[:, :, :], ALU.add, [T1, T2], [V])
                    for gi in range(8):
                        P.op("dve", (lambda gi=gi: (lambda e: e.tensor_tensor_scan(Gt[:, gi, :], RT[:, gi, :], V[:, gi, :], 0.0,
                                                                                    ALU.mult, ALU.add)))(), [RT, V], [Gt])
                    P.tt(T1[:, :, :], Gt[:, :, :], COST[:, :, :], ALU.mult, [Gt, COST], [T1])
                    P.copy(T2[0:64, :, :], Gt[64:128, :, :], [Gt], [T2])
                    P.copy(T2[64:128, :, :], Gt[0:64, :, :], [Gt], [T2])
                    P.tt(T2[:, :, :], T2[:, :, :], SINT[:, :, :], ALU.mult, [T2, SINT], [T2])
                    P.tt(T1[:, :, :], T1[:, :, :], T2[:, :, :], ALU.subtract, [T1, T2], [T1])
                    ZP = Zp[d]
                    if d == 0:
                        P.memset(ZP[:, :, 0:1], 0.0, [ZP])
                        P.copy(ZP[:, :, 1:288], T1[:, :, 0:287], [T1], [ZP])
                    else:
                        P.memset(ZP[:, :, 31:32], 0.0, [ZP])
                        P.copy(ZP[:, :, 0:31], T1[:, :, 0:31][:, :, ::-1], [T1], [ZP])
                        P.copy(ZP[:, :, 32:288], T1[:, :, 31:287][:, :, ::-1], [T1], [ZP])
                for gi in range(8):
                    p = py[gi % 2]
                    P.matmul(p[:, 0:288], WM[:, gi, :], UB[:, gi, :], True, False, [WM, UB], [p])
                    P.matmul(p[:, 0:288], WE[fc % 2][0][:, gi, :], Zp[0][:, gi, :], False, False, [WE[fc % 2][0], Zp[0]], [p])
                    P.matmul(p[:, 0:288], WE[fc % 2][1][:, gi, :], Zp[1][:, gi, :], False, True, [WE[fc % 2][1], Zp[1]], [p])
                    P.copy(Yg[:, gi, :], p[:, 0:288], [p], [Yg], q="act")
                Z = zfc[fc % 2]
                zv = Z[:, :].rearrange("p (j s) -> p s j", s=8)
                for t in range(8):
                    p = pt[t % 2]
                    for g8 in range(8):
                        P.matmul(p[:, 0:288], selT[:, g8 * 8 + t, :], Yg[:, g8, :], g8 == 0, g8 == 7, [selT, Yg], [p])
                    P.act(gt[0][:, :], p[:, 0:288], AF.Square, [p], [gt[0]])
                    P.ts(gt[0][:, :], gt[0][:, :], 0.044715, 1.0, ALU.mult, ALU.add, [gt[0]], [gt[0]])
                    P.tt(gt[1][:, :], gt[0][:, :], p[:, 0:288], ALU.mult, [gt[0], p], [gt[1]])
                    P.act(gt[2][:, :], gt[1][:, :], AF.Sigmoid, [gt[1]], [gt[2]], scale=1.5957691216057308)
                    P.tt(zv[:, t, :], gt[2][:, :], p[:, 0:288], ALU.mult, [gt[2], p], [Z])
                P.dma(zT_d[:, fc, :], Z[:, :], [Z], [zT_d])
        self.s5_glu(i, j, b)

    def s5_glu(self, i, j, b):
        P = self.P
        zT_d = self.scratch("hT_d", [128, 16, TOK], BF16)
        TBT = 6
        NTB = TBT * 128
        gw = self.W["s5_glu_w"].t[j].rearrange("(c p) n -> p c n", p=128)
        with P.phase():
            self._dg = [P.sbuf("dg%d" % k, [128, 128], F32) for k in range(2)]
            zT = P.sbuf("sg_zT", [128, 16, NTB], BF16)
            yacc = P.sbuf("sg_y", [128, TBT, D], F32)
            wp = [P.sbuf("sg_wp%d" % k, [128, 16, 512], BF16) for k in range(2)]
            G = [P.sbuf("sg_G%d" % k, [128, D], F32) for k in range(2)]
            xt = [P.sbuf("sg_xt%d" % k, [128, D], F32) for k in range(2)]
            sg = [P.sbuf("sg_sg%d" % k, [128, 256], F32) for k in range(2)]
            ps = [P.psum("sg_ps%d" % k, [128, 512], F32) for k in range(4)]
            self.gate_rows(i, 0, b, G[0], ps[0])
            self.gate_rows(i, 0, BPC, G[1], ps[1])
            cnt = 0
            for blk in range(NT // TBT):
                tiles = list(range(blk * TBT, (blk + 1) * TBT))
                P.dma(zT[:, :, :], zT_d[:, :, blk * NTB:(blk + 1) * NTB], [zT_d], [zT])
                for pn in range(8):
                    Wp = wp[pn % 2]
                    P.dma(Wp[:, :, 0:256], gw[:, :, pn * 256:(pn + 1) * 256], [], [Wp], q="pool")
                    P.dma(Wp[:, :, 256:512], gw[:, :, 2048 + pn * 256:2048 + (pn + 1) * 256], [], [Wp], q="pool")
                    for k, t in enumerate(tiles):
                        pp = ps[cnt % 4]
                        S = sg[cnt % 2]
                        cnt += 1
                        for kc in range(16):
                            P.matmul(pp[:, :], zT[:, kc, k * 128:(k + 1) * 128], Wp[:, kc, :], kc == 0, kc == 15, [zT, Wp], [pp])
                        P.act(S[:, :], pp[:, 256:512], AF.Sigmoid, [pp], [S])
                        P.tt(yacc[:, k, pn * 256:(pn + 1) * 256], pp[:, 0:256], S[:, :], ALU.mult, [pp, S], [yacc])
                for k, t in enumerate(tiles):
                    X = xt[k % 2]
                    g = G[1] if t < 2 else G[0]
                    P.dma(X[:, :], self.res_ap(b, t), [self.rtok[b][t]], [X])
                    P.tt(yacc[:, k, :], yacc[:, k, :], g[:, :], ALU.mult, [yacc, g], [yacc])
                    P.tt(X[:, :], X[:, :], yacc[:, k, :], ALU.add, [X, yacc], [X])
                    P.dma(self.res_ap(b, t), X[:, :], [X], [self.rtok[b][t]])


    def gdn(self, i, j, b):
        self.norm_to_dram(i, 0, b)
        self.gdn_proj(i, j, b)
        if self.cfg.get("gdn_stop") == "proj":
            return
        self.gdn_core(i, j, b)
        self.outproj_residual(i, b, self.scratch("r_ogT", [4096, TOK], BF16), self.W["gdn_w_out"].t[j])

    def gdn_proj(self, i, j, b):
        P = self.P
        hT_d = self.scratch("hT_d", [128, 16, TOK], BF16)
        g_qkvT = self.scratch("g_qkvT", [8192, TOK], BF16)
        g_z = self.scratch("r_gate", [TOK, 4096], BF16)
        g_ab = self.scratch("g_ab", [TOK, 128], F32)
        win = self.W["gdn_w_in"].t[j].rearrange("(c p) n -> p c n", p=128)
        blocks = [(0, 256)] + [(256 + 512 * k, 512) for k in range(4)]
        PADW = 4 + 256 + 4 + 2048 + 4
        with P.phase():
            hT = P.sbuf("gp_hT", [128, 16, TOK], BF16)
            P.dma(hT[:, :, :], hT_d[:, :, :], [hT_d], [hT])
            cwr = P.sbuf("gp_cwr", [5, 2048], F32)
            cw = P.sbuf("gp_cw", [128, 64, 5], F32)
            pcw = P.psum("gp_pcw", [128, 512], F32)
            for c4 in range(4):
                P.dma(cwr[:, :], self.W["gdn_conv_w"].t[j][:, c4 * 2048:(c4 + 1) * 2048], [], [cwr])
                for c in range(16):
                    cc = c4 * 16 + c
                    P.transpose(pcw[:, cc * 5:(cc + 1) * 5], cwr[0:5, c * 128:(c + 1) * 128], self.ident[0:5, 0:5],
                                [cwr, self.ident], [pcw])
            P.copy(cw[:, :, :], pcw[:, 0:320].rearrange("p (c k) -> p c k", k=5), [pcw], [cw])
            wp = [P.sbuf("gp_wp%d" % k, [128, 16, 512], BF16) for k in range(2)]
            pad = [P.sbuf("gp_pad%d" % k, [128, PADW], F32) for k in range(2)]
            acc = [P.sbuf("gp_acc%d" % k, [128, TOK], F32) for k in range(2)]
            sq = [P.sbuf("gp_sq%d" % k, [128, 512], F32) for k in range(2)]
            rin = [P.sbuf("gp_rin%d" % k, [128, 512], F32) for k in range(2)]
            ost = [P.sbuf("gp_ost%d" % k, [128, TOK], BF16) for k in range(2)]
            epsc = P.sbuf("gp_eps", [128, 1], F32)
            P.memset(epsc[:, :], EPS, [epsc])
            for k in range(2):
                P.memset(pad[k][:, :], 0.0, [pad[k]])
            psA = [P.psum("gp_psA%d" % k, [128, 512], F32) for k in range(4)]
            psN = [P.psum("gp_psN%d" % k, [128, 512], F32) for k in range(2)]
            cnt = 0
            for pn in range(16):
                Wp = wp[pn % 2]
                P.dma(Wp[:, :, :], win[:, :, pn * 512:(pn + 1) * 512], [], [Wp], q="pool")
                for nb in range(4):
                    chunk = pn * 4 + nb
                    PD, AC, OS = pad[chunk % 2], acc[chunk % 2], ost[chunk % 2]
                    for bi, (t0, n) in enumerate(blocks):
                        ps = psA[cnt % 4]
                        cnt += 1
                        for k in range(16):
                            P.matmul(ps[:, 0:n], Wp[:, k, nb * 128:(nb + 1) * 128], hT[:, k, t0:t0 + n], k == 0, k == 15,
                                     [Wp, hT], [ps])
                        o0 = 2 + t0 if bi == 0 else 264 + 2 + (t0 - 256)
                        P.copy(PD[:, o0:o0 + n], ps[:, 0:n], [ps], [PD], q="act")
                    for (po, ao, n) in ((0, 0, 256), (264, 256, 2048)):
                        for k in range(5):
                            src = PD[:, po + k:po + k + n]
                            if k == 0:
                                P.ts(AC[:, ao:ao + n], src, cw[:, chunk, 0:1], None, ALU.mult, None, [PD, cw], [AC])
                            else:
                                P.stt(AC[:, ao:ao + n], src, cw[:, chunk, k:k + 1], AC[:, ao:ao + n], ALU.mult, ALU.add,
                                      [PD, cw, AC], [AC])
                    if chunk >= 32:
                        P.act(OS[:, :], AC[:, :], AF.Silu, [AC], [OS])
                    else:
                        P.act(AC[:, :], AC[:, :], AF.Silu, [AC], [AC])
                        qs = (128.0 ** -0.5) if chunk < 16 else 1.0
                        for bi, (t0, n) in enumerate(blocks):
                            SQ, RI = sq[bi % 2], rin[bi % 2]
                            pN = psN[bi % 2]
                            P.tt(SQ[:, 0:n], AC[:, t0:t0 + n], AC[:, t0:t0 + n], ALU.mult, [AC], [SQ])
                            P.matmul(pN[:, 0:n], self.ones[:, :], SQ[:, 0:n], True, True, [self.ones, SQ], [pN])
                            P.act(RI[:, 0:n], pN[:, 0:n], AF.Sqrt, [pN, epsc], [RI], bias=epsc[:, 0:1])
                            P.recip(RI[:, 0:n], RI[:, 0:n], [RI], [RI])
                            P.stt(OS[:, t0:t0 + n], AC[:, t0:t0 + n], qs, RI[:, 0:n], ALU.mult, ALU.mult, [AC, RI], [OS])
                    P.dma(g_qkvT[chunk * 128:(chunk + 1) * 128, :], OS[:, :], [OS], [g_qkvT])
        with P.phase():
            hT = P.sbuf("gz_hT", [128, 16, TOK], BF16)
            P.dma(hT[:, :, :], hT_d[:, :, :], [hT_d], [hT])
            wp = [P.sbuf("gz_wp%d" % k, [128, 16, 512], BF16) for k in range(2)]
            stage = [P.sbuf("gz_st%d" % k, [128, NT, 512], BF16) for k in range(2)]
            psA = [P.psum("gz_psA%d" % k, [128, 512], F32) for k in range(4)]
            cnt = 0
            for pn in range(8):
                Wp = wp[pn % 2]
                ST = stage[pn % 2]
                c0 = 8192 + pn * 512
                P.dma(Wp[:, :, :], win[:, :, c0:c0 + 512], [], [Wp], q="pool")
                for t in range(NT):
                    ps = psA[cnt % 4]
                    cnt += 1
                    for k in range(16):
                        P.matmul(ps[:, :], hT[:, k, t * 128:(t + 1) * 128], Wp[:, k, :], k == 0, k == 15, [hT, Wp], [ps])
                    P.act(ST[:, t, :], ps[:, :], AF.Silu, [ps], [ST])
                P.dma(g_z.t.rearrange("(t p) e -> p t e", p=128)[:, :, pn * 512:(pn + 1) * 512], ST[:, :, :], [ST], [g_z])
            Wab = P.sbuf("gz_wab", [128, 16, 128], BF16)
            P.dma(Wab[:, :, :], win[:, :, 12288:12416], [], [Wab], q="pool")
            par = P.sbuf("gz_par", [128, 2, 64], F32)
            P.dma(par[:, 0, :], self.W["gdn_a_log"].t[j].rearrange("d h -> (d h)").partition_broadcast(128), [], [par])
            P.dma(par[:, 1, :], self.W["gdn_dt_bias"].t[j].rearrange("d h -> (d h)").partition_broadcast(128), [], [par])
            nA = P.sbuf("gz_nA", [128, 2, 32], F32)
            P.act(nA[:, :, :], par[:, 0, :].rearrange("p (d h) -> p d h", d=2), AF.Exp, [par], [nA])
            ab = P.sbuf("gz_ab", [128, NT, 128], F32)
            tmp = [P.sbuf("gz_tmp%d" % k, [128, 2, 32], F32) for k in range(2)]
            for t in range(NT):
                ps = psA[cnt % 4]
                cnt += 1
                for k in range(16):
                    P.matmul(ps[:, 0:128], hT[:, k, t * 128:(t + 1) * 128], Wab[:, k, :], k == 0, k == 15, [hT, Wab], [ps])
                pv = ps[:, 0:128].rearrange("p (d a h) -> p d a h", d=2, a=2)
                av = ab[:, t, :].rearrange("p (d a h) -> p d a h", d=2, a=2)
                T = tmp[t % 2]
                P.tt(T[:, :, :], pv[:, :, 0, :], par[:, 1, :].rearrange("p (d h) -> p d h", d=2), ALU.add, [ps, par], [T])
                P.act(T[:, :, :], T[:, :, :], AF.Exp, [T], [T])
                P.act(T[:, :, :], T[:, :, :], AF.Ln, [T], [T], bias=self.ones[:, 0:1])
                P.stt(av[:, :, 0, :], T[:, :, :], -1.0, nA[:, :, :], ALU.mult, ALU.mult, [T, nA], [ab])
                P.act(av[:, :, 1, :], pv[:, :, 1, :], AF.Sigmoid, [ps], [ab])
            P.dma(g_ab.t.rearrange("(t p) e -> p t e", p=128), ab[:, :, :], [ab], [g_ab])

    def gdn_core(self, i, j, b):
        P = self.P
        g_qkvT = self.scratch("g_qkvT", [8192, TOK], BF16)
        g_z = self.scratch("r_gate", [TOK, 4096], BF16)
        g_ab = self.scratch("g_ab", [TOK, 128], F32)
        r_ogT = self.scratch("r_ogT", [4096, TOK], BF16)

        class Ring:
            def __init__(self, items):
                self.items = items
                self.k = 0

            def next(self):
                it = self.items[self.k % len(self.items)]
                self.k += 1
                return it

        with P.phase():
            def cload(nm):
                t = P.sbuf("gc_" + nm, [128, 128], F32)
                P.dma(t[:, :], self.C[nm][:, :], [], [t])
                return t
            LI, LS, UI, US = cload("LI"), cload("LS"), cload("UI"), cload("US")
            gnw = P.sbuf("gc_gnw", [128, 128], F32)
            P.dma(gnw[:, :], self.W["gdn_norm_w"].t[j].partition_broadcast(128), [], [gnw])
            gb = P.sbuf("gc_gb", [128, NT, 128], F32)
            P.dma(gb[:, :, :], g_ab.t.rearrange("(t p) e -> p t e", p=128), [g_ab], [gb])
            qT = P.sbuf("gc_qT", [128, TOK], BF16)
            kT = P.sbuf("gc_kT", [128, TOK], BF16)
            vT = P.sbuf("gc_vT", [128, TOK], BF16)
            zh = P.sbuf("gc_zh", [128, NT, 128], BF16)
            ktm = P.sbuf("gc_ktm", [128, NT, 128], BF16)
            vtm = P.sbuf("gc_vtm", [128, NT, 128], BF16)
            oacc = P.sbuf("gc_oacc", [128, NT, 128], F32)
            ogT = P.sbuf("gc_ogT", [128, TOK], BF16)
            junk = P.sbuf("gc_junk", [128, 128], F32)
            def ring(nm, n, shape, dt):
                return Ring([P.sbuf("gc_%s%d" % (nm, k), shape, dt) for k in range(n)])
            rGm = ring("Gm", 4, [128, 128], F32)
            rcols = ring("cols", 6, [128, 8], F32)
            rDec = ring("Dec", 4, [128, 128], F32)
            rDecT = ring("DecT", 4, [128, 128], F32)
            rsq = ring("sqb", 24, [128, 128], BF16)
            rU = ring("U", 4, [128, 128], F32)
            rAT = ring("ATr", 4, [128, 128], BF16)
            rM = ring("Mr", 3, [128, 128], BF16)
            rN = ring("Nr", 3, [128, 128], BF16)
            def cloadb(nm):
                tf = P.sbuf("gc_f_" + nm, [128, 128], F32)
                P.dma(tf[:, :], self.C[nm][:, :], [], [tf])
                tb = P.sbuf("gc_b_" + nm, [128, 128], BF16)
                P.copy(tb[:, :], tf[:, :], [tf], [tb])
                return tb
            blk16 = cloadb("blk16")
            lvlm = [cloadb("lvl16"), cloadb("lvl32"), cloadb("lvl64")]
            rkd = ring("kdr", 4, [128, 128], BF16)
            rWT = ring("WTr", 4, [128, 128], BF16)
            rtmp = ring("tmp", 4, [128, 128], F32)
            S32 = [P.sbuf("gc_S32_%d" % k, [128, 128], F32) for k in range(2)]
            S16 = [P.sbuf("gc_S16_%d" % k, [128, 128], BF16) for k in range(2)]
            st = ring("st", 2, [128, 8], F32)
            banks = []
            for k in range(5):
                bank = P.psum("gc_pq%d" % k, [128, 512], F32)
                banks.append((bank.t, Tok("pq%d" % k)))
            rbank = Ring(banks)

            class _PS:
                def __init__(self):
                    self.cur = None
                    self.q = 0

                def next(self, newbank=True):
                    if newbank or self.cur is None or self.q >= 4:
                        self.cur = rbank.next()
                        self.q = 0
                    ap = self.cur[0][:, self.q * 128:(self.q + 1) * 128]
                    self.q += 1
                    return ap, self.cur[1]
            rps = _PS()
            pcb = P.psum("gc_pc", [128, 512], F32)
            rpc = Ring([(pcb.t[:, 0:4], Tok("pc0"))])
            pbb = [P.psum("gc_pb%d" % k, [128, 512], BF16) for k in range(2)]
            rpbank = Ring([(pbb[k].t, Tok("pb%d" % k)) for k in range(2)])

            class _PB:
                def next(self):
                    bk = rpbank.next()
                    return bk[0][:, 0:128], bk[1]
            rpb = _PB()
            qkv = g_qkvT.t.rearrange("(c p) t -> p c t", p=128)
            zv = g_z.t.rearrange("(t p) e -> p t e", p=128)
            ov = r_ogT.t.rearrange("(c p) t -> p c t", p=128)
            identb = self.identb
            order = [list(range(NT)), [1, 0] + list(range(NT - 1, 1, -1))]

            def pre(d, hv, t):
                sl = slice(t * 128, (t + 1) * 128)
                LU = UI if d == 0 else LI
                GMASK = LS if d == 0 else US
                mS = LS if d == 0 else US
                mTI = UI if d == 0 else LI
                gcol = gb[:, t, d * 64 + hv:d * 64 + hv + 1]
                bcol = gb[:, t, d * 64 + 32 + hv:d * 64 + 32 + hv + 1]
                Gm = rGm.next()
                P.ts(Gm[:, :], GMASK[:, :], gcol, None, ALU.mult, None, [GMASK, gb], [Gm])
                pgd, tgd = rps.next()
                pgdT, tgdT = rps.next(False)
                P.matmul(pgd, LU[:, :], Gm[:, :], True, True, [LU, Gm], [tgd])
                P.matmul(pgdT, Gm[:, :], LU[:, :], True, True, [LU, Gm], [tgdT])
                pc, tpc = rpc.next()
                gcol2 = gb[:, t, d * 64 + hv:d * 64 + hv + 2]
                P.matmul(pc[:, 0:2], LU[:, :], gcol2, True, True, [LU, gb], [tpc])
                P.matmul(pc[:, 2:4], self.ones[:, :], gcol2, True, True, [self.ones, gb], [tpc])
                cols = rcols.next()
                P.copy(cols[:, 0:2], pc[:, 0:4:2], [tpc], [cols], q="act")
                P.act(cols[:, 2:3], cols[:, 0:1], AF.Exp, [cols], [cols])
                P.act(cols[:, 3:4], cols[:, 0:1], AF.Exp, [cols], [cols], scale=-1.0, bias=cols[:, 1:2])
                P.act(cols[:, 4:5], cols[:, 1:2], AF.Exp, [cols], [cols])
                P.tt(cols[:, 5:6], cols[:, 2:3], bcol, ALU.mult, [cols, gb], [cols])
                if self.cfg.get("gdn_cut", 9) < 2:
                    return None
                Dec, DecT = rDec.next(), rDecT.next()
                P.act(Dec[:, :], pgd, AF.Exp, [tgd], [Dec])
                P.act(DecT[:, :], pgdT, AF.Exp, [tgdT], [DecT])
                P.tt(Dec[:, :], Dec[:, :], mS[:, :], ALU.mult, [Dec, mS], [Dec])
                P.tt(DecT[:, :], DecT[:, :], mTI[:, :], ALU.mult, [DecT, mTI], [DecT])
                pkk, tkk = rps.next()
                pkq, tkq = rps.next(False)
                P.matmul(pkk, kT[:, sl], kT[:, sl], True, True, [kT], [tkk])
                P.matmul(pkq, kT[:, sl], qT[:, sl], True, True, [kT, qT], [tkq])
                M = rM.next()
                AT = rAT.next()
                P.stt(M[:, :], pkk, bcol, Dec[:, :], ALU.mult, ALU.mult, [tkk, gb, Dec], [M])
                P.tt(AT[:, :], pkq, DecT[:, :], ALU.mult, [tkq, DecT], [AT])
                pN, tN = rpb.next()
                P.transpose(pN, M[:, :], identb[:, :], [M, identb], [tN])
                N = rN.next()
                P.copy(N[:, :], pN, [tN], [N], q="act")
                M0, N0 = rsq.next(), rsq.next()
                P.tt(M0[:, :], M[:, :], blk16[:, :], ALU.mult, [M, blk16], [M0])
                P.tt(N0[:, :], N[:, :], blk16[:, :], ALU.mult, [N, blk16], [N0])
                T_, R = rsq.next(), rsq.next()
                P.tt(T_[:, :], identb[:, :], M0[:, :], ALU.subtract, [identb, M0], [T_])
                P.tt(R[:, :], identb[:, :], N0[:, :], ALU.subtract, [identb, N0], [R])
                Pm, Qm = N0, M0
                for lvl in range(3):
                    pp, tp = rps.next()
                    P.matmul(pp, Qm[:, :], Pm[:, :], True, True, [Qm, Pm], [tp])
                    pq, tq = rps.next(False)
                    P.matmul(pq, Pm[:, :], Qm[:, :], True, True, [Qm, Pm], [tq])
                    P2, Q2 = rsq.next(), rsq.next()
                    P.copy(P2[:, :], pp, [tp], [P2], q="act")
                    P.copy(Q2[:, :], pq, [tq], [Q2], q="act")
                    pt_, tt_ = rps.next()
                    P.matmul(pt_, P2[:, :], T_[:, :], True, True, [P2, T_], [tt_])
                    prr, trr = rps.next(False)
                    P.matmul(prr, T_[:, :], P2[:, :], True, True, [P2, T_], [trr])
                    Tn, Rn = rsq.next(), rsq.next()
                    P.tt(Tn[:, :], pt_, T_[:, :], ALU.add, [tt_, T_], [Tn])
                    P.tt(Rn[:, :], prr, R[:, :], ALU.add, [trr, R], [Rn])
                    T_, R, Pm, Qm = Tn, Rn, P2, Q2
                for li, msk in enumerate(lvlm):
                    last = li == len(lvlm) - 1
                    B1, B1T = rsq.next(), rsq.next()
                    P.tt(B1[:, :], M[:, :], msk[:, :], ALU.mult, [M, msk], [B1])
                    P.tt(B1T[:, :], N[:, :], msk[:, :], ALU.mult, [N, msk], [B1T])
                    px, tx = rps.next()
                    P.matmul(px, B1[:, :], R[:, :], True, True, [B1, R], [tx])
                    if not last:
                        px2, tx2 = rps.next(False)
                        P.matmul(px2, B1T[:, :], T_[:, :], True, True, [B1T, T_], [tx2])
                    Xp = rsq.next()
                    P.copy(Xp[:, :], px, [tx], [Xp], q="act")
                    if not last:
                        X = rsq.next()
                        P.copy(X[:, :], px2, [tx2], [X], q="act")
                    pr, tr = rps.next()
                    P.matmul(pr, T_[:, :], Xp[:, :], True, True, [T_, Xp], [tr])
                    if not last:
                        pt_, tt_ = rps.next(False)
                        P.matmul(pt_, R[:, :], X[:, :], True, True, [R, X], [tt_])
                    Rn = rsq.next()
                    P.tt(Rn[:, :], R[:, :], pr, ALU.subtract, [tr, R], [Rn])
                    if not last:
                        Tn = rsq.next()
                        P.tt(Tn[:, :], T_[:, :], pt_, ALU.subtract, [tt_, T_], [Tn])
                        T_ = Tn
                    R = Rn
                RHSu, RHSw, kdec = rsq.next(), rsq.next(), rkd.next()
                P.ts(RHSu[:, :], vtm[:, t, :], bcol, None, ALU.mult, None, [vtm, gb], [RHSu])
                P.ts(RHSw[:, :], ktm[:, t, :], cols[:, 5:6], None, ALU.mult, None, [ktm, cols], [RHSw])
                P.ts(kdec[:, :], ktm[:, t, :], cols[:, 3:4], None, ALU.mult, None, [ktm, cols], [kdec])
                pU, tU = rps.next()
                pW, tW = rps.next(False)
                P.matmul(pU, R[:, :], RHSu[:, :], True, True, [R, RHSu], [tU])
                P.matmul(pW, RHSw[:, :], R[:, :], True, True, [R, RHSw], [tW])
                U = rU.next()
                WT = rWT.next()
                P.copy(U[:, :], pU, [tU], [U], q="act")
                P.copy(WT[:, :], pW, [tW], [WT], q="act")
                return dict(U=U, WT=WT, AT=AT, kdec=kdec, cols=cols, t=t)

            def seq(d, hv, pr_, first, first_dir):
                t = pr_["t"]
                sl = slice(t * 128, (t + 1) * 128)
                S, Sb = S32[d], S16[d]
                cols = pr_["cols"]
                vnew = rsq.next()
                if first:
                    P.copy(vnew[:, :], pr_["U"][:, :], [pr_["U"]], [vnew])
                else:
                    pws, tws = rps.next()
                    P.matmul(pws, pr_["WT"][:, :], Sb[:, :], True, True, [pr_["WT"], Sb], [tws])
                    P.tt(vnew[:, :], pr_["U"][:, :], pws, ALU.subtract, [pr_["U"], tws], [vnew])
                po2, to2 = rps.next()
                P.matmul(po2, pr_["AT"][:, :], vnew[:, :], True, True, [pr_["AT"], vnew], [to2])
                if not first:
                    po1, to1 = rps.next(False)
                    P.matmul(po1, qT[:, sl], Sb[:, :], True, True, [qT, Sb], [to1])
                    tm = rtmp.next()
                    P.act(tm[:, :], po1, AF.Copy, [to1, cols], [tm], scale=cols[:, 2:3])
                    if first_dir:
                        P.tt(oacc[:, t, :], po2, tm[:, :], ALU.add, [to2, tm], [oacc])
                    else:
                        P.tt(tm[:, :], po2, tm[:, :], ALU.add, [to2, tm], [tm])
                        P.tt(oacc[:, t, :], oacc[:, t, :], tm[:, :], ALU.add, [oacc, tm], [oacc])
                else:
                    if first_dir:
                        P.copy(oacc[:, t, :], po2, [to2], [oacc])
                    else:
                        P.tt(oacc[:, t, :], oacc[:, t, :], po2, ALU.add, [oacc, to2], [oacc])
                pS, tS = rps.next()
                P.matmul(pS, pr_["kdec"][:, :], vnew[:, :], True, True, [pr_["kdec"], vnew], [tS])
                if first:
                    P.copy(S[:, :], pS, [tS], [S])
                else:
                    P.stt(S[:, :], S[:, :], cols[:, 4:5], pS, ALU.mult, ALU.add, [S, cols, tS], [S])
                P.copy(Sb[:, :], S[:, :], [S], [Sb], q="act")

            for hv in range(self.cfg.get("gdn_heads", 32)):
                hq = hv // 2
                if hv % 2 == 0:
                    P.dma(qT[:, :], qkv[:, hq, :], [g_qkvT], [qT])
                    P.dma(kT[:, :], qkv[:, 16 + hq, :], [g_qkvT], [kT])
                    for t0 in range(0, NT, 4):
                        bk, tk_ = rpbank.next()
                        nt = min(4, NT - t0)
                        for q4 in range(nt):
                            t = t0 + q4
                            P.transpose(bk[:, q4 * 128:(q4 + 1) * 128], kT[:, t * 128:(t + 1) * 128], identb[:, :],
                                        [kT, identb], [tk_])
                        P.copy(ktm[:, t0:t0 + nt, :], bk[:, 0:nt * 128].rearrange("p (a b) -> p a b", b=128), [tk_], [ktm],
                               q="act")
                P.dma(vT[:, :], qkv[:, 32 + hv, :], [g_qkvT], [vT])
                P.dma(zh[:, :, :], zv[:, :, hv * 128:(hv + 1) * 128], [g_z], [zh])
                for t0 in range(0, NT, 4):
                    bk, tk_ = rpbank.next()
                    nt = min(4, NT - t0)
                    for q4 in range(nt):
                        t = t0 + q4
                        P.transpose(bk[:, q4 * 128:(q4 + 1) * 128], vT[:, t * 128:(t + 1) * 128], identb[:, :],
                                    [vT, identb], [tk_])
                    P.copy(vtm[:, t0:t0 + nt, :], bk[:, 0:nt * 128].rearrange("p (a b) -> p a b", b=128), [tk_], [vtm],
                           q="act")
                if self.cfg.get("gdn_cut", 9) < 1:
                    continue
                for d in range(2):
                    nxt = pre(d, hv, order[d][0])
                    if self.cfg.get("gdn_cut", 9) < 3:
                        continue
                    for n in range(NT):
                        cur = nxt
                        if n + 1 < NT:
                            nxt = pre(d, hv, order[d][n + 1])
                        seq(d, hv, cur, n == 0, d == 0)
                for t in range(NT):
                    S_ = st.next()
                    o = oacc[:, t, :]
                    P.memset(S_[:, 0:1], 0.0, [S_])
                    P.act(junk[:, :], o, AF.Square, [oacc], [junk, S_], accum_out=S_[:, 0:1])
                    P.ts(S_[:, 1:2], S_[:, 0:1], 1.0 / 128, EPS, ALU.mult, ALU.add, [S_], [S_])
                    P.act(S_[:, 2:3], S_[:, 1:2], AF.Sqrt, [S_], [S_])
                    P.recip(S_[:, 3:4], S_[:, 2:3], [S_], [S_])
                    tm = rtmp.next()
                    P.stt(tm[:, :], o, S_[:, 3:4], gnw[:, :], ALU.mult, ALU.mult, [oacc, S_, gnw], [tm])
                    og = rsq.next()
                    P.tt(og[:, :], tm[:, :], zh[:, t, :], ALU.mult, [tm, zh], [og])
                    pb_, tb_ = rpb.next()
                    P.transpose(pb_, og[:, :], identb[:, :], [og, identb], [tb_])
                    P.copy(ogT[:, t * 128:(t + 1) * 128], pb_, [tb_], [ogT], q="act")
                P.dma(ov[:, hv, :], ogT[:, :], [ogT], [r_ogT])

    def ret_proj(self, i, j, b):
        P = self.P
        hT_d = self.scratch("hT_d", [128, 16, TOK], BF16)
        r_qT = self.scratch("r_qT", [D, TOK], BF16)
        r_kT = self.scratch("r_kT", [D, TOK], BF16)
        r_v = self.scratch("r_v", [TOK, 4096], BF16)
        r_gate = self.scratch("r_gate", [TOK, 4096], BF16)
        win = self.W["ret_w_in"].t[j].rearrange("(c p) n -> p c n", p=128)
        blocks = [(0, 256)] + [(256 + 512 * k, 512) for k in range(4)]
        with P.phase():
            hT = P.sbuf("rp_hT", [128, 16, TOK], BF16)
            P.dma(hT[:, :, :], hT_d[:, :, :], [hT_d], [hT])
            rope = P.sbuf("rp_rope", [128, 4, SEQ], F32)
            P.dma(rope[:, :, :], self.C["rope"][:, :, :], [], [rope])
            rrf = P.sbuf("rp_rrf", [128, 128], F32)
            rrb = P.sbuf("rp_rrb", [128, 128], BF16)
            P.dma(rrf[:, :], self.C["rrotT"][:, :], [], [rrf])
            P.copy(rrb[:, :], rrf[:, :], [rrf], [rrb])
            wp = [P.sbuf("rp_wp%d" % k, [128, 16, 512], BF16) for k in range(2)]
            qraw = [P.sbuf("rp_qraw%d" % k, [128, 512], BF16) for k in range(2)]
            T1 = [P.sbuf("rp_t1%d" % k, [128, 512], F32) for k in range(2)]
            T2 = [P.sbuf("rp_t2%d" % k, [128, 512], F32) for k in range(2)]
            qst = [P.sbuf("rp_qst%d" % k, [128, TOK], BF16) for k in range(2)]
            psA = [P.psum("rp_psA%d" % k, [128, 512], F32) for k in range(4)]
            psR = [P.psum("rp_psR%d" % k, [128, 512], F32) for k in range(2)]
            cnt = 0
            for pn in range(8):
                Wp = wp[pn % 2]
                P.dma(Wp[:, :, :], win[:, :, pn * 512:(pn + 1) * 512], [], [Wp], q="pool")
                for nb in range(4):
                    chunk = pn * 4 + nb
                    isk = chunk >= 16
                    typ = chunk % 2
                    sc = 0.0625 if isk else 1.0
                    QS = qst[chunk % 2]
                    for bi, (t0, n) in enumerate(blocks):
                        ps = psA[cnt % 4]
                        for k in range(16):
                            P.matmul(ps[:, 0:n], Wp[:, k, nb * 128:(nb + 1) * 128], hT[:, k, t0:t0 + n], k == 0, k == 15,
                                     [Wp, hT], [ps])
                        if bi == 0:
                            P.act(QS[:, 0:256], ps[:, 0:256], AF.Copy, [ps], [QS], scale=sc)
                        else:
                            QR = qraw[cnt % 2]
                            P.act(QR[:, :], ps[:, :], AF.Copy, [ps], [QR], scale=sc)
                            pr = psR[cnt % 2]
                            P.matmul(pr[:, :], rrb[:, :], QR[:, :], True, True, [rrb, QR], [pr])
                            x0 = t0 - 256
                            P.tt(T1[cnt % 2][:, :], QR[:, :], rope[:, 2 * typ, x0:x0 + 512], ALU.mult, [QR, rope], [T1[cnt % 2]])
                            P.tt(T2[cnt % 2][:, :], pr[:, :], rope[:, 2 * typ + 1, x0:x0 + 512], ALU.mult, [pr, rope],
                                 [T2[cnt % 2]])
                            P.tt(QS[:, t0:t0 + 512], T1[cnt % 2][:, :], T2[cnt % 2][:, :], ALU.add,
                                 [T1[cnt % 2], T2[cnt % 2]], [QS])
                        cnt += 1
                    dst = r_kT if isk else r_qT
                    row0 = (chunk % 16) * 128
                    P.dma(dst[row0:row0 + 128, :], QS[:, :], [QS], [dst])
        with P.phase():
            hT = P.sbuf("rv_hT", [128, 16, TOK], BF16)
            P.dma(hT[:, :, :], hT_d[:, :, :], [hT_d], [hT])
            wp = [P.sbuf("rv_wp%d" % k, [128, 16, 512], BF16) for k in range(2)]
            stage = [P.sbuf("rv_st%d" % k, [128, NT, 512], BF16) for k in range(2)]
            psA = [P.psum("rv_psA%d" % k, [128, 512], F32) for k in range(4)]
            cnt = 0
            for pn in range(16):
                Wp = wp[pn % 2]
                ST = stage[pn % 2]
                c0 = 4096 + pn * 512
                P.dma(Wp[:, :, :], win[:, :, c0:c0 + 512], [], [Wp], q="pool")
                for t in range(NT):
                    ps = psA[cnt % 4]
                    cnt += 1
                    for k in range(16):
                        P.matmul(ps[:, :], hT[:, k, t * 128:(t + 1) * 128], Wp[:, k, :], k == 0, k == 15, [hT, Wp], [ps])
                    P.act(ST[:, t, :], ps[:, :], AF.Copy if pn < 8 else AF.Silu, [ps], [ST])
                dst = r_v if pn < 8 else r_gate
                cc = (pn % 8) * 512
                P.dma(dst.t.rearrange("(t p) e -> p t e", p=128)[:, :, cc:cc + 512], ST[:, :, :], [ST], [dst])

    def ret_core(self, i, j, b):
        P = self.P
        r_qT = self.scratch("r_qT", [D, TOK], BF16)
        r_kT = self.scratch("r_kT", [D, TOK], BF16)
        r_v = self.scratch("r_v", [TOK, 4096], BF16)
        r_gate = self.scratch("r_gate", [TOK, 4096], BF16)
        r_ogT = self.scratch("r_ogT", [4096, TOK], BF16)
        with P.phase():
            def cload(nm, shape):
                t = P.sbuf("rc_" + nm, shape, F32)
                P.dma(t[:, :], self.C[nm][:, :], [], [t])
                return t
            relF = cload("relF", [128, 128])
            maskF = cload("maskF", [128, 128])
            relB = cload("relB", [128, 128])
            maskB = cload("maskB", [128, 128])
            posrowF = cload("posrowF", [128, 128])
            posrowB = cload("posrowB", [128, 128])
            poscol = cload("poscol", [128, 4])
            lg = P.sbuf("rc_lg", [128, 16], F32)
            P.dma(lg[:, :], self.W["ret_log_decay"].t[j].rearrange("d h -> (d h)").partition_broadcast(128), [], [lg])
            gnw = P.sbuf("rc_gnw", [128, 512], F32)
            P.dma(gnw[:, :], self.W["ret_gn_w"].t[j].partition_broadcast(128), [], [gnw])
            E1 = P.sbuf("rc_E1", [128, 128], F32)
            E2 = P.sbuf("rc_E2", [128, 128], F32)
            DT = P.sbuf("rc_DT", [128, 128], F32)
            TFq = P.sbuf("rc_TFq", [128, 128], F32)
            TBq = P.sbuf("rc_TBq", [128, 128], F32)
            cols = P.sbuf("rc_cols", [128, 4], F32)
            qT = P.sbuf("rc_qT", [128, 2, TOK], BF16)
            kT = P.sbuf("rc_kT", [128, 2, TOK], BF16)
            vh = P.sbuf("rc_v", [128, NT, 512], BF16)
            gh = P.sbuf("rc_g", [128, NT, 512], BF16)
            qdf = P.sbuf("rc_qdf", [128, 2, TOK], BF16)
            qdb = P.sbuf("rc_qdb", [128, 2, TOK], BF16)
            kdb_all = P.sbuf("rc_kdb", [128, NT, 256], BF16)
            kdf = [P.sbuf("rc_kdf%d" % k, [128, 256], BF16) for k in range(2)]
            ATs = [P.sbuf("rc_AT%d" % k, [128, 128], BF16) for k in range(2)]
            oacc = P.sbuf("rc_oacc", [128, NT, 512], F32)
            S32 = [P.sbuf("rc_S32_%d" % k, [128, 2, 512], F32) for k in range(2)]
            S16 = [P.sbuf("rc_S16_%d" % k, [128, 2, 512], BF16) for k in range(2)]
            ogT = P.sbuf("rc_ogT", [128, 4, TOK], BF16)
            ogt = [P.sbuf("rc_ogt%d" % k, [128, 512], BF16) for k in range(2)]
            tmp = [P.sbuf("rc_tmp%d" % k, [128, 512], F32) for k in range(2)]
            junk = P.sbuf("rc_junk", [128, 512], F32)
            st = [P.sbuf("rc_st%d" % k, [128, 8], F32) for k in range(2)]
            psAT = [P.psum("rc_psAT%d" % k, [128, 512], F32) for k in range(2)]
            psK = [P.psum("rc_psK%d" % k, [128, 512], BF16) for k in range(2)]
            psO = [P.psum("rc_psO%d" % k, [128, 512], F32) for k in range(2)]
            psS = [P.psum("rc_psS%d" % k, [128, 512], F32) for k in range(2)]
            order_f = list(range(NT))
            order_b = [1, 0] + list(range(NT - 1, 1, -1))
            qv = r_qT.t.rearrange("(c p) t -> p c t", p=128)
            kv = r_kT.t.rearrange("(c p) t -> p c t", p=128)
            vv = r_v.t.rearrange("(t p) e -> p t e", p=128)
            gv = r_gate.t.rearrange("(t p) e -> p t e", p=128)
            ov = r_ogT.t.rearrange("(c p) t -> p c t", p=128)
            for h in range(8):
                lgf = lg[:, h:h + 1]
                lgb = lg[:, 8 + h:9 + h]
                P.dma(qT[:, :, :], qv[:, 2 * h:2 * h + 2, :], [r_qT], [qT])
                P.dma(kT[:, :, :], kv[:, 2 * h:2 * h + 2, :], [r_kT], [kT])
                P.dma(vh[:, :, :], vv[:, :, h * 512:(h + 1) * 512], [r_v], [vh])
                P.dma(gh[:, :, :], gv[:, :, h * 512:(h + 1) * 512], [r_gate], [gh])
                P.act(E1[:, :], relF[:, :], AF.Exp, [relF, lg], [E1], scale=lgf)
                P.tt(E1[:, :], E1[:, :], maskF[:, :], ALU.mult, [E1, maskF], [E1])
                P.act(E2[:, :], relB[:, :], AF.Exp, [relB, lg], [E2], scale=lgb)
                P.tt(E2[:, :], E2[:, :], maskB[:, :], ALU.mult, [E2, maskB], [E2])
                P.tt(DT[:, :], E1[:, :], E2[:, :], ALU.add, [E1, E2], [DT])
                P.act(TFq[:, :], posrowF[:, :], AF.Exp, [posrowF, lg], [TFq], scale=lgf)
                P.act(TBq[:, :], posrowB[:, :], AF.Exp, [posrowB, lg], [TBq], scale=lgb)
                P.act(cols[:, 0:1], poscol[:, 0:1], AF.Exp, [poscol, lg], [cols], scale=lgf)
                P.act(cols[:, 1:2], poscol[:, 1:2], AF.Exp, [poscol, lg], [cols], scale=lgb)
                P.act(cols[:, 2:3], lgf, AF.Exp, [lg], [cols], scale=128.0)
                P.act(cols[:, 3:4], lgb, AF.Exp, [lg], [cols], scale=128.0)
                for dc in range(2):
                    P.tt(qdf[:, dc, :].rearrange("p (t i) -> p t i", i=128), qT[:, dc, :].rearrange("p (t i) -> p t i", i=128),
                         TFq[:, :].unsqueeze(1).broadcast_to([128, NT, 128]), ALU.mult, [qT, TFq], [qdf])
                    P.tt(qdb[:, dc, :].rearrange("p (t i) -> p t i", i=128), qT[:, dc, :].rearrange("p (t i) -> p t i", i=128),
                         TBq[:, :].unsqueeze(1).broadcast_to([128, NT, 128]), ALU.mult, [qT, TBq], [qdb])
                Sf, Sfb = S32[0], S16[0]
                for n, t in enumerate(order_f):
                    sl = slice(t * 128, (t + 1) * 128)
                    pa = psAT[n % 2]
                    for dc in range(2):
                        P.matmul(pa[:, 0:128], kT[:, dc, sl], qT[:, dc, sl], dc == 0, dc == 1, [kT, qT], [pa])
                    AT = ATs[n % 2]
                    P.tt(AT[:, :], pa[:, 0:128], DT[:, :], ALU.mult, [pa, DT], [AT])
                    pk = psK[n % 2]
                    for dc in range(2):
                        P.transpose(pk[:, dc * 128:(dc + 1) * 128], kT[:, dc, sl], self.identb[:, :], [kT, self.identb], [pk])
                    KF = kdf[n % 2]
                    P.act(KF[:, :], pk[:, 0:256], AF.Copy, [pk, cols], [KF], scale=cols[:, 0:1])
                    P.ts(kdb_all[:, t, :], pk[:, 0:256], cols[:, 1:2], None, ALU.mult, None, [pk, cols], [kdb_all])
                    po = psO[n % 2]
                    P.matmul(po[:, :], AT[:, :], vh[:, t, :], True, n == 0, [AT, vh], [po])
                    if n > 0:
                        for dc in range(2):
                            P.matmul(po[:, :], qdf[:, dc, sl], Sfb[:, dc, :], False, dc == 1, [qdf, Sfb], [po])
                    P.copy(oacc[:, t, :], po[:, :], [po], [oacc], q="act")
                    for dc in range(2):
                        pS = psS[dc]
                        P.matmul(pS[:, :], KF[:, dc * 128:(dc + 1) * 128], vh[:, t, :], True, True, [KF, vh], [pS])
                        if n == 0:
                            P.copy(Sf[:, dc, :], pS[:, :], [pS], [Sf])
                        else:
                            P.stt(Sf[:, dc, :], Sf[:, dc, :], cols[:, 2:3], pS[:, :], ALU.mult, ALU.add, [Sf, cols, pS], [Sf])
                    if n < NT - 1:
                        P.copy(Sfb[:, :, :], Sf[:, :, :], [Sf], [Sfb], q="act")
                Sb, Sbb = S32[1], S16[1]
                for n, t in enumerate(order_b):
                    sl = slice(t * 128, (t + 1) * 128)
                    if n > 0:
                        po = psO[n % 2]
                        for dc in range(2):
                            P.matmul(po[:, :], qdb[:, dc, sl], Sbb[:, dc, :], dc == 0, dc == 1, [qdb, Sbb], [po])
                        P.tt(oacc[:, t, :], oacc[:, t, :], po[:, :], ALU.add, [oacc, po], [oacc])
                    for dc in range(2):
                        pS = psS[dc]
                        P.matmul(pS[:, :], kdb_all[:, t, dc * 128:(dc + 1) * 128], vh[:, t, :], True, True, [kdb_all, vh], [pS])
                        if n == 0:
                            P.copy(Sb[:, dc, :], pS[:, :], [pS], [Sb])
                        else:
                            P.stt(Sb[:, dc, :], Sb[:, dc, :], cols[:, 3:4], pS[:, :], ALU.mult, ALU.add, [Sb, cols, pS], [Sb])
                    if n < NT - 1:
                        P.copy(Sbb[:, :, :], Sb[:, :, :], [Sb], [Sbb], q="act")
                for t in range(NT):
                    S = st[t % 2]
                    o = oacc[:, t, :]
                    P.memset(S[:, 0:2], 0.0, [S])
                    P.act(junk[:, :], o, AF.Identity, [oacc], [junk, S], accum_out=S[:, 0:1])
                    P.act(junk[:, :], o, AF.Square, [oacc], [junk, S], accum_out=S[:, 1:2])
                    P.ts(S[:, 2:3], S[:, 0:1], 1.0 / 512, None, ALU.mult, None, [S], [S])
                    P.tt(S[:, 3:4], S[:, 2:3], S[:, 2:3], ALU.mult, [S], [S])
                    P.stt(S[:, 4:5], S[:, 1:2], 1.0 / 512, S[:, 3:4], ALU.mult, ALU.subtract, [S], [S])
                    P.ts(S[:, 4:5], S[:, 4:5], EPS, None, ALU.add, None, [S], [S])
                    P.act(S[:, 5:6], S[:, 4:5], AF.Sqrt, [S], [S])
                    P.recip(S[:, 6:7], S[:, 5:6], [S], [S])
                    TM = tmp[t % 2]
                    P.ts(TM[:, :], o, S[:, 2:3], S[:, 6:7], ALU.subtract, ALU.mult, [oacc, S], [TM])
                    P.tt(TM[:, :], TM[:, :], gnw[:, :], ALU.mult, [TM, gnw], [TM])
                    OG = ogt[t % 2]
                    P.tt(OG[:, :], TM[:, :], gh[:, t, :], ALU.mult, [TM, gh], [OG])
                    pk = psK[t % 2]
                    for ec in range(4):
                        P.transpose(pk[:, ec * 128:(ec + 1) * 128], OG[:, ec * 128:(ec + 1) * 128], self.identb[:, :],
                                    [OG, self.identb], [pk])
                    P.copy(ogT[:, :, t * 128:(t + 1) * 128], pk[:, :].rearrange("p (e i) -> p e i", e=4), [pk], [ogT],
                           q="act")
                P.dma(ov[:, 4 * h:4 * h + 4, :], ogT[:, :, :], [ogT], [r_ogT])


def build(cfg):
    nc = bass.Bass("TRN2", target_bir_lowering=False)
    with contextlib.ExitStack() as stack:
        M = Model(nc, stack, cfg)
        P = M.P
        M.copy_inputs()
        if cfg.get("only_gdn_core"):
            M.gdn_core(1, 0, 0)
            P.barrier()
            P.flush()
            return nc
        M.modulation()
        for i in cfg.get("layers", range(DEPTH)):
            if "mixer" in cfg.get("phases", ["mixer", "ffn"]):
                M.mixer(i)
            if "ffn" in cfg.get("phases", ["mixer", "ffn"]):
                M.ffn(i)
        if cfg.get("final", True):
            M.final_norm()
        for nm in cfg.get("dump", []):
            src = M._scr[nm]
            shp = list(src.t.shape)
            dt = src.t.dtype
            do = P.dram("dump_" + nm, shp, dt, kind="ExternalOutput")
            if len(shp) == 2:
                P.dma(do[:, :], src[:, :], [src], [do])
            else:
                P.dma(do[:, :, :], src[:, :, :], [src], [do])
        if cfg.get("debug_ctx"):
            co = P.dram("ctx_out", [BPC, CTXL, D], F32, kind="ExternalOutput")
            for b in range(BPC):
                P.dma(co[b, :, :], M.ctxs[b, :, :], [M.rtok[b][0], M.rtok[b][1]], [co])
        P.barrier()
        P.flush()
        print("ops", P.nops, "blocks", P.nblocks)
    return nc


def make_in_maps(inputs, ncores=NCORES):
    consts = host_constants()
    maps = []
    for r in range(ncores):
        m = {}
        m["x"] = np.ascontiguousarray(inputs["x"][r * BPC:(r + 1) * BPC])
        m["ctx"] = np.ascontiguousarray(inputs["ctx"][r * BPC:(r + 1) * BPC])
        m["cvec"] = np.ascontiguousarray(
            np.concatenate([inputs["c"][r * BPC:(r + 1) * BPC], inputs["c_ctx"][None, :]], axis=0))
        for nm in WEIGHT_NAMES:
            m[nm] = np.ascontiguousarray(inputs[nm])
        for nm, arr in consts.items():
            m["k_" + nm] = arr
        maps.append(m)
    return maps


GROUP = NCORES


def kernel(**inputs):
    inputs = {k: np.asarray(v) for k, v in inputs.items()}
    nc = build({})
    maps = make_in_maps(inputs)
    outs = []
    for g0 in range(0, NCORES, GROUP):
        res = run_bass_kernel_spmd(nc, maps[g0:g0 + GROUP], core_ids=list(range(GROUP)))
        outs += [res.results[r]["out"] for r in range(GROUP)]
    return np.concatenate(outs, axis=0).astype(np.float32)
```

```python
import contextlib
import math
import numpy as np
import ml_dtypes
import concourse.bass as bass
import concourse.mybir as mybir
from concourse.bass_utils import run_bass_kernel_spmd

F32 = mybir.dt.float32
BF16 = mybir.dt.bfloat16
ALU = mybir.AluOpType
AF = mybir.ActivationFunctionType
AX = mybir.AxisListType

D = 2048
SEQ = 2048
CTXL = 256
TOK = SEQ + CTXL
NT = TOK // 128
DEPTH = 4
DFF = 8192
EPS = 1e-6
NCORES = 8
BPC = 2
NM = BPC + 1


class Tok:
    __slots__ = ("name", "lw", "rs")

    def __init__(self, name=""):
        self.name = name
        self.lw = None
        self.rs = {}


class TB:
    def __init__(self, t, name=""):
        self.t = t
        self.tok = Tok(name)

    def __getitem__(self, idx):
        return self.t[idx]


def _toks(lst):
    out = []
    for x in lst:
        if x is None:
            continue
        out.append(x.tok if isinstance(x, TB) else x)
    return out


class Prog:
    STREAMS = ["pe", "act", "dve", "pool", "sp"]
    NDMA = 8

    def __init__(self, nc, stack):
        self.nc = nc
        self.stack = stack
        self.cur = stack
        self.streams = {q: [] for q in self.STREAMS}
        self.count = {}
        self.known = {q: {} for q in self.STREAMS}
        self.dma_rr = {q: 0 for q in self.STREAMS}
        self.semh = {}
        self.nops = 0
        self.nblocks = 0
        for q in self.STREAMS:
            self._sem((q, -1, 0))
        for q in ("sp", "pool"):
            for k in range(self.NDMA):
                self._sem((q, k, 0))

    def _sem(self, key):
        if key not in self.semh:
            nm = "s_%s_%d_%d" % (key[0], key[1] + 1, key[2])
            self.semh[key] = self.stack.enter_context(self.nc.semaphore(nm))
            self.count.setdefault(key, 0)
        return self.semh[key]

    def sbuf(self, name, shape, dtype):
        self.uid = getattr(self, "uid", 0) + 1
        name = "%s_u%d" % (name, self.uid)
        t = self.cur.enter_context(self.nc.sbuf_tensor(name, list(shape), dtype))
        return TB(t, name)

    def psum(self, name, shape, dtype=F32):
        self.uid = getattr(self, "uid", 0) + 1
        name = "%s_u%d" % (name, self.uid)
        t = self.cur.enter_context(self.nc.psum_tensor(name, list(shape), dtype))
        return TB(t, name)

    def dram(self, name, shape, dtype, kind="Internal"):
        t = self.nc.dram_tensor(name, list(shape), dtype, kind=kind)
        return TB(t.ap(), name)

    @contextlib.contextmanager
    def phase(self):
        prev = self.cur
        with contextlib.ExitStack() as st:
            self.cur = st
            yield
            self.barrier()
            self.flush()
        self.cur = prev

    LIMIT = 20000

    def op(self, stream, fn, reads=(), writes=(), dma=False):
        gens = self.__dict__.setdefault("gens", {})
        if dma:
            k = self.dma_rr[stream]
            self.dma_rr[stream] = (k + 1) % self.NDMA
            base = (stream, k)
            step = 16
        else:
            base = (stream, -1)
            step = 1
        g = gens.get(base, 0)
        key = base + (g,)
        if self.count.get(key, 0) + step > self.LIMIT:
            g += 1
            gens[base] = g
            key = base + (g,)
        self._sem(key)
        n = self.count.get(key, 0) + step
        self.count[key] = n
        deps = {}
        reads = _toks(reads)
        writes = _toks(writes)
        for b in reads:
            if b.lw is not None and deps.get(b.lw[0], 0) < b.lw[1]:
                deps[b.lw[0]] = b.lw[1]
        for b in writes:
            if b.lw is not None and deps.get(b.lw[0], 0) < b.lw[1]:
                deps[b.lw[0]] = b.lw[1]
            for kk, v in b.rs.items():
                if deps.get(kk, 0) < v:
                    deps[kk] = v
        known = self.known[stream]
        waits = []
        for kk, v in deps.items():
            if stream == "pe" and kk[0] == "pe" and kk[1] == -1:
                continue
            if known.get(kk, 0) >= v:
                continue
            known[kk] = v
            waits.append((kk, v))
        self.streams[stream].append((waits, fn, key, step))
        for b in reads:
            if b.rs.get(key, 0) < n:
                b.rs[key] = n
        for b in writes:
            b.lw = (key, n)
            b.rs = {}
        self.nops += 1

    def barrier(self):
        for s in self.STREAMS:
            known = self.known[s]
            waits = []
            for kk, v in self.count.items():
                if v > 0 and known.get(kk, 0) < v:
                    known[kk] = v
                    waits.append((kk, v))
            if waits:
                self.streams[s].append((waits, None, None, 0))

    def flush(self):
        nc = self.nc
        semh = self.semh
        st = self.streams
        self.streams = {q: [] for q in self.STREAMS}
        if not any(st.values()):
            return
        self.nblocks += 1

        def replay(eng, lst):
            for waits, fn, key, step in lst:
                for kk, v in waits:
                    eng.wait_ge(semh[kk], v)
                if fn is not None:
                    ins = fn(eng)
                    ins.then_inc(semh[key], step)

        with nc.Block() as block:
            @block.tensor
            def _(e):
                replay(e, st["pe"])

            @block.scalar
            def _(e):
                replay(e, st["act"])

            @block.vector
            def _(e):
                replay(e, st["dve"])

            @block.gpsimd
            def _(e):
                replay(e, st["pool"])

            @block.sync
            def _(e):
                replay(e, st["sp"])

    def dma(self, out, in_, reads, writes, q="sp"):
        self.op(q, lambda e: e.dma_start(out=out, in_=in_), reads, writes, dma=True)

    def matmul(self, out, lhsT, rhs, start, stop, reads, writes):
        self.op("pe", lambda e: e.matmul(out, lhsT, rhs, start=start, stop=stop), reads, writes)

    def transpose(self, out, in_, ident, reads, writes):
        self.op("pe", lambda e: e.transpose(out, in_, ident), reads, writes)

    def act(self, out, in_, func, reads, writes, bias=None, scale=None, accum_out=None):
        kw = {}
        if bias is not None:
            kw["bias"] = bias
        if scale is not None:
            kw["scale"] = scale
        if accum_out is not None:
            kw["accum_out"] = accum_out
        self.op("act", lambda e: e.activation(out, in_, func, **kw), reads, writes)

    def tt(self, out, in0, in1, op, reads, writes, q="dve"):
        self.op(q, lambda e: e.tensor_tensor(out, in0, in1, op), reads, writes)

    def ts(self, out, in0, s1, s2, op0, op1, reads, writes, q="dve"):
        if op1 is None:
            self.op(q, lambda e: e.tensor_scalar(out, in0, s1, None, op0), reads, writes)
        else:
            self.op(q, lambda e: e.tensor_scalar(out, in0, s1, s2, op0, op1), reads, writes)

    def stt(self, out, in0, scalar, in1, op0, op1, reads, writes, q="dve"):
        self.op(q, lambda e: e.scalar_tensor_tensor(out, in0, scalar, in1, op0, op1), reads, writes)

    def copy(self, out, in_, reads, writes, q="dve"):
        if q == "act":
            self.op(q, lambda e: e.activation(out, in_, AF.Copy), reads, writes)
        else:
            self.op(q, lambda e: e.tensor_copy(out, in_), reads, writes)

    def memset(self, ap, val, writes, q="dve"):
        self.op(q, lambda e: e.memset(ap, val), (), writes)

    def recip(self, out, in_, reads, writes):
        self.op("dve", lambda e: e.reciprocal(out, in_), reads, writes)


WEIGHT_NAMES = ["mod_w", "mod_b", "norm_w", "final_norm_w", "ffn_w1", "ffn_w2",
                "s5_lam_re", "s5_lam_im", "s5_log_dt", "s5_b_re", "s5_b_im", "s5_c_re", "s5_c_im", "s5_d",
                "s5_glu_w", "gdn_w_in", "gdn_conv_w", "gdn_a_log", "gdn_dt_bias", "gdn_norm_w", "gdn_w_out",
                "ret_w_in", "ret_log_decay", "ret_gn_w", "ret_w_out"]
WEIGHT_SHAPES = {
    "mod_w": (4, 2048, 12288), "mod_b": (4, 12288), "norm_w": (4, 2, 2048), "final_norm_w": (2048,),
    "ffn_w1": (4, 2048, 8192), "ffn_w2": (4, 8192, 2048),
    "s5_lam_re": (2, 2, 128, 64), "s5_lam_im": (2, 2, 128, 64), "s5_log_dt": (2, 2, 128),
    "s5_b_re": (2, 2, 128, 64, 16), "s5_b_im": (2, 2, 128, 64, 16),
    "s5_c_re": (2, 2, 128, 16, 64), "s5_c_im": (2, 2, 128, 16, 64), "s5_d": (2, 2048),
    "s5_glu_w": (2, 2048, 4096), "gdn_w_in": (1, 2048, 12416), "gdn_conv_w": (1, 5, 8192),
    "gdn_a_log": (1, 2, 32), "gdn_dt_bias": (1, 2, 32), "gdn_norm_w": (1, 128), "gdn_w_out": (1, 4096, 2048),
    "ret_w_in": (1, 2048, 12288), "ret_log_decay": (1, 2, 8), "ret_gn_w": (1, 512), "ret_w_out": (1, 4096, 2048),
}


def host_constants():
    c = {}
    c["ident"] = np.eye(128, dtype=np.float32)
    c["ones"] = np.ones((128, 128), dtype=np.float32)
    R = np.zeros((128, 128), np.float32)
    for m in range(64):
        R[m, m + 64] = -1.0
        R[m + 64, m] = 1.0
    c["rrotT"] = np.ascontiguousarray(R.T)
    half = 64
    inv_freq = (10000.0 ** (-np.arange(half, dtype=np.float32) / half)).astype(np.float32)
    tok = np.arange(SEQ)
    row = (tok // 64).astype(np.float32)
    col = (tok % 64).astype(np.float32)
    fr = np.concatenate([inv_freq, inv_freq])
    ar = (row[None, :] * fr[:, None]).astype(np.float32)
    ac = (col[None, :] * fr[:, None]).astype(np.float32)
    c["rope"] = np.stack([np.cos(ar), np.sin(ar), np.cos(ac), np.sin(ac)], axis=1).astype(np.float32)
    jj = np.arange(128)[:, None]
    ii = np.arange(128)[None, :]
    c["relF"] = np.maximum(ii - jj, 0).astype(np.float32)
    c["maskF"] = (ii >= jj).astype(np.float32)
    c["relB"] = np.maximum(jj - ii, 0).astype(np.float32)
    c["maskB"] = (jj >= ii).astype(np.float32)
    c["posrowF"] = np.broadcast_to((ii + 1.0), (128, 128)).astype(np.float32).copy()
    c["posrowB"] = np.broadcast_to((128.0 - ii), (128, 128)).astype(np.float32).copy()
    pc = np.zeros((128, 4), np.float32)
    pc[:, 0] = 127.0 - np.arange(128)
    pc[:, 1] = np.arange(128)
    c["poscol"] = pc
    r_ = np.arange(128)[:, None]
    c_ = np.arange(128)[None, :]
    c["LI"] = (r_ >= c_).astype(np.float32)
    c["LS"] = (r_ > c_).astype(np.float32)
    c["UI"] = (r_ <= c_).astype(np.float32)
    c["US"] = (r_ < c_).astype(np.float32)
    idx = np.arange(128)
    c["blk16"] = ((idx[:, None] // 16) == (idx[None, :] // 16)).astype(np.float32)
    for bs in (16, 32, 64):
        c["lvl%d" % bs] = (((idx[:, None] // (2 * bs)) == (idx[None, :] // (2 * bs)))
                           & ((idx[:, None] // bs) != (idx[None, :] // bs))).astype(np.float32)
    rr = np.arange(128)
    s_of = rr // 16
    c["s5maskF"] = (s_of[None, :] >= s_of[:, None]).astype(np.float32)
    c["s5maskB"] = (s_of[:, None] >= s_of[None, :]).astype(np.float32)
    c["iota288"] = np.broadcast_to(np.arange(288, dtype=np.float32), (128, 288)).copy()
    sel16 = np.zeros((16, 128), np.float32)
    for r in range(128):
        sel16[r % 16, r] = 1.0
    c["s5sel16"] = sel16
    selT = np.zeros((128, 64, 128), np.float32)
    for g8 in range(8):
        for t in range(8):
            for cc in range(16):
                selT[t * 16 + cc, g8 * 8 + t, g8 * 16 + cc] = 1.0
    c["s5selT"] = selT
    return c


class Model:
    def __init__(self, nc, stack, cfg):
        self.cfg = cfg
        P = self.P = Prog(nc, stack)
        self.x_in = P.dram("x", [BPC, SEQ, D], F32, kind="ExternalInput")
        self.ctx_in = P.dram("ctx", [BPC, CTXL, D], F32, kind="ExternalInput")
        self.cvec = P.dram("cvec", [NM, D], F32, kind="ExternalInput")
        self.W = {}
        for nm in WEIGHT_NAMES:
            kind = "ExternalInput"
            if cfg.get("internal_weights") and nm not in cfg.get("keep_weights", []):
                self.W[nm] = P.dram(nm, [2, 2], F32, kind="Internal")
                continue
            self.W[nm] = P.dram(nm, list(WEIGHT_SHAPES[nm]), F32, kind=kind)
        self.C = {}
        for nm, arr in host_constants().items():
            self.C[nm] = P.dram("k_" + nm, list(arr.shape), F32, kind="ExternalInput")
        self.out = P.dram("out", [BPC, SEQ, D], F32, kind="ExternalOutput")
        self.ctxs = P.dram("ctxs", [BPC, CTXL, D], F32)
        self.rtok = [[Tok("r%d_%d" % (b, t)) for t in range(NT)] for b in range(BPC)]
        self.ident = P.sbuf("ident", [128, 128], F32)
        self.ones = P.sbuf("ones", [128, 128], F32)
        self.identb = P.sbuf("identb", [128, 128], BF16)
        self.modT = P.sbuf("modT", [128, DEPTH, 96, NM], F32)
        self.nwT = P.sbuf("nwT", [128, DEPTH * 2 + 1, 16], F32)
        self.A = P.sbuf("Amod", [128, DEPTH, 2, 16, NM], F32)
        P.dma(self.ident[:, :], self.C["ident"][:, :], [self.C["ident"]], [self.ident])
        P.dma(self.ones[:, :], self.C["ones"][:, :], [self.C["ones"]], [self.ones])
        P.copy(self.identb[:, :], self.ident[:, :], [self.ident], [self.identb])

    def res_ap(self, b, t):
        if t < 2:
            return self.ctxs[b, t * 128:(t + 1) * 128, :]
        return self.out[b, (t - 2) * 128:(t - 1) * 128, :]

    def copy_inputs(self):
        P = self.P
        for b in range(BPC):
            P.dma(self.ctxs[b, :, :], self.ctx_in[b, :, :], [self.ctx_in], [self.rtok[b][0], self.rtok[b][1]])
            for q in range(4):
                P.dma(self.out[b, q * 512:(q + 1) * 512, :], self.x_in[b, q * 512:(q + 1) * 512, :], [self.x_in],
                      [self.rtok[b][2 + q * 4 + k] for k in range(4)])

    def modulation(self):
        P = self.P
        with P.phase():
            cs = P.sbuf("cs", [NM, D], F32)
            scT = P.sbuf("scT", [128, 16, NM], F32)
            pst = P.psum("pst", [128, 512], F32)
            P.dma(cs[:, :], self.cvec[:, :], [self.cvec], [cs])
            P.act(cs[:, :], cs[:, :], AF.Silu, [cs], [cs])
            for k in range(16):
                P.transpose(pst[:, k * NM:(k + 1) * NM], cs[0:NM, k * 128:(k + 1) * 128], self.ident[0:NM, 0:NM],
                            [cs, self.ident], [pst])
            P.copy(scT[:, :, :], pst[:, 0:16 * NM].rearrange("p (k m) -> p k m", m=NM), [pst], [scT])
            nwr = P.sbuf("nwr", [128, 128], F32)
            nwr2 = P.sbuf("nwr2", [16, 128], F32)
            P.dma(nwr[:, :], self.W["norm_w"].t.rearrange("l w (c p) -> (l w c) p", p=128),
                  [self.W["norm_w"]], [nwr])
            P.dma(nwr2[:, :], self.W["final_norm_w"].t.rearrange("(c p) -> c p", p=128),
                  [self.W["final_norm_w"]], [nwr2])
            pst2 = P.psum("pst2", [128, 512], F32)
            P.transpose(pst2[:, 0:128], nwr[:, :], self.ident[:, :], [nwr, self.ident], [pst2])
            P.transpose(pst2[:, 128:144], nwr2[:, :], self.ident[0:16, 0:16], [nwr2, self.ident], [pst2])
            P.copy(self.nwT[:, :, :], pst2[:, 0:144].rearrange("p (l c) -> p l c", c=16), [pst2], [self.nwT])
            wp = [P.sbuf("modw%d" % i, [128, 16, 512], F32) for i in range(2)]
            pm = [P.psum("pm%d" % i, [128, 512], F32) for i in range(2)]
            mbr = P.sbuf("mbr", [96, 128], F32)
            mbT = P.sbuf("mbT", [128, 96], F32)
            pb = P.psum("pb", [128, 512], F32)
            for i in self.cfg.get("layers", range(DEPTH)):
                P.dma(mbr[:, :], self.W["mod_b"].t[i].rearrange("(c p) -> c p", p=128), [self.W["mod_b"]], [mbr])
                P.transpose(pb[:, 0:96], mbr[0:96, :], self.ident[0:96, 0:96], [mbr, self.ident], [pb])
                P.copy(mbT[:, :], pb[:, 0:96], [pb], [mbT])
                wv = self.W["mod_w"].t[i].rearrange("(c p) n -> p c n", p=128)
                for pn in range(24):
                    w = wp[pn % 2]
                    P.dma(w[:, :, :], wv[:, :, pn * 512:(pn + 1) * 512], [self.W["mod_w"]], [w])
                    ps = pm[pn % 2]
                    for nb in range(4):
                        for k in range(16):
                            P.matmul(ps[:, nb * NM:(nb + 1) * NM], w[:, k, nb * 128:(nb + 1) * 128], scT[:, k, :],
                                     k == 0, k == 15, [w, scT], [ps])
                    for nb in range(4):
                        P.ts(self.modT[:, i, pn * 4 + nb, :], ps[:, nb * NM:(nb + 1) * NM], mbT[:, pn * 4 + nb:pn * 4 + nb + 1],
                             None, ALU.add, None, [ps, mbT], [self.modT])
                for wh in range(2):
                    sc = self.modT[:, i, (1 + 3 * wh) * 16:(2 + 3 * wh) * 16, :]
                    for m in range(NM):
                        P.stt(self.A[:, i, wh, :, m], self.modT[:, i, (1 + 3 * wh) * 16:(2 + 3 * wh) * 16, m], 1.0,
                              self.nwT[:, i * 2 + wh, :], ALU.add, ALU.mult, [self.modT, self.nwT], [self.A])

    def gate_rows(self, i, wh, m, gtile, ps):
        P = self.P
        base = (2 + 3 * wh) * 16
        for c4 in range(4):
            for cc in range(4):
                c = c4 * 4 + cc
                dg = self._dg[c % 2]
                P.ts(dg[:, :], self.ident[:, :], self.modT[:, i, base + c, m:m + 1], None, ALU.mult, None,
                     [self.ident, self.modT], [dg])
                P.matmul(ps[:, cc * 128:(cc + 1) * 128], self.ones[:, :], dg[:, :], True, True, [self.ones, dg], [ps])
            P.copy(gtile[:, c4 * 512:(c4 + 1) * 512], ps[:, :], [ps], [gtile], q="act")

    def norm_tiles(self, i, wh, b, tiles, hT, hT_off, bufs, deint=False):
        P = self.P
        xt, xn, ss, pss = bufs
        for k, t in enumerate(tiles):
            m = BPC if t < 2 else b
            X = xt[k % 2]
            XN = xn[k % len(xn)]
            S = ss[k % 2]
            P.dma(X[:, :], self.res_ap(b, t), [self.rtok[b][t]], [X])
            P.memset(S[:, 0:1], 0.0, [S])
            P.act(XN[:, :], X[:, :], AF.Square, [X], [XN, S], accum_out=S[:, 0:1])
            P.ts(S[:, 1:2], S[:, 0:1], 1.0 / D, EPS, ALU.mult, ALU.add, [S], [S])
            P.act(S[:, 2:3], S[:, 1:2], AF.Sqrt, [S], [S])
            P.recip(S[:, 3:4], S[:, 2:3], [S], [S])
            P.act(XN[:, :], X[:, :], AF.Copy, [X, S], [XN], scale=S[:, 3:4])
            for c4 in range(4):
                ps = pss[(k * 4 + c4) % len(pss)]
                for cc in range(4):
                    c = c4 * 4 + cc
                    P.transpose(ps[:, cc * 128:(cc + 1) * 128], XN[:, c * 128:(c + 1) * 128], self.ident[:, :],
                                [XN, self.ident], [ps])
                for cc in range(4):
                    c = c4 * 4 + cc
                    if deint:
                        o = hT[:, c, :, k * 16:(k + 1) * 16]
                        pin = ps[:, cc * 128:(cc + 1) * 128].rearrange("p (j s) -> p s j", s=8)
                    else:
                        o = hT[:, c, hT_off + k * 128:hT_off + (k + 1) * 128]
                        pin = ps[:, cc * 128:(cc + 1) * 128]
                    P.act(o, pin, AF.Identity, [ps, self.A, self.modT], [hT],
                          scale=self.A[:, i, wh, c, m:m + 1], bias=self.modT[:, i, (3 * wh) * 16 + c, m:m + 1])

    def ffn(self, i):
        P = self.P
        TBT = 6
        NTOK = TBT * 128
        with P.phase():
            self._dg = [P.sbuf("dg%d" % k, [128, 128], F32) for k in range(2)]
            hT = P.sbuf("f_hT", [128, 16, NTOK], BF16)
            yacc = P.sbuf("f_yacc", [128, TBT, D], F32)
            uT = [P.sbuf("f_uT%d" % k, [128, 4, NTOK], BF16) for k in range(2)]
            w1p = [P.sbuf("f_w1p%d" % k, [128, 16, 512], BF16) for k in range(2)]
            w2p = [P.sbuf("f_w2p%d" % k, [128, 4, D], BF16) for k in range(2)]
            rl = [P.sbuf("f_rl%d" % k, [128, NTOK], F32) for k in range(1)]
            G = [P.sbuf("f_G%d" % k, [128, D], F32) for k in range(2)]
            xt = [P.sbuf("f_xt%d" % k, [128, D], F32) for k in range(2)]
            xn = [P.sbuf("f_xn%d" % k, [128, D], F32) for k in range(1)]
            ss = [P.sbuf("f_ss%d" % k, [128, 4], F32) for k in range(2)]
            psu = [P.psum("f_psu%d" % k, [128, 1024], F32) for k in range(2)]
            psd = [P.psum("f_psd%d" % k, [128, 512], F32) for k in range(4)]
            w1v = self.W["ffn_w1"].t[i].rearrange("(c p) n -> p c n", p=128)
            w2v = self.W["ffn_w2"].t[i].rearrange("(c p) n -> p c n", p=128)
            for b in range(BPC):
                self.gate_rows(i, 1, b, G[0], psd[0])
                self.gate_rows(i, 1, BPC, G[1], psd[1])
                for blk in range(NT // TBT):
                    tiles = list(range(blk * TBT, (blk + 1) * TBT))
                    self.norm_tiles(i, 1, b, tiles, hT, 0, (xt, xn, ss, psd))
                    for hb in range(DFF // 512):
                        W1 = w1p[hb % 2]
                        W2 = w2p[hb % 2]
                        U = uT[hb % 2]
                        P.dma(W1[:, :, :], w1v[:, :, hb * 512:(hb + 1) * 512], [self.W["ffn_w1"]], [W1], q="pool")
                        P.dma(W2[:, :, :], w2v[:, hb * 4:(hb + 1) * 4, :], [self.W["ffn_w2"]], [W2], q="pool")
                        for nb in range(4):
                            ps = psu[nb % 2]
                            R = rl[0]
                            for k in range(16):
                                P.matmul(ps[:, 0:512], W1[:, k, nb * 128:(nb + 1) * 128], hT[:, k, 0:512], k == 0, k == 15,
                                         [W1, hT], [ps])
                            for k in range(16):
                                P.matmul(ps[:, 512:512 + NTOK - 512], W1[:, k, nb * 128:(nb + 1) * 128], hT[:, k, 512:NTOK],
                                         k == 0, k == 15, [W1, hT], [ps])
                            P.act(R[:, :], ps[:, 0:NTOK], AF.Relu, [ps], [R])
                            P.tt(U[:, nb, :], R[:, :], R[:, :], ALU.mult, [R], [U])
                        for k, t in enumerate(tiles):
                            for db in range(4):
                                ps = psd[(k * 4 + db) % 4]
                                for kc in range(4):
                                    P.matmul(ps[:, :], U[:, kc, k * 128:(k + 1) * 128], W2[:, kc, db * 512:(db + 1) * 512],
                                             kc == 0, kc == 3, [U, W2], [ps])
                                ya = yacc[:, k, db * 512:(db + 1) * 512]
                                if hb == 0:
                                    P.copy(ya, ps[:, :], [ps], [yacc])
                                else:
                                    P.tt(ya, ps[:, :], ya, ALU.add, [ps, yacc], [yacc])
                    for k, t in enumerate(tiles):
                        X = xt[k % 2]
                        g = G[1] if t < 2 else G[0]
                        P.dma(X[:, :], self.res_ap(b, t), [self.rtok[b][t]], [X])
                        P.tt(yacc[:, k, :], yacc[:, k, :], g[:, :], ALU.mult, [yacc, g], [yacc])
                        P.tt(X[:, :], X[:, :], yacc[:, k, :], ALU.add, [X, yacc], [X])
                        P.dma(self.res_ap(b, t), X[:, :], [X], [self.rtok[b][t]])

    def final_norm(self):
        P = self.P
        with P.phase():
            xt = [P.sbuf("n_xt%d" % k, [128, D], F32) for k in range(2)]
            xn = [P.sbuf("n_xn%d" % k, [128, D], F32) for k in range(2)]
            ss = [P.sbuf("n_ss%d" % k, [128, 4], F32) for k in range(2)]
            wrow = P.sbuf("n_wrow", [128, D], F32)
            ps = P.psum("n_ps", [128, 512], F32)
            dg = [P.sbuf("n_dg%d" % k, [128, 128], F32) for k in range(2)]
            for c4 in range(4):
                for cc in range(4):
                    c = c4 * 4 + cc
                    P.ts(dg[c % 2][:, :], self.ident[:, :], self.nwT[:, 2 * DEPTH, c:c + 1], None, ALU.mult, None,
                         [self.ident, self.nwT], [dg[c % 2]])
                    P.matmul(ps[:, cc * 128:(cc + 1) * 128], self.ones[:, :], dg[c % 2][:, :], True, True,
                             [self.ones, dg[c % 2]], [ps])
                P.copy(wrow[:, c4 * 512:(c4 + 1) * 512], ps[:, :], [ps], [wrow])
            k = 0
            for b in range(BPC):
                for t in range(2, NT):
                    X = xt[k % 2]
                    XN = xn[k % 2]
                    S = ss[k % 2]
                    k += 1
                    P.dma(X[:, :], self.res_ap(b, t), [self.rtok[b][t]], [X])
                    P.memset(S[:, 0:1], 0.0, [S])
                    P.act(XN[:, :], X[:, :], AF.Square, [X], [XN, S], accum_out=S[:, 0:1])
                    P.ts(S[:, 1:2], S[:, 0:1], 1.0 / D, EPS, ALU.mult, ALU.add, [S], [S])
                    P.act(S[:, 2:3], S[:, 1:2], AF.Sqrt, [S], [S])
                    P.recip(S[:, 3:4], S[:, 2:3], [S], [S])
                    P.stt(XN[:, :], X[:, :], S[:, 3:4], wrow[:, :], ALU.mult, ALU.mult, [X, S, wrow], [XN])
                    P.dma(self.res_ap(b, t), XN[:, :], [XN], [self.rtok[b][t]])


    def scratch(self, name, shape, dtype):
        if not hasattr(self, "_scr"):
            self._scr = {}
        if name not in self._scr:
            self._scr[name] = self.P.dram(name, shape, dtype)
        return self._scr[name]

    def norm_to_dram(self, i, wh, b):
        P = self.P
        hT_d = self.scratch("hT_d", [128, 16, TOK], BF16)
        with P.phase():
            hT = P.sbuf("nd_hT", [128, 16, TOK], BF16)
            xt = [P.sbuf("nd_xt%d" % k, [128, D], F32) for k in range(2)]
            xn = [P.sbuf("nd_xn%d" % k, [128, D], F32) for k in range(2)]
            ss = [P.sbuf("nd_ss%d" % k, [128, 4], F32) for k in range(2)]
            pss = [P.psum("nd_ps%d" % k, [128, 512], F32) for k in range(4)]
            self.norm_tiles(i, wh, b, list(range(NT)), hT, 0, (xt, xn, ss, pss))
            P.dma(hT_d[:, :, :], hT[:, :, :], [hT], [hT_d])
        return hT_d

    def outproj_residual(self, i, b, aT_d, wout):
        P = self.P
        TBT = 6
        NTB = TBT * 128
        with P.phase():
            self._dg = [P.sbuf("dg%d" % k, [128, 128], F32) for k in range(2)]
            aT = P.sbuf("op_aT", [128, 32, NTB], BF16)
            yacc = P.sbuf("op_y", [128, TBT, D], F32)
            wp = [P.sbuf("op_wp%d" % k, [128, 32, 256], BF16) for k in range(2)]
            G = [P.sbuf("op_G%d" % k, [128, D], F32) for k in range(2)]
            xt = [P.sbuf("op_xt%d" % k, [128, D], F32) for k in range(2)]
            ps = [P.psum("op_ps%d" % k, [128, 512], F32) for k in range(4)]
            wv = wout.rearrange("(c p) n -> p c n", p=128)
            av = aT_d.t.rearrange("(c p) t -> p c t", p=128)
            self.gate_rows(i, 0, b, G[0], ps[0])
            self.gate_rows(i, 0, BPC, G[1], ps[1])
            cnt = 0
            for blk in range(NT // TBT):
                tiles = list(range(blk * TBT, (blk + 1) * TBT))
                P.dma(aT[:, :, :], av[:, :, blk * NTB:(blk + 1) * NTB], [aT_d], [aT])
                for pn in range(8):
                    Wp = wp[pn % 2]
                    P.dma(Wp[:, :, :], wv[:, :, pn * 256:(pn + 1) * 256], [], [Wp], q="pool")
                    for k, t in enumerate(tiles):
                        pp = ps[cnt % 4]
                        cnt += 1
                        for kc in range(32):
                            P.matmul(pp[:, 0:256], aT[:, kc, k * 128:(k + 1) * 128], Wp[:, kc, :], kc == 0, kc == 31,
                                     [aT, Wp], [pp])
                        P.copy(yacc[:, k, pn * 256:(pn + 1) * 256], pp[:, 0:256], [pp], [yacc],
                               q="act" if cnt % 2 else "dve")
                for k, t in enumerate(tiles):
                    X = xt[k % 2]
                    g = G[1] if t < 2 else G[0]
                    P.dma(X[:, :], self.res_ap(b, t), [self.rtok[b][t]], [X])
                    P.tt(yacc[:, k, :], yacc[:, k, :], g[:, :], ALU.mult, [yacc, g], [yacc])
                    P.tt(X[:, :], X[:, :], yacc[:, k, :], ALU.add, [X, yacc], [X])
                    P.dma(self.res_ap(b, t), X[:, :], [X], [self.rtok[b][t]])

    def mixer(self, i):
        kind, j = i % 3, i // 3
        for b in range(BPC):
            if kind == 2:
                self.norm_to_dram(i, 0, b)
                self.ret_proj(i, j, b)
                self.ret_core(i, j, b)
                self.outproj_residual(i, b, self.scratch("r_ogT", [4096, TOK], BF16), self.W["ret_w_out"].t[j])
            elif kind == 0:
                self.s5(i, j, b)
            else:
                self.gdn(i, j, b)


    def sincos(self, x, t, r, sin_out, cos_out, halfpi, rd, tk):
        P = self.P
        TWO_PI = 2.0 * math.pi
        MAGIC = 12582912.0
        P.ts(t, x, 1.0 / TWO_PI, MAGIC, ALU.mult, ALU.add, rd, [tk["t"]])
        P.ts(t, t, -MAGIC, None, ALU.add, None, [tk["t"]], [tk["t"]])
        P.stt(r, t, -TWO_PI, x, ALU.mult, ALU.add, [tk["t"]] + rd, [tk["r"]])
        P.act(sin_out, r, AF.Sin, [tk["r"]], [tk["sin"]])
        P.stt(t, r, -1.0, r, ALU.mult, ALU.max, [tk["r"]], [tk["t"]])
        P.act(cos_out, t, AF.Sin, [tk["t"], halfpi], [tk["cos"]], scale=-1.0, bias=halfpi[:, 0:1])

    def s5_prep(self, i, j):
        P = self.P
        PI = math.pi
        s5F = [self.scratch("s5F%d" % d, [128, 16384], F32) for d in range(2)]
        s5FF = [self.scratch("s5FF%d" % d, [128, 16384], F32) for d in range(2)]
        s5EN = [self.scratch("s5EN%d" % d, [128, 16384], F32) for d in range(2)]
        s5E = [self.scratch("s5E%d" % d, [128, 16384], F32) for d in range(2)]
        s5M = self.scratch("s5M", [128, 16384], F32)
        if not hasattr(self, "s5cols"):
            self.s5cols = P.sbuf("s5cols", [128, 2, 2, 128], F32)
        with P.phase():
            big = P.sbuf("sp_big", [128, 16384], F32)
            PW = P.sbuf("sp_PW", [128, 16, 2, 64], F32)
            bre = P.sbuf("sp_bre", [128, 64, 16], F32)
            bim = P.sbuf("sp_bim", [128, 64, 16], F32)
            Btr = P.sbuf("sp_Btr", [128, 64, 16], F32)
            Bti = P.sbuf("sp_Bti", [128, 64, 16], F32)
            cre = P.sbuf("sp_cre", [128, 16, 64], F32)
            cim = P.sbuf("sp_cim", [128, 16, 64], F32)
            tA = [P.sbuf("sp_t%d" % k, [128, 1024], F32) for k in range(4)]
            sm = P.sbuf("sp_sm", [128, 16, 64], F32)
            dtc = P.sbuf("sp_dt", [128, 4], F32)
            dup = P.sbuf("sp_dup", [128, 128], F32)
            negpi = P.sbuf("sp_negpi", [128, 1], F32)
            P.memset(negpi[:, :], 0.5 * PI, [negpi])
            pst = P.psum("sp_pst", [128, 512], F32)
            X1, Y1, T, MAG, SN, CS, U2, V2, DEN, XR = [sm[:, k, :] for k in range(10)]
            XK, RR = sm[:, 14, :], sm[:, 15, :]
            for d in range(2):
                P.dma(sm[:, 10, :], self.W["s5_lam_re"].t[j, d], [], [sm])
                P.dma(sm[:, 11, :], self.W["s5_lam_im"].t[j, d], [], [sm])
                P.dma(dtc[:, 0:1], self.W["s5_log_dt"].t[j, d].rearrange("(g o) -> g o", o=1), [], [dtc])
                P.dma(bre[:, :, :], self.W["s5_b_re"].t[j, d], [], [bre])
                P.dma(bim[:, :, :], self.W["s5_b_im"].t[j, d], [], [bim])
                P.dma(cre[:, :, :], self.W["s5_c_re"].t[j, d], [], [cre])
                P.dma(cim[:, :, :], self.W["s5_c_im"].t[j, d], [], [cim])
                LRE, LIM = sm[:, 10, :], sm[:, 11, :]
                P.ts(LRE, LRE, -1e-4, None, ALU.min, None, [sm], [sm])
                P.act(dtc[:, 1:2], dtc[:, 0:1], AF.Exp, [dtc], [dtc])
                P.ts(X1, LRE, dtc[:, 1:2], None, ALU.mult, None, [sm, dtc], [sm])
                P.ts(Y1, LIM, dtc[:, 1:2], None, ALU.mult, None, [sm, dtc], [sm])
                P.copy(dup[:, 0:64], Y1, [sm], [dup])
                P.copy(dup[:, 64:128], Y1, [sm], [dup])
                P.transpose(pst[:, 0:128], dup[:, :], self.ident[:, :], [dup, self.ident], [pst])
                P.ts(self.s5cols[:, d, 0, :], pst[:, 0:128], 8.0, None, ALU.mult, None, [pst], [self.s5cols])
                P.copy(dup[:, 0:64], X1, [sm], [dup])
                P.copy(dup[:, 64:128], X1, [sm], [dup])
                P.transpose(pst[:, 128:256], dup[:, :], self.ident[:, :], [dup, self.ident], [pst])
                P.act(self.s5cols[:, d, 1, :], pst[:, 128:256], AF.Exp, [pst], [self.s5cols], scale=8.0)
                for k in range(0, 9):
                    if k == 0:
                        P.memset(PW[:, 7, 0, :], 1.0, [PW])
                        P.memset(PW[:, 7, 1, :], 0.0, [PW])
                        continue
                    P.ts(XK, Y1, float(k), None, ALU.mult, None, [sm], [sm])
                    self.sincos(XK, T, RR, SN, CS, negpi, [sm], {"t": sm, "r": sm, "sin": sm, "cos": sm})
                    P.act(MAG, X1, AF.Exp, [sm], [sm], scale=float(k))
                    P.tt(PW[:, 7 + k, 0, :], MAG, CS, ALU.mult, [sm], [PW])
                    P.tt(PW[:, 7 + k, 1, :], MAG, SN, ALU.mult, [sm], [PW])
                    if k <= 7:
                        P.act(MAG, X1, AF.Exp, [sm], [sm], scale=-float(k))
                        P.tt(PW[:, 7 - k, 0, :], MAG, CS, ALU.mult, [sm], [PW])
                        P.stt(PW[:, 7 - k, 1, :], MAG, -1.0, SN, ALU.mult, ALU.mult, [sm], [PW])
                P.ts(XR, PW[:, 8, 0, :], -1.0, None, ALU.add, None, [PW], [sm])
                YI = PW[:, 8, 1, :]
                P.tt(U2, LRE, LRE, ALU.mult, [sm], [sm])
                P.tt(V2, LIM, LIM, ALU.mult, [sm], [sm])
                P.tt(DEN, U2, V2, ALU.add, [sm], [sm])
                P.recip(DEN, DEN, [sm], [sm])
                BSR, BSI = sm[:, 12, :], sm[:, 13, :]
                P.tt(U2, XR, LRE, ALU.mult, [sm], [sm])
                P.tt(V2, YI, LIM, ALU.mult, [sm, PW], [sm])
                P.tt(U2, U2, V2, ALU.add, [sm], [sm])
                P.tt(BSR, U2, DEN, ALU.mult, [sm], [sm])
                P.tt(U2, YI, LRE, ALU.mult, [sm, PW], [sm])
                P.tt(V2, XR, LIM, ALU.mult, [sm], [sm])
                P.tt(U2, U2, V2, ALU.subtract, [sm], [sm])
                P.tt(BSI, U2, DEN, ALU.mult, [sm], [sm])

                def bc_n_c(ap):
                    return ap.unsqueeze(2).broadcast_to([128, 64, 16])

                def bc_c_n(ap):
                    return ap.unsqueeze(1).broadcast_to([128, 16, 64])

                t3 = [t[:, :].rearrange("g (a b) -> g a b", b=16) for t in tA]
                t3c = [t[:, :].rearrange("g (a b) -> g a b", b=64) for t in tA]

                def cprod(o_re, o_im, a_re, a_im, m_re, m_im, tv, neg_im, rds):
                    P.tt(tv[0], a_re, m_re, ALU.mult, rds, [tA[0]])
                    P.tt(tv[1], a_im, m_im, ALU.mult, rds, [tA[1]])
                    P.tt(o_re, tv[0], tv[1], ALU.subtract, [tA[0], tA[1]], [big])
                    P.tt(tv[2], a_re, m_im, ALU.mult, rds, [tA[2]])
                    P.tt(tv[3], a_im, m_re, ALU.mult, rds, [tA[3]])
                    if neg_im:
                        P.stt(o_im, tv[2], -1.0, tv[3], ALU.mult, ALU.subtract, [tA[2], tA[3]], [big])
                    else:
                        P.tt(o_im, tv[2], tv[3], ALU.add, [tA[2], tA[3]], [big])

                P.tt(t3[0], bc_n_c(BSR), bre[:, :, :], ALU.mult, [sm, bre], [tA[0]])
                P.tt(t3[1], bc_n_c(BSI), bim[:, :, :], ALU.mult, [sm, bim], [tA[1]])
                P.tt(Btr[:, :, :], t3[0], t3[1], ALU.subtract, [tA[0], tA[1]], [Btr])
                P.tt(t3[2], bc_n_c(BSR), bim[:, :, :], ALU.mult, [sm, bim], [tA[2]])
                P.tt(t3[3], bc_n_c(BSI), bre[:, :, :], ALU.mult, [sm, bre], [tA[3]])
                P.tt(Bti[:, :, :], t3[2], t3[3], ALU.add, [tA[2], tA[3]], [Bti])

                v1 = big[:, :].rearrange("g (s c r n) -> g s c r n", s=8, c=16, r=2, n=64)
                v2 = big[:, :].rearrange("g (r n s c) -> g r n s c", s=8, c=16, r=2, n=64)
                for layout, dst in ((1, s5F[d]), (2, s5FF[d])):
                    for sidx in range(8):
                        k = (7 - sidx) if d == 0 else sidx
                        if layout == 1:
                            o_re = v1[:, sidx, :, 0, :].rearrange("g c n -> g n c")
                            o_im = v1[:, sidx, :, 1, :].rearrange("g c n -> g n c")
                        else:
                            o_re = v2[:, 0, :, sidx, :]
                            o_im = v2[:, 1, :, sidx, :]
                        cprod(o_re, o_im, bc_n_c(PW[:, 7 + k, 0, :]), bc_n_c(PW[:, 7 + k, 1, :]), Btr[:, :, :], Bti[:, :, :],
                              t3, False, [PW, Btr, Bti])
                    P.dma(dst[:, :], big[:, :], [big], [dst])
                for layout, dst in ((3, s5EN[d]), (4, s5E[d])):
                    for tidx in range(8):
                        if layout == 3:
                            k = (tidx - 7) if d == 0 else (-tidx)
                        else:
                            k = (tidx + 1) if d == 0 else (8 - tidx)
                        o_re = v2[:, 0, :, tidx, :].rearrange("g n c -> g c n")
                        o_im = v2[:, 1, :, tidx, :].rearrange("g n c -> g c n")
                        cprod(o_re, o_im, bc_c_n(PW[:, 7 + k, 0, :]), bc_c_n(PW[:, 7 + k, 1, :]), cre[:, :, :], cim[:, :, :],
                              t3c, True, [PW, cre, cim])
                    P.dma(dst[:, :], big[:, :], [big], [dst])
        with P.phase():
            maskF = P.sbuf("sm_maskF", [128, 128], F32)
            maskB = P.sbuf("sm_maskB", [128, 128], F32)
            P.dma(maskF[:, :], self.C["s5maskF"][:, :], [], [maskF])
            P.dma(maskB[:, :], self.C["s5maskB"][:, :], [], [maskB])
            sel16 = P.sbuf("sm_sel16", [16, 128], F32)
            P.dma(sel16[:, :], self.C["s5sel16"][:, :], [], [sel16])
            Dg = P.sbuf("sm_Dg", [128, 16], F32)
            P.dma(Dg[:, :], self.W["s5_d"].t[j].rearrange("(g c) -> g c", c=16), [], [Dg])
            DgT = P.sbuf("sm_DgT", [16, 128], F32)
            Dall = P.sbuf("sm_Dall", [128, 128], F32)
            ps0 = P.psum("sm_ps0", [128, 512], F32)
            P.transpose(ps0[0:16, 0:128], Dg[:, :], self.ident[:, :], [Dg, self.ident], [ps0])
            P.copy(DgT[:, :], ps0[0:16, 0:128], [ps0], [DgT])
            P.matmul(ps0[:, 128:256], sel16[:, :], DgT[:, :], True, True, [sel16, DgT], [ps0])
            P.copy(Dall[:, :], ps0[:, 128:256], [ps0], [Dall])
            GB = 8
            ld = [[P.sbuf("sm_ld%d_%d" % (k, q), [128, GB, 128], F32) for q in range(4)] for k in range(2)]
            Mst = [P.sbuf("sm_Mst%d" % k, [128, GB, 128], F32) for k in range(2)]
            T1 = [P.sbuf("sm_T1%d" % k, [128, 128], F32) for k in range(2)]
            T2 = [P.sbuf("sm_T2%d" % k, [128, 128], F32) for k in range(2)]
            psf = [P.psum("sm_psf%d" % k, [128, 512], F32) for k in range(2)]
            psb = [P.psum("sm_psb%d" % k, [128, 512], F32) for k in range(2)]
            srcs = [s5FF[0], s5EN[0], s5FF[1], s5EN[1]]
            for bt in range(128 // GB):
                L = ld[bt % 2]
                for q in range(4):
                    P.dma(L[q][:, :, :], srcs[q].t[bt * GB:(bt + 1) * GB, :].rearrange("g (r c) -> r g c", c=128),
                          [srcs[q]], [L[q]])
                MS = Mst[bt % 2]
                for gi in range(GB):
                    g = bt * GB + gi
                    pf, pb = psf[gi % 2], psb[gi % 2]
                    P.matmul(pf[:, 0:128], L[0][:, gi, :], L[1][:, gi, :], True, True, [L[0], L[1]], [pf])
                    P.matmul(pb[:, 0:128], L[2][:, gi, :], L[3][:, gi, :], True, True, [L[2], L[3]], [pb])
                    a, c2 = T1[gi % 2], T2[gi % 2]
                    P.tt(a[:, :], pf[:, 0:128], maskF[:, :], ALU.mult, [pf, maskF], [a])
                    P.tt(c2[:, :], pb[:, 0:128], maskB[:, :], ALU.mult, [pb, maskB], [c2])
                    P.tt(a[:, :], a[:, :], c2[:, :], ALU.add, [a, c2], [a])
                    P.stt(MS[:, gi, :], self.ident[:, :], Dall[:, g:g + 1], a[:, :], ALU.mult, ALU.add,
                          [self.ident, Dall, a], [MS])
                P.dma(s5M.t[bt * GB:(bt + 1) * GB, :].rearrange("g (r c) -> r g c", c=128), MS[:, :, :], [MS], [s5M])

    def s5(self, i, j, b):
        P = self.P
        PI = math.pi
        if b == 0:
            self.s5_prep(i, j)
        s5F = [self.scratch("s5F%d" % d, [128, 16384], F32) for d in range(2)]
        s5E = [self.scratch("s5E%d" % d, [128, 16384], F32) for d in range(2)]
        s5M = self.scratch("s5M", [128, 16384], F32)
        U_d = self.scratch("s5U", [128, 8, 16, 288], BF16)
        zT_d = self.scratch("hT_d", [128, 16, TOK], BF16)
        with P.phase():
            hTd = P.sbuf("s1_hTd", [128, 16, 8, 288], BF16)
            xt = [P.sbuf("s1_xt%d" % k, [128, D], F32) for k in range(2)]
            xn = [P.sbuf("s1_xn%d" % k, [128, D], F32) for k in range(2)]
            ss = [P.sbuf("s1_ss%d" % k, [128, 4], F32) for k in range(2)]
            pss = [P.psum("s1_ps%d" % k, [128, 512], F32) for k in range(4)]
            self.norm_tiles(i, 0, b, list(range(NT)), hTd, 0, (xt, xn, ss, pss), deint=True)
            for fc in range(16):
                for g8 in range(8):
                    g = fc * 8 + g8
                    P.dma(U_d[g].rearrange("s c j -> c s j"), hTd[g8 * 16:(g8 + 1) * 16, fc, :, :], [hTd], [U_d])
        with P.phase():
            iota = P.sbuf("s2_iota", [128, 288], F32)
            P.dma(iota[:, :], self.C["iota288"][:, :], [], [iota])
            negpi = P.sbuf("s2_negpi", [128, 1], F32)
            P.memset(negpi[:, :], 0.5 * PI, [negpi])
            selT = P.sbuf("s2_selT", [128, 64, 128], BF16)
            P.dma(selT[:, :, :], self.C["s5selT"][:, :, :], [], [selT], q="pool")
            Ub = [P.sbuf("s2_Ub%d" % k, [128, 8, 288], BF16) for k in range(2)]
            Wm = [P.sbuf("s2_Wm%d" % k, [128, 8, 128], BF16) for k in range(2)]
            WF = [[P.sbuf("s2_WF%d_%d" % (k, d), [128, 8, 128], BF16) for d in range(2)] for k in range(2)]
            WE = [[P.sbuf("s2_WE%d_%d" % (k, d), [128, 8, 128], BF16) for d in range(2)] for k in range(2)]
            ANG = P.sbuf("s2_ANG", [128, 8, 288], F32)
            ANG2 = P.sbuf("s2_ANG2", [128, 8, 288], F32)
            SINT = P.sbuf("s2_SINT", [128, 8, 288], F32)
            COST = P.sbuf("s2_COST", [128, 8, 288], F32)
            V = P.sbuf("s2_V", [128, 8, 288], F32)
            T1 = P.sbuf("s2_T1", [128, 8, 288], F32)
            T2 = P.sbuf("s2_T2", [128, 8, 288], F32)
            Gt = P.sbuf("s2_Gt", [128, 8, 288], F32)
            RT = P.sbuf("s2_RT", [128, 8, 288], F32)
            Zp = [P.sbuf("s2_Zp%d" % d, [128, 8, 288], BF16) for d in range(2)]
            Yg = P.sbuf("s2_Yg", [128, 8, 288], BF16)
            zfc = [P.sbuf("s2_zfc%d" % k, [128, TOK], BF16) for k in range(2)]
            gt = [P.sbuf("s2_gt%d" % k, [128, 288], F32) for k in range(3)]
            pv = [P.psum("s2_pv%d" % k, [128, 512], F32) for k in range(2)]
            py = [P.psum("s2_py%d" % k, [128, 512], F32) for k in range(2)]
            pt = [P.psum("s2_pt%d" % k, [128, 512], F32) for k in range(2)]
            for fc in range(16):
                UB, WM = Ub[fc % 2], Wm[fc % 2]
                P.dma(UB[:, :, :], U_d[fc * 8:(fc + 1) * 8].rearrange("g s c j -> (s c) g j"), [U_d], [UB])
                P.dma(WM[:, :, :], s5M.t[fc * 8:(fc + 1) * 8, :].rearrange("g (r c) -> r g c", c=128), [s5M], [WM], q="pool")
                for d in range(2):
                    P.dma(WF[fc % 2][d][:, :, :], s5F[d].t[fc * 8:(fc + 1) * 8, :].rearrange("g (r c) -> r g c", c=128),
                          [s5F[d]], [WF[fc % 2][d]], q="pool")
                    P.dma(WE[fc % 2][d][:, :, :], s5E[d].t[fc * 8:(fc + 1) * 8, :].rearrange("g (r c) -> r g c", c=128),
                          [s5E[d]], [WE[fc % 2][d]], q="pool")
                for d in range(2):
                    wf, we = WF[fc % 2][d], WE[fc % 2][d]
                    for gi in range(8):
                        g = fc * 8 + gi
                        P.ts(ANG[:, gi, :], iota[:, :], self.s5cols[:, d, 0, g:g + 1], None, ALU.mult, None,
                             [iota, self.s5cols], [ANG])
                        P.ts(RT[:, gi, :], iota[:, :], 0.0, self.s5cols[:, d, 1, g:g + 1], ALU.mult, ALU.add,
                             [iota, self.s5cols], [RT])
                    self.sincos(ANG[:, :, :], ANG2[:, :, :], T1[:, :, :], SINT[:, :, :], COST[:, :, :], negpi, [ANG],
                                {"t": ANG2, "r": T1, "sin": SINT, "cos": COST})
                    P.ts(SINT[64:128, :, :], SINT[64:128, :, :], -1.0, None, ALU.mult, None, [SINT], [SINT])
                    for gi in range(8):
                        p = pv[gi % 2]
                        P.matmul(p[:, 0:288], wf[:, gi, :], UB[:, gi, :], True, True, [wf, UB], [p])
                        if d == 0:
                            P.copy(V[:, gi, :], p[:, 0:288], [p], [V], q="act")
                        else:
                            P.copy(V[:, gi, 0:32], p[:, 0:32][:, ::-1], [p], [V], q="act")
                            P.copy(V[:, gi, 32:288], p[:, 32:288][:, ::-1], [p], [V], q="act")
                    P.tt(T1[:, :, :], V[:, :, :], COST[:, :, :], ALU.mult, [V, COST], [T1])
                    P.copy(T2[0:64, :, :], V[64:128, :, :], [V], [T2])
                    P.copy(T2[64:128, :, :], V[0:64, :, :], [V], [T2])
                    P.tt(T2[:, :, :], T2[:, :, :], SINT[:, :, :], ALU.mult, [T2, SINT], [T2])
                    P.tt(V[:, :, :], T1[:, :, :], T2[:, :, :], ALU.add, [T1, T2], [V])
                    for gi in range(8):
                        P.op("dve", (lambda gi=gi: (lambda e: e.tensor_tensor_scan(Gt[:, gi, :], RT[:, gi, :], V[:, gi, :], 0.0,
                                                                                    ALU.mult, ALU.add)))(), [RT, V], [Gt])
                    P.tt(T1[:, :, :], Gt[:, :, :], COST[:, :, :], ALU.mult, [Gt, COST], [T1])
                    P.copy(T2[0:64, :, :], Gt[64:128, :, :], [Gt], [T2])
                    P.copy(T2[64:128, :, :], Gt[0:64, :, :], [Gt], [T2])
                    P.tt(T2[:, :, :], T2[:, :, :], SINT[:, :, :], ALU.mult, [T2, SINT], [T2])
                    P.tt(T1[:, :, :], T1[:, :, :], T2[:, :, :], ALU.subtract, [T1, T2], [T1])
                    ZP = Zp[d]
                    if d == 0:
                        P.memset(ZP[:, :, 0:1], 0.0, [ZP])
                        P.copy(ZP[:, :, 1:288], T1[:, :, 0:287], [T1], [ZP])
                    else:
                        P.memset(ZP[:, :, 31:32], 0.0, [ZP])
                        P.copy(ZP[:, :, 0:31], T1[:, :, 0:31][:, :, ::-1], [T1], [ZP])
                        P.copy(ZP[:, :, 32:288], T1[:, :, 31:287][:, :, ::-1], [T1], [ZP])
                for gi in range(8):
                    p = py[gi % 2]
                    P.matmul(p[:, 0:288], WM[:, gi, :], UB[:, gi, :], True, False, [WM, UB], [p])
                    P.matmul(p[:, 0:288], WE[fc % 2][0][:, gi, :], Zp[0][:, gi, :], False, False, [WE[fc % 2][0], Zp[0]], [p])
                    P.matmul(p[:, 0:288], WE[fc % 2][1][:, gi, :], Zp[1][:, gi, :], False, True, [WE[fc % 2][1], Zp[1]], [p])
                    P.copy(Yg[:, gi, :], p[:, 0:288], [p], [Yg], q="act")
                Z = zfc[fc % 2]
                zv = Z[:, :].rearrange("p (j s) -> p s j", s=8)
                for t in range(8):
                    p = pt[t % 2]
                    for g8 in range(8):
                        P.matmul(p[:, 0:288], selT[:, g8 * 8 + t, :], Yg[:, g8, :], g8 == 0, g8 == 7, [selT, Yg], [p])
                    P.act(gt[0][:, :], p[:, 0:288], AF.Square, [p], [gt[0]])
                    P.ts(gt[0][:, :], gt[0][:, :], 0.044715, 1.0, ALU.mult, ALU.add, [gt[0]], [gt[0]])
                    P.tt(gt[1][:, :], gt[0][:, :], p[:, 0:288], ALU.mult, [gt[0], p], [gt[1]])
                    P.act(gt[2][:, :], gt[1][:, :], AF.Sigmoid, [gt[1]], [gt[2]], scale=1.5957691216057308)
                    P.tt(zv[:, t, :], gt[2][:, :], p[:, 0:288], ALU.mult, [gt[2], p], [Z])
                P.dma(zT_d[:, fc, :], Z[:, :], [Z], [zT_d])
        self.s5_glu(i, j, b)

    def s5_glu(self, i, j, b):
        P = self.P
        zT_d = self.scratch("hT_d", [128, 16, TOK], BF16)
        TBT = 6
        NTB = TBT * 128
        gw = self.W["s5_glu_w"].t[j].rearrange("(c p) n -> p c n", p=128)
        with P.phase():
            self._dg = [P.sbuf("dg%d" % k, [128, 128], F32) for k in range(2)]
            zT = P.sbuf("sg_zT", [128, 16, NTB], BF16)
            yacc = P.sbuf("sg_y", [128, TBT, D], F32)
            wp = [P.sbuf("sg_wp%d" % k, [128, 16, 512], BF16) for k in range(2)]
            G = [P.sbuf("sg_G%d" % k, [128, D], F32) for k in range(2)]
            xt = [P.sbuf("sg_xt%d" % k, [128, D], F32) for k in range(2)]
            sg = [P.sbuf("sg_sg%d" % k, [128, 256], F32) for k in range(2)]
            ps = [P.psum("sg_ps%d" % k, [128, 512], F32) for k in range(4)]
            self.gate_rows(i, 0, b, G[0], ps[0])
            self.gate_rows(i, 0, BPC, G[1], ps[1])
            cnt = 0
            for blk in range(NT // TBT):
                tiles = list(range(blk * TBT, (blk + 1) * TBT))
                P.dma(zT[:, :, :], zT_d[:, :, blk * NTB:(blk + 1) * NTB], [zT_d], [zT])
                for pn in range(8):
                    Wp = wp[pn % 2]
                    P.dma(Wp[:, :, 0:256], gw[:, :, pn * 256:(pn + 1) * 256], [], [Wp], q="pool")
                    P.dma(Wp[:, :, 256:512], gw[:, :, 2048 + pn * 256:2048 + (pn + 1) * 256], [], [Wp], q="pool")
                    for k, t in enumerate(tiles):
                        pp = ps[cnt % 4]
                        S = sg[cnt % 2]
                        cnt += 1
                        for kc in range(16):
                            P.matmul(pp[:, :], zT[:, kc, k * 128:(k + 1) * 128], Wp[:, kc, :], kc == 0, kc == 15, [zT, Wp], [pp])
                        P.act(S[:, :], pp[:, 256:512], AF.Sigmoid, [pp], [S])
                        P.tt(yacc[:, k, pn * 256:(pn + 1) * 256], pp[:, 0:256], S[:, :], ALU.mult, [pp, S], [yacc])
                for k, t in enumerate(tiles):
                    X = xt[k % 2]
                    g = G[1] if t < 2 else G[0]
                    P.dma(X[:, :], self.res_ap(b, t), [self.rtok[b][t]], [X])
                    P.tt(yacc[:, k, :], yacc[:, k, :], g[:, :], ALU.mult, [yacc, g], [yacc])
                    P.tt(X[:, :], X[:, :], yacc[:, k, :], ALU.add, [X, yacc], [X])
                    P.dma(self.res_ap(b, t), X[:, :], [X], [self.rtok[b][t]])


    def gdn(self, i, j, b):
        self.norm_to_dram(i, 0, b)
        self.gdn_proj(i, j, b)
        if self.cfg.get("gdn_stop") == "proj":
            return
        self.gdn_core(i, j, b)
        self.outproj_residual(i, b, self.scratch("r_ogT", [4096, TOK], BF16), self.W["gdn_w_out"].t[j])

    def gdn_proj(self, i, j, b):
        P = self.P
        hT_d = self.scratch("hT_d", [128, 16, TOK], BF16)
        g_qkvT = self.scratch("g_qkvT", [8192, TOK], BF16)
        g_z = self.scratch("r_gate", [TOK, 4096], BF16)
        g_ab = self.scratch("g_ab", [TOK, 128], F32)
        win = self.W["gdn_w_in"].t[j].rearrange("(c p) n -> p c n", p=128)
        blocks = [(0, 256)] + [(256 + 512 * k, 512) for k in range(4)]
        PADW = 4 + 256 + 4 + 2048 + 4
        with P.phase():
            hT = P.sbuf("gp_hT", [128, 16, TOK], BF16)
            P.dma(hT[:, :, :], hT_d[:, :, :], [hT_d], [hT])
            cwr = P.sbuf("gp_cwr", [5, 2048], F32)
            cw = P.sbuf("gp_cw", [128, 64, 5], F32)
            pcw = P.psum("gp_pcw", [128, 512], F32)
            for c4 in range(4):
                P.dma(cwr[:, :], self.W["gdn_conv_w"].t[j][:, c4 * 2048:(c4 + 1) * 2048], [], [cwr])
                for c in range(16):
                    cc = c4 * 16 + c
                    P.transpose(pcw[:, cc * 5:(cc + 1) * 5], cwr[0:5, c * 128:(c + 1) * 128], self.ident[0:5, 0:5],
                                [cwr, self.ident], [pcw])
            P.copy(cw[:, :, :], pcw[:, 0:320].rearrange("p (c k) -> p c k", k=5), [pcw], [cw])
            wp = [P.sbuf("gp_wp%d" % k, [128, 16, 512], BF16) for k in range(2)]
            pad = [P.sbuf("gp_pad%d" % k, [128, PADW], F32) for k in range(2)]
            acc = [P.sbuf("gp_acc%d" % k, [128, TOK], F32) for k in range(2)]
            sq = [P.sbuf("gp_sq%d" % k, [128, 512], F32) for k in range(2)]
            rin = [P.sbuf("gp_rin%d" % k, [128, 512], F32) for k in range(2)]
            ost = [P.sbuf("gp_ost%d" % k, [128, TOK], BF16) for k in range(2)]
            epsc = P.sbuf("gp_eps", [128, 1], F32)
            P.memset(epsc[:, :], EPS, [epsc])
            for k in range(2):
                P.memset(pad[k][:, :], 0.0, [pad[k]])
            psA = [P.psum("gp_psA%d" % k, [128, 512], F32) for k in range(4)]
            psN = [P.psum("gp_psN%d" % k, [128, 512], F32) for k in range(2)]
            cnt = 0
            for pn in range(16):
                Wp = wp[pn % 2]
                P.dma(Wp[:, :, :], win[:, :, pn * 512:(pn + 1) * 512], [], [Wp], q="pool")
                for nb in range(4):
                    chunk = pn * 4 + nb
                    PD, AC, OS = pad[chunk % 2], acc[chunk % 2], ost[chunk % 2]
                    for bi, (t0, n) in enumerate(blocks):
                        ps = psA[cnt % 4]
                        cnt += 1
                        for k in range(16):
                            P.matmul(ps[:, 0:n], Wp[:, k, nb * 128:(nb + 1) * 128], hT[:, k, t0:t0 + n], k == 0, k == 15,
                                     [Wp, hT], [ps])
                        o0 = 2 + t0 if bi == 0 else 264 + 2 + (t0 - 256)
                        P.copy(PD[:, o0:o0 + n], ps[:, 0:n], [ps], [PD], q="act")
                    for (po, ao, n) in ((0, 0, 256), (264, 256, 2048)):
                        for k in range(5):
                            src = PD[:, po + k:po + k + n]
                            if k == 0:
                                P.ts(AC[:, ao:ao + n], src, cw[:, chunk, 0:1], None, ALU.mult, None, [PD, cw], [AC])
                            else:
                                P.stt(AC[:, ao:ao + n], src, cw[:, chunk, k:k + 1], AC[:, ao:ao + n], ALU.mult, ALU.add,
                                      [PD, cw, AC], [AC])
                    if chunk >= 32:
                        P.act(OS[:, :], AC[:, :], AF.Silu, [AC], [OS])
                    else:
                        P.act(AC[:, :], AC[:, :], AF.Silu, [AC], [AC])
                        qs = (128.0 ** -0.5) if chunk < 16 else 1.0
                        for bi, (t0, n) in enumerate(blocks):
                            SQ, RI = sq[bi % 2], rin[bi % 2]
                            pN = psN[bi % 2]
                            P.tt(SQ[:, 0:n], AC[:, t0:t0 + n], AC[:, t0:t0 + n], ALU.mult, [AC], [SQ])
                            P.matmul(pN[:, 0:n], self.ones[:, :], SQ[:, 0:n], True, True, [self.ones, SQ], [pN])
                            P.act(RI[:, 0:n], pN[:, 0:n], AF.Sqrt, [pN, epsc], [RI], bias=epsc[:, 0:1])
                            P.recip(RI[:, 0:n], RI[:, 0:n], [RI], [RI])
                            P.stt(OS[:, t0:t0 + n], AC[:, t0:t0 + n], qs, RI[:, 0:n], ALU.mult, ALU.mult, [AC, RI], [OS])
                    P.dma(g_qkvT[chunk * 128:(chunk + 1) * 128, :], OS[:, :], [OS], [g_qkvT])
        with P.phase():
            hT = P.sbuf("gz_hT", [128, 16, TOK], BF16)
            P.dma(hT[:, :, :], hT_d[:, :, :], [hT_d], [hT])
            wp = [P.sbuf("gz_wp%d" % k, [128, 16, 512], BF16) for k in range(2)]
            stage = [P.sbuf("gz_st%d" % k, [128, NT, 512], BF16) for k in range(2)]
            psA = [P.psum("gz_psA%d" % k, [128, 512], F32) for k in range(4)]
            cnt = 0
            for pn in range(8):
                Wp = wp[pn % 2]
                ST = stage[pn % 2]
                c0 = 8192 + pn * 512
                P.dma(Wp[:, :, :], win[:, :, c0:c0 + 512], [], [Wp], q="pool")
                for t in range(NT):
                    ps = psA[cnt % 4]
                    cnt += 1
                    for k in range(16):
                        P.matmul(ps[:, :], hT[:, k, t * 128:(t + 1) * 128], Wp[:, k, :], k == 0, k == 15, [hT, Wp], [ps])
                    P.act(ST[:, t, :], ps[:, :], AF.Silu, [ps], [ST])
                P.dma(g_z.t.rearrange("(t p) e -> p t e", p=128)[:, :, pn * 512:(pn + 1) * 512], ST[:, :, :], [ST], [g_z])
            Wab = P.sbuf("gz_wab", [128, 16, 128], BF16)
            P.dma(Wab[:, :, :], win[:, :, 12288:12416], [], [Wab], q="pool")
            par = P.sbuf("gz_par", [128, 2, 64], F32)
            P.dma(par[:, 0, :], self.W["gdn_a_log"].t[j].rearrange("d h -> (d h)").partition_broadcast(128), [], [par])
            P.dma(par[:, 1, :], self.W["gdn_dt_bias"].t[j].rearrange("d h -> (d h)").partition_broadcast(128), [], [par])
            nA = P.sbuf("gz_nA", [128, 2, 32], F32)
            P.act(nA[:, :, :], par[:, 0, :].rearrange("p (d h) -> p d h", d=2), AF.Exp, [par], [nA])
            ab = P.sbuf("gz_ab", [128, NT, 128], F32)
            tmp = [P.sbuf("gz_tmp%d" % k, [128, 2, 32], F32) for k in range(2)]
            for t in range(NT):
                ps = psA[cnt % 4]
                cnt += 1
                for k in range(16):
                    P.matmul(ps[:, 0:128], hT[:, k, t * 128:(t + 1) * 128], Wab[:, k, :], k == 0, k == 15, [hT, Wab], [ps])
                pv = ps[:, 0:128].rearrange("p (d a h) -> p d a h", d=2, a=2)
                av = ab[:, t, :].rearrange("p (d a h) -> p d a h", d=2, a=2)
                T = tmp[t % 2]
                P.tt(T[:, :, :], pv[:, :, 0, :], par[:, 1, :].rearrange("p (d h) -> p d h", d=2), ALU.add, [ps, par], [T])
                P.act(T[:, :, :], T[:, :, :], AF.Exp, [T], [T])
                P.act(T[:, :, :], T[:, :, :], AF.Ln, [T], [T], bias=self.ones[:, 0:1])
                P.stt(av[:, :, 0, :], T[:, :, :], -1.0, nA[:, :, :], ALU.mult, ALU.mult, [T, nA], [ab])
                P.act(av[:, :, 1, :], pv[:, :, 1, :], AF.Sigmoid, [ps], [ab])
            P.dma(g_ab.t.rearrange("(t p) e -> p t e", p=128), ab[:, :, :], [ab], [g_ab])

    def gdn_core(self, i, j, b):
        P = self.P
        g_qkvT = self.scratch("g_qkvT", [8192, TOK], BF16)
        g_z = self.scratch("r_gate", [TOK, 4096], BF16)
        g_ab = self.scratch("g_ab", [TOK, 128], F32)
        r_ogT = self.scratch("r_ogT", [4096, TOK], BF16)

        class Ring:
            def __init__(self, items):
                self.items = items
                self.k = 0

            def next(self):
                it = self.items[self.k % len(self.items)]
                self.k += 1
                return it

        with P.phase():
            def cload(nm):
                t = P.sbuf("gc_" + nm, [128, 128], F32)
                P.dma(t[:, :], self.C[nm][:, :], [], [t])
                return t
            LI, LS, UI, US = cload("LI"), cload("LS"), cload("UI"), cload("US")
            gnw = P.sbuf("gc_gnw", [128, 128], F32)
            P.dma(gnw[:, :], self.W["gdn_norm_w"].t[j].partition_broadcast(128), [], [gnw])
            gb = P.sbuf("gc_gb", [128, NT, 128], F32)
            P.dma(gb[:, :, :], g_ab.t.rearrange("(t p) e -> p t e", p=128), [g_ab], [gb])
            qT = P.sbuf("gc_qT", [128, TOK], BF16)
            kT = P.sbuf("gc_kT", [128, TOK], BF16)
            vT2 = [P.sbuf("gc_vT%d" % k, [128, TOK], BF16) for k in range(2)]
            zh2 = [P.sbuf("gc_zh%d" % k, [128, NT, 128], BF16) for k in range(2)]
            ktm = P.sbuf("gc_ktm", [128, NT, 128], BF16)
            vtm2 = [P.sbuf("gc_vtm%d" % k, [128, NT, 128], BF16) for k in range(2)]
            oacc2 = [P.sbuf("gc_oacc%d" % k, [128, NT, 128], F32) for k in range(2)]
            ogT2 = [P.sbuf("gc_ogT%d" % k, [128, TOK], BF16) for k in range(2)]
            junk = P.sbuf("gc_junk", [128, 128], F32)
            def ring(nm, n, shape, dt):
                return Ring([P.sbuf("gc_%s%d" % (nm, k), shape, dt) for k in range(n)])
            rGm = ring("Gm", 16, [128, 128], F32)
            rcols = ring("cols", 20, [128, 8], F32)
            rDec = ring("Dec", 16, [128, 128], F32)
            rDecT = ring("DecT", 16, [128, 128], F32)
            rsq = ring("sqb", 144, [128, 128], BF16)
            rU = ring("U", 16, [128, 128], F32)
            rAT = ring("ATr", 16, [128, 128], BF16)
            rM = ring("Mr", 12, [128, 128], BF16)
            rN = ring("Nr", 12, [128, 128], BF16)
            def cloadb(nm):
                tf = P.sbuf("gc_f_" + nm, [128, 128], F32)
                P.dma(tf[:, :], self.C[nm][:, :], [], [tf])
                tb = P.sbuf("gc_b_" + nm, [128, 128], BF16)
                P.copy(tb[:, :], tf[:, :], [tf], [tb])
                return tb
            blk16 = cloadb("blk16")
            lvlm = [cloadb("lvl16"), cloadb("lvl32"), cloadb("lvl64")]
            rkd = ring("kdr", 16, [128, 128], BF16)
            rWT = ring("WTr", 16, [128, 128], BF16)
            rtmp = ring("tmp", 16, [128, 128], F32)
            S32 = [P.sbuf("gc_S32_%d" % k, [128, 128], F32) for k in range(4)]
            S16 = [P.sbuf("gc_S16_%d" % k, [128, 128], BF16) for k in range(4)]
            st = ring("st", 4, [128, 8], F32)
            banks = []
            for k in range(5):
                bank = P.psum("gc_pq%d" % k, [128, 512], F32)
                banks.append((bank.t, Tok("pq%d" % k)))
            rbank = Ring(banks)

            class _PS:
                def __init__(self):
                    self.cur = None
                    self.q = 0

                def next(self, newbank=True):
                    if newbank or self.cur is None or self.q >= 4:
                        self.cur = rbank.next()
                        self.q = 0
                    ap = self.cur[0][:, self.q * 128:(self.q + 1) * 128]
                    self.q += 1
                    return ap, self.cur[1]
            rps = _PS()
            pcb = P.psum("gc_pc", [128, 512], F32)
            rpc = Ring([(pcb.t[:, 0:4], Tok("pc0"))])
            pbb = [P.psum("gc_pb%d" % k, [128, 512], BF16) for k in range(2)]
            rpbank = Ring([(pbb[k].t, Tok("pb%d" % k)) for k in range(2)])

            class _PB:
                def next(self):
                    bk = rpbank.next()
                    return bk[0][:, 0:128], bk[1]
            rpb = _PB()
            qkv = g_qkvT.t.rearrange("(c p) t -> p c t", p=128)
            zv = g_z.t.rearrange("(t p) e -> p t e", p=128)
            ov = r_ogT.t.rearrange("(c p) t -> p c t", p=128)
            identb = self.identb
            order = [list(range(NT)), [1, 0] + list(range(NT - 1, 1, -1))]

            def pre_gen(d, hv, t, out):
                vtm = vtm2[hv % 2]
                sl = slice(t * 128, (t + 1) * 128)
                LU = UI if d == 0 else LI
                GMASK = LS if d == 0 else US
                mS = LS if d == 0 else US
                mTI = UI if d == 0 else LI
                gcol = gb[:, t, d * 64 + hv:d * 64 + hv + 1]
                bcol = gb[:, t, d * 64 + 32 + hv:d * 64 + 32 + hv + 1]
                Gm = rGm.next()
                P.ts(Gm[:, :], GMASK[:, :], gcol, None, ALU.mult, None, [GMASK, gb], [Gm])
                pgd, tgd = rps.next()
                pgdT, tgdT = rps.next(False)
                P.matmul(pgd, LU[:, :], Gm[:, :], True, True, [LU, Gm], [tgd])
                P.matmul(pgdT, Gm[:, :], LU[:, :], True, True, [LU, Gm], [tgdT])
                pc, tpc = rpc.next()
                gcol2 = gb[:, t, d * 64 + hv:d * 64 + hv + 2]
                P.matmul(pc[:, 0:2], LU[:, :], gcol2, True, True, [LU, gb], [tpc])
                P.matmul(pc[:, 2:4], self.ones[:, :], gcol2, True, True, [self.ones, gb], [tpc])
                cols = rcols.next()
                P.copy(cols[:, 0:2], pc[:, 0:4:2], [tpc], [cols], q="act")
                P.act(cols[:, 2:3], cols[:, 0:1], AF.Exp, [cols], [cols])
                P.act(cols[:, 3:4], cols[:, 0:1], AF.Exp, [cols], [cols], scale=-1.0, bias=cols[:, 1:2])
                P.act(cols[:, 4:5], cols[:, 1:2], AF.Exp, [cols], [cols])
                P.tt(cols[:, 5:6], cols[:, 2:3], bcol, ALU.mult, [cols, gb], [cols])
                yield
                Dec, DecT = rDec.next(), rDecT.next()
                P.act(Dec[:, :], pgd, AF.Exp, [tgd], [Dec])
                P.act(DecT[:, :], pgdT, AF.Exp, [tgdT], [DecT])
                P.tt(Dec[:, :], Dec[:, :], mS[:, :], ALU.mult, [Dec, mS], [Dec])
                P.tt(DecT[:, :], DecT[:, :], mTI[:, :], ALU.mult, [DecT, mTI], [DecT])
                yield
                pkk, tkk = rps.next()
                pkq, tkq = rps.next(False)
                P.matmul(pkk, kT[:, sl], kT[:, sl], True, True, [kT], [tkk])
                P.matmul(pkq, kT[:, sl], qT[:, sl], True, True, [kT, qT], [tkq])
                M = rM.next()
                AT = rAT.next()
                P.stt(M[:, :], pkk, bcol, Dec[:, :], ALU.mult, ALU.mult, [tkk, gb, Dec], [M])
                P.tt(AT[:, :], pkq, DecT[:, :], ALU.mult, [tkq, DecT], [AT])
                yield
                pN, tN = rpb.next()
                P.transpose(pN, M[:, :], identb[:, :], [M, identb], [tN])
                N = rN.next()
                P.copy(N[:, :], pN, [tN], [N], q="act")
                yield
                M0, N0 = rsq.next(), rsq.next()
                P.tt(M0[:, :], M[:, :], blk16[:, :], ALU.mult, [M, blk16], [M0])
                P.tt(N0[:, :], N[:, :], blk16[:, :], ALU.mult, [N, blk16], [N0])
                T_, R = rsq.next(), rsq.next()
                P.tt(T_[:, :], identb[:, :], M0[:, :], ALU.subtract, [identb, M0], [T_])
                P.tt(R[:, :], identb[:, :], N0[:, :], ALU.subtract, [identb, N0], [R])
                Pm, Qm = N0, M0
                for lvl in range(3):
                    pp, tp = rps.next()
                    P.matmul(pp, Qm[:, :], Pm[:, :], True, True, [Qm, Pm], [tp])
                    pq, tq = rps.next(False)
                    P.matmul(pq, Pm[:, :], Qm[:, :], True, True, [Qm, Pm], [tq])
                    P2, Q2 = rsq.next(), rsq.next()
                    P.copy(P2[:, :], pp, [tp], [P2], q="act")
                    P.copy(Q2[:, :], pq, [tq], [Q2], q="act")
                    yield
                    pt_, tt_ = rps.next()
                    P.matmul(pt_, P2[:, :], T_[:, :], True, True, [P2, T_], [tt_])
                    prr, trr = rps.next(False)
                    P.matmul(prr, T_[:, :], P2[:, :], True, True, [P2, T_], [trr])
                    Tn, Rn = rsq.next(), rsq.next()
                    P.tt(Tn[:, :], pt_, T_[:, :], ALU.add, [tt_, T_], [Tn])
                    P.tt(Rn[:, :], prr, R[:, :], ALU.add, [trr, R], [Rn])
                    T_, R, Pm, Qm = Tn, Rn, P2, Q2
                    yield
                for li, msk in enumerate(lvlm):
                    last = li == len(lvlm) - 1
                    B1, B1T = rsq.next(), rsq.next()
                    P.tt(B1[:, :], M[:, :], msk[:, :], ALU.mult, [M, msk], [B1])
                    P.tt(B1T[:, :], N[:, :], msk[:, :], ALU.mult, [N, msk], [B1T])
                    px, tx = rps.next()
                    P.matmul(px, B1[:, :], R[:, :], True, True, [B1, R], [tx])
                    if not last:
                        px2, tx2 = rps.next(False)
                        P.matmul(px2, B1T[:, :], T_[:, :], True, True, [B1T, T_], [tx2])
                    Xp = rsq.next()
                    P.copy(Xp[:, :], px, [tx], [Xp], q="act")
                    if not last:
                        X = rsq.next()
                        P.copy(X[:, :], px2, [tx2], [X], q="act")
                    yield
                    pr, tr = rps.next()
                    P.matmul(pr, T_[:, :], Xp[:, :], True, True, [T_, Xp], [tr])
                    if not last:
                        pt_, tt_ = rps.next(False)
                        P.matmul(pt_, R[:, :], X[:, :], True, True, [R, X], [tt_])
                    Rn = rsq.next()
                    P.tt(Rn[:, :], R[:, :], pr, ALU.subtract, [tr, R], [Rn])
                    if not last:
                        Tn = rsq.next()
                        P.tt(Tn[:, :], T_[:, :], pt_, ALU.subtract, [tt_, T_], [Tn])
                        T_ = Tn
                    R = Rn
                    yield
                RHSu, RHSw, kdec = rsq.next(), rsq.next(), rkd.next()
                P.ts(RHSu[:, :], vtm[:, t, :], bcol, None, ALU.mult, None, [vtm, gb], [RHSu])
                P.ts(RHSw[:, :], ktm[:, t, :], cols[:, 5:6], None, ALU.mult, None, [ktm, cols], [RHSw])
                P.ts(kdec[:, :], ktm[:, t, :], cols[:, 3:4], None, ALU.mult, None, [ktm, cols], [kdec])
                pU, tU = rps.next()
                pW, tW = rps.next(False)
                P.matmul(pU, R[:, :], RHSu[:, :], True, True, [R, RHSu], [tU])
                P.matmul(pW, RHSw[:, :], R[:, :], True, True, [R, RHSw], [tW])
                U = rU.next()
                WT = rWT.next()
                P.copy(U[:, :], pU, [tU], [U], q="act")
                P.copy(WT[:, :], pW, [tW], [WT], q="act")
                out.update(dict(U=U, WT=WT, AT=AT, kdec=kdec, cols=cols, t=t))

            def seq(d, hv, pr_, first, first_dir):
                t = pr_["t"]
                sl = slice(t * 128, (t + 1) * 128)
                S, Sb = S32[(hv % 2) * 2 + d], S16[(hv % 2) * 2 + d]
                oacc = oacc2[hv % 2]
                cols = pr_["cols"]
                vnew = rsq.next()
                if first:
                    P.copy(vnew[:, :], pr_["U"][:, :], [pr_["U"]], [vnew])
                else:
                    pws, tws = rps.next()
                    P.matmul(pws, pr_["WT"][:, :], Sb[:, :], True, True, [pr_["WT"], Sb], [tws])
                    P.tt(vnew[:, :], pr_["U"][:, :], pws, ALU.subtract, [pr_["U"], tws], [vnew])
                po2, to2 = rps.next()
                P.matmul(po2, pr_["AT"][:, :], vnew[:, :], True, True, [pr_["AT"], vnew], [to2])
                if not first:
                    po1, to1 = rps.next(False)
                    P.matmul(po1, qT[:, sl], Sb[:, :], True, True, [qT, Sb], [to1])
                    tm = rtmp.next()
                    P.act(tm[:, :], po1, AF.Copy, [to1, cols], [tm], scale=cols[:, 2:3])
                    P.tt(tm[:, :], po2, tm[:, :], ALU.add, [to2, tm], [tm])
                    P.tt(oacc[:, t, :], oacc[:, t, :], tm[:, :], ALU.add, [oacc, tm], [oacc])
                else:
                    P.tt(oacc[:, t, :], oacc[:, t, :], po2, ALU.add, [oacc, to2], [oacc])
                pS, tS = rps.next()
                P.matmul(pS, pr_["kdec"][:, :], vnew[:, :], True, True, [pr_["kdec"], vnew], [tS])
                if first:
                    P.copy(S[:, :], pS, [tS], [S])
                else:
                    P.stt(S[:, :], S[:, :], cols[:, 4:5], pS, ALU.mult, ALU.add, [S, cols, tS], [S])
                P.copy(Sb[:, :], S[:, :], [S], [Sb], q="act")

            for hv0 in range(0, self.cfg.get("gdn_heads", 32), 2):
                hq = hv0 // 2
                P.dma(qT[:, :], qkv[:, hq, :], [g_qkvT], [qT])
                P.dma(kT[:, :], qkv[:, 16 + hq, :], [g_qkvT], [kT])
                for t0 in range(0, NT, 4):
                    bk, tk_ = rpbank.next()
                    nt = min(4, NT - t0)
                    for q4 in range(nt):
                        t = t0 + q4
                        P.transpose(bk[:, q4 * 128:(q4 + 1) * 128], kT[:, t * 128:(t + 1) * 128], identb[:, :],
                                    [kT, identb], [tk_])
                    P.copy(ktm[:, t0:t0 + nt, :], bk[:, 0:nt * 128].rearrange("p (a b) -> p a b", b=128), [tk_], [ktm],
                           q="act")
                for hb in range(2):
                    hv = hv0 + hb
                    vT, zh, vtm = vT2[hb], zh2[hb], vtm2[hb]
                    P.dma(vT[:, :], qkv[:, 32 + hv, :], [g_qkvT], [vT])
                    P.dma(zh[:, :, :], zv[:, :, hv * 128:(hv + 1) * 128], [g_z], [zh])
                    for t0 in range(0, NT, 4):
                        bk, tk_ = rpbank.next()
                        nt = min(4, NT - t0)
                        for q4 in range(nt):
                            t = t0 + q4
                            P.transpose(bk[:, q4 * 128:(q4 + 1) * 128], vT[:, t * 128:(t + 1) * 128], identb[:, :],
                                        [vT, identb], [tk_])
                        P.copy(vtm[:, t0:t0 + nt, :], bk[:, 0:nt * 128].rearrange("p (a b) -> p a b", b=128), [tk_], [vtm],
                               q="act")
                    P.memset(oacc2[hb][:, :, :], 0.0, [oacc2[hb]])

                def run_quad(n):
                    outs = [dict() for _ in range(4)]
                    gens = [pre_gen(k % 2, hv0 + k // 2, order[k % 2][n], outs[k]) for k in range(4)]
                    alive = [True] * 4
                    while any(alive):
                        for k in range(4):
                            if alive[k]:
                                try:
                                    next(gens[k])
                                except StopIteration:
                                    alive[k] = False
                    return outs
                nxt = run_quad(0)
                for n in range(NT):
                    cur = nxt
                    if n + 1 < NT:
                        nxt = run_quad(n + 1)
                    for k in range(4):
                        seq(k % 2, hv0 + k // 2, cur[k], n == 0, False)
                for hb in range(2):
                    hv = hv0 + hb
                    oacc, zh, ogT = oacc2[hb], zh2[hb], ogT2[hb]
                    for t in range(NT):
                        S_ = st.next()
                        o = oacc[:, t, :]
                        P.memset(S_[:, 0:1], 0.0, [S_])
                        P.act(junk[:, :], o, AF.Square, [oacc], [junk, S_], accum_out=S_[:, 0:1])
                        P.ts(S_[:, 1:2], S_[:, 0:1], 1.0 / 128, EPS, ALU.mult, ALU.add, [S_], [S_])
                        P.act(S_[:, 2:3], S_[:, 1:2], AF.Sqrt, [S_], [S_])
                        P.recip(S_[:, 3:4], S_[:, 2:3], [S_], [S_])
                        tm = rtmp.next()
                        P.stt(tm[:, :], o, S_[:, 3:4], gnw[:, :], ALU.mult, ALU.mult, [oacc, S_, gnw], [tm])
                        og = rsq.next()
                        P.tt(og[:, :], tm[:, :], zh[:, t, :], ALU.mult, [tm, zh], [og])
                        pb_, tb_ = rpb.next()
                        P.transpose(pb_, og[:, :], identb[:, :], [og, identb], [tb_])
                        P.copy(ogT[:, t * 128:(t + 1) * 128], pb_, [tb_], [ogT], q="act")
                    P.dma(ov[:, hv, :], ogT[:, :], [ogT], [r_ogT])
    def ret_proj(self, i, j, b):
        P = self.P
        hT_d = self.scratch("hT_d", [128, 16, TOK], BF16)
        r_qT = self.scratch("r_qT", [D, TOK], BF16)
        r_kT = self.scratch("r_kT", [D, TOK], BF16)
        r_v = self.scratch("r_v", [TOK, 4096], BF16)
        r_gate = self.scratch("r_gate", [TOK, 4096], BF16)
        win = self.W["ret_w_in"].t[j].rearrange("(c p) n -> p c n", p=128)
        blocks = [(0, 256)] + [(256 + 512 * k, 512) for k in range(4)]
        with P.phase():
            hT = P.sbuf("rp_hT", [128, 16, TOK], BF16)
            P.dma(hT[:, :, :], hT_d[:, :, :], [hT_d], [hT])
            rope = P.sbuf("rp_rope", [128, 4, SEQ], F32)
            P.dma(rope[:, :, :], self.C["rope"][:, :, :], [], [rope])
            rrf = P.sbuf("rp_rrf", [128, 128], F32)
            rrb = P.sbuf("rp_rrb", [128, 128], BF16)
            P.dma(rrf[:, :], self.C["rrotT"][:, :], [], [rrf])
            P.copy(rrb[:, :], rrf[:, :], [rrf], [rrb])
            wp = [P.sbuf("rp_wp%d" % k, [128, 16, 512], BF16) for k in range(2)]
            qraw = [P.sbuf("rp_qraw%d" % k, [128, 512], BF16) for k in range(2)]
            T1 = [P.sbuf("rp_t1%d" % k, [128, 512], F32) for k in range(2)]
            T2 = [P.sbuf("rp_t2%d" % k, [128, 512], F32) for k in range(2)]
            qst = [P.sbuf("rp_qst%d" % k, [128, TOK], BF16) for k in range(2)]
            psA = [P.psum("rp_psA%d" % k, [128, 512], F32) for k in range(4)]
            psR = [P.psum("rp_psR%d" % k, [128, 512], F32) for k in range(2)]
            cnt = 0
            for pn in range(8):
                Wp = wp[pn % 2]
                P.dma(Wp[:, :, :], win[:, :, pn * 512:(pn + 1) * 512], [], [Wp], q="pool")
                for nb in range(4):
                    chunk = pn * 4 + nb
                    isk = chunk >= 16
                    typ = chunk % 2
                    sc = 0.0625 if isk else 1.0
                    QS = qst[chunk % 2]
                    for bi, (t0, n) in enumerate(blocks):
                        ps = psA[cnt % 4]
                        for k in range(16):
                            P.matmul(ps[:, 0:n], Wp[:, k, nb * 128:(nb + 1) * 128], hT[:, k, t0:t0 + n], k == 0, k == 15,
                                     [Wp, hT], [ps])
                        if bi == 0:
                            P.act(QS[:, 0:256], ps[:, 0:256], AF.Copy, [ps], [QS], scale=sc)
                        else:
                            QR = qraw[cnt % 2]
                            P.act(QR[:, :], ps[:, :], AF.Copy, [ps], [QR], scale=sc)
                            pr = psR[cnt % 2]
                            P.matmul(pr[:, :], rrb[:, :], QR[:, :], True, True, [rrb, QR], [pr])
                            x0 = t0 - 256
                            P.tt(T1[cnt % 2][:, :], QR[:, :], rope[:, 2 * typ, x0:x0 + 512], ALU.mult, [QR, rope], [T1[cnt % 2]])
                            P.tt(T2[cnt % 2][:, :], pr[:, :], rope[:, 2 * typ + 1, x0:x0 + 512], ALU.mult, [pr, rope],
                                 [T2[cnt % 2]])
                            P.tt(QS[:, t0:t0 + 512], T1[cnt % 2][:, :], T2[cnt % 2][:, :], ALU.add,
                                 [T1[cnt % 2], T2[cnt % 2]], [QS])
                        cnt += 1
                    dst = r_kT if isk else r_qT
                    row0 = (chunk % 16) * 128
                    P.dma(dst[row0:row0 + 128, :], QS[:, :], [QS], [dst])
        with P.phase():
            hT = P.sbuf("rv_hT", [128, 16, TOK], BF16)
            P.dma(hT[:, :, :], hT_d[:, :, :], [hT_d], [hT])
            wp = [P.sbuf("rv_wp%d" % k, [128, 16, 512], BF16) for k in range(2)]
            stage = [P.sbuf("rv_st%d" % k, [128, NT, 512], BF16) for k in range(2)]
            psA = [P.psum("rv_psA%d" % k, [128, 512], F32) for k in range(4)]
            cnt = 0
            for pn in range(16):
                Wp = wp[pn % 2]
                ST = stage[pn % 2]
                c0 = 4096 + pn * 512
                P.dma(Wp[:, :, :], win[:, :, c0:c0 + 512], [], [Wp], q="pool")
                for t in range(NT):
                    ps = psA[cnt % 4]
                    cnt += 1
                    for k in range(16):
                        P.matmul(ps[:, :], hT[:, k, t * 128:(t + 1) * 128], Wp[:, k, :], k == 0, k == 15, [hT, Wp], [ps])
                    P.act(ST[:, t, :], ps[:, :], AF.Copy if pn < 8 else AF.Silu, [ps], [ST])
                dst = r_v if pn < 8 else r_gate
                cc = (pn % 8) * 512
                P.dma(dst.t.rearrange("(t p) e -> p t e", p=128)[:, :, cc:cc + 512], ST[:, :, :], [ST], [dst])

    def ret_core(self, i, j, b):
        P = self.P
        r_qT = self.scratch("r_qT", [D, TOK], BF16)
        r_kT = self.scratch("r_kT", [D, TOK], BF16)
        r_v = self.scratch("r_v", [TOK, 4096], BF16)
        r_gate = self.scratch("r_gate", [TOK, 4096], BF16)
        r_ogT = self.scratch("r_ogT", [4096, TOK], BF16)
        with P.phase():
            def cload(nm, shape):
                t = P.sbuf("rc_" + nm, shape, F32)
                P.dma(t[:, :], self.C[nm][:, :], [], [t])
                return t
            relF = cload("relF", [128, 128])
            maskF = cload("maskF", [128, 128])
            relB = cload("relB", [128, 128])
            maskB = cload("maskB", [128, 128])
            posrowF = cload("posrowF", [128, 128])
            posrowB = cload("posrowB", [128, 128])
            poscol = cload("poscol", [128, 4])
            lg = P.sbuf("rc_lg", [128, 16], F32)
            P.dma(lg[:, :], self.W["ret_log_decay"].t[j].rearrange("d h -> (d h)").partition_broadcast(128), [], [lg])
            gnw = P.sbuf("rc_gnw", [128, 512], F32)
            P.dma(gnw[:, :], self.W["ret_gn_w"].t[j].partition_broadcast(128), [], [gnw])
            E1 = P.sbuf("rc_E1", [128, 128], F32)
            E2 = P.sbuf("rc_E2", [128, 128], F32)
            DT = P.sbuf("rc_DT", [128, 128], F32)
            TFq = P.sbuf("rc_TFq", [128, 128], F32)
            TBq = P.sbuf("rc_TBq", [128, 128], F32)
            cols = P.sbuf("rc_cols", [128, 4], F32)
            qT = P.sbuf("rc_qT", [128, 2, TOK], BF16)
            kT = P.sbuf("rc_kT", [128, 2, TOK], BF16)
            vh = P.sbuf("rc_v", [128, NT, 512], BF16)
            gh = P.sbuf("rc_g", [128, NT, 512], BF16)
            qdf = P.sbuf("rc_qdf", [128, 2, TOK], BF16)
            qdb = P.sbuf("rc_qdb", [128, 2, TOK], BF16)
            kdb_all = P.sbuf("rc_kdb", [128, NT, 256], BF16)
            kdf = [P.sbuf("rc_kdf%d" % k, [128, 256], BF16) for k in range(2)]
            ATs = [P.sbuf("rc_AT%d" % k, [128, 128], BF16) for k in range(2)]
            oacc = P.sbuf("rc_oacc", [128, NT, 512], F32)
            S32 = [P.sbuf("rc_S32_%d" % k, [128, 2, 512], F32) for k in range(2)]
            S16 = [P.sbuf("rc_S16_%d" % k, [128, 2, 512], BF16) for k in range(2)]
            ogT = P.sbuf("rc_ogT", [128, 4, TOK], BF16)
            ogt = [P.sbuf("rc_ogt%d" % k, [128, 512], BF16) for k in range(2)]
            tmp = [P.sbuf("rc_tmp%d" % k, [128, 512], F32) for k in range(2)]
            junk = P.sbuf("rc_junk", [128, 512], F32)
            st = [P.sbuf("rc_st%d" % k, [128, 8], F32) for k in range(2)]
            psAT = [P.psum("rc_psAT%d" % k, [128, 512], F32) for k in range(2)]
            psK = [P.psum("rc_psK%d" % k, [128, 512], BF16) for k in range(2)]
            psO = [P.psum("rc_psO%d" % k, [128, 512], F32) for k in range(2)]
            psS = [P.psum("rc_psS%d" % k, [128, 512], F32) for k in range(2)]
            order_f = list(range(NT))
            order_b = [1, 0] + list(range(NT - 1, 1, -1))
            qv = r_qT.t.rearrange("(c p) t -> p c t", p=128)
            kv = r_kT.t.rearrange("(c p) t -> p c t", p=128)
            vv = r_v.t.rearrange("(t p) e -> p t e", p=128)
            gv = r_gate.t.rearrange("(t p) e -> p t e", p=128)
            ov = r_ogT.t.rearrange("(c p) t -> p c t", p=128)
            for h in range(8):
                lgf = lg[:, h:h + 1]
                lgb = lg[:, 8 + h:9 + h]
                P.dma(qT[:, :, :], qv[:, 2 * h:2 * h + 2, :], [r_qT], [qT])
                P.dma(kT[:, :, :], kv[:, 2 * h:2 * h + 2, :], [r_kT], [kT])
                P.dma(vh[:, :, :], vv[:, :, h * 512:(h + 1) * 512], [r_v], [vh])
                P.dma(gh[:, :, :], gv[:, :, h * 512:(h + 1) * 512], [r_gate], [gh])
                P.act(E1[:, :], relF[:, :], AF.Exp, [relF, lg], [E1], scale=lgf)
                P.tt(E1[:, :], E1[:, :], maskF[:, :], ALU.mult, [E1, maskF], [E1])
                P.act(E2[:, :], relB[:, :], AF.Exp, [relB, lg], [E2], scale=lgb)
                P.tt(E2[:, :], E2[:, :], maskB[:, :], ALU.mult, [E2, maskB], [E2])
                P.tt(DT[:, :], E1[:, :], E2[:, :], ALU.add, [E1, E2], [DT])
                P.act(TFq[:, :], posrowF[:, :], AF.Exp, [posrowF, lg], [TFq], scale=lgf)
                P.act(TBq[:, :], posrowB[:, :], AF.Exp, [posrowB, lg], [TBq], scale=lgb)
                P.act(cols[:, 0:1], poscol[:, 0:1], AF.Exp, [poscol, lg], [cols], scale=lgf)
                P.act(cols[:, 1:2], poscol[:, 1:2], AF.Exp, [poscol, lg], [cols], scale=lgb)
                P.act(cols[:, 2:3], lgf, AF.Exp, [lg], [cols], scale=128.0)
                P.act(cols[:, 3:4], lgb, AF.Exp, [lg], [cols], scale=128.0)
                for dc in range(2):
                    P.tt(qdf[:, dc, :].rearrange("p (t i) -> p t i", i=128), qT[:, dc, :].rearrange("p (t i) -> p t i", i=128),
                         TFq[:, :].unsqueeze(1).broadcast_to([128, NT, 128]), ALU.mult, [qT, TFq], [qdf])
                    P.tt(qdb[:, dc, :].rearrange("p (t i) -> p t i", i=128), qT[:, dc, :].rearrange("p (t i) -> p t i", i=128),
                         TBq[:, :].unsqueeze(1).broadcast_to([128, NT, 128]), ALU.mult, [qT, TBq], [qdb])
                Sf, Sfb = S32[0], S16[0]
                for n, t in enumerate(order_f):
                    sl = slice(t * 128, (t + 1) * 128)
                    pa = psAT[n % 2]
                    for dc in range(2):
                        P.matmul(pa[:, 0:128], kT[:, dc, sl], qT[:, dc, sl], dc == 0, dc == 1, [kT, qT], [pa])
                    AT = ATs[n % 2]
                    P.tt(AT[:, :], pa[:, 0:128], DT[:, :], ALU.mult, [pa, DT], [AT])
                    pk = psK[n % 2]
                    for dc in range(2):
                        P.transpose(pk[:, dc * 128:(dc + 1) * 128], kT[:, dc, sl], self.identb[:, :], [kT, self.identb], [pk])
                    KF = kdf[n % 2]
                    P.act(KF[:, :], pk[:, 0:256], AF.Copy, [pk, cols], [KF], scale=cols[:, 0:1])
                    P.ts(kdb_all[:, t, :], pk[:, 0:256], cols[:, 1:2], None, ALU.mult, None, [pk, cols], [kdb_all])
                    po = psO[n % 2]
                    P.matmul(po[:, :], AT[:, :], vh[:, t, :], True, n == 0, [AT, vh], [po])
                    if n > 0:
                        for dc in range(2):
                            P.matmul(po[:, :], qdf[:, dc, sl], Sfb[:, dc, :], False, dc == 1, [qdf, Sfb], [po])
                    P.copy(oacc[:, t, :], po[:, :], [po], [oacc], q="act")
                    for dc in range(2):
                        pS = psS[dc]
                        P.matmul(pS[:, :], KF[:, dc * 128:(dc + 1) * 128], vh[:, t, :], True, True, [KF, vh], [pS])
                        if n == 0:
                            P.copy(Sf[:, dc, :], pS[:, :], [pS], [Sf])
                        else:
                            P.stt(Sf[:, dc, :], Sf[:, dc, :], cols[:, 2:3], pS[:, :], ALU.mult, ALU.add, [Sf, cols, pS], [Sf])
                    if n < NT - 1:
                        P.copy(Sfb[:, :, :], Sf[:, :, :], [Sf], [Sfb], q="act")
                Sb, Sbb = S32[1], S16[1]
                for n, t in enumerate(order_b):
                    sl = slice(t * 128, (t + 1) * 128)
                    if n > 0:
                        po = psO[n % 2]
                        for dc in range(2):
                            P.matmul(po[:, :], qdb[:, dc, sl], Sbb[:, dc, :], dc == 0, dc == 1, [qdb, Sbb], [po])
                        P.tt(oacc[:, t, :], oacc[:, t, :], po[:, :], ALU.add, [oacc, po], [oacc])
                    for dc in range(2):
                        pS = psS[dc]
                        P.matmul(pS[:, :], kdb_all[:, t, dc * 128:(dc + 1) * 128], vh[:, t, :], True, True, [kdb_all, vh], [pS])
                        if n == 0:
                            P.copy(Sb[:, dc, :], pS[:, :], [pS], [Sb])
                        else:
                            P.stt(Sb[:, dc, :], Sb[:, dc, :], cols[:, 3:4], pS[:, :], ALU.mult, ALU.add, [Sb, cols, pS], [Sb])
                    if n < NT - 1:
                        P.copy(Sbb[:, :, :], Sb[:, :, :], [Sb], [Sbb], q="act")
                for t in range(NT):
                    S = st[t % 2]
                    o = oacc[:, t, :]
                    P.memset(S[:, 0:2], 0.0, [S])
                    P.act(junk[:, :], o, AF.Identity, [oacc], [junk, S], accum_out=S[:, 0:1])
                    P.act(junk[:, :], o, AF.Square, [oacc], [junk, S], accum_out=S[:, 1:2])
                    P.ts(S[:, 2:3], S[:, 0:1], 1.0 / 512, None, ALU.mult, None, [S], [S])
                    P.tt(S[:, 3:4], S[:, 2:3], S[:, 2:3], ALU.mult, [S], [S])
                    P.stt(S[:, 4:5], S[:, 1:2], 1.0 / 512, S[:, 3:4], ALU.mult, ALU.subtract, [S], [S])
                    P.ts(S[:, 4:5], S[:, 4:5], EPS, None, ALU.add, None, [S], [S])
                    P.act(S[:, 5:6], S[:, 4:5], AF.Sqrt, [S], [S])
                    P.recip(S[:, 6:7], S[:, 5:6], [S], [S])
                    TM = tmp[t % 2]
                    P.ts(TM[:, :], o, S[:, 2:3], S[:, 6:7], ALU.subtract, ALU.mult, [oacc, S], [TM])
                    P.tt(TM[:, :], TM[:, :], gnw[:, :], ALU.mult, [TM, gnw], [TM])
                    OG = ogt[t % 2]
                    P.tt(OG[:, :], TM[:, :], gh[:, t, :], ALU.mult, [TM, gh], [OG])
                    pk = psK[t % 2]
                    for ec in range(4):
                        P.transpose(pk[:, ec * 128:(ec + 1) * 128], OG[:, ec * 128:(ec + 1) * 128], self.identb[:, :],
                                    [OG, self.identb], [pk])
                    P.copy(ogT[:, :, t * 128:(t + 1) * 128], pk[:, :].rearrange("p (e i) -> p e i", e=4), [pk], [ogT],
                           q="act")
                P.dma(ov[:, 4 * h:4 * h + 4, :], ogT[:, :, :], [ogT], [r_ogT])


def build(cfg):
    nc = bass.Bass("TRN2", target_bir_lowering=False)
    with contextlib.ExitStack() as stack:
        M = Model(nc, stack, cfg)
        P = M.P
        M.copy_inputs()
        if cfg.get("only_gdn_core"):
            M.gdn_core(1, 0, 0)
            P.barrier()
            P.flush()
            return nc
        M.modulation()
        for i in cfg.get("layers", range(DEPTH)):
            if "mixer" in cfg.get("phases", ["mixer", "ffn"]):
                M.mixer(i)
            if "ffn" in cfg.get("phases", ["mixer", "ffn"]):
                M.ffn(i)
        if cfg.get("final", True):
            M.final_norm()
        for nm in cfg.get("dump", []):
            src = M._scr[nm]
            shp = list(src.t.shape)
            dt = src.t.dtype
            do = P.dram("dump_" + nm, shp, dt, kind="ExternalOutput")
            if len(shp) == 2:
                P.dma(do[:, :], src[:, :], [src], [do])
            else:
                P.dma(do[:, :, :], src[:, :, :], [src], [do])
        if cfg.get("debug_ctx"):
            co = P.dram("ctx_out", [BPC, CTXL, D], F32, kind="ExternalOutput")
            for b in range(BPC):
                P.dma(co[b, :, :], M.ctxs[b, :, :], [M.rtok[b][0], M.rtok[b][1]], [co])
        P.barrier()
        P.flush()
        print("ops", P.nops, "blocks", P.nblocks)
    return nc


def make_in_maps(inputs, ncores=NCORES):
    consts = host_constants()
    maps = []
    for r in range(ncores):
        m = {}
        m["x"] = np.ascontiguousarray(inputs["x"][r * BPC:(r + 1) * BPC])
        m["ctx"] = np.ascontiguousarray(inputs["ctx"][r * BPC:(r + 1) * BPC])
        m["cvec"] = np.ascontiguousarray(
            np.concatenate([inputs["c"][r * BPC:(r + 1) * BPC], inputs["c_ctx"][None, :]], axis=0))
        for nm in WEIGHT_NAMES:
            m[nm] = np.ascontiguousarray(inputs[nm])
        for nm, arr in consts.items():
            m["k_" + nm] = arr
        maps.append(m)
    return maps


GROUP = NCORES


def kernel(**inputs):
    inputs = {k: np.asarray(v) for k, v in inputs.items()}
    nc = build({})
    maps = make_in_maps(inputs)
    outs = []
    for g0 in range(0, NCORES, GROUP):
        res = run_bass_kernel_spmd(nc, maps[g0:g0 + GROUP], core_ids=list(range(GROUP)))
        outs += [res.results[r]["out"] for r in range(GROUP)]
    return np.concatenate(outs, axis=0).astype(np.float32)
```
